# Optimizing a Trainium2 kernel written in Bass

```python
import jax
import jax.numpy as jnp
from jax import lax
import numpy as np

D_MODEL = 1024
BATCH = 8
SEQ = 4096
DEPTH = 2

CTX_LEN = 256
GRID_W = 64
NORM_EPS = 1e-6

HEAD_DIM = 64
MIX_WIDTH = D_MODEL
NA_WIDTH = MIX_WIDTH // 2
NA_HEADS = NA_WIDTH // HEAD_DIM
NA_WIN_H_MAX = 8
NA_WIN_W = 16
GLA_WIDTH = MIX_WIDTH // 4
GLA_HEADS = GLA_WIDTH // HEAD_DIM
GLA_HEAD_V = HEAD_DIM
GLA_HEAD_K = HEAD_DIM // 2
GLA_KEY = GLA_HEADS * GLA_HEAD_K
GLA_GK_RANK = 16
GLA_GATE_NORM = 16.0
HGRN_WIDTH = MIX_WIDTH - NA_WIDTH - GLA_WIDTH
HGRN_HEADS = HGRN_WIDTH // HEAD_DIM
HGRN_HEAD_DIM = HEAD_DIM

CHUNK = 16
ROPE_BASE = 10000.0

IN_SPLITS = (NA_WIDTH, NA_WIDTH, NA_WIDTH, NA_WIDTH,
             GLA_KEY, GLA_KEY, GLA_WIDTH, 2 * GLA_GK_RANK, GLA_WIDTH,
             HGRN_WIDTH, HGRN_WIDTH, HGRN_WIDTH, HGRN_WIDTH, HGRN_WIDTH)
IN_WIDTH = sum(IN_SPLITS)

kernel_name = 'hybrid_na_gla_hgrn2_prefix_dit'


def rmsnorm(x, w):
    xf = x.astype(jnp.float32)
    y = xf * lax.rsqrt(jnp.mean(xf * xf, axis=-1, keepdims=True) + NORM_EPS)
    return (y * w.astype(jnp.float32)).astype(x.dtype)


def split_columns(p):
    parts, start = [], 0
    for width in IN_SPLITS:
        parts.append(p[..., start:start + width])
        start += width
    return parts


def axial_rope(x):
    n_tok, dk = x.shape[1], x.shape[-1]
    half = dk // 2
    t = jnp.arange(n_tok)
    inv_freq = 1.0 / (ROPE_BASE ** (jnp.arange(0, half, 2, dtype=jnp.float32) / half))

    def rot(xa, pos):
        ang = pos.astype(jnp.float32)[:, None] * inv_freq[None, :]
        cos = jnp.cos(ang)[None, :, None, :]
        sin = jnp.sin(ang)[None, :, None, :]
        x1, x2 = jnp.split(xa.astype(jnp.float32), 2, axis=-1)
        return jnp.concatenate([x1 * cos - x2 * sin, x1 * sin + x2 * cos], axis=-1)

    x_row, x_col = jnp.split(x, 2, axis=-1)
    out = jnp.concatenate([rot(x_row, t // GRID_W), rot(x_col, t % GRID_W)], axis=-1)
    return out.astype(x.dtype)


def chunk_gated_scan(q, k, v, log_f, s0):
    b, h, t, dk = q.shape
    dv = v.shape[-1]
    n = t // CHUNK
    f32 = jnp.float32
    qc = q.reshape(b, h, n, CHUNK, dk).astype(f32)
    kc = k.reshape(b, h, n, CHUNK, dk).astype(f32)
    vc = v.reshape(b, h, n, CHUNK, dv).astype(f32)
    G = jnp.cumsum(log_f.reshape(b, h, n, CHUNK, dk).astype(f32), axis=3)
    g_ref = G[:, :, :, CHUNK // 2 - 1:CHUNK // 2, :]
    g_last = G[:, :, :, -1:, :]
    scores = jnp.einsum('bhnid,bhnjd->bhnij', qc * jnp.exp(G - g_ref), kc * jnp.exp(g_ref - G))
    tri = jnp.tril(jnp.ones((CHUNK, CHUNK), dtype=bool))
    scores = jnp.where(tri, scores, 0.0)
    o_intra = jnp.einsum('bhnij,bhnjv->bhniv', scores, vc)
    q_in = jnp.moveaxis(qc * jnp.exp(G), 2, 0)
    k_out = jnp.moveaxis(kc * jnp.exp(g_last - G), 2, 0)
    v_s = jnp.moveaxis(vc, 2, 0)
    decay = jnp.moveaxis(jnp.exp(g_last[:, :, :, 0, :]), 2, 0)

    def step(state, xs):
        q_i, k_i, v_i, d_i = xs
        o_i = jnp.einsum('bhcd,bhdv->bhcv', q_i, state)
        state = d_i[..., None] * state + jnp.einsum('bhcd,bhcv->bhdv', k_i, v_i)
        return state, o_i

    s_final, o_inter = lax.scan(step, s0.astype(f32), (q_in, k_out, v_s, decay))
    o = o_intra + jnp.moveaxis(o_inter, 0, 2)
    return o.reshape(b, h, t, dv).astype(v.dtype), s_final


def bidirectional_prefix_scan(q_c, k_c, v_c, lf_c, q_x, k_x, v_x, lf_x):
    b, h, _, dk = q_c.shape
    dv = v_c.shape[-1]
    s0 = jnp.zeros((b, h, dk, dv), jnp.float32)
    fl = lambda a: jnp.flip(a, axis=2)
    oc_f, sc_f = chunk_gated_scan(q_c, k_c[0], v_c, lf_c[0], s0)
    ox_f, _ = chunk_gated_scan(q_x, k_x[0], v_x, lf_x[0], sc_f)
    oc_b, sc_b = chunk_gated_scan(fl(q_c), fl(k_c[1]), fl(v_c), fl(lf_c[1]), s0)
    ox_b, _ = chunk_gated_scan(fl(q_x), fl(k_x[1]), fl(v_x), fl(lf_x[1]), sc_b)
    return oc_f + fl(oc_b), ox_f + fl(ox_b)


def head_norm_gate(o, w, g):
    o = rmsnorm(jnp.swapaxes(o, 1, 2), w)
    b, t, h, dv = o.shape
    return o.reshape(b, t, h * dv) * jax.nn.silu(g)


def neighborhood_attention(q, k, v, k_ctx, v_ctx, rpb):
    b, t, h, dh = q.shape
    rows = t // GRID_W
    win_h = min(NA_WIN_H_MAX, rows)
    scale = dh ** -0.5
    grid = lambda a: a.reshape(b, rows, GRID_W, h, dh)
    qg, kg, vg = grid(q), grid(k), grid(v)
    col = jnp.arange(GRID_W)
    col_start = jnp.clip(col - NA_WIN_W // 2, 0, GRID_W - NA_WIN_W)
    col_in = (col[None, :] >= col_start[:, None]) & (col[None, :] < col_start[:, None] + NA_WIN_W)
    col_idx = jnp.clip(col[None, :] - col[:, None], 1 - NA_WIN_W, NA_WIN_W - 1) + NA_WIN_W - 1
    rpb_cols = rpb[:, :, col_idx].astype(jnp.float32)
    n_nb = win_h * GRID_W

    def row_block(r):
        row_start = jnp.clip(r - win_h // 2, 0, rows - win_h)
        q_r = lax.dynamic_index_in_dim(qg, r, axis=1, keepdims=False)
        k_r = lax.dynamic_slice_in_dim(kg, row_start, win_h, axis=1)
        v_r = lax.dynamic_slice_in_dim(vg, row_start, win_h, axis=1)
        row_idx = row_start + jnp.arange(win_h) - r + NA_WIN_H_MAX - 1
        bias = jnp.transpose(rpb_cols[:, row_idx], (0, 2, 1, 3))
        s_nb = jnp.einsum('bqhd,brkhd->bhqrk', q_r, k_r).astype(jnp.float32) * scale + bias[None]
        s_nb = jnp.where(col_in[None, None, :, None, :], s_nb, -jnp.inf)
        s_cx = jnp.einsum('bqhd,blhd->bhql', q_r, k_ctx).astype(jnp.float32) * scale
        s = jnp.concatenate([s_nb.reshape(b, h, GRID_W, n_nb), s_cx], axis=-1)
        p = jax.nn.softmax(s, axis=-1).astype(v.dtype)
        p_nb = p[..., :n_nb].reshape(b, h, GRID_W, win_h, GRID_W)
        return (jnp.einsum('bhqrk,brkhd->bqhd', p_nb, v_r)
                + jnp.einsum('bhql,blhd->bqhd', p[..., n_nb:], v_ctx))

    out = lax.map(row_block, jnp.arange(rows))
    return jnp.moveaxis(out, 0, 1).reshape(b, t, h * dh)


def context_attention(q, k, v):
    b, l, h, dh = q.shape
    s = jnp.einsum('bqhd,bkhd->bhqk', q, k).astype(jnp.float32) * dh ** -0.5
    p = jax.nn.softmax(s, axis=-1).astype(v.dtype)
    return jnp.einsum('bhqk,bkhd->bqhd', p, v).reshape(b, l, h * dh)


def gla_inputs(q, k, v, gk, w_gk, b_gk, use_rope):
    b, t, _ = q.shape
    q = q.reshape(b, t, GLA_HEADS, GLA_HEAD_K)
    k = k.reshape(b, t, GLA_HEADS, GLA_HEAD_K)
    if use_rope:
        q = axial_rope(q)
        k = axial_rope(k)
    q = q * GLA_HEAD_K ** -0.5
    z = jnp.einsum('btzr,zrk->zbtk', gk.reshape(b, t, 2, GLA_GK_RANK), w_gk) + b_gk[:, None, None, :]
    lf = jax.nn.log_sigmoid(z.astype(jnp.float32)) / GLA_GATE_NORM
    lf = lf.reshape(2, b, t, GLA_HEADS, GLA_HEAD_K).transpose(0, 1, 3, 2, 4)
    kh = jnp.swapaxes(k, 1, 2)
    vh = jnp.swapaxes(v.reshape(b, t, GLA_HEADS, GLA_HEAD_V), 1, 2)
    return jnp.swapaxes(q, 1, 2), (kh, kh), vh, (lf[0], lf[1])


def hgrn_inputs(q, f_fwd, f_bwd, i, lb):
    b, t, _ = q.shape
    heads = lambda a: a.reshape(b, t, HGRN_HEADS, HGRN_HEAD_DIM).transpose(0, 2, 1, 3)
    f_f = lb[0] + (1.0 - lb[0]) * jax.nn.sigmoid(f_fwd.astype(jnp.float32))
    f_b = lb[1] + (1.0 - lb[1]) * jax.nn.sigmoid(f_bwd.astype(jnp.float32))
    k_pair = (heads((1.0 - f_f).astype(q.dtype)), heads((1.0 - f_b).astype(q.dtype)))
    lf_pair = (heads(jnp.log(f_f)), heads(jnp.log(f_b)))
    return heads(q) * HGRN_HEAD_DIM ** -0.5, k_pair, heads(i), lf_pair


def hybrid_mixer(h_x, h_c, w_in, rpb, w_gk, b_gk, gla_norm, lb, hgrn_norm, w_out, need_ctx):
    b, t, _ = h_x.shape
    px = split_columns(h_x @ w_in)
    pc = split_columns(h_c @ w_in)
    na_heads = lambda a: a.reshape(a.shape[0], a.shape[1], NA_HEADS, HEAD_DIM)
    qa_x, ka_x, va_x = na_heads(px[0]), na_heads(px[1]), na_heads(px[2])
    qa_c, ka_c, va_c = na_heads(pc[0]), na_heads(pc[1]), na_heads(pc[2])
    o_na_x = neighborhood_attention(qa_x, ka_x, va_x, ka_c, va_c, rpb) * jax.nn.silu(px[3])
    gla_c = gla_inputs(pc[4], pc[5], pc[6], pc[7], w_gk, b_gk, False)
    gla_x = gla_inputs(px[4], px[5], px[6], px[7], w_gk, b_gk, True)
    oc_gla, ox_gla = bidirectional_prefix_scan(*gla_c, *gla_x)
    o_gla_x = head_norm_gate(ox_gla, gla_norm, px[8])
    hg_c = hgrn_inputs(pc[9], pc[10], pc[11], pc[12], lb)
    hg_x = hgrn_inputs(px[9], px[10], px[11], px[12], lb)
    oc_hg, ox_hg = bidirectional_prefix_scan(*hg_c, *hg_x)
    o_hg_x = head_norm_gate(ox_hg, hgrn_norm, px[13])
    y_x = jnp.concatenate([o_na_x, o_gla_x, o_hg_x], axis=-1) @ w_out
    if not need_ctx:
        return y_x, None
    o_na_c = context_attention(qa_c, ka_c, va_c) * jax.nn.silu(pc[3])
    o_gla_c = head_norm_gate(oc_gla, gla_norm, pc[8])
    o_hg_c = head_norm_gate(oc_hg, hgrn_norm, pc[13])
    y_c = jnp.concatenate([o_na_c, o_gla_c, o_hg_c], axis=-1) @ w_out
    return y_x, y_c


def setup_inputs(seed: int = 0) -> dict:
    key = jax.random.key(seed)
    ks = jax.random.split(key, 16)
    nrm = lambda k, shape, s: jax.random.normal(k, shape, jnp.float32) * s
    return {
        'x': nrm(ks[0], (BATCH, SEQ, D_MODEL), 1.0),
        'c': nrm(ks[1], (BATCH, D_MODEL), 1.0),
        'ctx': nrm(ks[2], (BATCH, CTX_LEN, D_MODEL), 1.0),
        'c_ctx': nrm(ks[3], (D_MODEL,), 1.0),
        'ada_w': nrm(ks[4], (DEPTH, D_MODEL, 3 * D_MODEL), 0.5 * D_MODEL ** -0.5),
        'ada_b': nrm(ks[5], (DEPTH, 3 * D_MODEL), 0.02),
        'norm_w': 1.0 + nrm(ks[6], (DEPTH, D_MODEL), 0.02),
        'w_in': nrm(ks[7], (DEPTH, D_MODEL, IN_WIDTH), D_MODEL ** -0.5),
        'na_rpb': nrm(ks[8], (DEPTH, NA_HEADS, 2 * NA_WIN_H_MAX - 1, 2 * NA_WIN_W - 1), 0.1),
        'gla_w_gk': nrm(ks[9], (DEPTH, 2, GLA_GK_RANK, GLA_KEY), GLA_GK_RANK ** -0.5),
        'gla_b_gk': nrm(ks[10], (DEPTH, 2, GLA_KEY), 0.1),
        'gla_norm_w': 1.0 + nrm(ks[11], (DEPTH, GLA_HEAD_V), 0.02),
        'hgrn_lower_bounds': nrm(ks[12], (DEPTH, 2, HGRN_WIDTH), 0.1),
        'hgrn_norm_w': 1.0 + nrm(ks[13], (DEPTH, HGRN_HEAD_DIM), 0.02),
        'w_out': nrm(ks[14], (DEPTH, MIX_WIDTH, D_MODEL), MIX_WIDTH ** -0.5),
        'final_norm_w': 1.0 + nrm(ks[15], (D_MODEL,), 0.02),
    }


def reference(x, c, ctx, c_ctx, ada_w, ada_b, norm_w, w_in, na_rpb, gla_w_gk, gla_b_gk,
              gla_norm_w, hgrn_lower_bounds, hgrn_norm_w, w_out, final_norm_w):
    lbs = jnp.cumsum(jax.nn.softmax(hgrn_lower_bounds.astype(jnp.float32), axis=0), axis=0)
    lbs = lbs - lbs[0:1]
    xc = ctx
    for layer in range(DEPTH):
        need_ctx = layer < DEPTH - 1
        mod_x = jax.nn.silu(c) @ ada_w[layer] + ada_b[layer]
        mod_c = jax.nn.silu(c_ctx) @ ada_w[layer] + ada_b[layer]
        sh_x, sc_x, gt_x = jnp.split(mod_x[:, None, :], 3, axis=-1)
        sh_c, sc_c, gt_c = jnp.split(mod_c, 3, axis=-1)
        h_x = rmsnorm(x, norm_w[layer]) * (1.0 + sc_x) + sh_x
        h_c = rmsnorm(xc, norm_w[layer]) * (1.0 + sc_c) + sh_c
        y_x, y_c = hybrid_mixer(h_x, h_c, w_in[layer], na_rpb[layer], gla_w_gk[layer], gla_b_gk[layer],
                                gla_norm_w[layer], lbs[layer], hgrn_norm_w[layer], w_out[layer], need_ctx)
        x = x + gt_x * y_x
        if need_ctx:
            xc = xc + gt_c * y_c
    return rmsnorm(x, final_norm_w)
```

```python
import numpy as np
import ml_dtypes
from contextlib import ExitStack
import concourse.bass as bass
import concourse.mybir as mybir
from concourse.bass_utils import run_bass_kernel_spmd

F32 = mybir.dt.float32
BF16 = mybir.dt.bfloat16
AF = mybir.ActivationFunctionType
ALU = mybir.AluOpType
NPBF = ml_dtypes.bfloat16

ENGS = ("pe", "act", "dve", "pool", "sp")
SAME_ENGINE_SYNC = {"pe": False, "act": True, "dve": True, "pool": True, "sp": False}
N_DMA_SEMS = {"sp": 20, "act": 4, "pool": 12}

D = 1024
KT = 8
T = 4096
CTX = 256
NTOK = T + CTX
NT = NTOK // 128
NEXT = 4992
N_FMB = 26
N_FMF = 5
EPS = 1e-6


class Ins:
    __slots__ = ("eng", "fn", "deps", "is_dma", "dma_sem", "dma_val", "inc_idx", "needed", "pos", "guard")

    def __init__(self, eng, fn, is_dma):
        self.eng = eng
        self.fn = fn
        self.deps = []
        self.is_dma = is_dma
        self.dma_sem = None
        self.dma_val = None
        self.inc_idx = None
        self.needed = False
        self.guard = None


class Prog:
    def __init__(self):
        self.nc = bass.Bass("TRN2", target_bir_lowering=False)
        self.es = ExitStack()
        self.streams = {e: [] for e in ENGS}
        self.last_w = {}
        self.readers = {}
        self.dma_rr = {e: 0 for e in N_DMA_SEMS}
        self.dma_last = {}
        self.dma_cnt = {}
        self.n_ins = 0
        self.scopes = []

    def push(self):
        self.scopes.append(ExitStack())

    def pop(self):
        self.barrier()
        self.scopes.pop().close()

    def _ctx(self):
        return self.scopes[-1] if self.scopes else self.es

    def sb(self, name, shape, dtype=F32):
        self.uid = getattr(self, "uid", 0) + 1
        return self._ctx().enter_context(self.nc.sbuf_tensor(f"sb{self.uid}_{name}", list(shape), dtype))

    def ps(self, name, shape=(128, 512), dtype=F32):
        self.uid = getattr(self, "uid", 0) + 1
        return self._ctx().enter_context(self.nc.psum_tensor(f"ps{self.uid}_{name}", list(shape), dtype))

    def dram(self, name, shape, dtype=F32, kind="Internal"):
        return self.nc.dram_tensor(name, list(shape), dtype, kind=kind)

    def op(self, eng, fn, reads=(), writes=(), dma=False):
        ins = Ins(eng, fn, dma)
        ins.pos = self.n_ins
        self.n_ins += 1
        deps = []
        for r in reads:
            w = self.last_w.get(r)
            if w is not None:
                deps.append(w)
        for wkey in writes:
            w = self.last_w.get(wkey)
            if w is not None:
                deps.append(w)
            deps.extend(self.readers.get(wkey, {}).values())
        if dma:
            slot = self.dma_rr[eng] % N_DMA_SEMS[eng]
            self.dma_rr[eng] += 1
            key = (eng, slot)
            prev = self.dma_last.get(key)
            if prev is not None:
                ins.guard = prev
            self.dma_cnt[key] = self.dma_cnt.get(key, 0) + 1
            ins.dma_sem = key
            ins.dma_val = 16 * self.dma_cnt[key]
            self.dma_last[key] = ins
        best = {}
        for d in deps:
            if d is ins:
                continue
            k = d.dma_sem if d.is_dma else d.eng
            if k not in best or best[k].pos < d.pos:
                best[k] = d
        for d in best.values():
            ins.deps.append(d)
            d.needed = True
        if ins.guard is not None:
            ins.guard.needed = True
        mykey = ins.dma_sem if dma else eng
        for r in reads:
            self.readers.setdefault(r, {})[mykey] = ins
        for wkey in writes:
            self.last_w[wkey] = ins
            self.readers[wkey] = {}
        self.streams[eng].append(ins)
        return ins

    def barrier(self):
        lasts = []
        for e in ENGS:
            for ins in reversed(self.streams[e]):
                if ins.fn is not None and not ins.is_dma:
                    lasts.append(ins)
                    break
        lasts += list(self.dma_last.values())
        for e in ENGS:
            b = Ins(e, None, False)
            b.pos = self.n_ins
            self.n_ins += 1
            for d in lasts:
                b.deps.append(d)
                d.needed = True
            self.streams[e].append(b)
        self.last_w = {}
        self.readers = {}

    def dma(self, out, in_, reads=(), writes=(), eng="sp"):
        return self.op(eng, lambda e: e.dma_start(out=out, in_=in_), reads, writes, dma=True)

    def mm(self, out, lhsT, rhs, reads, writes, start=True, stop=True, tp=None, nogrp=False):
        if tp is None and nogrp:
            return self.op("pe", lambda e: e.matmul(out, lhsT, rhs, start=start, stop=stop, skip_group_check=True), reads, writes)
        if tp is None:
            return self.op("pe", lambda e: e.matmul(out, lhsT, rhs, start=start, stop=stop), reads, writes)
        return self.op("pe", lambda e: e.matmul(out, lhsT, rhs, start=start, stop=stop, tile_position=tp,
                                                skip_group_check=True), reads, writes)

    def tr(self, out, in_, ident, reads, writes):
        return self.op("pe", lambda e: e.transpose(out, in_, ident), reads, writes)

    def act(self, out, in_, func, reads, writes, bias=None, scale=None, accum_out=None):
        kw = {}
        if bias is not None:
            kw["bias"] = bias
        if scale is not None:
            kw["scale"] = scale
        if accum_out is not None:
            kw["accum_out"] = accum_out
        return self.op("act", lambda e: e.activation(out, in_, func, **kw), reads, writes)

    def tt(self, eng, out, in0, in1, op, reads, writes):
        return self.op(eng, lambda e: e.tensor_tensor(out, in0, in1, op), reads, writes)

    def ts(self, eng, out, in0, s1, s2, op0, op1, reads, writes):
        if s2 is None:
            return self.op(eng, lambda e: e.tensor_scalar(out, in0, s1, None, op0=op0), reads, writes)
        return self.op(eng, lambda e: e.tensor_scalar(out, in0, s1, s2, op0=op0, op1=op1), reads, writes)

    def stt(self, eng, out, in0, scalar, in1, op0, op1, reads, writes):
        return self.op(eng, lambda e: e.scalar_tensor_tensor(out, in0, scalar, in1, op0=op0, op1=op1), reads, writes)

    def cp(self, eng, out, in_, reads, writes):
        if eng == "act":
            return self.op("act", lambda e: e.copy(out, in_), reads, writes)
        return self.op(eng, lambda e: e.tensor_copy(out, in_), reads, writes)

    def recip(self, out, in_, reads, writes):
        return self.op("dve", lambda e: e.reciprocal(out, in_), reads, writes)

    def memset(self, eng, ap, val, writes):
        return self.op(eng, lambda e: e.memset(ap, val), (), writes)

    def finish(self, final_waits=()):
        nc = self.nc
        es = self.es
        sems = {e: es.enter_context(nc.semaphore("s_" + e)) for e in ENGS}
        dsems = {}
        for e, n in N_DMA_SEMS.items():
            for i in range(n):
                dsems[(e, i)] = es.enter_context(nc.semaphore(f"d_{e}{i}"))
        for fw in final_waits:
            fw.needed = True
        for e in ENGS:
            c = 0
            for ins in self.streams[e]:
                if ins.is_dma or ins.fn is None:
                    continue
                if ins.needed:
                    c += 1
                    ins.inc_idx = c
        streams = self.streams

        def emit(e, eng_obj):
            known = {}

            def wait_for(d):
                if d.is_dma:
                    k = ("d",) + d.dma_sem
                    if known.get(k, 0) >= d.dma_val:
                        return
                    eng_obj.wait_ge(dsems[d.dma_sem], d.dma_val)
                    known[k] = d.dma_val
                else:
                    if d.eng == e and not SAME_ENGINE_SYNC[e]:
                        return
                    k = ("e", d.eng)
                    if known.get(k, 0) >= d.inc_idx:
                        return
                    eng_obj.wait_ge(sems[d.eng], d.inc_idx)
                    known[k] = d.inc_idx

            for ins in streams[e]:
                for d in ins.deps:
                    wait_for(d)
                if ins.guard is not None:
                    wait_for(ins.guard)
                if ins.fn is None:
                    continue
                bi = ins.fn(eng_obj)
                if ins.is_dma:
                    bi.then_inc(dsems[ins.dma_sem], 16)
                elif ins.needed:
                    bi.then_inc(sems[e], 1)
            if e == "sp":
                for fw in final_waits:
                    wait_for(fw)

        with nc.Block() as block:
            @block.tensor
            def _(eng):
                emit("pe", eng)

            @block.scalar
            def _(eng):
                emit("act", eng)

            @block.vector
            def _(eng):
                emit("dve", eng)

            @block.gpsimd
            def _(eng):
                emit("pool", eng)

            @block.sync
            def _(eng):
                emit("sp", eng)
        while self.scopes:
            self.scopes.pop().close()
        self.es.close()
        return nc


class Ring:
    def __init__(self, P, name, n, shape, dtype=F32, psum=False):
        self.name = name
        if psum:
            self.bufs = [P.ps(f"{name}{i}", shape, dtype) for i in range(n)]
        else:
            self.bufs = [P.sb(f"{name}{i}", shape, dtype) for i in range(n)]
        self.i = 0

    def next(self):
        j = self.i % len(self.bufs)
        self.i += 1
        return self.bufs[j], f"{self.name}{j}"


def ext_col_index():
    cols = []
    cols += list(range(0, 512)) + list(range(512, 1024)) + list(range(1536, 2048))

    def pad_heads(base, swap):
        out = []
        for h in range(4):
            for i in range(32):
                j = (i + 8 if (i % 16) < 8 else i - 8) if swap else i
                out.append(base + h * 32 + j)
            out += [-1] * 32
        return out
    cols += pad_heads(2048, False) + pad_heads(2048, True) + pad_heads(2176, False) + pad_heads(2176, True)
    cols += list(range(2592, 2848)) + list(range(2848, 3104)) + list(range(3872, 4128))
    cols += list(range(3104, 3360)) + list(range(3360, 3616)) + list(range(2560, 2592)) + [-1] * 96
    cols += list(range(1024, 1536)) + list(range(2304, 2560)) + list(range(3616, 3872))
    assert len(cols) == NEXT
    return np.array(cols)


def host_consts():
    c = {}
    c["identf"] = np.eye(128, dtype=np.float32)
    c["identb"] = np.eye(128, dtype=np.float32).astype(NPBF)
    i = np.arange(128)
    m16 = np.ones((128, 128), np.float32); m16[:, i % 16 == 0] = 0
    m128 = np.ones((128, 128), np.float32); m128[:, 0] = 0
    c["scanmask"] = np.stack([m16, m128], 0)
    s = i[:, None]; t = i[None, :]
    tri = []
    for C in (16, 128):
        same = (s // C) == (t // C)
        tri.append((same & (s <= t)).astype(np.float32))
        tri.append((same & (s >= t)).astype(np.float32))
    c["trimask"] = np.stack(tri, 0)
    tex = np.zeros((4, 2, 128, 256), np.float32)
    for v_, C in enumerate((16, 16, 128, 128)):
        nch = 128 // C
        m = tri[v_].reshape(128, nch, C)
        for h in range(2):
            t4 = np.zeros((128, nch, 2, C), np.float32)
            t4[:, :, h, :] = m
            tex[v_, h] = t4.reshape(128, 256)
    c["trimaskx"] = tex
    c["cm"] = ((i[:, None] // 16) == np.arange(8)[None, :]).astype(np.float32).astype(NPBF)
    c["onesblk"] = (((i[:, None] // 64) == (i[None, :] // 64)).astype(np.float32) / 64.0).astype(NPBF)
    c["ones64"] = np.ones((128, 64), np.float32).astype(NPBF)
    inv_freq = (1.0 / (10000.0 ** (np.arange(0, 16, 2, dtype=np.float32) / np.float32(16)))).astype(np.float32)
    tt_ = np.arange(T)
    cos = np.zeros((128, T), np.float32); sins = np.zeros((128, T), np.float32)
    for p in range(128):
        ii = p % 64
        if ii >= 32:
            continue
        blk = ii // 16
        j = ii % 16
        f = j % 8
        pos = (tt_ // 64 if blk == 0 else tt_ % 64).astype(np.float32)
        ang = (pos * inv_freq[f]).astype(np.float32)
        cos[p] = np.cos(ang)
        sins[p] = -np.sin(ang) if j < 8 else np.sin(ang)
    c["ropecos"] = cos
    c["ropesin"] = sins
    return c


def host_layout(inp):
    f32 = np.float32
    common = dict(host_consts())
    cols = ext_col_index()
    valid = cols >= 0
    DEPTH = inp["w_in"].shape[0]
    q = np.arange(64)
    col_start = np.clip(q - 8, 0, 48)
    kc = np.arange(64)
    inwin = (kc[:, None] >= col_start[None, :]) & (kc[:, None] < col_start[None, :] + 16)
    didx = np.clip(kc[:, None] - q[None, :], -15, 15) + 15
    pidx = np.arange(128)
    for l in range(DEPTH):
        w_ext = np.zeros((D, NEXT), f32)
        w_ext[:, valid] = inp["w_in"][l][:, cols[valid]]
        common[f"w_in{l}"] = w_ext
        common[f"ada_w{l}"] = np.ascontiguousarray(inp["ada_w"][l], dtype=f32)
        common[f"ada_bT{l}"] = np.ascontiguousarray(inp["ada_b"][l].reshape(24, 128).T, dtype=f32)
        common[f"ada_bgt{l}"] = np.ascontiguousarray(inp["ada_b"][l][2048:3072].reshape(1, 1024), dtype=f32)
        common[f"norm_wT{l}"] = np.ascontiguousarray(inp["norm_w"][l].reshape(8, 128).T, dtype=f32)
        common[f"w_out{l}"] = np.ascontiguousarray(inp["w_out"][l], dtype=f32)
        rpb = inp["na_rpb"][l]
        g = rpb[:, ::-1, :][:, :, didx]
        g = np.where(inwin[None, None], g, f32(-30000.0))
        g = g.transpose(0, 2, 1, 3).reshape(4, 128, 15 * 64)
        common[f"rpbt{l}"] = np.ascontiguousarray(g, dtype=f32)
        wgk = inp["gla_w_gk"][l]
        bgk = inp["gla_b_gk"][l]
        wp = np.zeros((32, 2, 2, 128), f32)
        bp = np.zeros((128, 2, 2), f32)
        for d_ in range(2):
            for g_ in range(2):
                for p in range(128):
                    h = 2 * g_ + p // 64
                    ii = p % 64
                    if ii < 32:
                        wp[d_ * 16:(d_ + 1) * 16, d_, g_, p] = wgk[d_, :, h * 32 + ii]
                        bp[p, d_, g_] = bgk[d_, h * 32 + ii]
        common[f"wgk{l}"] = wp
        common[f"bgk{l}"] = bp
        common[f"glanw{l}"] = np.ascontiguousarray(inp["gla_norm_w"][l][pidx % 64].reshape(128, 1), dtype=f32)
        common[f"hgnw{l}"] = np.ascontiguousarray(inp["hgrn_norm_w"][l][pidx % 64].reshape(128, 1), dtype=f32)
    lb = inp["hgrn_lower_bounds"]
    common["lbraw"] = np.ascontiguousarray(lb.reshape(2, 2, 2, 128).transpose(3, 0, 1, 2).reshape(128, 8), dtype=f32)
    common["fnw"] = np.ascontiguousarray(inp["final_norm_w"].reshape(1, D), dtype=f32)
    per = []
    B = inp["x"].shape[0]
    cc = inp["c_ctx"].reshape(8, 128).T
    for b in range(B):
        cv = np.concatenate([inp["c"][b].reshape(8, 128).T, cc], axis=1)
        per.append({"x": np.ascontiguousarray(inp["x"][b], dtype=f32),
                    "ctx": np.ascontiguousarray(inp["ctx"][b], dtype=f32),
                    "cvec": np.ascontiguousarray(cv, dtype=f32)})
    return common, per


FMB_EVAC = {}
for _t in range(0, 4):
    FMB_EVAC[_t] = ("scale", 0.125)
for _t in range(4, 8):
    FMB_EVAC[_t] = ("copy", 1.0)
for _t in range(8, 12):
    FMB_EVAC[_t] = ("silu", 1.0)
for _t in range(12, 16):
    FMB_EVAC[_t] = ("scale", 32.0 ** -0.5)
for _t in range(16, 20):
    FMB_EVAC[_t] = ("copy", 1.0)
for _t in (20, 21, 24, 25):
    FMB_EVAC[_t] = ("silu", 1.0)
for _t in (22, 23):
    FMB_EVAC[_t] = ("scale", 0.125)


import os
DBG = {"skip_p1": os.environ.get("K_DBG_SKIP_P1") == "1",
       "p2_tiles": int(os.environ.get("K_DBG_P2_TILES", "0")),
       "p2_groups": [int(x) for x in os.environ.get("K_DBG_P2_GROUPS", "0,1,2,3").split(",")],
       "p2_dirs": [int(x) for x in os.environ.get("K_DBG_P2_DIRS", "1,0").split(",")],
       "cut": int(os.environ.get("K_DBG_P2_CUT", "99"))}


def build_program(depth=2, stop_after=None, debug=False):
    P = Prog()
    nc = P.nc
    kdbg = "ExternalOutput" if debug else "Internal"

    def din(name, shape, dt=F32):
        return P.dram(name, shape, dt, kind="ExternalInput").ap()

    x_d = din("x", [T, D]); ctx_d = din("ctx", [CTX, D]); cvec_d = din("cvec", [128, 16])
    identf_d = din("identf", [128, 128]); identb_d = din("identb", [128, 128], BF16)
    scanmask_d = din("scanmask", [2, 128, 128]); trimask_d = din("trimask", [4, 128, 128]); trimaskx_d = din("trimaskx", [4, 2, 128, 256])
    cm_d = din("cm", [128, 8], BF16); onesblk_d = din("onesblk", [128, 128], BF16); ones64_d = din("ones64", [128, 64], BF16)
    ropecos_d = din("ropecos", [128, T]); ropesin_d = din("ropesin", [128, T])
    lbraw_d = din("lbraw", [128, 8]); fnw_d = din("fnw", [1, D])
    LW = []
    for l in range(2):
        LW.append(dict(
            w_in=din(f"w_in{l}", [D, NEXT]), ada_w=din(f"ada_w{l}", [D, 3 * D]), ada_bT=din(f"ada_bT{l}", [128, 24]),
            ada_bgt=din(f"ada_bgt{l}", [1, D]), norm_wT=din(f"norm_wT{l}", [128, 8]), w_out=din(f"w_out{l}", [D, D]),
            rpbt=din(f"rpbt{l}", [4, 128, 960]), wgk=din(f"wgk{l}", [32, 2, 2, 128]), bgk=din(f"bgk{l}", [128, 2, 2]),
            glanw=din(f"glanw{l}", [128, 1]), hgnw=din(f"hgnw{l}", [128, 1])))
    out_d = P.dram("out", [T, D], F32, kind="ExternalOutput").ap()
    FMB = P.dram("FMB", [N_FMB, 128, NTOK], BF16, kind=kdbg).ap()
    FMF = P.dram("FMF", [N_FMF, 128, NTOK], F32, kind=kdbg).ap()
    TM = P.dram("TM", [NTOK, 1024], BF16, kind=kdbg).ap()
    OB = P.dram("OB", [4, 128, NTOK], F32, kind=kdbg).ap()
    OT = P.dram("OT", [8, 128, NTOK], BF16, kind=kdbg).ap()
    XN = P.dram("XN", [T, D], F32, kind=kdbg).ap()
    XC = P.dram("XC", [CTX, D], F32, kind=kdbg).ap()
    MODDBG = P.dram("MODDBG", [128, 64], F32, kind=kdbg).ap()

    identf = P.sb("identf", [128, 128]); identb = P.sb("identb", [128, 128], BF16)
    onesblk = P.sb("onesblk", [128, 128], BF16); ones64 = P.sb("ones64", [128, 64], BF16)
    cvec = P.sb("cvec", [128, 16]); sil = P.sb("sil", [128, 16]); silrep = P.sb("silrep", [128, 16, 128])
    lbraw = P.sb("lbraw", [128, 8]); lbt = P.sb("lbt", [128, 8]); omlb = P.sb("omlb", [128, 8])
    eps_t = P.sb("eps_t", [128, 1])
    P.memset("dve", eps_t[:], EPS, ["eps"])
    for nm, dst, src in (("identf", identf, identf_d), ("identb", identb, identb_d), ("onesblk", onesblk, onesblk_d),
                         ("ones64", ones64, ones64_d), ("cvec", cvec, cvec_d), ("lbraw", lbraw, lbraw_d)):
        P.dma(dst[:], src, writes=[nm])
    tmp16 = P.sb("tmp16", [128, 16])
    P.act(tmp16[:], cvec[:], AF.Exp, ["cvec"], ["tmp16"], scale=-1.0)
    P.ts("dve", tmp16[:], tmp16[:], 1.0, None, ALU.add, None, ["tmp16"], ["tmp16"])
    P.recip(tmp16[:], tmp16[:], ["tmp16"], ["tmp16"])
    P.tt("dve", sil[:], cvec[:], tmp16[:], ALU.mult, ["cvec", "tmp16"], ["sil"])
    for j in range(16):
        P.cp("dve", silrep[:, j, :], sil[:, j:j + 1].to_broadcast([128, 128]), ["sil"], ["silrep"])
    P.memset("dve", lbt[:], 0.0, ["lbt"])
    P.tt("dve", lbt[:, 4:8], lbraw[:, 0:4], lbraw[:, 4:8], ALU.subtract, ["lbraw", "lbt"], ["lbt"])
    P.act(lbt[:, 4:8], lbt[:, 4:8], AF.Exp, ["lbt"], ["lbt"])
    P.ts("dve", lbt[:, 4:8], lbt[:, 4:8], 1.0, None, ALU.add, None, ["lbt"], ["lbt"])
    P.recip(lbt[:, 4:8], lbt[:, 4:8], ["lbt"], ["lbt"])
    P.ts("dve", omlb[:], lbt[:], -1.0, 1.0, ALU.mult, ALU.add, ["lbt"], ["omlb"])

    last_out = []

    for l in range(depth):
        W = LW[l]
        last = (l == depth - 1)
        need_ctx = not last
        x_src = x_d if l == 0 else XN
        c_src = ctx_d if l == 0 else XC
        tiles = list(range(NT))

        P.push()
        gsh = P.sb(f"gsh{l}", [128, 2, 2, 8])
        gtrep = P.sb(f"gtrep{l}", [128, 2, D])
        woutb = P.sb(f"woutb{l}", [128, 2 if need_ctx else 1, KT, D], BF16)
        ttab = P.sb(f"ttab{l}", [128, 4, 960], BF16)
        wgk = P.sb(f"wgk{l}", [32, 2, 2, 128]); nbgk = P.sb(f"nbgk{l}", [128, 4])
        glanw = P.sb(f"glanw{l}", [128, 1]); hgnw = P.sb(f"hgnw{l}", [128, 1])

        P.push()
        winb = P.sb("winb", [128, KT, NEXT], BF16)
        tr_ps = Ring(P, "tr_ps", 2, [128, 512], psum=True)
        fm_ps = Ring(P, "fm_ps", 3, [128, 512], psum=True)
        tm_ps = Ring(P, "tm_ps", 2, [128, 512], psum=True)
        modT = P.sb("modT", [128, 16, 2]); adabT = P.sb("adabT", [128, 24]); normwT = P.sb("normwT", [128, 8])
        adabgt = P.sb("adabgt", [128, D])
        stg = Ring(P, "stg", 2, [128, KT, 256])
        mod_ps = tm_ps.bufs[0]; gt_ps = fm_ps
        P.dma(adabT[:], W["ada_bT"], writes=["adabT"]); P.dma(normwT[:], W["norm_wT"], writes=["normwT"])
        P.dma(adabgt[:], W["ada_bgt"].partition_broadcast(128), writes=["adabgt"])
        P.dma(wgk[:], W["wgk"], writes=["wgk"]); P.dma(nbgk[:], W["bgk"].rearrange("p a b -> p (a b)"), writes=["nbgk"])
        P.dma(glanw[:], W["glanw"], writes=["glanw"]); P.dma(hgnw[:], W["hgnw"], writes=["hgnw"])
        P.ts("dve", nbgk[:], nbgk[:], -1.0, None, ALU.mult, None, ["nbgk"], ["nbgk"])
        adaw_v = W["ada_w"].rearrange("(kt p) j -> p kt j", p=128)
        sil2 = sil[:].rearrange("p (w k) -> p w k", w=2)
        for ch in range(12):
            sbuf, skey = stg.next()
            P.dma(sbuf[:], adaw_v[:, :, ch * 256:(ch + 1) * 256], writes=[skey])
            if ch < 8:
                for jl in range(2):
                    jt = ch * 2 + jl
                    for kt in range(KT):
                        P.mm(mod_ps[:, jt * 2:jt * 2 + 2], sbuf[:, kt, jl * 128:(jl + 1) * 128], sil2[:, :, kt],
                             [skey, "sil"], ["tm_ps0"], start=(kt == 0), stop=(kt == KT - 1))
                    P.ts("dve", modT[:, jt, :], mod_ps[:, jt * 2:jt * 2 + 2], adabT[:, jt:jt + 1], None, ALU.add, None,
                         ["tm_ps0", "adabT"], ["modT"])
            else:
                q4 = ch - 8
                for which in range(2):
                    gp, gk_ = gt_ps.next()
                    for kt in range(KT):
                        P.mm(gp[:, 0:256], silrep[:, which * 8 + kt, :], sbuf[:, kt, :], ["silrep", skey], [gk_],
                             start=(kt == 0), stop=(kt == KT - 1))
                    P.tt("dve", gtrep[:, which, q4 * 256:(q4 + 1) * 256], gp[:, 0:256], adabgt[:, q4 * 256:(q4 + 1) * 256],
                         ALU.add, [gk_, "adabgt"], ["gtrep"])
        for which in range(2):
            P.stt("dve", gsh[:, which, 0, :], modT[:, 8:16, which], 1.0, normwT[:], ALU.add, ALU.mult,
                  ["modT", "normwT"], ["gsh"])
            P.cp("dve", gsh[:, which, 1, :], modT[:, 0:8, which], ["modT"], ["gsh"])
        if debug:
            P.dma(MODDBG[:, 0:32], gsh[:].rearrange("p a b c -> p (a b c)"), reads=["gsh"])
        wout_v = W["w_out"].rearrange("(kt p) j -> p kt j", p=128)
        for q4 in range(4):
            sbuf, skey = stg.next()
            P.dma(sbuf[:], wout_v[:, :, q4 * 256:(q4 + 1) * 256], writes=[skey])
            for which in range(2 if need_ctx else 1):
                for kt in range(KT):
                    eng = "dve" if kt % 2 == 0 else "pool"
                    P.tt(eng, woutb[:, which, kt, q4 * 256:(q4 + 1) * 256], sbuf[:, kt, :],
                         gtrep[:, which, q4 * 256:(q4 + 1) * 256], ALU.mult, [skey, "gtrep"], ["woutb"])
        for hp in range(4):
            sbuf, skey = stg.next()
            rv = sbuf[:].rearrange("p k c -> p (k c)")
            P.dma(rv[:, 0:960], W["rpbt"][hp, :, :], writes=[skey])
            P.act(ttab[:, hp, :], rv[:, 0:960], AF.Exp, [skey], ["ttab"])
        if stop_after == (l, "prep"):
            break

        win_v = W["w_in"].rearrange("(kt p) j -> p kt j", p=128)
        nchunk = (NEXT + 255) // 256
        for ch in range(0 if (DBG["skip_p1"] and l == 0) else nchunk):
            c0 = ch * 256
            cw = min(256, NEXT - c0)
            sbuf, skey = stg.next()
            P.dma(sbuf[:, :, 0:cw], win_v[:, :, c0:c0 + cw], writes=[skey])
            for kt in range(KT):
                eng = ("dve", "pool", "act")[kt % 3]
                P.cp(eng, winb[:, kt, c0:c0 + cw], sbuf[:, kt, 0:cw], [skey], ["winb"])
        xt_r = Ring(P, "xt", 2, [128, D]); xh_r = Ring(P, "xh", 2, [128, D]); sq_scr = P.sb("sq_scr", [128, D])
        ss_r = Ring(P, "ss", 4, [128, 2])
        hT_r = Ring(P, "hT", 2, [128, KT, 512], BF16)
        fmo_b = Ring(P, "fmo_b", 4, [128, 512], BF16); fmo_f = Ring(P, "fmo_f", 2, [128, 512])
        tmo = Ring(P, "tmo", 2, [128, 1024], BF16)
        supers = [(0, 2)] + [(2 + 4 * i, 4) for i in range(8)]
        if DBG["skip_p1"] and l == 0:
            supers = []
        def prologue(t0, ntl):
            is_ctx = (t0 == 0)
            which = 1 if is_ctx else 0
            ntok = ntl * 128
            hT, hkey = hT_r.next()
            for ti in range(ntl):
                tau = t0 + ti
                xt, xkey = xt_r.next()
                src = c_src[tau * 128:(tau + 1) * 128, :] if is_ctx else x_src[(tau - 2) * 128:(tau - 1) * 128, :]
                P.dma(xt[:], src, writes=[xkey])
                ss, sskey = ss_r.next()
                P.memset("dve", ss[:], 0.0, [sskey])
                P.act(sq_scr[:], xt[:], AF.Square, [xkey, sskey], ["sq_scr", sskey], accum_out=ss[:, 0:1])
                P.act(ss[:, 1:2], ss[:, 0:1], AF.Ln, [sskey, "eps"], [sskey], bias=eps_t[:], scale=1.0 / D)
                P.act(ss[:, 1:2], ss[:, 1:2], AF.Exp, [sskey], [sskey], scale=-0.5)
                xh, xhkey = xh_r.next()
                P.ts("dve", xh[:], xt[:], ss[:, 1:2], None, ALU.mult, None, [xkey, sskey], [xhkey])
                for half in range(2):
                    tp, tpkey = tr_ps.next()
                    for q4 in range(4):
                        kt = half * 4 + q4
                        P.tr(tp[:, q4 * 128:(q4 + 1) * 128], xh[:, kt * 128:(kt + 1) * 128], identf[:], [xhkey, "identf"], [tpkey])
                    for q4 in range(4):
                        kt = half * 4 + q4
                        P.act(hT[:, kt, ti * 128:(ti + 1) * 128], tp[:, q4 * 128:(q4 + 1) * 128], AF.Identity,
                              [tpkey, "gsh"], [hkey], bias=gsh[:, which, 1, kt:kt + 1], scale=gsh[:, which, 0, kt:kt + 1])
            return hT, hkey

        def mainbody(t0, ntl, hT, hkey):
            ntok = ntl * 128
            for ft in range(N_FMB + N_FMF):
                fp, fpkey = fm_ps.next()
                for kt in range(KT):
                    P.mm(fp[:, 0:ntok], winb[:, kt, ft * 128:(ft + 1) * 128], hT[:, kt, 0:ntok], ["winb", hkey], [fpkey],
                         start=(kt == 0), stop=(kt == KT - 1))
                if ft < N_FMB:
                    ob, obkey = fmo_b.next()
                    kind, sc_ = FMB_EVAC[ft]
                    if kind == "silu":
                        P.act(ob[:, 0:ntok], fp[:, 0:ntok], AF.Silu, [fpkey], [obkey])
                    else:
                        P.act(ob[:, 0:ntok], fp[:, 0:ntok], AF.Copy, [fpkey], [obkey], scale=float(sc_))
                    P.dma(FMB[ft, :, t0 * 128:t0 * 128 + ntok], ob[:, 0:ntok], reads=[obkey], writes=[("FMB", ft, t0)], eng="pool")
                else:
                    ob, obkey = fmo_f.next()
                    P.cp("dve", ob[:, 0:ntok], fp[:, 0:ntok], [fpkey], [obkey])
                    P.dma(FMF[ft - N_FMB, :, t0 * 128:t0 * 128 + ntok], ob[:, 0:ntok], reads=[obkey],
                          writes=[("FMF", ft - N_FMB, t0)], eng="pool")
            cbase = (N_FMB + N_FMF) * 128
            for ti in range(ntl):
                tau = t0 + ti
                ob, obkey = tmo.next()
                for half in range(2):
                    tp2, tp2key = tm_ps.next()
                    for kt in range(KT):
                        P.mm(tp2[:], hT[:, kt, ti * 128:(ti + 1) * 128], winb[:, kt, cbase + half * 512:cbase + (half + 1) * 512],
                             [hkey, "winb"], [tp2key], start=(kt == 0), stop=(kt == KT - 1))
                    P.cp("dve", ob[:, half * 512:(half + 1) * 512], tp2[:], [tp2key], [obkey])
                P.dma(TM[tau * 128:(tau + 1) * 128, :], ob[:], reads=[obkey], writes=[("TM", tau)], eng="pool")

        cur = prologue(*supers[0]) if supers else None
        for i_, (t0, ntl) in enumerate(supers):
            nxt = prologue(*supers[i_ + 1]) if i_ + 1 < len(supers) else None
            mainbody(t0, ntl, *cur)
            cur = nxt
        P.pop()
        if stop_after == (l, "p1"):
            break

        P.push()
        build_p2(P, l, need_ctx, FMB, FMF, TM, OB, OT, dict(
            identb=identb, onesblk=onesblk, wgk=wgk, nbgk=nbgk, glanw=glanw, hgnw=hgnw, lbt=lbt, omlb=omlb, eps_t=eps_t,
            scanmask_d=scanmask_d, trimask_d=trimask_d, trimaskx_d=trimaskx_d, cm_d=cm_d, ropecos_d=ropecos_d, ropesin_d=ropesin_d))
        P.pop()
        if stop_after == (l, "p2"):
            break

        P.push()
        build_p3(P, l, need_ctx, FMB, TM, OT, ttab, ones64)
        P.pop()
        if stop_after == (l, "p3"):
            break

        P.push()
        ot_r = Ring(P, "ot", 2, [128, 8, 128], BF16); xr = Ring(P, "x4", 2, [128, D]); xo = Ring(P, "xo", 2, [128, D])
        y_ps = Ring(P, "y_ps", 4, [128, 512], psum=True)
        ss_r = Ring(P, "ss4", 4, [128, 2]); sq_scr = P.sb("sq4", [128, D]); fnw = P.sb("fnw", [128, D])
        if last:
            P.dma(fnw[:], fnw_d.partition_broadcast(128), writes=["fnw"])
        otv = OT.rearrange("f p t -> p f t")
        for tau in (range(NT) if need_ctx else range(2, NT)):
            is_ctx = tau < 2
            which = 1 if is_ctx else 0
            ot, otkey = ot_r.next()
            P.dma(ot[:], otv[:, :, tau * 128:(tau + 1) * 128], reads=[("OT", f, tau) for f in range(8)], writes=[otkey])
            xt, xkey = xr.next()
            rows = slice(tau * 128, (tau + 1) * 128) if is_ctx else slice((tau - 2) * 128, (tau - 1) * 128)
            src = c_src[rows, :] if is_ctx else x_src[rows, :]
            P.dma(xt[:], src, reads=[("XR", l, tau)], writes=[xkey])
            xn, xnkey = xo.next()
            for half in range(2):
                yp, ypkey = y_ps.next()
                for f in range(8):
                    P.mm(yp[:], ot[:, f, :], woutb[:, which, f, half * 512:(half + 1) * 512], [otkey, "woutb"], [ypkey],
                         start=(f == 0), stop=(f == 7))
                P.tt("dve", xn[:, half * 512:(half + 1) * 512], yp[:], xt[:, half * 512:(half + 1) * 512], ALU.add,
                     [ypkey, xkey], [xnkey])
            if not last:
                dst = XC[rows, :] if is_ctx else XN[rows, :]
                P.dma(dst, xn[:], reads=[xnkey], writes=[("XR", l + 1, tau)], eng="pool")
            else:
                ss, sskey = ss_r.next()
                P.memset("dve", ss[:], 0.0, [sskey])
                P.act(sq_scr[:], xn[:], AF.Square, [xnkey, sskey], ["sq4", sskey], accum_out=ss[:, 0:1])
                P.act(ss[:, 1:2], ss[:, 0:1], AF.Ln, [sskey, "eps"], [sskey], bias=eps_t[:], scale=1.0 / D)
                P.act(ss[:, 1:2], ss[:, 1:2], AF.Exp, [sskey], [sskey], scale=-0.5)
                P.stt("dve", xn[:], xn[:], ss[:, 1:2], fnw[:], ALU.mult, ALU.mult, [xnkey, sskey, "fnw"], [xnkey])
                last_out.append(P.dma(out_d[rows, :], xn[:], reads=[xnkey], eng="pool"))
        P.pop()
        P.pop()
        if stop_after == (l, "p4"):
            break

    finals = list(last_out) + [d for d in P.dma_last.values()]
    return P.finish(final_waits=finals)


def _interleave(gens):
    gens = list(gens)
    while gens:
        for gen in list(gens):
            try:
                next(gen)
            except StopIteration:
                gens.remove(gen)


def build_p2(P, l, need_ctx, FMB, FMF, TM, OB, OT, G):
    identb = G["identb"]; onesblk = G["onesblk"]; wgk = G["wgk"]; nbgk = G["nbgk"]
    lbt = G["lbt"]; omlb = G["omlb"]; eps_t = G["eps_t"]
    scanmask = P.sb("scanmask", [128, 2, 128]); trimask = P.sb("trimask", [128, 8, 256]); cm = P.sb("cm", [128, 8], BF16)
    ropecos = P.sb("ropecos", [128, T]); ropesin = P.sb("ropesin", [128, T])
    P.dma(scanmask[:], G["scanmask_d"].rearrange("n p t -> p n t"), writes=["scanmask"])
    P.dma(trimask[:], G["trimaskx_d"].rearrange("n h p t -> p (n h) t"), writes=["trimask"])
    P.dma(cm[:], G["cm_d"], writes=["cm"])
    P.dma(ropecos[:], G["ropecos_d"], writes=["ropecos"]); P.dma(ropesin[:], G["ropesin_d"], writes=["ropesin"])
    NSL = [2, 2, 9, 9]
    S = [P.sb(f"S{g}", [128, NSL[g], 64]) for g in range(4)]
    RD = 6
    qk_r = Ring(P, "qk", RD, [128, 4, 128], BF16); gk_r = Ring(P, "gkr", RD, [32, 128]); v_r = Ring(P, "vr", RD, [128, 128], BF16)
    qh_r = Ring(P, "qh", RD, [128, 128], BF16); ff_r = Ring(P, "ff", RD, [128, 128]); gate_r = Ring(P, "gate", RD, [64, 256], BF16)
    ob_r = Ring(P, "obr", RD, [64, 256])
    wr = [Ring(P, f"w{i}", 6, [128, 128]) for i in range(9)]
    sm_r = Ring(P, "sm", RD, [128, 4, 8])
    kt_r = Ring(P, "ktr", RD, [128, 128], BF16)
    qbd_rs = [Ring(P, "qbdg", 4, [128, 2, 128], BF16), Ring(P, "qbdh", 4, [128, 2, 128], BF16)]
    qin_rs = [Ring(P, "qing", 4, [128, 2, 128], BF16), Ring(P, "qinh", 4, [128, 2, 128], BF16)]
    sbf_r = Ring(P, "sbf", 12, [128, 64], BF16)
    kout_r = Ring(P, "kout", RD, [128, 128], BF16)
    koT_r = Ring(P, "koT", RD, [128, 128], BF16); pt_r = Ring(P, "pt", RD, [128, 2, 256], BF16)
    vexp_r = Ring(P, "vexp", 4, [128, 2, 8, 64], BF16)
    o_r = Ring(P, "osb", 8, [64, 256]); sq_r = Ring(P, "osq", RD, [64, 256], BF16); og_r = Ring(P, "og", RD, [64, 256], BF16)
    z_ps = Ring(P, "z_ps", 1, [128, 512], psum=True); tp_ps = Ring(P, "tp_ps", 1, [128, 1024], BF16, psum=True)
    sc_ps = Ring(P, "sc_ps", 2, [128, 512], psum=True); kv_ps = Ring(P, "kv_ps", 2, [128, 512], psum=True)
    o_ps = Ring(P, "o_ps", 2, [128, 512], psum=True)
    for rr in qbd_rs + qin_rs:
        for i, b in enumerate(rr.bufs):
            P.memset("pool", b[:], 0.0, [f"{rr.name}{i}"])
    base = [0, 0, 0, 0]

    def group_front(dirn, tau, g, ctx):
        bwd = dirn == 1
        is_ctx = tau < 2
        cs = slice(tau * 128, (tau + 1) * 128)
        gla = g < 2
        gi = g % 2
        C = 128 if gla else 16
        nch = 128 // C
        mi = 1 if gla else 0
        kappa = -1.0 / 16.0 if gla else 1.0
        nsl = NSL[g]
        b0 = base[g]
        base[g] = (b0 + nch) % nsl
        c3 = lambda ap: ap.rearrange("p (n c) -> p n c", c=C)
        vt, vkey = v_r.next()
        voff = (512 + gi * 128) if gla else (768 + gi * 128)
        P.dma(vt[:], TM[cs, voff:voff + 128], reads=[("TM", tau)], writes=[vkey])
        lf, lfkey = wr[0].next()
        if gla:
            qk, qkkey = qk_r.next()
            for n_, ft in enumerate((12 + gi, 14 + gi, 16 + gi, 18 + gi)):
                P.dma(qk[:, n_, :], FMB[ft, :, cs], reads=[("FMB", ft, _st(tau))], writes=[qkkey])
            gkt, gkkey = gk_r.next()
            P.dma(gkt[:], FMF[4, 0:32, cs], reads=[("FMF", 4, _st(tau))], writes=[gkkey])
            yield
            zp, zkey = z_ps.next()
            P.mm(zp[:, 0:128], wgk[:, dirn, gi, :], gkt[:], ["wgk", gkkey], [zkey])
            e_, ekey = wr[1].next()
            P.act(e_[:], zp[:, 0:128], AF.Exp, [zkey, "nbgk"], [ekey], bias=nbgk[:, dirn * 2 + gi:dirn * 2 + gi + 1], scale=-1.0)
            yield
            P.act(lf[:], e_[:], AF.Ln, [ekey], [lfkey], bias=1.0)
            yield
        else:
            qh, qhkey = qh_r.next()
            P.dma(qh[:], FMB[22 + gi, :, cs], reads=[("FMB", 22 + gi, _st(tau))], writes=[qhkey])
            fr, frkey = ff_r.next()
            fft = (2 if bwd else 0) + gi
            P.dma(fr[:], FMF[fft, :, cs], reads=[("FMF", fft, _st(tau))], writes=[frkey])
            yield
            e_, ekey = wr[1].next()
            P.act(e_[:], fr[:], AF.Exp, [frkey], [ekey], scale=-1.0)
            yield
            P.act(e_[:], e_[:], AF.Ln, [ekey], [ekey], bias=1.0)
            yield
            P.act(e_[:], e_[:], AF.Exp, [ekey], [ekey], scale=-1.0)
            yield
            col = l * 4 + dirn * 2 + gi
            fgate, fkey = wr[2].next()
            P.ts("dve", fgate[:], e_[:], omlb[:, col:col + 1], lbt[:, col:col + 1], ALU.mult, ALU.add,
                 [ekey, "omlb", "lbt"], [fkey])
            yield
            P.act(lf[:], fgate[:], AF.Ln, [fkey], [lfkey])
            kk, kkkey = wr[3].next()
            P.ts("pool", kk[:], fgate[:], -1.0, 1.0, ALU.mult, ALU.add, [fkey], [kkkey])
            yield
        Gc, Gkey = wr[4].next()
        if bwd:
            P.op("dve", _scan(Gc[:, ::-1], scanmask[:, mi, :], lf[:, ::-1]), [lfkey, "scanmask"], [Gkey])
        else:
            P.op("dve", _scan(Gc[:], scanmask[:, mi, :], lf[:]), [lfkey, "scanmask"], [Gkey])
        yield
        Gv = c3(Gc[:])
        mid = (C // 2) if bwd else (C // 2 - 1)
        lastp = 0 if bwd else C - 1
        sm, smkey = sm_r.next()
        seng = "dve" if gla else "pool"
        P.tt(seng, sm[:, 1, 0:nch], Gv[:, :, lastp], Gv[:, :, mid], ALU.subtract, [Gkey], [smkey])
        P.cp(seng, sm[:, 0, 0:nch], Gv[:, :, mid], [Gkey], [smkey])
        P.cp(seng, sm[:, 2, 0:nch], Gv[:, :, lastp], [Gkey], [smkey])
        Gm, Gmkey = wr[5].next()
        P.tt("dve", c3(Gm[:]), Gv, Gv[:, :, mid:mid + 1].to_broadcast([128, nch, C]), ALU.subtract, [Gkey], [Gmkey])
        yield
        P.act(sm[:, 0:3, 0:nch], sm[:, 0:3, 0:nch], AF.Exp, [smkey], [smkey], scale=kappa)
        A_, Akey = wr[6].next(); B_, Bkey = wr[7].next()
        P.act(A_[:], Gm[:], AF.Exp, [Gmkey], [Akey], scale=kappa)
        yield
        P.act(B_[:], Gm[:], AF.Exp, [Gmkey], [Bkey], scale=-kappa)
        yield
        Qf, Qfkey = wr[8].next()
        Kt, Ktkey = kt_r.next(); Kout, Koutkey = kout_r.next()
        if gla and not is_ctx:
            ts_ = slice((tau - 2) * 128, (tau - 1) * 128)
            r1, r1key = wr[2].next(); r2, r2key = wr[3].next()
            P.tt("dve", r1[:], qk[:, 0, :], ropecos[:, ts_], ALU.mult, [qkkey, "ropecos"], [r1key])
            P.tt("pool", r2[:], qk[:, 1, :], ropesin[:, ts_], ALU.mult, [qkkey, "ropesin"], [r2key])
            yield
            P.tt("dve", r1[:], r1[:], r2[:], ALU.add, [r1key, r2key], [r1key])
            r3, r3key = wr[2].next(); r4, r4key = wr[3].next()
            P.tt("pool", r3[:], qk[:, 2, :], ropecos[:, ts_], ALU.mult, [qkkey, "ropecos"], [r3key])
            yield
            P.tt("dve", Qf[:], r1[:], A_[:], ALU.mult, [r1key, Akey], [Qfkey])
            P.tt("pool", r4[:], qk[:, 3, :], ropesin[:, ts_], ALU.mult, [qkkey, "ropesin"], [r4key])
            yield
            P.tt("pool", r3[:], r3[:], r4[:], ALU.add, [r3key, r4key], [r3key])
            yield
            P.tt("pool", Kt[:], r3[:], B_[:], ALU.mult, [r3key, Bkey], [Ktkey])
            yield
        elif gla:
            P.tt("dve", Qf[:], qk[:, 0, :], A_[:], ALU.mult, [qkkey, Akey], [Qfkey])
            P.tt("pool", Kt[:], qk[:, 2, :], B_[:], ALU.mult, [qkkey, Bkey], [Ktkey])
            yield
        else:
            P.tt("dve", Qf[:], qh[:], A_[:], ALU.mult, [qhkey, Akey], [Qfkey])
            P.tt("pool", Kt[:], kk[:], B_[:], ALU.mult, [kkkey, Bkey], [Ktkey])
            yield
        Qbd, Qbdkey = qbd_rs[0 if gla else 1].next()
        jhc = lambda ap: ap.rearrange("p (j h c) -> p j h c", h=2, c=C)
        P.cp("act", jhc(Qbd[0:64].rearrange("p a b -> p (a b)"))[:, :, 0, :], c3(Qf[0:64, :]), [Qfkey], [Qbdkey])
        yield
        P.cp("act", jhc(Qbd[64:128].rearrange("p a b -> p (a b)"))[:, :, 1, :], c3(Qf[64:128, :]), [Qfkey], [Qbdkey])
        Qin, Qinkey = qin_rs[0 if gla else 1].next()
        for h in range(2):
            hs = slice(h * 64, (h + 1) * 64)
            P.tt("dve" if gla else "pool", jhc(Qin[hs].rearrange("p a b -> p (a b)"))[:, :, h, :], c3(Qf[hs, :]),
                 sm[hs, 0, 0:nch].unsqueeze(2).to_broadcast([64, nch, C]), ALU.mult, [Qfkey, smkey], [Qinkey])
        P.tt("pool", c3(Kout[:]), c3(Kt[:]), sm[:, 1, 0:nch].unsqueeze(2).to_broadcast([128, nch, C]), ALU.mult,
             [Ktkey, smkey], [Koutkey])
        yield
        tpp, tpkey = tp_ps.next()
        P.tr(tpp[:, 0:128], Kout[:], identb[:], [Koutkey, "identb"], [tpkey])
        koT, koTkey = koT_r.next()
        P.cp("act", koT[:], tpp[:, 0:128], [tpkey], [koTkey])
        yield
        scp, sckey = sc_ps.next()
        P.mm(scp[:, 0:256], Kt[:], Qbd[:].rearrange("p h t -> p (h t)"), [Ktkey, Qbdkey], [sckey])
        PT, PTkey = pt_r.next()
        tmi = (2 if gla else 0) + (1 if bwd else 0)
        for h in range(2):
            P.tt("dve", PT[:, h, :], scp[:, 0:256], trimask[:, tmi * 2 + h, :], ALU.mult, [sckey, "trimask"], [PTkey])
            yield
        vx = vxkey = None
        if not gla:
            vx, vxkey = vexp_r.next()
            P.tt("pool", vx[:], vt[:].rearrange("p (h d) -> p h d", h=2).unsqueeze(2).to_broadcast([128, 2, 8, 64]),
                 cm[:].unsqueeze(1).unsqueeze(3).to_broadcast([128, 2, 8, 64]), ALU.mult, [vkey, "cm"], [vxkey])
            yield
        ctx.update(dict(vt=vt, vkey=vkey, koT=koT, koTkey=koTkey, vx=vx, vxkey=vxkey, PT=PT, PTkey=PTkey, Qin=Qin,
                        Qinkey=Qinkey, sm=sm, smkey=smkey, b0=b0, nsl=nsl, C=C, nch=nch, gla=gla, gi=gi, bwd=bwd,
                        is_ctx=is_ctx, cs=cs, g=g, tau=tau))

    def group_back(ctx):
        vt = ctx["vt"]; vkey = ctx["vkey"]; koT = ctx["koT"]; koTkey = ctx["koTkey"]; vx = ctx["vx"]; vxkey = ctx["vxkey"]
        PT = ctx["PT"]; PTkey = ctx["PTkey"]; Qin = ctx["Qin"]; Qinkey = ctx["Qinkey"]; sm = ctx["sm"]; smkey = ctx["smkey"]
        b0 = ctx["b0"]; nsl = ctx["nsl"]; C = ctx["C"]; nch = ctx["nch"]; gla = ctx["gla"]; gi = ctx["gi"]; bwd = ctx["bwd"]
        is_ctx = ctx["is_ctx"]; cs = ctx["cs"]; g = ctx["g"]; tau = ctx["tau"]
        kvp, kvkey = kv_ps.next()
        if gla:
            for h in range(2):
                P.mm(kvp[h * 64:(h + 1) * 64, 0:64], koT[:, h * 64:(h + 1) * 64], vt[:, h * 64:(h + 1) * 64],
                     [koTkey, vkey], [kvkey], tp=(0, h * 64))
        else:
            for h in range(2):
                P.mm(kvp[h * 64:(h + 1) * 64, :], koT[:, h * 64:(h + 1) * 64],
                     vx[:, h, :, :].rearrange("p j d -> p (j d)"), [koTkey, vxkey], [kvkey], tp=(0, h * 64))
        yield
        op_, okey = o_ps.next()
        for h in range(2):
            P.mm(op_[0:64, 0:256], vt[:, h * 64:(h + 1) * 64], PT[:, h, :], [vkey, PTkey], [okey],
                 start=(h == 0), stop=False, nogrp=True)
        yield
        opv = op_[0:64, 0:256].rearrange("p (j h c) -> p h j c", h=2, c=C)
        jorder = list(range(nch - 1, -1, -1)) if bwd else list(range(nch))
        for jj, j in enumerate(jorder):
            s_in = (b0 + jj) % nsl
            s_out = (b0 + jj + 1) % nsl
            sbf, sbkey = sbf_r.next()
            P.cp("act", sbf[:], S[g][:, s_in, :], [f"S{g}_{s_in}"], [sbkey])
            P.mm(op_[0:64, j * 2 * C:(j + 1) * 2 * C], sbf[:], Qin[:].rearrange("p a b -> p (a b)")[:, j * 2 * C:(j + 1) * 2 * C],
                 [sbkey, Qinkey], [okey], start=False, stop=True, nogrp=True)
            P.stt("dve", S[g][:, s_out, :], S[g][:, s_in, :], sm[:, 2, j:j + 1], kvp[:, j * 64:(j + 1) * 64], ALU.mult, ALU.add,
                  [f"S{g}_{s_in}", smkey, kvkey], [f"S{g}_{s_out}"])
            yield
        hv = lambda ap: ap.rearrange("(h d) t -> d h t", h=2)
        if bwd:
            osb, oskey = o_r.next()
            P.cp("act", osb[:].rearrange("p (h j c) -> p h j c", h=2, c=C), opv, [okey], [oskey])
            P.dma(hv(OB[g, :, cs]), osb[:].rearrange("p (h t) -> p h t", h=2), reads=[oskey], writes=[("OB", g, tau)], eng="pool")
            yield
        elif not (is_ctx and not need_ctx):
            obt, obtkey = ob_r.next()
            P.dma(obt[:].rearrange("p (h t) -> p h t", h=2), hv(OB[g, :, cs]), reads=[("OB", g, tau)], writes=[obtkey])
            gt_, gtkey = gate_r.next()
            gft = (20 + gi) if gla else (24 + gi)
            P.dma(gt_[:].rearrange("p (h t) -> p h t", h=2), hv(FMB[gft, :, cs]), reads=[("FMB", gft, _st(tau))], writes=[gtkey])
            osb, oskey = o_r.next()
            P.tt("dve", osb[:].rearrange("p (h j c) -> p h j c", h=2, c=C), opv,
                 obt[:].rearrange("p (h j c) -> p h j c", h=2, c=C), ALU.add, [okey, obtkey], [oskey])
            yield
            osq, osqkey = sq_r.next()
            P.act(osq[:], osb[:], AF.Square, [oskey], [osqkey])
            yield
            zp, zkey = z_ps.next()
            P.mm(zp[0:64, 0:256], onesblk[0:64, 0:64], osq[:], ["onesblk", osqkey], [zkey])
            rs, rskey = o_r.next()
            P.act(rs[:], zp[0:64, 0:256], AF.Ln, [zkey, "eps"], [rskey], bias=eps_t[0:64, :])
            yield
            P.act(rs[:], rs[:], AF.Exp, [rskey], [rskey], scale=-0.5)
            yield
            nw = G["glanw"] if gla else G["hgnw"]
            P.stt("dve", osb[:], osb[:], nw[0:64, :], rs[:], ALU.mult, ALU.mult, [oskey, rskey, "glanw", "hgnw"], [oskey])
            yield
            og, ogkey = og_r.next()
            P.tt("pool", og[:], osb[:], gt_[:], ALU.mult, [oskey, gtkey], [ogkey])
            P.dma(hv(OT[4 + g, :, cs]), og[:].rearrange("p (h t) -> p h t", h=2), reads=[ogkey], writes=[("OT", 4 + g, tau)], eng="pool")
            yield

    for dirn in DBG["p2_dirs"]:
        bwd = dirn == 1
        order = [1, 0] + list(range(NT - 1, 1, -1)) if bwd else list(range(NT))
        for g in range(4):
            base[g] = 0
            P.memset("dve", S[g][:, 0, :], 0.0, [f"S{g}_0"])
        if DBG["p2_tiles"]:
            order = order[:DBG["p2_tiles"]]
        backs = []
        for tau in order:
            for pair in ((0, 2), (1, 3)):
                ctxs = [dict() for g in pair if g in DBG["p2_groups"]]
                fronts = [group_front(dirn, tau, g, c_) for g, c_ in zip([g for g in pair if g in DBG["p2_groups"]], ctxs)]
                _interleave(backs + fronts)
                backs = [group_back(c_) for c_ in ctxs]
        _interleave(backs)


def _st(tau):
    return 0 if tau < 2 else 2 + 4 * ((tau - 2) // 4)


def _scan(out, d0, d1):
    return lambda e: e.tensor_tensor_scan(out, d0, d1, 0.0, ALU.mult, ALU.add)


def build_p3(P, l, need_ctx, FMB, TM, OT, ttab, ones64):
    KTb = Ring(P, "KTb", 2, [128, NTOK], BF16)
    Vd = Ring(P, "Vd", 2, [128, 68, 64], BF16)
    q_r = Ring(P, "q3", 2, [128, 512], BF16); g_r = Ring(P, "g3", 2, [128, 512], BF16)
    pe_r = Ring(P, "pe3", 4, [128, 512], BF16); p_r = Ring(P, "p3", 4, [128, 512], BF16)
    rc_r = Ring(P, "rc3", 2, [128, 512]); o_r = Ring(P, "o3", 2, [128, 512]); og_r = Ring(P, "og3", 2, [128, 512], BF16)
    s_ps = Ring(P, "s_ps", 3, [128, 512], psum=True)
    ao_ps = Ring(P, "ao_ps", 2, [128, 512], psum=True); as_ps = Ring(P, "as_ps", 2, [128, 512], psum=True)
    TMv = TM.rearrange("(r k) c -> k r c", k=64)

    def row_interval(rp):
        if rp <= 7:
            return 0, rp + 4
        if rp >= 56:
            return rp - 3, 63
        return rp - 3, rp + 4

    for hp in range(4):
        kt_, ktkey = KTb.next()
        P.dma(kt_[:], FMB[4 + hp, :, :], reads=[("FMB", 4 + hp, s) for s in [0] + [2 + 4 * i for i in range(8)]], writes=[ktkey])
        vd, vdkey = Vd.next()
        for hh in range(2):
            for part in range(4):
                r0, r1 = part * 17, (part + 1) * 17
                P.dma(vd[hh * 64:(hh + 1) * 64, r0:r1, :], TMv[:, r0:r1, hp * 128 + hh * 64:hp * 128 + (hh + 1) * 64],
                      reads=[("TM", t_) for t_ in range(NT)], writes=[vdkey])
        blocks = [("lat", b) for b in range(8)] + ([("ctx", 0)] if need_ctx else [])
        for kind, b in blocks:
            if kind == "lat":
                tok0 = CTX + b * 512
                nq = 512
                pieces = [(rho, 0, 512, None) for rho in range(4)]
                for rp in range(64):
                    lo, hi = row_interval(rp)
                    a = max(lo, 8 * b); e_ = min(hi, 8 * b + 7)
                    if a > e_:
                        continue
                    u0 = 7 - rp + a
                    pieces.append((4 + rp, (a - 8 * b) * 64, (e_ - 8 * b + 1) * 64, u0))
            else:
                tok0 = 0
                nq = 256
                pieces = [(rho, 0, 256, None) for rho in range(4)]
            qt, qkey = q_r.next(); gt_, gkey = g_r.next()
            st_reads = sorted(set(_st(t_) for t_ in range(tok0 // 128, (tok0 + nq) // 128)))
            P.dma(qt[:, 0:nq], FMB[hp, :, tok0:tok0 + nq], reads=[("FMB", hp, s) for s in st_reads], writes=[qkey])
            P.dma(gt_[:, 0:nq], FMB[8 + hp, :, tok0:tok0 + nq], reads=[("FMB", 8 + hp, s) for s in st_reads], writes=[gkey])
            ao, aokey = ao_ps.next(); as_, askey = as_ps.next()
            staged = {}

            def stage_a(pi):
                rho, c0, c1, u0 = pieces[pi]
                sp_, spkey = s_ps.next()
                for hh in range(2):
                    P.mm(sp_[hh * 64:(hh + 1) * 64, c0:c1], kt_[hh * 64:(hh + 1) * 64, rho * 64:(rho + 1) * 64],
                         qt[hh * 64:(hh + 1) * 64, c0:c1], [ktkey, qkey], [spkey], tp=(hh * 64, hh * 64))
                pe_, pekey = pe_r.next()
                P.act(pe_[:, c0:c1], sp_[:, c0:c1], AF.Exp, [spkey], [pekey])
                if u0 is not None:
                    pp, ppkey = p_r.next()
                    nr = (c1 - c0) // 64
                    P.tt("dve", pp[:, c0:c1], pe_[:, c0:c1], ttab[:, hp, u0 * 64:(u0 + nr) * 64], ALU.mult,
                         [pekey, "ttab"], [ppkey])
                else:
                    pp, ppkey = pe_, pekey
                staged[pi] = (pp, ppkey)

            def stage_b(pi):
                rho, c0, c1, u0 = pieces[pi]
                pp, ppkey = staged.pop(pi)
                for hh in range(2):
                    P.mm(ao[hh * 64:(hh + 1) * 64, c0:c1], vd[hh * 64:(hh + 1) * 64, rho, :], pp[hh * 64:(hh + 1) * 64, c0:c1],
                         [vdkey, ppkey], [aokey], start=(pi == 0), stop=(pi == len(pieces) - 1), tp=(hh * 64, hh * 64))
                    P.mm(as_[hh * 64:(hh + 1) * 64, c0:c1], ones64[hh * 64:(hh + 1) * 64, :], pp[hh * 64:(hh + 1) * 64, c0:c1],
                         ["ones64", ppkey], [askey], start=(pi == 0), stop=(pi == len(pieces) - 1), tp=(hh * 64, hh * 64))

            LA = 2
            for pi in range(min(LA, len(pieces))):
                stage_a(pi)
            for pi in range(len(pieces)):
                if pi + LA < len(pieces):
                    stage_a(pi + LA)
                stage_b(pi)
            rc, rckey = rc_r.next()
            P.recip(rc[:, 0:nq], as_[:, 0:nq], [askey], [rckey])
            ob, obkey = o_r.next()
            P.tt("dve", ob[:, 0:nq], ao[:, 0:nq], rc[:, 0:nq], ALU.mult, [aokey, rckey], [obkey])
            og, ogkey = og_r.next()
            P.tt("pool", og[:, 0:nq], ob[:, 0:nq], gt_[:, 0:nq], ALU.mult, [obkey, gkey], [ogkey])
            P.dma(OT[hp, :, tok0:tok0 + nq], og[:, 0:nq], reads=[ogkey],
                  writes=[("OT", hp, t_) for t_ in range(tok0 // 128, (tok0 + nq) // 128)], eng="pool")


_CACHE = {}


def kernel(**inputs):
    inputs = {k: np.asarray(v) for k, v in inputs.items()}
    common, per = host_layout(inputs)
    if "nc" not in _CACHE:
        _CACHE["nc"] = build_program()
    nc = _CACHE["nc"]
    in_maps = [dict(common, **p) for p in per]
    res = run_bass_kernel_spmd(nc, in_maps, core_ids=list(range(len(per))))
    out = np.stack([np.asarray(r["out"], dtype=np.float32) for r in res.results], axis=0)
    return out
```

```python
import numpy as np
import ml_dtypes
from contextlib import ExitStack
import concourse.bass as bass
import concourse.mybir as mybir
from concourse.bass_utils import run_bass_kernel_spmd

F32 = mybir.dt.float32
BF16 = mybir.dt.bfloat16
AF = mybir.ActivationFunctionType
ALU = mybir.AluOpType
NPBF = ml_dtypes.bfloat16

ENGS = ("pe", "act", "dve", "pool", "sp")
SAME_ENGINE_SYNC = {"pe": False, "act": True, "dve": True, "pool": True, "sp": False}
N_DMA_SEMS = {"sp": 20, "act": 4, "pool": 12}

D = 1024
KT = 8
T = 4096
CTX = 256
NTOK = T + CTX
NT = NTOK // 128
NEXT = 4992
N_FMB = 26
N_FMF = 5
EPS = 1e-6
CHG = 32


class Ins:
    __slots__ = ("eng", "fn", "deps", "is_dma", "dma_sem", "dma_val", "inc_idx", "needed", "pos", "guard")

    def __init__(self, eng, fn, is_dma):
        self.eng = eng
        self.fn = fn
        self.deps = []
        self.is_dma = is_dma
        self.dma_sem = None
        self.dma_val = None
        self.inc_idx = None
        self.needed = False
        self.guard = None


class Prog:
    def __init__(self):
        self.nc = bass.Bass("TRN2", target_bir_lowering=False)
        self.es = ExitStack()
        self.streams = {e: [] for e in ENGS}
        self.last_w = {}
        self.readers = {}
        self.dma_rr = {e: 0 for e in N_DMA_SEMS}
        self.dma_last = {}
        self.dma_cnt = {}
        self.n_ins = 0
        self.scopes = []

    def push(self):
        self.scopes.append(ExitStack())

    def pop(self):
        self.barrier()
        self.scopes.pop().close()

    def _ctx(self):
        return self.scopes[-1] if self.scopes else self.es

    def sb(self, name, shape, dtype=F32):
        self.uid = getattr(self, "uid", 0) + 1
        return self._ctx().enter_context(self.nc.sbuf_tensor(f"sb{self.uid}_{name}", list(shape), dtype))

    def ps(self, name, shape=(128, 512), dtype=F32):
        self.uid = getattr(self, "uid", 0) + 1
        return self._ctx().enter_context(self.nc.psum_tensor(f"ps{self.uid}_{name}", list(shape), dtype))

    def dram(self, name, shape, dtype=F32, kind="Internal"):
        return self.nc.dram_tensor(name, list(shape), dtype, kind=kind)

    def op(self, eng, fn, reads=(), writes=(), dma=False):
        ins = Ins(eng, fn, dma)
        ins.pos = self.n_ins
        self.n_ins += 1
        deps = []
        for r in reads:
            w = self.last_w.get(r)
            if w is not None:
                deps.append(w)
        for wkey in writes:
            w = self.last_w.get(wkey)
            if w is not None:
                deps.append(w)
            deps.extend(self.readers.get(wkey, {}).values())
        if dma:
            slot = self.dma_rr[eng] % N_DMA_SEMS[eng]
            self.dma_rr[eng] += 1
            key = (eng, slot)
            prev = self.dma_last.get(key)
            if prev is not None:
                ins.guard = prev
            self.dma_cnt[key] = self.dma_cnt.get(key, 0) + 1
            ins.dma_sem = key
            ins.dma_val = 16 * self.dma_cnt[key]
            self.dma_last[key] = ins
        best = {}
        for d in deps:
            if d is ins:
                continue
            k = d.dma_sem if d.is_dma else d.eng
            if k not in best or best[k].pos < d.pos:
                best[k] = d
        for d in best.values():
            ins.deps.append(d)
            d.needed = True
        if ins.guard is not None:
            ins.guard.needed = True
        mykey = ins.dma_sem if dma else eng
        for r in reads:
            self.readers.setdefault(r, {})[mykey] = ins
        for wkey in writes:
            self.last_w[wkey] = ins
            self.readers[wkey] = {}
        self.streams[eng].append(ins)
        return ins

    def barrier(self):
        lasts = []
        for e in ENGS:
            for ins in reversed(self.streams[e]):
                if ins.fn is not None and not ins.is_dma:
                    lasts.append(ins)
                    break
        lasts += list(self.dma_last.values())
        for e in ENGS:
            b = Ins(e, None, False)
            b.pos = self.n_ins
            self.n_ins += 1
            for d in lasts:
                b.deps.append(d)
                d.needed = True
            self.streams[e].append(b)
        self.last_w = {}
        self.readers = {}

    def dma(self, out, in_, reads=(), writes=(), eng="sp"):
        return self.op(eng, lambda e: e.dma_start(out=out, in_=in_), reads, writes, dma=True)

    def mm(self, out, lhsT, rhs, reads, writes, start=True, stop=True, tp=None):
        if tp is None:
            return self.op("pe", lambda e: e.matmul(out, lhsT, rhs, start=start, stop=stop), reads, writes)
        return self.op("pe", lambda e: e.matmul(out, lhsT, rhs, start=start, stop=stop, tile_position=tp,
                                                skip_group_check=True), reads, writes)

    def tr(self, out, in_, ident, reads, writes):
        return self.op("pe", lambda e: e.transpose(out, in_, ident), reads, writes)

    def act(self, out, in_, func, reads, writes, bias=None, scale=None, accum_out=None):
        kw = {}
        if bias is not None:
            kw["bias"] = bias
        if scale is not None:
            kw["scale"] = scale
        if accum_out is not None:
            kw["accum_out"] = accum_out
        return self.op("act", lambda e: e.activation(out, in_, func, **kw), reads, writes)

    def tt(self, eng, out, in0, in1, op, reads, writes):
        return self.op(eng, lambda e: e.tensor_tensor(out, in0, in1, op), reads, writes)

    def ts(self, eng, out, in0, s1, s2, op0, op1, reads, writes):
        if s2 is None:
            return self.op(eng, lambda e: e.tensor_scalar(out, in0, s1, None, op0=op0), reads, writes)
        return self.op(eng, lambda e: e.tensor_scalar(out, in0, s1, s2, op0=op0, op1=op1), reads, writes)

    def stt(self, eng, out, in0, scalar, in1, op0, op1, reads, writes):
        return self.op(eng, lambda e: e.scalar_tensor_tensor(out, in0, scalar, in1, op0=op0, op1=op1), reads, writes)

    def cp(self, eng, out, in_, reads, writes):
        if eng == "act":
            return self.op("act", lambda e: e.copy(out, in_), reads, writes)
        return self.op(eng, lambda e: e.tensor_copy(out, in_), reads, writes)

    def recip(self, out, in_, reads, writes):
        return self.op("dve", lambda e: e.reciprocal(out, in_), reads, writes)

    def memset(self, eng, ap, val, writes):
        return self.op(eng, lambda e: e.memset(ap, val), (), writes)

    def finish(self, final_waits=()):
        nc = self.nc
        es = self.es
        sems = {e: es.enter_context(nc.semaphore("s_" + e)) for e in ENGS}
        dsems = {}
        for e, n in N_DMA_SEMS.items():
            for i in range(n):
                dsems[(e, i)] = es.enter_context(nc.semaphore(f"d_{e}{i}"))
        for fw in final_waits:
            fw.needed = True
        for e in ENGS:
            c = 0
            for ins in self.streams[e]:
                if ins.is_dma or ins.fn is None:
                    continue
                if ins.needed:
                    c += 1
                    ins.inc_idx = c
        streams = self.streams

        def emit(e, eng_obj):
            known = {}

            def wait_for(d):
                if d.is_dma:
                    k = ("d",) + d.dma_sem
                    if known.get(k, 0) >= d.dma_val:
                        return
                    eng_obj.wait_ge(dsems[d.dma_sem], d.dma_val)
                    known[k] = d.dma_val
                else:
                    if d.eng == e and not SAME_ENGINE_SYNC[e]:
                        return
                    k = ("e", d.eng)
                    if known.get(k, 0) >= d.inc_idx:
                        return
                    eng_obj.wait_ge(sems[d.eng], d.inc_idx)
                    known[k] = d.inc_idx

            for ins in streams[e]:
                for d in ins.deps:
                    wait_for(d)
                if ins.guard is not None:
                    wait_for(ins.guard)
                if ins.fn is None:
                    continue
                bi = ins.fn(eng_obj)
                if ins.is_dma:
                    bi.then_inc(dsems[ins.dma_sem], 16)
                elif ins.needed:
                    bi.then_inc(sems[e], 1)
            if e == "sp":
                for fw in final_waits:
                    wait_for(fw)

        with nc.Block() as block:
            @block.tensor
            def _(eng):
                emit("pe", eng)

            @block.scalar
            def _(eng):
                emit("act", eng)

            @block.vector
            def _(eng):
                emit("dve", eng)

            @block.gpsimd
            def _(eng):
                emit("pool", eng)

            @block.sync
            def _(eng):
                emit("sp", eng)
        while self.scopes:
            self.scopes.pop().close()
        self.es.close()
        return nc


class Ring:
    def __init__(self, P, name, n, shape, dtype=F32, psum=False):
        self.name = name
        if psum:
            self.bufs = [P.ps(f"{name}{i}", shape, dtype) for i in range(n)]
        else:
            self.bufs = [P.sb(f"{name}{i}", shape, dtype) for i in range(n)]
        self.i = 0

    def next(self):
        j = self.i % len(self.bufs)
        self.i += 1
        return self.bufs[j], f"{self.name}{j}"


def ext_col_index():
    cols = []
    cols += list(range(0, 512)) + list(range(512, 1024)) + list(range(1536, 2048))

    def pad_heads(base, swap):
        out = []
        for h in range(4):
            for i in range(32):
                j = (i + 8 if (i % 16) < 8 else i - 8) if swap else i
                out.append(base + h * 32 + j)
            out += [-1] * 32
        return out
    cols += pad_heads(2048, False) + pad_heads(2048, True) + pad_heads(2176, False) + pad_heads(2176, True)
    cols += list(range(2592, 2848)) + list(range(2848, 3104)) + list(range(3872, 4128))
    cols += list(range(3104, 3360)) + list(range(3360, 3616)) + list(range(2560, 2592)) + [-1] * 96
    cols += list(range(1024, 1536)) + list(range(2304, 2560)) + list(range(3616, 3872))
    assert len(cols) == NEXT
    return np.array(cols)


def host_consts():
    c = {}
    c["identf"] = np.eye(128, dtype=np.float32)
    c["identb"] = np.eye(128, dtype=np.float32).astype(NPBF)
    i = np.arange(128)
    m16 = np.ones((128, 128), np.float32); m16[:, i % 16 == 0] = 0
    m128 = np.ones((128, 128), np.float32); m128[:, 0] = 0
    m32 = np.ones((128, 128), np.float32); m32[:, i % 32 == 0] = 0
    c["scanmask"] = np.stack([m16, m128, m32], 0)
    s = i[:, None]; t = i[None, :]
    tri = []
    for C in (16, 128, 32):
        same = (s // C) == (t // C)
        tri.append((same & (s <= t)).astype(np.float32))
        tri.append((same & (s >= t)).astype(np.float32))
    c["trimask"] = np.stack(tri, 0)
    c["cm"] = ((i[:, None] // CHG) == np.arange(128 // CHG)[None, :]).astype(np.float32).astype(NPBF)
    c["onesblk"] = (((i[:, None] // 64) == (i[None, :] // 64)).astype(np.float32) / 64.0).astype(NPBF)
    c["ones64"] = np.ones((128, 64), np.float32).astype(NPBF)
    inv_freq = (1.0 / (10000.0 ** (np.arange(0, 16, 2, dtype=np.float32) / np.float32(16)))).astype(np.float32)
    tt_ = np.arange(T)
    cos = np.zeros((128, T), np.float32); sins = np.zeros((128, T), np.float32)
    for p in range(128):
        ii = p % 64
        if ii >= 32:
            continue
        blk = ii // 16
        j = ii % 16
        f = j % 8
        pos = (tt_ // 64 if blk == 0 else tt_ % 64).astype(np.float32)
        ang = (pos * inv_freq[f]).astype(np.float32)
        cos[p] = np.cos(ang)
        sins[p] = -np.sin(ang) if j < 8 else np.sin(ang)
    c["ropecos"] = cos
    c["ropesin"] = sins
    return c


def host_layout(inp):
    f32 = np.float32
    common = dict(host_consts())
    cols = ext_col_index()
    valid = cols >= 0
    DEPTH = inp["w_in"].shape[0]
    q = np.arange(64)
    col_start = np.clip(q - 8, 0, 48)
    kc = np.arange(64)
    inwin = (kc[:, None] >= col_start[None, :]) & (kc[:, None] < col_start[None, :] + 16)
    didx = np.clip(kc[:, None] - q[None, :], -15, 15) + 15
    pidx = np.arange(128)
    for l in range(DEPTH):
        w_ext = np.zeros((D, NEXT), f32)
        w_ext[:, valid] = inp["w_in"][l][:, cols[valid]]
        common[f"w_in{l}"] = w_ext
        common[f"ada_w{l}"] = np.ascontiguousarray(inp["ada_w"][l], dtype=f32)
        common[f"ada_bT{l}"] = np.ascontiguousarray(inp["ada_b"][l].reshape(24, 128).T, dtype=f32)
        common[f"ada_bgt{l}"] = np.ascontiguousarray(inp["ada_b"][l][2048:3072].reshape(1, 1024), dtype=f32)
        common[f"norm_wT{l}"] = np.ascontiguousarray(inp["norm_w"][l].reshape(8, 128).T, dtype=f32)
        common[f"w_out{l}"] = np.ascontiguousarray(inp["w_out"][l], dtype=f32)
        rpb = inp["na_rpb"][l]
        g = rpb[:, ::-1, :][:, :, didx]
        g = np.where(inwin[None, None], g, f32(-30000.0))
        g = g.transpose(0, 2, 1, 3).reshape(4, 128, 15 * 64)
        common[f"rpbt{l}"] = np.ascontiguousarray(g, dtype=f32)
        wgk = inp["gla_w_gk"][l]
        bgk = inp["gla_b_gk"][l]
        wp = np.zeros((32, 2, 2, 128), f32)
        bp = np.zeros((128, 2, 2), f32)
        for d_ in range(2):
            for g_ in range(2):
                for p in range(128):
                    h = 2 * g_ + p // 64
                    ii = p % 64
                    if ii < 32:
                        wp[d_ * 16:(d_ + 1) * 16, d_, g_, p] = wgk[d_, :, h * 32 + ii]
                        bp[p, d_, g_] = bgk[d_, h * 32 + ii]
        common[f"wgk{l}"] = wp
        common[f"bgk{l}"] = bp
        common[f"glanw{l}"] = np.ascontiguousarray(inp["gla_norm_w"][l][pidx % 64].reshape(128, 1), dtype=f32)
        common[f"hgnw{l}"] = np.ascontiguousarray(inp["hgrn_norm_w"][l][pidx % 64].reshape(128, 1), dtype=f32)
    lb = inp["hgrn_lower_bounds"]
    common["lbraw"] = np.ascontiguousarray(lb.reshape(2, 2, 2, 128).transpose(3, 0, 1, 2).reshape(128, 8), dtype=f32)
    common["fnw"] = np.ascontiguousarray(inp["final_norm_w"].reshape(1, D), dtype=f32)
    per = []
    B = inp["x"].shape[0]
    cc = inp["c_ctx"].reshape(8, 128).T
    for b in range(B):
        cv = np.concatenate([inp["c"][b].reshape(8, 128).T, cc], axis=1)
        per.append({"x": np.ascontiguousarray(inp["x"][b], dtype=f32),
                    "ctx": np.ascontiguousarray(inp["ctx"][b], dtype=f32),
                    "cvec": np.ascontiguousarray(cv, dtype=f32)})
    return common, per


FMB_EVAC = {}
for _t in range(0, 4):
    FMB_EVAC[_t] = ("scale", 0.125)
for _t in range(4, 8):
    FMB_EVAC[_t] = ("copy", 1.0)
for _t in range(8, 12):
    FMB_EVAC[_t] = ("silu", 1.0)
for _t in range(12, 16):
    FMB_EVAC[_t] = ("scale", 32.0 ** -0.5)
for _t in range(16, 20):
    FMB_EVAC[_t] = ("copy", 1.0)
for _t in (20, 21, 24, 25):
    FMB_EVAC[_t] = ("silu", 1.0)
for _t in (22, 23):
    FMB_EVAC[_t] = ("scale", 0.125)


import os
DBG = {"skip_p1": os.environ.get("K_DBG_SKIP_P1") == "1",
       "p2_tiles": int(os.environ.get("K_DBG_P2_TILES", "0")),
       "p2_groups": [int(x) for x in os.environ.get("K_DBG_P2_GROUPS", "0,1,2,3").split(",")],
       "p2_dirs": [int(x) for x in os.environ.get("K_DBG_P2_DIRS", "1,0").split(",")],
       "cut": int(os.environ.get("K_DBG_P2_CUT", "99"))}


def build_program(depth=2, stop_after=None, debug=False):
    P = Prog()
    nc = P.nc
    kdbg = "ExternalOutput" if debug else "Internal"

    def din(name, shape, dt=F32):
        return P.dram(name, shape, dt, kind="ExternalInput").ap()

    x_d = din("x", [T, D]); ctx_d = din("ctx", [CTX, D]); cvec_d = din("cvec", [128, 16])
    identf_d = din("identf", [128, 128]); identb_d = din("identb", [128, 128], BF16)
    scanmask_d = din("scanmask", [3, 128, 128]); trimask_d = din("trimask", [6, 128, 128])
    cm_d = din("cm", [128, 128 // CHG], BF16); onesblk_d = din("onesblk", [128, 128], BF16); ones64_d = din("ones64", [128, 64], BF16)
    ropecos_d = din("ropecos", [128, T]); ropesin_d = din("ropesin", [128, T])
    lbraw_d = din("lbraw", [128, 8]); fnw_d = din("fnw", [1, D])
    LW = []
    for l in range(2):
        LW.append(dict(
            w_in=din(f"w_in{l}", [D, NEXT]), ada_w=din(f"ada_w{l}", [D, 3 * D]), ada_bT=din(f"ada_bT{l}", [128, 24]),
            ada_bgt=din(f"ada_bgt{l}", [1, D]), norm_wT=din(f"norm_wT{l}", [128, 8]), w_out=din(f"w_out{l}", [D, D]),
            rpbt=din(f"rpbt{l}", [4, 128, 960]), wgk=din(f"wgk{l}", [32, 2, 2, 128]), bgk=din(f"bgk{l}", [128, 2, 2]),
            glanw=din(f"glanw{l}", [128, 1]), hgnw=din(f"hgnw{l}", [128, 1])))
    out_d = P.dram("out", [T, D], F32, kind="ExternalOutput").ap()
    FMB = P.dram("FMB", [N_FMB, 128, NTOK], BF16, kind=kdbg).ap()
    FMF = P.dram("FMF", [N_FMF, 128, NTOK], F32, kind=kdbg).ap()
    TM = P.dram("TM", [NTOK, 1024], BF16, kind=kdbg).ap()
    OB = P.dram("OB", [4, 128, NTOK], F32, kind=kdbg).ap()
    OT = P.dram("OT", [8, 128, NTOK], BF16, kind=kdbg).ap()
    XN = P.dram("XN", [T, D], F32, kind=kdbg).ap()
    XC = P.dram("XC", [CTX, D], F32, kind=kdbg).ap()
    MODDBG = P.dram("MODDBG", [128, 64], F32, kind=kdbg).ap()

    identf = P.sb("identf", [128, 128]); identb = P.sb("identb", [128, 128], BF16)
    onesblk = P.sb("onesblk", [128, 128], BF16); ones64 = P.sb("ones64", [128, 64], BF16)
    cvec = P.sb("cvec", [128, 16]); sil = P.sb("sil", [128, 16]); silrep = P.sb("silrep", [128, 16, 128])
    lbraw = P.sb("lbraw", [128, 8]); lbt = P.sb("lbt", [128, 8]); omlb = P.sb("omlb", [128, 8])
    eps_t = P.sb("eps_t", [128, 1])
    P.memset("dve", eps_t[:], EPS, ["eps"])
    for nm, dst, src in (("identf", identf, identf_d), ("identb", identb, identb_d), ("onesblk", onesblk, onesblk_d),
                         ("ones64", ones64, ones64_d), ("cvec", cvec, cvec_d), ("lbraw", lbraw, lbraw_d)):
        P.dma(dst[:], src, writes=[nm])
    tmp16 = P.sb("tmp16", [128, 16])
    P.act(tmp16[:], cvec[:], AF.Exp, ["cvec"], ["tmp16"], scale=-1.0)
    P.ts("dve", tmp16[:], tmp16[:], 1.0, None, ALU.add, None, ["tmp16"], ["tmp16"])
    P.recip(tmp16[:], tmp16[:], ["tmp16"], ["tmp16"])
    P.tt("dve", sil[:], cvec[:], tmp16[:], ALU.mult, ["cvec", "tmp16"], ["sil"])
    for j in range(16):
        P.cp("dve", silrep[:, j, :], sil[:, j:j + 1].to_broadcast([128, 128]), ["sil"], ["silrep"])
    P.memset("dve", lbt[:], 0.0, ["lbt"])
    P.tt("dve", lbt[:, 4:8], lbraw[:, 0:4], lbraw[:, 4:8], ALU.subtract, ["lbraw", "lbt"], ["lbt"])
    P.act(lbt[:, 4:8], lbt[:, 4:8], AF.Exp, ["lbt"], ["lbt"])
    P.ts("dve", lbt[:, 4:8], lbt[:, 4:8], 1.0, None, ALU.add, None, ["lbt"], ["lbt"])
    P.recip(lbt[:, 4:8], lbt[:, 4:8], ["lbt"], ["lbt"])
    P.ts("dve", omlb[:], lbt[:], -1.0, 1.0, ALU.mult, ALU.add, ["lbt"], ["omlb"])

    last_out = []

    for l in range(depth):
        W = LW[l]
        last = (l == depth - 1)
        need_ctx = not last
        x_src = x_d if l == 0 else XN
        c_src = ctx_d if l == 0 else XC
        tiles = list(range(NT))

        P.push()
        gsh = P.sb(f"gsh{l}", [128, 2, 2, 8])
        gtrep = P.sb(f"gtrep{l}", [128, 2, D])
        woutb = P.sb(f"woutb{l}", [128, 2 if need_ctx else 1, KT, D], BF16)
        ttab = P.sb(f"ttab{l}", [128, 4, 960], BF16)
        wgk = P.sb(f"wgk{l}", [32, 2, 2, 128]); nbgk = P.sb(f"nbgk{l}", [128, 4])
        glanw = P.sb(f"glanw{l}", [128, 1]); hgnw = P.sb(f"hgnw{l}", [128, 1])

        P.push()
        winb = P.sb("winb", [128, KT, NEXT], BF16)
        tr_ps = Ring(P, "tr_ps", 2, [128, 512], psum=True)
        fm_ps = Ring(P, "fm_ps", 3, [128, 512], psum=True)
        tm_ps = Ring(P, "tm_ps", 2, [128, 512], psum=True)
        modT = P.sb("modT", [128, 16, 2]); adabT = P.sb("adabT", [128, 24]); normwT = P.sb("normwT", [128, 8])
        adabgt = P.sb("adabgt", [128, D])
        stg = Ring(P, "stg", 2, [128, KT, 256])
        mod_ps = tm_ps.bufs[0]; gt_ps = fm_ps
        P.dma(adabT[:], W["ada_bT"], writes=["adabT"]); P.dma(normwT[:], W["norm_wT"], writes=["normwT"])
        P.dma(adabgt[:], W["ada_bgt"].partition_broadcast(128), writes=["adabgt"])
        P.dma(wgk[:], W["wgk"], writes=["wgk"]); P.dma(nbgk[:], W["bgk"].rearrange("p a b -> p (a b)"), writes=["nbgk"])
        P.dma(glanw[:], W["glanw"], writes=["glanw"]); P.dma(hgnw[:], W["hgnw"], writes=["hgnw"])
        P.ts("dve", nbgk[:], nbgk[:], -1.0, None, ALU.mult, None, ["nbgk"], ["nbgk"])
        adaw_v = W["ada_w"].rearrange("(kt p) j -> p kt j", p=128)
        sil2 = sil[:].rearrange("p (w k) -> p w k", w=2)
        for ch in range(12):
            sbuf, skey = stg.next()
            P.dma(sbuf[:], adaw_v[:, :, ch * 256:(ch + 1) * 256], writes=[skey])
            if ch < 8:
                for jl in range(2):
                    jt = ch * 2 + jl
                    for kt in range(KT):
                        P.mm(mod_ps[:, jt * 2:jt * 2 + 2], sbuf[:, kt, jl * 128:(jl + 1) * 128], sil2[:, :, kt],
                             [skey, "sil"], ["tm_ps0"], start=(kt == 0), stop=(kt == KT - 1))
                    P.ts("dve", modT[:, jt, :], mod_ps[:, jt * 2:jt * 2 + 2], adabT[:, jt:jt + 1], None, ALU.add, None,
                         ["tm_ps0", "adabT"], ["modT"])
            else:
                q4 = ch - 8
                for which in range(2):
                    gp, gk_ = gt_ps.next()
                    for kt in range(KT):
                        P.mm(gp[:, 0:256], silrep[:, which * 8 + kt, :], sbuf[:, kt, :], ["silrep", skey], [gk_],
                             start=(kt == 0), stop=(kt == KT - 1))
                    P.tt("dve", gtrep[:, which, q4 * 256:(q4 + 1) * 256], gp[:, 0:256], adabgt[:, q4 * 256:(q4 + 1) * 256],
                         ALU.add, [gk_, "adabgt"], ["gtrep"])
        for which in range(2):
            P.stt("dve", gsh[:, which, 0, :], modT[:, 8:16, which], 1.0, normwT[:], ALU.add, ALU.mult,
                  ["modT", "normwT"], ["gsh"])
            P.cp("dve", gsh[:, which, 1, :], modT[:, 0:8, which], ["modT"], ["gsh"])
        if debug:
            P.dma(MODDBG[:, 0:32], gsh[:].rearrange("p a b c -> p (a b c)"), reads=["gsh"])
        wout_v = W["w_out"].rearrange("(kt p) j -> p kt j", p=128)
        for q4 in range(4):
            sbuf, skey = stg.next()
            P.dma(sbuf[:], wout_v[:, :, q4 * 256:(q4 + 1) * 256], writes=[skey])
            for which in range(2 if need_ctx else 1):
                for kt in range(KT):
                    eng = "dve" if kt % 2 == 0 else "pool"
                    P.tt(eng, woutb[:, which, kt, q4 * 256:(q4 + 1) * 256], sbuf[:, kt, :],
                         gtrep[:, which, q4 * 256:(q4 + 1) * 256], ALU.mult, [skey, "gtrep"], ["woutb"])
        for hp in range(4):
            sbuf, skey = stg.next()
            rv = sbuf[:].rearrange("p k c -> p (k c)")
            P.dma(rv[:, 0:960], W["rpbt"][hp, :, :], writes=[skey])
            P.act(ttab[:, hp, :], rv[:, 0:960], AF.Exp, [skey], ["ttab"])
        if stop_after == (l, "prep"):
            break

        win_v = W["w_in"].rearrange("(kt p) j -> p kt j", p=128)
        nchunk = (NEXT + 255) // 256
        for ch in range(0 if (DBG["skip_p1"] and l == 0) else nchunk):
            c0 = ch * 256
            cw = min(256, NEXT - c0)
            sbuf, skey = stg.next()
            P.dma(sbuf[:, :, 0:cw], win_v[:, :, c0:c0 + cw], writes=[skey])
            for kt in range(KT):
                eng = ("dve", "pool", "act")[kt % 3]
                P.cp(eng, winb[:, kt, c0:c0 + cw], sbuf[:, kt, 0:cw], [skey], ["winb"])
        xt_r = Ring(P, "xt", 2, [128, D]); xh_r = Ring(P, "xh", 2, [128, D]); sq_scr = P.sb("sq_scr", [128, D])
        ss_r = Ring(P, "ss", 4, [128, 2])
        hT_r = Ring(P, "hT", 2, [128, KT, 512], BF16)
        fmo_b = Ring(P, "fmo_b", 4, [128, 512], BF16); fmo_f = Ring(P, "fmo_f", 2, [128, 512])
        tmo = Ring(P, "tmo", 2, [128, 1024], BF16)
        supers = [(0, 2)] + [(2 + 4 * i, 4) for i in range(8)]
        if DBG["skip_p1"] and l == 0:
            supers = []
        def prologue(t0, ntl):
            is_ctx = (t0 == 0)
            which = 1 if is_ctx else 0
            ntok = ntl * 128
            hT, hkey = hT_r.next()
            for ti in range(ntl):
                tau = t0 + ti
                xt, xkey = xt_r.next()
                src = c_src[tau * 128:(tau + 1) * 128, :] if is_ctx else x_src[(tau - 2) * 128:(tau - 1) * 128, :]
                P.dma(xt[:], src, writes=[xkey])
                ss, sskey = ss_r.next()
                P.memset("dve", ss[:], 0.0, [sskey])
                P.act(sq_scr[:], xt[:], AF.Square, [xkey, sskey], ["sq_scr", sskey], accum_out=ss[:, 0:1])
                P.act(ss[:, 1:2], ss[:, 0:1], AF.Ln, [sskey, "eps"], [sskey], bias=eps_t[:], scale=1.0 / D)
                P.act(ss[:, 1:2], ss[:, 1:2], AF.Exp, [sskey], [sskey], scale=-0.5)
                xh, xhkey = xh_r.next()
                P.ts("dve", xh[:], xt[:], ss[:, 1:2], None, ALU.mult, None, [xkey, sskey], [xhkey])
                for half in range(2):
                    tp, tpkey = tr_ps.next()
                    for q4 in range(4):
                        kt = half * 4 + q4
                        P.tr(tp[:, q4 * 128:(q4 + 1) * 128], xh[:, kt * 128:(kt + 1) * 128], identf[:], [xhkey, "identf"], [tpkey])
                    for q4 in range(4):
                        kt = half * 4 + q4
                        P.act(hT[:, kt, ti * 128:(ti + 1) * 128], tp[:, q4 * 128:(q4 + 1) * 128], AF.Identity,
                              [tpkey, "gsh"], [hkey], bias=gsh[:, which, 1, kt:kt + 1], scale=gsh[:, which, 0, kt:kt + 1])
            return hT, hkey

        def mainbody(t0, ntl, hT, hkey):
            ntok = ntl * 128
            for ft in range(N_FMB + N_FMF):
                fp, fpkey = fm_ps.next()
                for kt in range(KT):
                    P.mm(fp[:, 0:ntok], winb[:, kt, ft * 128:(ft + 1) * 128], hT[:, kt, 0:ntok], ["winb", hkey], [fpkey],
                         start=(kt == 0), stop=(kt == KT - 1))
                if ft < N_FMB:
                    ob, obkey = fmo_b.next()
                    kind, sc_ = FMB_EVAC[ft]
                    if kind == "silu":
                        P.act(ob[:, 0:ntok], fp[:, 0:ntok], AF.Silu, [fpkey], [obkey])
                    else:
                        P.act(ob[:, 0:ntok], fp[:, 0:ntok], AF.Copy, [fpkey], [obkey], scale=float(sc_))
                    P.dma(FMB[ft, :, t0 * 128:t0 * 128 + ntok], ob[:, 0:ntok], reads=[obkey], writes=[("FMB", ft, t0)], eng="pool")
                else:
                    ob, obkey = fmo_f.next()
                    P.cp("dve", ob[:, 0:ntok], fp[:, 0:ntok], [fpkey], [obkey])
                    P.dma(FMF[ft - N_FMB, :, t0 * 128:t0 * 128 + ntok], ob[:, 0:ntok], reads=[obkey],
                          writes=[("FMF", ft - N_FMB, t0)], eng="pool")
            cbase = (N_FMB + N_FMF) * 128
            for ti in range(ntl):
                tau = t0 + ti
                ob, obkey = tmo.next()
                for half in range(2):
                    tp2, tp2key = tm_ps.next()
                    for kt in range(KT):
                        P.mm(tp2[:], hT[:, kt, ti * 128:(ti + 1) * 128], winb[:, kt, cbase + half * 512:cbase + (half + 1) * 512],
                             [hkey, "winb"], [tp2key], start=(kt == 0), stop=(kt == KT - 1))
                    P.cp("dve", ob[:, half * 512:(half + 1) * 512], tp2[:], [tp2key], [obkey])
                P.dma(TM[tau * 128:(tau + 1) * 128, :], ob[:], reads=[obkey], writes=[("TM", tau)], eng="pool")

        cur = prologue(*supers[0]) if supers else None
        for i_, (t0, ntl) in enumerate(supers):
            nxt = prologue(*supers[i_ + 1]) if i_ + 1 < len(supers) else None
            mainbody(t0, ntl, *cur)
            cur = nxt
        P.pop()
        if stop_after == (l, "p1"):
            break

        P.push()
        build_p2(P, l, need_ctx, FMB, FMF, TM, OB, OT, dict(
            identb=identb, onesblk=onesblk, wgk=wgk, nbgk=nbgk, glanw=glanw, hgnw=hgnw, lbt=lbt, omlb=omlb, eps_t=eps_t,
            scanmask_d=scanmask_d, trimask_d=trimask_d, cm_d=cm_d, ropecos_d=ropecos_d, ropesin_d=ropesin_d))
        P.pop()
        if stop_after == (l, "p2"):
            break

        P.push()
        build_p3(P, l, need_ctx, FMB, TM, OT, ttab, ones64)
        P.pop()
        if stop_after == (l, "p3"):
            break

        P.push()
        ot_r = Ring(P, "ot", 2, [128, 8, 128], BF16); xr = Ring(P, "x4", 2, [128, D]); xo = Ring(P, "xo", 2, [128, D])
        y_ps = Ring(P, "y_ps", 4, [128, 512], psum=True)
        ss_r = Ring(P, "ss4", 4, [128, 2]); sq_scr = P.sb("sq4", [128, D]); fnw = P.sb("fnw", [128, D])
        if last:
            P.dma(fnw[:], fnw_d.partition_broadcast(128), writes=["fnw"])
        otv = OT.rearrange("f p t -> p f t")
        for tau in (range(NT) if need_ctx else range(2, NT)):
            is_ctx = tau < 2
            which = 1 if is_ctx else 0
            ot, otkey = ot_r.next()
            P.dma(ot[:], otv[:, :, tau * 128:(tau + 1) * 128], reads=[("OT", f, tau) for f in range(8)], writes=[otkey])
            xt, xkey = xr.next()
            rows = slice(tau * 128, (tau + 1) * 128) if is_ctx else slice((tau - 2) * 128, (tau - 1) * 128)
            src = c_src[rows, :] if is_ctx else x_src[rows, :]
            P.dma(xt[:], src, reads=[("XR", l, tau)], writes=[xkey])
            xn, xnkey = xo.next()
            for half in range(2):
                yp, ypkey = y_ps.next()
                for f in range(8):
                    P.mm(yp[:], ot[:, f, :], woutb[:, which, f, half * 512:(half + 1) * 512], [otkey, "woutb"], [ypkey],
                         start=(f == 0), stop=(f == 7))
                P.tt("dve", xn[:, half * 512:(half + 1) * 512], yp[:], xt[:, half * 512:(half + 1) * 512], ALU.add,
                     [ypkey, xkey], [xnkey])
            if not last:
                dst = XC[rows, :] if is_ctx else XN[rows, :]
                P.dma(dst, xn[:], reads=[xnkey], writes=[("XR", l + 1, tau)], eng="pool")
            else:
                ss, sskey = ss_r.next()
                P.memset("dve", ss[:], 0.0, [sskey])
                P.act(sq_scr[:], xn[:], AF.Square, [xnkey, sskey], ["sq4", sskey], accum_out=ss[:, 0:1])
                P.act(ss[:, 1:2], ss[:, 0:1], AF.Ln, [sskey, "eps"], [sskey], bias=eps_t[:], scale=1.0 / D)
                P.act(ss[:, 1:2], ss[:, 1:2], AF.Exp, [sskey], [sskey], scale=-0.5)
                P.stt("dve", xn[:], xn[:], ss[:, 1:2], fnw[:], ALU.mult, ALU.mult, [xnkey, sskey, "fnw"], [xnkey])
                last_out.append(P.dma(out_d[rows, :], xn[:], reads=[xnkey], eng="pool"))
        P.pop()
        P.pop()
        if stop_after == (l, "p4"):
            break

    finals = list(last_out) + [d for d in P.dma_last.values()]
    return P.finish(final_waits=finals)


def _interleave(gens):
    gens = list(gens)
    while gens:
        for gen in list(gens):
            try:
                next(gen)
            except StopIteration:
                gens.remove(gen)


VEXP_ENG = os.environ.get("K_VEXP_ENG", "dve")


def build_p2(P, l, need_ctx, FMB, FMF, TM, OB, OT, G):
    identb = G["identb"]; onesblk = G["onesblk"]; wgk = G["wgk"]; nbgk = G["nbgk"]
    lbt = G["lbt"]; omlb = G["omlb"]; eps_t = G["eps_t"]
    scanmask = P.sb("scanmask", [128, 3, 128]); trimask = P.sb("trimask", [128, 6, 128]); cm = P.sb("cm", [128, 128 // CHG], BF16)
    ropecos = P.sb("ropecos", [128, T]); ropesin = P.sb("ropesin", [128, T])
    P.dma(scanmask[:], G["scanmask_d"].rearrange("n p t -> p n t"), writes=["scanmask"])
    P.dma(trimask[:], G["trimask_d"].rearrange("n p t -> p n t"), writes=["trimask"])
    P.dma(cm[:], G["cm_d"], writes=["cm"])
    P.dma(ropecos[:], G["ropecos_d"], writes=["ropecos"]); P.dma(ropesin[:], G["ropesin_d"], writes=["ropesin"])
    NHC = 128 // CHG
    NSL = [3, 3, 2 * NHC + 1, 2 * NHC + 1]
    S = [P.sb(f"S{g}", [128, NSL[g], 64]) for g in range(4)]
    RD = 6
    qk_r = Ring(P, "qk", RD, [128, 4, 128], BF16); gk_r = Ring(P, "gkr", RD, [32, 128]); v_r = Ring(P, "vr", 8, [128, 128], BF16)
    qh_r = Ring(P, "qh", RD, [128, 128], BF16); ff_r = Ring(P, "ff", RD, [128, 128]); gate_r = Ring(P, "gate", RD, [128, 128], BF16)
    ob_r = Ring(P, "obr", RD, [128, 128])
    wr = [Ring(P, f"w{i}", 8, [128, 128]) for i in range(10)]
    sm_r = Ring(P, "sm", 8, [128, 4, 8])
    qbd_r = Ring(P, "qbd", RD, [128, 2, 128], BF16); kt_r = Ring(P, "ktr", RD, [128, 128], BF16)
    kout_r = Ring(P, "kout", RD, [128, 128], BF16)
    koT_r = Ring(P, "koT", RD, [128, 128], BF16); pt_r = Ring(P, "pt", 8, [128, 2, 128], BF16)
    vexp_r = Ring(P, "vexp", RD, [128, 2, NHC, 64], BF16)
    o_r = Ring(P, "osb", RD, [128, 128]); sq_r = Ring(P, "osq", RD, [128, 128], BF16); og_r = Ring(P, "og", RD, [128, 128], BF16)
    z_ps = Ring(P, "z_ps", 1, [128, 512], psum=True); tp_ps = Ring(P, "tp_ps", 1, [128, 1024], BF16, psum=True)
    sc_ps = Ring(P, "sc_ps", 2, [128, 512], psum=True); kv_ps = Ring(P, "kv_ps", 2, [128, 512], psum=True)
    o_ps = Ring(P, "o_ps", 2, [128, 512], psum=True)
    for i, b in enumerate(qbd_r.bufs):
        P.memset("pool", b[:], 0.0, [f"qbd{i}"])
    base = [0, 0, 0, 0]

    def group_front(dirn, tau, g, ctx):
        bwd = dirn == 1
        is_ctx = tau < 2
        cs = slice(tau * 128, (tau + 1) * 128)
        gla = g < 2
        gi = g % 2
        C = 128 if gla else CHG
        nch = 128 // C
        mi = 1 if gla else 2
        kappa = -1.0 / 16.0 if gla else 1.0
        nsl = NSL[g]
        b0 = base[g]
        base[g] = (b0 + nch) % nsl
        c3 = lambda ap: ap.rearrange("p (n c) -> p n c", c=C)
        vt, vkey = v_r.next()
        voff = (512 + gi * 128) if gla else (768 + gi * 128)
        P.dma(vt[:], TM[cs, voff:voff + 128], reads=[("TM", tau)], writes=[vkey])
        lf, lfkey = wr[0].next()
        if gla:
            qk, qkkey = qk_r.next()
            for n_, ft in enumerate((12 + gi, 14 + gi, 16 + gi, 18 + gi)):
                P.dma(qk[:, n_, :], FMB[ft, :, cs], reads=[("FMB", ft, _st(tau))], writes=[qkkey])
            gkt, gkkey = gk_r.next()
            P.dma(gkt[:], FMF[4, 0:32, cs], reads=[("FMF", 4, _st(tau))], writes=[gkkey])
            yield
            zp, zkey = z_ps.next()
            P.mm(zp[:, 0:128], wgk[:, dirn, gi, :], gkt[:], ["wgk", gkkey], [zkey])
            e_, ekey = wr[1].next()
            P.act(e_[:], zp[:, 0:128], AF.Exp, [zkey, "nbgk"], [ekey], bias=nbgk[:, dirn * 2 + gi:dirn * 2 + gi + 1], scale=-1.0)
            yield
            P.act(lf[:], e_[:], AF.Ln, [ekey], [lfkey], bias=1.0)
            yield
        else:
            qh, qhkey = qh_r.next()
            P.dma(qh[:], FMB[22 + gi, :, cs], reads=[("FMB", 22 + gi, _st(tau))], writes=[qhkey])
            fr, frkey = ff_r.next()
            fft = (2 if bwd else 0) + gi
            P.dma(fr[:], FMF[fft, :, cs], reads=[("FMF", fft, _st(tau))], writes=[frkey])
            yield
            e_, ekey = wr[1].next()
            P.act(e_[:], fr[:], AF.Exp, [frkey], [ekey], scale=-1.0)
            yield
            P.act(e_[:], e_[:], AF.Ln, [ekey], [ekey], bias=1.0)
            yield
            P.act(e_[:], e_[:], AF.Exp, [ekey], [ekey], scale=-1.0)
            yield
            col = l * 4 + dirn * 2 + gi
            fgate, fkey = wr[2].next()
            P.ts("dve", fgate[:], e_[:], omlb[:, col:col + 1], lbt[:, col:col + 1], ALU.mult, ALU.add,
                 [ekey, "omlb", "lbt"], [fkey])
            yield
            P.act(lf[:], fgate[:], AF.Ln, [fkey], [lfkey])
            kk, kkkey = wr[3].next()
            P.act(kk[:], fgate[:], AF.Identity, [fkey], [kkkey], bias=1.0, scale=-1.0)
            yield
        Gc, Gkey = wr[4].next()
        if bwd:
            P.op("dve", _scan(Gc[:, ::-1], scanmask[:, mi, :], lf[:, ::-1]), [lfkey, "scanmask"], [Gkey])
        else:
            P.op("dve", _scan(Gc[:], scanmask[:, mi, :], lf[:]), [lfkey, "scanmask"], [Gkey])
        yield
        Gv = c3(Gc[:])
        mid = (C // 2) if bwd else (C // 2 - 1)
        lastp = 0 if bwd else C - 1
        sm, smkey = sm_r.next()
        P.tt("dve", sm[:, 1, 0:nch], Gv[:, :, lastp], Gv[:, :, mid], ALU.subtract, [Gkey], [smkey])
        Gm, Gmkey = wr[5].next()
        P.tt("dve", c3(Gm[:]), Gv, Gv[:, :, mid:mid + 1].to_broadcast([128, nch, C]), ALU.subtract, [Gkey], [Gmkey])
        yield
        P.act(sm[:, 0, 0:nch], Gv[:, :, mid], AF.Exp, [Gkey], [smkey], scale=kappa)
        P.act(sm[:, 2, 0:nch], Gv[:, :, lastp], AF.Exp, [Gkey], [smkey], scale=kappa)
        P.act(sm[:, 1, 0:nch], sm[:, 1, 0:nch], AF.Exp, [smkey], [smkey], scale=kappa)
        A_, Akey = wr[6].next(); B_, Bkey = wr[7].next()
        P.act(A_[:], Gm[:], AF.Exp, [Gmkey], [Akey], scale=kappa)
        yield
        P.act(B_[:], Gm[:], AF.Exp, [Gmkey], [Bkey], scale=-kappa)
        yield
        Qf, Qfkey = wr[8].next()
        Kt, Ktkey = kt_r.next(); Kout, Koutkey = kout_r.next()
        if gla and not is_ctx:
            ts_ = slice((tau - 2) * 128, (tau - 1) * 128)
            r1, r1key = wr[2].next(); r2, r2key = wr[3].next()
            P.tt("dve", r1[:], qk[:, 0, :], ropecos[:, ts_], ALU.mult, [qkkey, "ropecos"], [r1key])
            P.tt("pool", r2[:], qk[:, 1, :], ropesin[:, ts_], ALU.mult, [qkkey, "ropesin"], [r2key])
            yield
            P.tt("dve", r1[:], r1[:], r2[:], ALU.add, [r1key, r2key], [r1key])
            r3, r3key = wr[2].next(); r4, r4key = wr[3].next()
            P.tt("pool", r3[:], qk[:, 2, :], ropecos[:, ts_], ALU.mult, [qkkey, "ropecos"], [r3key])
            yield
            P.tt("dve", Qf[:], r1[:], A_[:], ALU.mult, [r1key, Akey], [Qfkey])
            P.tt("pool", r4[:], qk[:, 3, :], ropesin[:, ts_], ALU.mult, [qkkey, "ropesin"], [r4key])
            yield
            P.tt("pool", r3[:], r3[:], r4[:], ALU.add, [r3key, r4key], [r3key])
            yield
            P.tt("pool", Kt[:], r3[:], B_[:], ALU.mult, [r3key, Bkey], [Ktkey])
            yield
        elif gla:
            P.tt("dve", Qf[:], qk[:, 0, :], A_[:], ALU.mult, [qkkey, Akey], [Qfkey])
            P.tt("pool", Kt[:], qk[:, 2, :], B_[:], ALU.mult, [qkkey, Bkey], [Ktkey])
            yield
        else:
            P.tt("dve", Qf[:], qh[:], A_[:], ALU.mult, [qhkey, Akey], [Qfkey])
            P.tt("pool", Kt[:], kk[:], B_[:], ALU.mult, [kkkey, Bkey], [Ktkey])
            yield
        Qbd, Qbdkey = qbd_r.next()
        P.cp("act", Qbd[0:64, 0, :], Qf[0:64, :], [Qfkey], [Qbdkey])
        yield
        P.cp("act", Qbd[64:128, 1, :], Qf[64:128, :], [Qfkey], [Qbdkey])
        Qin, Qinkey = wr[9].next()
        P.tt("dve" if gla else "pool", c3(Qin[:]), c3(Qf[:]), sm[:, 0, 0:nch].unsqueeze(2).to_broadcast([128, nch, C]), ALU.mult,
             [Qfkey, smkey], [Qinkey])
        P.tt("pool", c3(Kout[:]), c3(Kt[:]), sm[:, 1, 0:nch].unsqueeze(2).to_broadcast([128, nch, C]), ALU.mult,
             [Ktkey, smkey], [Koutkey])
        yield
        tpp, tpkey = tp_ps.next()
        P.tr(tpp[:, 0:128], Kout[:], identb[:], [Koutkey, "identb"], [tpkey])
        koT, koTkey = koT_r.next()
        P.cp("act", koT[:], tpp[:, 0:128], [tpkey], [koTkey])
        yield
        scp, sckey = sc_ps.next()
        P.mm(scp[:, 0:256], Kt[:], Qbd[:].rearrange("p h t -> p (h t)"), [Ktkey, Qbdkey], [sckey])
        PT, PTkey = pt_r.next()
        tmi = (2 if gla else 4) + (1 if bwd else 0)
        P.tt("dve", PT[:], scp[:, 0:256].rearrange("p (h t) -> p h t", h=2),
             trimask[:, tmi, :].unsqueeze(1).to_broadcast([128, 2, 128]), ALU.mult, [sckey, "trimask"], [PTkey])
        yield
        vx = vxkey = None
        if not gla:
            vx, vxkey = vexp_r.next()
            P.tt(VEXP_ENG, vx[:], vt[:].rearrange("p (h d) -> p h d", h=2).unsqueeze(2).to_broadcast([128, 2, NHC, 64]),
                 cm[:].unsqueeze(1).unsqueeze(3).to_broadcast([128, 2, NHC, 64]), ALU.mult, [vkey, "cm"], [vxkey])
            yield
        ctx.update(dict(vt=vt, vkey=vkey, koT=koT, koTkey=koTkey, vx=vx, vxkey=vxkey, PT=PT, PTkey=PTkey, Qin=Qin,
                        Qinkey=Qinkey, sm=sm, smkey=smkey, b0=b0, nsl=nsl, C=C, nch=nch, gla=gla, gi=gi, bwd=bwd,
                        is_ctx=is_ctx, cs=cs, g=g, tau=tau))

    def group_back(ctx):
        vt = ctx["vt"]; vkey = ctx["vkey"]; koT = ctx["koT"]; koTkey = ctx["koTkey"]; vx = ctx["vx"]; vxkey = ctx["vxkey"]
        PT = ctx["PT"]; PTkey = ctx["PTkey"]; Qin = ctx["Qin"]; Qinkey = ctx["Qinkey"]; sm = ctx["sm"]; smkey = ctx["smkey"]
        b0 = ctx["b0"]; nsl = ctx["nsl"]; C = ctx["C"]; nch = ctx["nch"]; gla = ctx["gla"]; gi = ctx["gi"]; bwd = ctx["bwd"]
        is_ctx = ctx["is_ctx"]; cs = ctx["cs"]; g = ctx["g"]; tau = ctx["tau"]
        kvp, kvkey = kv_ps.next()
        if gla:
            for h in range(2):
                P.mm(kvp[h * 64:(h + 1) * 64, 0:64], koT[:, h * 64:(h + 1) * 64], vt[:, h * 64:(h + 1) * 64],
                     [koTkey, vkey], [kvkey], tp=(0, h * 64))
        else:
            for h in range(2):
                P.mm(kvp[h * 64:(h + 1) * 64, 0:NHC * 64], koT[:, h * 64:(h + 1) * 64],
                     vx[:, h, :, :].rearrange("p j d -> p (j d)"), [koTkey, vxkey], [kvkey], tp=(0, h * 64))
        yield
        jorder = list(range(nch - 1, -1, -1)) if bwd else list(range(nch))
        for jj, j in enumerate(jorder):
            s_in = (b0 + jj) % nsl
            s_out = (b0 + jj + 1) % nsl
            P.stt("dve", S[g][:, s_out, :], S[g][:, s_in, :], sm[:, 2, j:j + 1], kvp[:, j * 64:(j + 1) * 64], ALU.mult, ALU.add,
                  [f"S{g}_{s_in}", smkey, kvkey], [f"S{g}_{s_out}"])
            yield

    def group_out(ctx):
        vt = ctx["vt"]; vkey = ctx["vkey"]
        PT = ctx["PT"]; PTkey = ctx["PTkey"]; Qin = ctx["Qin"]; Qinkey = ctx["Qinkey"]
        b0 = ctx["b0"]; nsl = ctx["nsl"]; C = ctx["C"]; nch = ctx["nch"]; gla = ctx["gla"]; gi = ctx["gi"]; bwd = ctx["bwd"]
        is_ctx = ctx["is_ctx"]; cs = ctx["cs"]; g = ctx["g"]; tau = ctx["tau"]
        op_, okey = o_ps.next()
        for h in range(2):
            P.mm(op_[h * 64:(h + 1) * 64, 0:128], vt[:, h * 64:(h + 1) * 64], PT[:, h, :], [vkey, PTkey], [okey],
                 start=True, stop=False, tp=(0, h * 64))
        yield
        jorder = list(range(nch - 1, -1, -1)) if bwd else list(range(nch))
        for jj, j in enumerate(jorder):
            s_in = (b0 + jj) % nsl
            for h in range(2):
                P.mm(op_[h * 64:(h + 1) * 64, j * C:(j + 1) * C], S[g][h * 64:(h + 1) * 64, s_in, :],
                     Qin[h * 64:(h + 1) * 64, j * C:(j + 1) * C], [f"S{g}_{s_in}", Qinkey], [okey],
                     start=False, stop=True, tp=(h * 64, h * 64))
            yield
        if bwd:
            osb, oskey = o_r.next()
            P.cp("act", osb[:], op_[:, 0:128], [okey], [oskey])
            P.dma(OB[g, :, cs], osb[:], reads=[oskey], writes=[("OB", g, tau)], eng="pool")
            yield
        elif not (is_ctx and not need_ctx):
            obt, obtkey = ob_r.next()
            P.dma(obt[:], OB[g, :, cs], reads=[("OB", g, tau)], writes=[obtkey])
            gt_, gtkey = gate_r.next()
            gft = (20 + gi) if gla else (24 + gi)
            P.dma(gt_[:], FMB[gft, :, cs], reads=[("FMB", gft, _st(tau))], writes=[gtkey])
            osb, oskey = o_r.next()
            P.tt("dve", osb[:], op_[:, 0:128], obt[:], ALU.add, [okey, obtkey], [oskey])
            yield
            osq, osqkey = sq_r.next()
            P.act(osq[:], osb[:], AF.Square, [oskey], [osqkey])
            yield
            zp, zkey = z_ps.next()
            P.mm(zp[:, 0:128], onesblk[:], osq[:], ["onesblk", osqkey], [zkey])
            rs, rskey = wr[1].next()
            P.act(rs[:], zp[:, 0:128], AF.Ln, [zkey, "eps"], [rskey], bias=eps_t[:])
            yield
            P.act(rs[:], rs[:], AF.Exp, [rskey], [rskey], scale=-0.5)
            yield
            nw = G["glanw"] if gla else G["hgnw"]
            P.stt("dve", osb[:], osb[:], nw[:], rs[:], ALU.mult, ALU.mult, [oskey, rskey, "glanw", "hgnw"], [oskey])
            yield
            og, ogkey = og_r.next()
            P.tt("pool", og[:], osb[:], gt_[:], ALU.mult, [oskey, gtkey], [ogkey])
            P.dma(OT[4 + g, :, cs], og[:], reads=[ogkey], writes=[("OT", 4 + g, tau)], eng="pool")
            yield

    for dirn in DBG["p2_dirs"]:
        bwd = dirn == 1
        order = [1, 0] + list(range(NT - 1, 1, -1)) if bwd else list(range(NT))
        for g in range(4):
            base[g] = 0
            P.memset("dve", S[g][:, 0, :], 0.0, [f"S{g}_0"])
        if DBG["p2_tiles"]:
            order = order[:DBG["p2_tiles"]]
        mids = []
        outs = []
        for tau in order:
            for pair in ((0, 2), (1, 3)):
                ctxs = [dict() for g in pair if g in DBG["p2_groups"]]
                fronts = [group_front(dirn, tau, g, c_) for g, c_ in zip([g for g in pair if g in DBG["p2_groups"]], ctxs)]
                _interleave(outs + mids + fronts)
                outs = [group_out(c_) for c_ in mid_ctxs] if mids else []
                mids = [group_back(c_) for c_ in ctxs]
                mid_ctxs = ctxs
        _interleave(outs + mids)
        _interleave([group_out(c_) for c_ in mid_ctxs])


def _st(tau):
    return 0 if tau < 2 else 2 + 4 * ((tau - 2) // 4)


def _scan(out, d0, d1):
    return lambda e: e.tensor_tensor_scan(out, d0, d1, 0.0, ALU.mult, ALU.add)


def build_p3(P, l, need_ctx, FMB, TM, OT, ttab, ones64):
    KTb = Ring(P, "KTb", 2, [128, NTOK], BF16)
    Vd = Ring(P, "Vd", 2, [128, 68, 64], BF16)
    q_r = Ring(P, "q3", 2, [128, 512], BF16); g_r = Ring(P, "g3", 2, [128, 512], BF16)
    pe_r = Ring(P, "pe3", 4, [128, 512], BF16); p_r = Ring(P, "p3", 4, [128, 512], BF16)
    rc_r = Ring(P, "rc3", 2, [128, 512]); o_r = Ring(P, "o3", 2, [128, 512]); og_r = Ring(P, "og3", 2, [128, 512], BF16)
    s_ps = Ring(P, "s_ps", 3, [128, 512], psum=True)
    ao_ps = Ring(P, "ao_ps", 2, [128, 512], psum=True); as_ps = Ring(P, "as_ps", 2, [128, 512], psum=True)
    TMv = TM.rearrange("(r k) c -> k r c", k=64)

    def row_interval(rp):
        if rp <= 7:
            return 0, rp + 4
        if rp >= 56:
            return rp - 3, 63
        return rp - 3, rp + 4

    for hp in range(4):
        kt_, ktkey = KTb.next()
        P.dma(kt_[:], FMB[4 + hp, :, :], reads=[("FMB", 4 + hp, s) for s in [0] + [2 + 4 * i for i in range(8)]], writes=[ktkey])
        vd, vdkey = Vd.next()
        for hh in range(2):
            for part in range(4):
                r0, r1 = part * 17, (part + 1) * 17
                P.dma(vd[hh * 64:(hh + 1) * 64, r0:r1, :], TMv[:, r0:r1, hp * 128 + hh * 64:hp * 128 + (hh + 1) * 64],
                      reads=[("TM", t_) for t_ in range(NT)], writes=[vdkey])
        blocks = [("lat", b) for b in range(8)] + ([("ctx", 0)] if need_ctx else [])
        for kind, b in blocks:
            if kind == "lat":
                tok0 = CTX + b * 512
                nq = 512
                pieces = [(rho, 0, 512, None) for rho in range(4)]
                for rp in range(64):
                    lo, hi = row_interval(rp)
                    a = max(lo, 8 * b); e_ = min(hi, 8 * b + 7)
                    if a > e_:
                        continue
                    u0 = 7 - rp + a
                    pieces.append((4 + rp, (a - 8 * b) * 64, (e_ - 8 * b + 1) * 64, u0))
            else:
                tok0 = 0
                nq = 256
                pieces = [(rho, 0, 256, None) for rho in range(4)]
            qt, qkey = q_r.next(); gt_, gkey = g_r.next()
            st_reads = sorted(set(_st(t_) for t_ in range(tok0 // 128, (tok0 + nq) // 128)))
            P.dma(qt[:, 0:nq], FMB[hp, :, tok0:tok0 + nq], reads=[("FMB", hp, s) for s in st_reads], writes=[qkey])
            P.dma(gt_[:, 0:nq], FMB[8 + hp, :, tok0:tok0 + nq], reads=[("FMB", 8 + hp, s) for s in st_reads], writes=[gkey])
            ao, aokey = ao_ps.next(); as_, askey = as_ps.next()
            staged = {}

            def stage_a(pi):
                rho, c0, c1, u0 = pieces[pi]
                sp_, spkey = s_ps.next()
                for hh in range(2):
                    P.mm(sp_[hh * 64:(hh + 1) * 64, c0:c1], kt_[hh * 64:(hh + 1) * 64, rho * 64:(rho + 1) * 64],
                         qt[hh * 64:(hh + 1) * 64, c0:c1], [ktkey, qkey], [spkey], tp=(hh * 64, hh * 64))
                pe_, pekey = pe_r.next()
                P.act(pe_[:, c0:c1], sp_[:, c0:c1], AF.Exp, [spkey], [pekey])
                if u0 is not None:
                    pp, ppkey = p_r.next()
                    nr = (c1 - c0) // 64
                    P.tt("dve", pp[:, c0:c1], pe_[:, c0:c1], ttab[:, hp, u0 * 64:(u0 + nr) * 64], ALU.mult,
                         [pekey, "ttab"], [ppkey])
                else:
                    pp, ppkey = pe_, pekey
                staged[pi] = (pp, ppkey)

            def stage_b(pi):
                rho, c0, c1, u0 = pieces[pi]
                pp, ppkey = staged.pop(pi)
                for hh in range(2):
                    P.mm(ao[hh * 64:(hh + 1) * 64, c0:c1], vd[hh * 64:(hh + 1) * 64, rho, :], pp[hh * 64:(hh + 1) * 64, c0:c1],
                         [vdkey, ppkey], [aokey], start=(pi == 0), stop=(pi == len(pieces) - 1), tp=(hh * 64, hh * 64))
                    P.mm(as_[hh * 64:(hh + 1) * 64, c0:c1], ones64[hh * 64:(hh + 1) * 64, :], pp[hh * 64:(hh + 1) * 64, c0:c1],
                         ["ones64", ppkey], [askey], start=(pi == 0), stop=(pi == len(pieces) - 1), tp=(hh * 64, hh * 64))

            LA = 2
            for pi in range(min(LA, len(pieces))):
                stage_a(pi)
            for pi in range(len(pieces)):
                if pi + LA < len(pieces):
                    stage_a(pi + LA)
                stage_b(pi)
            rc, rckey = rc_r.next()
            P.recip(rc[:, 0:nq], as_[:, 0:nq], [askey], [rckey])
            ob, obkey = o_r.next()
            P.tt("dve", ob[:, 0:nq], ao[:, 0:nq], rc[:, 0:nq], ALU.mult, [aokey, rckey], [obkey])
            og, ogkey = og_r.next()
            P.tt("pool", og[:, 0:nq], ob[:, 0:nq], gt_[:, 0:nq], ALU.mult, [obkey, gkey], [ogkey])
            P.dma(OT[hp, :, tok0:tok0 + nq], og[:, 0:nq], reads=[ogkey],
                  writes=[("OT", hp, t_) for t_ in range(tok0 // 128, (tok0 + nq) // 128)], eng="pool")


_CACHE = {}


def kernel(**inputs):
    inputs = {k: np.asarray(v) for k, v in inputs.items()}
    common, per = host_layout(inputs)
    if "nc" not in _CACHE:
        _CACHE["nc"] = build_program()
    nc = _CACHE["nc"]
    in_maps = [dict(common, **p) for p in per]
    res = run_bass_kernel_spmd(nc, in_maps, core_ids=list(range(len(per))))
    out = np.stack([np.asarray(r["out"], dtype=np.float32) for r in res.results], axis=0)
    return out
```

```python
import numpy as np
import ml_dtypes
from contextlib import ExitStack
import concourse.bass as bass
import concourse.mybir as mybir
from concourse.bass_utils import run_bass_kernel_spmd

F32 = mybir.dt.float32
BF16 = mybir.dt.bfloat16
AF = mybir.ActivationFunctionType
ALU = mybir.AluOpType
NPBF = ml_dtypes.bfloat16

ENGS = ("pe", "act", "dve", "pool", "sp")
SAME_ENGINE_SYNC = {"pe": False, "act": True, "dve": True, "pool": True, "sp": False}
N_DMA_SEMS = {"sp": 20, "act": 4, "pool": 12}

D = 1024
KT = 8
T = 4096
CTX = 256
NTOK = T + CTX
NT = NTOK // 128
NEXT = 4992
N_FMB = 26
N_FMF = 5
EPS = 1e-6
CHG = 32


class Ins:
    __slots__ = ("eng", "fn", "deps", "is_dma", "dma_sem", "dma_val", "inc_idx", "needed", "pos", "guard")

    def __init__(self, eng, fn, is_dma):
        self.eng = eng
        self.fn = fn
        self.deps = []
        self.is_dma = is_dma
        self.dma_sem = None
        self.dma_val = None
        self.inc_idx = None
        self.needed = False
        self.guard = None


class Prog:
    def __init__(self):
        self.nc = bass.Bass("TRN2", target_bir_lowering=False)
        self.es = ExitStack()
        self.streams = {e: [] for e in ENGS}
        self.last_w = {}
        self.readers = {}
        self.dma_rr = {e: 0 for e in N_DMA_SEMS}
        self.dma_last = {}
        self.dma_cnt = {}
        self.n_ins = 0
        self.scopes = []

    def push(self):
        self.scopes.append(ExitStack())

    def pop(self):
        self.barrier()
        self.scopes.pop().close()

    def _ctx(self):
        return self.scopes[-1] if self.scopes else self.es

    def sb(self, name, shape, dtype=F32):
        self.uid = getattr(self, "uid", 0) + 1
        return self._ctx().enter_context(self.nc.sbuf_tensor(f"sb{self.uid}_{name}", list(shape), dtype))

    def ps(self, name, shape=(128, 512), dtype=F32):
        self.uid = getattr(self, "uid", 0) + 1
        return self._ctx().enter_context(self.nc.psum_tensor(f"ps{self.uid}_{name}", list(shape), dtype))

    def dram(self, name, shape, dtype=F32, kind="Internal"):
        return self.nc.dram_tensor(name, list(shape), dtype, kind=kind)

    def op(self, eng, fn, reads=(), writes=(), dma=False):
        ins = Ins(eng, fn, dma)
        ins.pos = self.n_ins
        self.n_ins += 1
        deps = []
        for r in reads:
            w = self.last_w.get(r)
            if w is not None:
                deps.append(w)
        for wkey in writes:
            w = self.last_w.get(wkey)
            if w is not None:
                deps.append(w)
            deps.extend(self.readers.get(wkey, {}).values())
        if dma:
            slot = self.dma_rr[eng] % N_DMA_SEMS[eng]
            self.dma_rr[eng] += 1
            key = (eng, slot)
            prev = self.dma_last.get(key)
            if prev is not None:
                ins.guard = prev
            self.dma_cnt[key] = self.dma_cnt.get(key, 0) + 1
            ins.dma_sem = key
            ins.dma_val = 16 * self.dma_cnt[key]
            self.dma_last[key] = ins
        best = {}
        for d in deps:
            if d is ins:
                continue
            k = d.dma_sem if d.is_dma else d.eng
            if k not in best or best[k].pos < d.pos:
                best[k] = d
        for d in best.values():
            ins.deps.append(d)
            d.needed = True
        if ins.guard is not None:
            ins.guard.needed = True
        mykey = ins.dma_sem if dma else eng
        for r in reads:
            self.readers.setdefault(r, {})[mykey] = ins
        for wkey in writes:
            self.last_w[wkey] = ins
            self.readers[wkey] = {}
        self.streams[eng].append(ins)
        return ins

    def barrier(self):
        lasts = []
        for e in ENGS:
            for ins in reversed(self.streams[e]):
                if ins.fn is not None and not ins.is_dma:
                    lasts.append(ins)
                    break
        lasts += list(self.dma_last.values())
        for e in ENGS:
            b = Ins(e, None, False)
            b.pos = self.n_ins
            self.n_ins += 1
            for d in lasts:
                b.deps.append(d)
                d.needed = True
            self.streams[e].append(b)
        self.last_w = {}
        self.readers = {}

    def dma(self, out, in_, reads=(), writes=(), eng="sp"):
        return self.op(eng, lambda e: e.dma_start(out=out, in_=in_), reads, writes, dma=True)

    def mm(self, out, lhsT, rhs, reads, writes, start=True, stop=True, tp=None):
        if tp is None:
            return self.op("pe", lambda e: e.matmul(out, lhsT, rhs, start=start, stop=stop), reads, writes)
        return self.op("pe", lambda e: e.matmul(out, lhsT, rhs, start=start, stop=stop, tile_position=tp,
                                                skip_group_check=True), reads, writes)

    def tr(self, out, in_, ident, reads, writes):
        return self.op("pe", lambda e: e.transpose(out, in_, ident), reads, writes)

    def act(self, out, in_, func, reads, writes, bias=None, scale=None, accum_out=None):
        kw = {}
        if bias is not None:
            kw["bias"] = bias
        if scale is not None:
            kw["scale"] = scale
        if accum_out is not None:
            kw["accum_out"] = accum_out
        return self.op("act", lambda e: e.activation(out, in_, func, **kw), reads, writes)

    def tt(self, eng, out, in0, in1, op, reads, writes):
        return self.op(eng, lambda e: e.tensor_tensor(out, in0, in1, op), reads, writes)

    def ts(self, eng, out, in0, s1, s2, op0, op1, reads, writes):
        if s2 is None:
            return self.op(eng, lambda e: e.tensor_scalar(out, in0, s1, None, op0=op0), reads, writes)
        return self.op(eng, lambda e: e.tensor_scalar(out, in0, s1, s2, op0=op0, op1=op1), reads, writes)

    def stt(self, eng, out, in0, scalar, in1, op0, op1, reads, writes):
        return self.op(eng, lambda e: e.scalar_tensor_tensor(out, in0, scalar, in1, op0=op0, op1=op1), reads, writes)

    def cp(self, eng, out, in_, reads, writes):
        if eng == "act":
            return self.op("act", lambda e: e.copy(out, in_), reads, writes)
        return self.op(eng, lambda e: e.tensor_copy(out, in_), reads, writes)

    def recip(self, out, in_, reads, writes):
        return self.op("dve", lambda e: e.reciprocal(out, in_), reads, writes)

    def memset(self, eng, ap, val, writes):
        return self.op(eng, lambda e: e.memset(ap, val), (), writes)

    def finish(self, final_waits=()):
        nc = self.nc
        es = self.es
        sems = {e: es.enter_context(nc.semaphore("s_" + e)) for e in ENGS}
        dsems = {}
        for e, n in N_DMA_SEMS.items():
            for i in range(n):
                dsems[(e, i)] = es.enter_context(nc.semaphore(f"d_{e}{i}"))
        for fw in final_waits:
            fw.needed = True
        for e in ENGS:
            c = 0
            for ins in self.streams[e]:
                if ins.is_dma or ins.fn is None:
                    continue
                if ins.needed:
                    c += 1
                    ins.inc_idx = c
        streams = self.streams

        def emit(e, eng_obj):
            known = {}

            def wait_for(d):
                if d.is_dma:
                    k = ("d",) + d.dma_sem
                    if known.get(k, 0) >= d.dma_val:
                        return
                    eng_obj.wait_ge(dsems[d.dma_sem], d.dma_val)
                    known[k] = d.dma_val
                else:
                    if d.eng == e and not SAME_ENGINE_SYNC[e]:
                        return
                    k = ("e", d.eng)
                    if known.get(k, 0) >= d.inc_idx:
                        return
                    eng_obj.wait_ge(sems[d.eng], d.inc_idx)
                    known[k] = d.inc_idx

            for ins in streams[e]:
                for d in ins.deps:
                    wait_for(d)
                if ins.guard is not None:
                    wait_for(ins.guard)
                if ins.fn is None:
                    continue
                bi = ins.fn(eng_obj)
                if ins.is_dma:
                    bi.then_inc(dsems[ins.dma_sem], 16)
                elif ins.needed:
                    bi.then_inc(sems[e], 1)
            if e == "sp":
                for fw in final_waits:
                    wait_for(fw)

        with nc.Block() as block:
            @block.tensor
            def _(eng):
                emit("pe", eng)

            @block.scalar
            def _(eng):
                emit("act", eng)

            @block.vector
            def _(eng):
                emit("dve", eng)

            @block.gpsimd
            def _(eng):
                emit("pool", eng)

            @block.sync
            def _(eng):
                emit("sp", eng)
        while self.scopes:
            self.scopes.pop().close()
        self.es.close()
        return nc


class Ring:
    def __init__(self, P, name, n, shape, dtype=F32, psum=False):
        self.name = name
        if psum:
            self.bufs = [P.ps(f"{name}{i}", shape, dtype) for i in range(n)]
        else:
            self.bufs = [P.sb(f"{name}{i}", shape, dtype) for i in range(n)]
        self.i = 0

    def next(self):
        j = self.i % len(self.bufs)
        self.i += 1
        return self.bufs[j], f"{self.name}{j}"


def ext_col_index():
    cols = []
    cols += list(range(0, 512)) + list(range(512, 1024)) + list(range(1536, 2048))

    def pad_heads(base, swap):
        out = []
        for h in range(4):
            for i in range(32):
                j = (i + 8 if (i % 16) < 8 else i - 8) if swap else i
                out.append(base + h * 32 + j)
            out += [-1] * 32
        return out
    cols += pad_heads(2048, False) + pad_heads(2048, True) + pad_heads(2176, False) + pad_heads(2176, True)
    cols += list(range(2592, 2848)) + list(range(2848, 3104)) + list(range(3872, 4128))
    cols += list(range(3104, 3360)) + list(range(3360, 3616)) + list(range(2560, 2592)) + [-1] * 96
    cols += list(range(1024, 1536)) + list(range(2304, 2560)) + list(range(3616, 3872))
    assert len(cols) == NEXT
    return np.array(cols)


def host_consts():
    c = {}
    c["identf"] = np.eye(128, dtype=np.float32)
    c["identb"] = np.eye(128, dtype=np.float32).astype(NPBF)
    i = np.arange(128)
    m16 = np.ones((128, 128), np.float32); m16[:, i % 16 == 0] = 0
    m128 = np.ones((128, 128), np.float32); m128[:, 0] = 0
    m32 = np.ones((128, 128), np.float32); m32[:, i % 32 == 0] = 0
    c["scanmask"] = np.stack([m16, m128, m32], 0)
    s = i[:, None]; t = i[None, :]
    tri = []
    for C in (16, 128, 32):
        same = (s // C) == (t // C)
        tri.append((same & (s <= t)).astype(np.float32))
        tri.append((same & (s >= t)).astype(np.float32))
    c["trimask"] = np.stack(tri, 0)
    c["cm"] = ((i[:, None] // CHG) == np.arange(128 // CHG)[None, :]).astype(np.float32).astype(NPBF)
    c["onesblk"] = (((i[:, None] // 64) == (i[None, :] // 64)).astype(np.float32) / 64.0).astype(NPBF)
    c["ones64"] = np.ones((128, 64), np.float32).astype(NPBF)
    inv_freq = (1.0 / (10000.0 ** (np.arange(0, 16, 2, dtype=np.float32) / np.float32(16)))).astype(np.float32)
    tt_ = np.arange(T)
    cos = np.zeros((128, T), np.float32); sins = np.zeros((128, T), np.float32)
    for p in range(128):
        ii = p % 64
        if ii >= 32:
            continue
        blk = ii // 16
        j = ii % 16
        f = j % 8
        pos = (tt_ // 64 if blk == 0 else tt_ % 64).astype(np.float32)
        ang = (pos * inv_freq[f]).astype(np.float32)
        cos[p] = np.cos(ang)
        sins[p] = -np.sin(ang) if j < 8 else np.sin(ang)
    c["ropecos"] = cos
    c["ropesin"] = sins
    return c


def host_layout(inp):
    f32 = np.float32
    common = dict(host_consts())
    cols = ext_col_index()
    valid = cols >= 0
    DEPTH = inp["w_in"].shape[0]
    q = np.arange(64)
    col_start = np.clip(q - 8, 0, 48)
    kc = np.arange(64)
    inwin = (kc[:, None] >= col_start[None, :]) & (kc[:, None] < col_start[None, :] + 16)
    didx = np.clip(kc[:, None] - q[None, :], -15, 15) + 15
    pidx = np.arange(128)
    for l in range(DEPTH):
        w_ext = np.zeros((D, NEXT), f32)
        w_ext[:, valid] = inp["w_in"][l][:, cols[valid]]
        common[f"w_in{l}"] = w_ext
        common[f"ada_w{l}"] = np.ascontiguousarray(inp["ada_w"][l], dtype=f32)
        common[f"ada_bT{l}"] = np.ascontiguousarray(inp["ada_b"][l].reshape(24, 128).T, dtype=f32)
        common[f"ada_bgt{l}"] = np.ascontiguousarray(inp["ada_b"][l][2048:3072].reshape(1, 1024), dtype=f32)
        common[f"norm_wT{l}"] = np.ascontiguousarray(inp["norm_w"][l].reshape(8, 128).T, dtype=f32)
        common[f"w_out{l}"] = np.ascontiguousarray(inp["w_out"][l], dtype=f32)
        rpb = inp["na_rpb"][l]
        g = rpb[:, ::-1, :][:, :, didx]
        g = np.where(inwin[None, None], g, f32(-30000.0))
        g = g.transpose(0, 2, 1, 3).reshape(4, 128, 15 * 64)
        common[f"rpbt{l}"] = np.ascontiguousarray(g, dtype=f32)
        wgk = inp["gla_w_gk"][l]
        bgk = inp["gla_b_gk"][l]
        wp = np.zeros((32, 2, 2, 128), f32)
        bp = np.zeros((128, 2, 2), f32)
        for d_ in range(2):
            for g_ in range(2):
                for p in range(128):
                    h = 2 * g_ + p // 64
                    ii = p % 64
                    if ii < 32:
                        wp[d_ * 16:(d_ + 1) * 16, d_, g_, p] = wgk[d_, :, h * 32 + ii]
                        bp[p, d_, g_] = bgk[d_, h * 32 + ii]
        common[f"wgk{l}"] = wp
        common[f"bgk{l}"] = bp
        common[f"glanw{l}"] = np.ascontiguousarray(inp["gla_norm_w"][l][pidx % 64].reshape(128, 1), dtype=f32)
        common[f"hgnw{l}"] = np.ascontiguousarray(inp["hgrn_norm_w"][l][pidx % 64].reshape(128, 1), dtype=f32)
    lb = inp["hgrn_lower_bounds"]
    common["lbraw"] = np.ascontiguousarray(lb.reshape(2, 2, 2, 128).transpose(3, 0, 1, 2).reshape(128, 8), dtype=f32)
    common["fnw"] = np.ascontiguousarray(inp["final_norm_w"].reshape(1, D), dtype=f32)
    per = []
    B = inp["x"].shape[0]
    cc = inp["c_ctx"].reshape(8, 128).T
    for b in range(B):
        cv = np.concatenate([inp["c"][b].reshape(8, 128).T, cc], axis=1)
        per.append({"x": np.ascontiguousarray(inp["x"][b], dtype=f32),
                    "ctx": np.ascontiguousarray(inp["ctx"][b], dtype=f32),
                    "cvec": np.ascontiguousarray(cv, dtype=f32)})
    return common, per


FMB_EVAC = {}
for _t in range(0, 4):
    FMB_EVAC[_t] = ("scale", 0.125)
for _t in range(4, 8):
    FMB_EVAC[_t] = ("copy", 1.0)
for _t in range(8, 12):
    FMB_EVAC[_t] = ("silu", 1.0)
for _t in range(12, 16):
    FMB_EVAC[_t] = ("scale", 32.0 ** -0.5)
for _t in range(16, 20):
    FMB_EVAC[_t] = ("copy", 1.0)
for _t in (20, 21, 24, 25):
    FMB_EVAC[_t] = ("silu", 1.0)
for _t in (22, 23):
    FMB_EVAC[_t] = ("scale", 0.125)


import os
DBG = {"skip_p1": os.environ.get("K_DBG_SKIP_P1") == "1",
       "p2_tiles": int(os.environ.get("K_DBG_P2_TILES", "0")),
       "p2_groups": [int(x) for x in os.environ.get("K_DBG_P2_GROUPS", "0,1,2,3").split(",")],
       "p2_dirs": [int(x) for x in os.environ.get("K_DBG_P2_DIRS", "1,0").split(",")],
       "cut": int(os.environ.get("K_DBG_P2_CUT", "99"))}


def build_program(depth=2, stop_after=None, debug=False):
    P = Prog()
    nc = P.nc
    kdbg = "ExternalOutput" if debug else "Internal"

    def din(name, shape, dt=F32):
        return P.dram(name, shape, dt, kind="ExternalInput").ap()

    x_d = din("x", [T, D]); ctx_d = din("ctx", [CTX, D]); cvec_d = din("cvec", [128, 16])
    identf_d = din("identf", [128, 128]); identb_d = din("identb", [128, 128], BF16)
    scanmask_d = din("scanmask", [3, 128, 128]); trimask_d = din("trimask", [6, 128, 128])
    cm_d = din("cm", [128, 128 // CHG], BF16); onesblk_d = din("onesblk", [128, 128], BF16); ones64_d = din("ones64", [128, 64], BF16)
    ropecos_d = din("ropecos", [128, T]); ropesin_d = din("ropesin", [128, T])
    lbraw_d = din("lbraw", [128, 8]); fnw_d = din("fnw", [1, D])
    LW = []
    for l in range(2):
        LW.append(dict(
            w_in=din(f"w_in{l}", [D, NEXT]), ada_w=din(f"ada_w{l}", [D, 3 * D]), ada_bT=din(f"ada_bT{l}", [128, 24]),
            ada_bgt=din(f"ada_bgt{l}", [1, D]), norm_wT=din(f"norm_wT{l}", [128, 8]), w_out=din(f"w_out{l}", [D, D]),
            rpbt=din(f"rpbt{l}", [4, 128, 960]), wgk=din(f"wgk{l}", [32, 2, 2, 128]), bgk=din(f"bgk{l}", [128, 2, 2]),
            glanw=din(f"glanw{l}", [128, 1]), hgnw=din(f"hgnw{l}", [128, 1])))
    out_d = P.dram("out", [T, D], F32, kind="ExternalOutput").ap()
    FMB = P.dram("FMB", [N_FMB, 128, NTOK], BF16, kind=kdbg).ap()
    FMF = P.dram("FMF", [N_FMF, 128, NTOK], F32, kind=kdbg).ap()
    TM = P.dram("TM", [NTOK, 1024], BF16, kind=kdbg).ap()
    OB = P.dram("OB", [4, 128, NTOK], F32, kind=kdbg).ap()
    OT = P.dram("OT", [8, 128, NTOK], BF16, kind=kdbg).ap()
    XN = P.dram("XN", [T, D], F32, kind=kdbg).ap()
    XC = P.dram("XC", [CTX, D], F32, kind=kdbg).ap()
    MODDBG = P.dram("MODDBG", [128, 64], F32, kind=kdbg).ap()

    identf = P.sb("identf", [128, 128]); identb = P.sb("identb", [128, 128], BF16)
    onesblk = P.sb("onesblk", [128, 128], BF16); ones64 = P.sb("ones64", [128, 64], BF16)
    cvec = P.sb("cvec", [128, 16]); sil = P.sb("sil", [128, 16]); silrep = P.sb("silrep", [128, 16, 128])
    lbraw = P.sb("lbraw", [128, 8]); lbt = P.sb("lbt", [128, 8]); omlb = P.sb("omlb", [128, 8])
    eps_t = P.sb("eps_t", [128, 1])
    P.memset("dve", eps_t[:], EPS, ["eps"])
    for nm, dst, src in (("identf", identf, identf_d), ("identb", identb, identb_d), ("onesblk", onesblk, onesblk_d),
                         ("ones64", ones64, ones64_d), ("cvec", cvec, cvec_d), ("lbraw", lbraw, lbraw_d)):
        P.dma(dst[:], src, writes=[nm])
    tmp16 = P.sb("tmp16", [128, 16])
    P.act(tmp16[:], cvec[:], AF.Exp, ["cvec"], ["tmp16"], scale=-1.0)
    P.ts("dve", tmp16[:], tmp16[:], 1.0, None, ALU.add, None, ["tmp16"], ["tmp16"])
    P.recip(tmp16[:], tmp16[:], ["tmp16"], ["tmp16"])
    P.tt("dve", sil[:], cvec[:], tmp16[:], ALU.mult, ["cvec", "tmp16"], ["sil"])
    for j in range(16):
        P.cp("dve", silrep[:, j, :], sil[:, j:j + 1].to_broadcast([128, 128]), ["sil"], ["silrep"])
    P.memset("dve", lbt[:], 0.0, ["lbt"])
    P.tt("dve", lbt[:, 4:8], lbraw[:, 0:4], lbraw[:, 4:8], ALU.subtract, ["lbraw", "lbt"], ["lbt"])
    P.act(lbt[:, 4:8], lbt[:, 4:8], AF.Exp, ["lbt"], ["lbt"])
    P.ts("dve", lbt[:, 4:8], lbt[:, 4:8], 1.0, None, ALU.add, None, ["lbt"], ["lbt"])
    P.recip(lbt[:, 4:8], lbt[:, 4:8], ["lbt"], ["lbt"])
    P.ts("dve", omlb[:], lbt[:], -1.0, 1.0, ALU.mult, ALU.add, ["lbt"], ["omlb"])

    last_out = []

    for l in range(depth):
        W = LW[l]
        last = (l == depth - 1)
        need_ctx = not last
        x_src = x_d if l == 0 else XN
        c_src = ctx_d if l == 0 else XC
        tiles = list(range(NT))

        P.push()
        gsh = P.sb(f"gsh{l}", [128, 2, 2, 8])
        gtrep = P.sb(f"gtrep{l}", [128, 2, D])
        woutb = P.sb(f"woutb{l}", [128, 2 if need_ctx else 1, KT, D], BF16)
        ttab = P.sb(f"ttab{l}", [128, 4, 960], BF16)
        wgk = P.sb(f"wgk{l}", [32, 2, 2, 128]); nbgk = P.sb(f"nbgk{l}", [128, 4])
        glanw = P.sb(f"glanw{l}", [128, 1]); hgnw = P.sb(f"hgnw{l}", [128, 1])

        P.push()
        winb = P.sb("winb", [128, KT, NEXT], BF16)
        tr_ps = Ring(P, "tr_ps", 2, [128, 512], psum=True)
        fm_ps = Ring(P, "fm_ps", 3, [128, 512], psum=True)
        tm_ps = Ring(P, "tm_ps", 2, [128, 512], psum=True)
        modT = P.sb("modT", [128, 16, 2]); adabT = P.sb("adabT", [128, 24]); normwT = P.sb("normwT", [128, 8])
        adabgt = P.sb("adabgt", [128, D])
        stg = Ring(P, "stg", 2, [128, KT, 256])
        mod_ps = tm_ps.bufs[0]; gt_ps = fm_ps
        P.dma(adabT[:], W["ada_bT"], writes=["adabT"]); P.dma(normwT[:], W["norm_wT"], writes=["normwT"])
        P.dma(adabgt[:], W["ada_bgt"].partition_broadcast(128), writes=["adabgt"])
        P.dma(wgk[:], W["wgk"], writes=["wgk"]); P.dma(nbgk[:], W["bgk"].rearrange("p a b -> p (a b)"), writes=["nbgk"])
        P.dma(glanw[:], W["glanw"], writes=["glanw"]); P.dma(hgnw[:], W["hgnw"], writes=["hgnw"])
        P.ts("dve", nbgk[:], nbgk[:], -1.0, None, ALU.mult, None, ["nbgk"], ["nbgk"])
        adaw_v = W["ada_w"].rearrange("(kt p) j -> p kt j", p=128)
        sil2 = sil[:].rearrange("p (w k) -> p w k", w=2)
        for ch in range(12):
            sbuf, skey = stg.next()
            P.dma(sbuf[:], adaw_v[:, :, ch * 256:(ch + 1) * 256], writes=[skey])
            if ch < 8:
                for jl in range(2):
                    jt = ch * 2 + jl
                    for kt in range(KT):
                        P.mm(mod_ps[:, jt * 2:jt * 2 + 2], sbuf[:, kt, jl * 128:(jl + 1) * 128], sil2[:, :, kt],
                             [skey, "sil"], ["tm_ps0"], start=(kt == 0), stop=(kt == KT - 1))
                    P.ts("dve", modT[:, jt, :], mod_ps[:, jt * 2:jt * 2 + 2], adabT[:, jt:jt + 1], None, ALU.add, None,
                         ["tm_ps0", "adabT"], ["modT"])
            else:
                q4 = ch - 8
                for which in range(2):
                    gp, gk_ = gt_ps.next()
                    for kt in range(KT):
                        P.mm(gp[:, 0:256], silrep[:, which * 8 + kt, :], sbuf[:, kt, :], ["silrep", skey], [gk_],
                             start=(kt == 0), stop=(kt == KT - 1))
                    P.tt("dve", gtrep[:, which, q4 * 256:(q4 + 1) * 256], gp[:, 0:256], adabgt[:, q4 * 256:(q4 + 1) * 256],
                         ALU.add, [gk_, "adabgt"], ["gtrep"])
        for which in range(2):
            P.stt("dve", gsh[:, which, 0, :], modT[:, 8:16, which], 1.0, normwT[:], ALU.add, ALU.mult,
                  ["modT", "normwT"], ["gsh"])
            P.cp("dve", gsh[:, which, 1, :], modT[:, 0:8, which], ["modT"], ["gsh"])
        if debug:
            P.dma(MODDBG[:, 0:32], gsh[:].rearrange("p a b c -> p (a b c)"), reads=["gsh"])
        wout_v = W["w_out"].rearrange("(kt p) j -> p kt j", p=128)
        for q4 in range(4):
            sbuf, skey = stg.next()
            P.dma(sbuf[:], wout_v[:, :, q4 * 256:(q4 + 1) * 256], writes=[skey])
            for which in range(2 if need_ctx else 1):
                for kt in range(KT):
                    eng = "dve" if kt % 2 == 0 else "pool"
                    P.tt(eng, woutb[:, which, kt, q4 * 256:(q4 + 1) * 256], sbuf[:, kt, :],
                         gtrep[:, which, q4 * 256:(q4 + 1) * 256], ALU.mult, [skey, "gtrep"], ["woutb"])
        for hp in range(4):
            sbuf, skey = stg.next()
            rv = sbuf[:].rearrange("p k c -> p (k c)")
            P.dma(rv[:, 0:960], W["rpbt"][hp, :, :], writes=[skey])
            P.act(ttab[:, hp, :], rv[:, 0:960], AF.Exp, [skey], ["ttab"])
        if stop_after == (l, "prep"):
            break

        win_v = W["w_in"].rearrange("(kt p) j -> p kt j", p=128)
        nchunk = (NEXT + 255) // 256
        for ch in range(0 if (DBG["skip_p1"] and l == 0) else nchunk):
            c0 = ch * 256
            cw = min(256, NEXT - c0)
            sbuf, skey = stg.next()
            P.dma(sbuf[:, :, 0:cw], win_v[:, :, c0:c0 + cw], writes=[skey])
            for kt in range(KT):
                eng = ("dve", "pool", "act")[kt % 3]
                P.cp(eng, winb[:, kt, c0:c0 + cw], sbuf[:, kt, 0:cw], [skey], ["winb"])
        xt_r = Ring(P, "xt", 2, [128, D]); xh_r = Ring(P, "xh", 2, [128, D]); sq_scr = P.sb("sq_scr", [128, D])
        ss_r = Ring(P, "ss", 4, [128, 2])
        hT_r = Ring(P, "hT", 2, [128, KT, 512], BF16)
        fmo_b = Ring(P, "fmo_b", 4, [128, 512], BF16); fmo_f = Ring(P, "fmo_f", 2, [128, 512])
        tmo = Ring(P, "tmo", 2, [128, 1024], BF16)
        supers = [(0, 2)] + [(2 + 4 * i, 4) for i in range(8)]
        if DBG["skip_p1"] and l == 0:
            supers = []
        def prologue(t0, ntl):
            is_ctx = (t0 == 0)
            which = 1 if is_ctx else 0
            ntok = ntl * 128
            hT, hkey = hT_r.next()
            for ti in range(ntl):
                tau = t0 + ti
                xt, xkey = xt_r.next()
                src = c_src[tau * 128:(tau + 1) * 128, :] if is_ctx else x_src[(tau - 2) * 128:(tau - 1) * 128, :]
                P.dma(xt[:], src, writes=[xkey])
                ss, sskey = ss_r.next()
                P.memset("dve", ss[:], 0.0, [sskey])
                P.act(sq_scr[:], xt[:], AF.Square, [xkey, sskey], ["sq_scr", sskey], accum_out=ss[:, 0:1])
                P.act(ss[:, 1:2], ss[:, 0:1], AF.Ln, [sskey, "eps"], [sskey], bias=eps_t[:], scale=1.0 / D)
                P.act(ss[:, 1:2], ss[:, 1:2], AF.Exp, [sskey], [sskey], scale=-0.5)
                xh, xhkey = xh_r.next()
                P.ts("dve", xh[:], xt[:], ss[:, 1:2], None, ALU.mult, None, [xkey, sskey], [xhkey])
                for half in range(2):
                    tp, tpkey = tr_ps.next()
                    for q4 in range(4):
                        kt = half * 4 + q4
                        P.tr(tp[:, q4 * 128:(q4 + 1) * 128], xh[:, kt * 128:(kt + 1) * 128], identf[:], [xhkey, "identf"], [tpkey])
                    for q4 in range(4):
                        kt = half * 4 + q4
                        P.act(hT[:, kt, ti * 128:(ti + 1) * 128], tp[:, q4 * 128:(q4 + 1) * 128], AF.Identity,
                              [tpkey, "gsh"], [hkey], bias=gsh[:, which, 1, kt:kt + 1], scale=gsh[:, which, 0, kt:kt + 1])
            return hT, hkey

        def mainbody(t0, ntl, hT, hkey):
            ntok = ntl * 128
            for ft in range(N_FMB + N_FMF):
                fp, fpkey = fm_ps.next()
                for kt in range(KT):
                    P.mm(fp[:, 0:ntok], winb[:, kt, ft * 128:(ft + 1) * 128], hT[:, kt, 0:ntok], ["winb", hkey], [fpkey],
                         start=(kt == 0), stop=(kt == KT - 1))
                if ft < N_FMB:
                    ob, obkey = fmo_b.next()
                    kind, sc_ = FMB_EVAC[ft]
                    if kind == "silu":
                        P.act(ob[:, 0:ntok], fp[:, 0:ntok], AF.Silu, [fpkey], [obkey])
                    else:
                        P.act(ob[:, 0:ntok], fp[:, 0:ntok], AF.Copy, [fpkey], [obkey], scale=float(sc_))
                    P.dma(FMB[ft, :, t0 * 128:t0 * 128 + ntok], ob[:, 0:ntok], reads=[obkey], writes=[("FMB", ft, t0)], eng="pool")
                else:
                    ob, obkey = fmo_f.next()
                    P.cp("dve", ob[:, 0:ntok], fp[:, 0:ntok], [fpkey], [obkey])
                    P.dma(FMF[ft - N_FMB, :, t0 * 128:t0 * 128 + ntok], ob[:, 0:ntok], reads=[obkey],
                          writes=[("FMF", ft - N_FMB, t0)], eng="pool")
            cbase = (N_FMB + N_FMF) * 128
            for ti in range(ntl):
                tau = t0 + ti
                ob, obkey = tmo.next()
                for half in range(2):
                    tp2, tp2key = tm_ps.next()
                    for kt in range(KT):
                        P.mm(tp2[:], hT[:, kt, ti * 128:(ti + 1) * 128], winb[:, kt, cbase + half * 512:cbase + (half + 1) * 512],
                             [hkey, "winb"], [tp2key], start=(kt == 0), stop=(kt == KT - 1))
                    P.cp("dve", ob[:, half * 512:(half + 1) * 512], tp2[:], [tp2key], [obkey])
                P.dma(TM[tau * 128:(tau + 1) * 128, :], ob[:], reads=[obkey], writes=[("TM", tau)], eng="pool")

        cur = prologue(*supers[0]) if supers else None
        for i_, (t0, ntl) in enumerate(supers):
            nxt = prologue(*supers[i_ + 1]) if i_ + 1 < len(supers) else None
            mainbody(t0, ntl, *cur)
            cur = nxt
        P.pop()
        if stop_after == (l, "p1"):
            break

        P.push()
        build_p2(P, l, need_ctx, FMB, FMF, TM, OB, OT, dict(
            identb=identb, onesblk=onesblk, wgk=wgk, nbgk=nbgk, glanw=glanw, hgnw=hgnw, lbt=lbt, omlb=omlb, eps_t=eps_t,
            scanmask_d=scanmask_d, trimask_d=trimask_d, cm_d=cm_d, ropecos_d=ropecos_d, ropesin_d=ropesin_d))
        P.pop()
        if stop_after == (l, "p2"):
            break

        P.push()
        build_p3(P, l, need_ctx, FMB, TM, OT, ttab, ones64)
        P.pop()
        if stop_after == (l, "p3"):
            break

        P.push()
        ot_r = Ring(P, "ot", 2, [128, 8, 128], BF16); xr = Ring(P, "x4", 2, [128, D]); xo = Ring(P, "xo", 2, [128, D])
        y_ps = Ring(P, "y_ps", 4, [128, 512], psum=True)
        ss_r = Ring(P, "ss4", 4, [128, 2]); sq_scr = P.sb("sq4", [128, D]); fnw = P.sb("fnw", [128, D])
        if last:
            P.dma(fnw[:], fnw_d.partition_broadcast(128), writes=["fnw"])
        otv = OT.rearrange("f p t -> p f t")
        for tau in (range(NT) if need_ctx else range(2, NT)):
            is_ctx = tau < 2
            which = 1 if is_ctx else 0
            ot, otkey = ot_r.next()
            P.dma(ot[:], otv[:, :, tau * 128:(tau + 1) * 128], reads=[("OT", f, tau) for f in range(8)], writes=[otkey])
            xt, xkey = xr.next()
            rows = slice(tau * 128, (tau + 1) * 128) if is_ctx else slice((tau - 2) * 128, (tau - 1) * 128)
            src = c_src[rows, :] if is_ctx else x_src[rows, :]
            P.dma(xt[:], src, reads=[("XR", l, tau)], writes=[xkey])
            xn, xnkey = xo.next()
            for half in range(2):
                yp, ypkey = y_ps.next()
                for f in range(8):
                    P.mm(yp[:], ot[:, f, :], woutb[:, which, f, half * 512:(half + 1) * 512], [otkey, "woutb"], [ypkey],
                         start=(f == 0), stop=(f == 7))
                P.tt("dve", xn[:, half * 512:(half + 1) * 512], yp[:], xt[:, half * 512:(half + 1) * 512], ALU.add,
                     [ypkey, xkey], [xnkey])
            if not last:
                dst = XC[rows, :] if is_ctx else XN[rows, :]
                P.dma(dst, xn[:], reads=[xnkey], writes=[("XR", l + 1, tau)], eng="pool")
            else:
                ss, sskey = ss_r.next()
                P.memset("dve", ss[:], 0.0, [sskey])
                P.act(sq_scr[:], xn[:], AF.Square, [xnkey, sskey], ["sq4", sskey], accum_out=ss[:, 0:1])
                P.act(ss[:, 1:2], ss[:, 0:1], AF.Ln, [sskey, "eps"], [sskey], bias=eps_t[:], scale=1.0 / D)
                P.act(ss[:, 1:2], ss[:, 1:2], AF.Exp, [sskey], [sskey], scale=-0.5)
                P.stt("dve", xn[:], xn[:], ss[:, 1:2], fnw[:], ALU.mult, ALU.mult, [xnkey, sskey, "fnw"], [xnkey])
                last_out.append(P.dma(out_d[rows, :], xn[:], reads=[xnkey], eng="pool"))
        P.pop()
        P.pop()
        if stop_after == (l, "p4"):
            break

    finals = list(last_out) + [d for d in P.dma_last.values()]
    return P.finish(final_waits=finals)


def _interleave(gens):
    gens = list(gens)
    while gens:
        for gen in list(gens):
            try:
                next(gen)
            except StopIteration:
                gens.remove(gen)


VEXP_ENG = os.environ.get("K_VEXP_ENG", "dve")


def build_p2(P, l, need_ctx, FMB, FMF, TM, OB, OT, G):
    identb = G["identb"]; onesblk = G["onesblk"]; wgk = G["wgk"]; nbgk = G["nbgk"]
    lbt = G["lbt"]; omlb = G["omlb"]; eps_t = G["eps_t"]
    scanmask = P.sb("scanmask", [128, 3, 128]); trimask = P.sb("trimask", [128, 6, 128]); cm = P.sb("cm", [128, 128 // CHG], BF16)
    ropecos = P.sb("ropecos", [128, T]); ropesin = P.sb("ropesin", [128, T])
    P.dma(scanmask[:], G["scanmask_d"].rearrange("n p t -> p n t"), writes=["scanmask"])
    P.dma(trimask[:], G["trimask_d"].rearrange("n p t -> p n t"), writes=["trimask"])
    P.dma(cm[:], G["cm_d"], writes=["cm"])
    P.dma(ropecos[:], G["ropecos_d"], writes=["ropecos"]); P.dma(ropesin[:], G["ropesin_d"], writes=["ropesin"])
    NHC = 128 // CHG
    NSL = [3, 3, 2 * NHC + 1, 2 * NHC + 1]
    S = [P.sb(f"S{g}", [128, NSL[g], 64]) for g in range(4)]
    RD = 6
    qk_r = Ring(P, "qk", 3, [128, 8, 128], BF16); gk_r = Ring(P, "gkr", 3, [32, 128]); v2_r = Ring(P, "vr", 8, [128, 256], BF16)
    qh_r = Ring(P, "qh", 3, [128, 2, 128], BF16); ff_r = Ring(P, "ff", 3, [128, 2, 128]); gate_r = Ring(P, "gate", 4, [128, 256], BF16)
    ob_r = Ring(P, "obr", 4, [128, 256])
    w2 = [Ring(P, f"w{i}", 3, [128, 2, 128]) for i in range(9)]
    qin_r = Ring(P, "qin", 7, [128, 2, 128])
    sm_r = Ring(P, "sm", 8, [128, 4, 8])
    kt_r = Ring(P, "ktr", 6, [128, 2, 128], BF16); kout_r = Ring(P, "kout", 6, [128, 2, 128], BF16)
    qbd_r = Ring(P, "qbd", 6, [128, 2, 2, 128], BF16)
    koT_r = Ring(P, "koT", 6, [128, 256], BF16); pt_r = Ring(P, "pt", 7, [128, 4, 128], BF16)
    vexp_r = Ring(P, "vexp", 4, [128, 4, NHC, 64], BF16)
    o_r = Ring(P, "osb", 6, [128, 256]); sq_r = Ring(P, "osq", 4, [128, 256], BF16); og_r = Ring(P, "og", 4, [128, 256], BF16)
    scanmask2 = P.sb("scanmask2", [128, 3, 256])
    for k_ in range(3):
        P.cp("pool", scanmask2[:, k_, :].rearrange("p (a t) -> p a t", a=2), scanmask[:, k_, :].unsqueeze(1).to_broadcast([128, 2, 128]),
             ["scanmask"], ["scanmask2"])
    z_ps = Ring(P, "z_ps", 1, [128, 512], psum=True); tp_ps = Ring(P, "tp_ps", 1, [128, 1024], BF16, psum=True)
    sc_ps = Ring(P, "sc_ps", 2, [128, 512], psum=True); kv_ps = Ring(P, "kv_ps", 2, [128, 512], psum=True)
    o_ps = Ring(P, "o_ps", 2, [128, 512], psum=True)
    for i, b in enumerate(qbd_r.bufs):
        P.memset("pool", b[:], 0.0, [f"qbd{i}"])
    base = [0, 0, 0, 0]

    def super_front(dirn, tau, gla, ctxs):
        bwd = dirn == 1
        is_ctx = tau < 2
        cs = slice(tau * 128, (tau + 1) * 128)
        C = 128 if gla else CHG
        nch = 128 // C
        mi = 1 if gla else 2
        kappa = -1.0 / 16.0 if gla else 1.0
        g0 = 0 if gla else 2
        c3 = lambda ap: ap.rearrange("p (n c) -> p n c", c=C)
        fl = lambda t: t[:].rearrange("p a b -> p (a b)")
        b0s = []
        for gg in range(2):
            g = g0 + gg
            b0s.append(base[g])
            base[g] = (base[g] + nch) % NSL[g]
        vt2, vkey = v2_r.next()
        voff = 512 if gla else 768
        P.dma(vt2[:], TM[cs, voff:voff + 256], reads=[("TM", tau)], writes=[vkey])
        lf, lfkey = w2[0].next()
        if gla:
            qk, qkkey = qk_r.next()
            P.dma(qk[:], FMB[12:20, :, cs].rearrange("n p t -> p n t"), reads=[("FMB", ft, _st(tau)) for ft in range(12, 20)],
                  writes=[qkkey])
            gkt, gkkey = gk_r.next()
            P.dma(gkt[:], FMF[4, 0:32, cs], reads=[("FMF", 4, _st(tau))], writes=[gkkey])
            yield
            zp, zkey = z_ps.next()
            e_, ekey = w2[1].next()
            for gg in range(2):
                P.mm(zp[:, gg * 128:(gg + 1) * 128], wgk[:, dirn, gg, :], gkt[:], ["wgk", gkkey], [zkey])
            for gg in range(2):
                P.act(e_[:, gg, :], zp[:, gg * 128:(gg + 1) * 128], AF.Exp, [zkey, "nbgk"], [ekey],
                      bias=nbgk[:, dirn * 2 + gg:dirn * 2 + gg + 1], scale=-1.0)
                yield
            P.act(fl(lf), fl(e_), AF.Ln, [ekey], [lfkey], bias=1.0)
            yield
            kk = kkkey = None
        else:
            qh, qhkey = qh_r.next()
            P.dma(qh[:], FMB[22:24, :, cs].rearrange("n p t -> p n t"), reads=[("FMB", 22, _st(tau)), ("FMB", 23, _st(tau))],
                  writes=[qhkey])
            fr, frkey = ff_r.next()
            fft = 2 if bwd else 0
            P.dma(fr[:], FMF[fft:fft + 2, :, cs].rearrange("n p t -> p n t"),
                  reads=[("FMF", fft, _st(tau)), ("FMF", fft + 1, _st(tau))], writes=[frkey])
            yield
            e_, ekey = w2[1].next()
            P.act(fl(e_), fl(fr), AF.Exp, [frkey], [ekey], scale=-1.0)
            yield
            P.act(fl(e_), fl(e_), AF.Ln, [ekey], [ekey], bias=1.0)
            yield
            P.act(fl(e_), fl(e_), AF.Exp, [ekey], [ekey], scale=-1.0)
            yield
            if l == 0:
                fgate, fkey = e_, ekey
            else:
                fgate, fkey = w2[2].next()
                for gg in range(2):
                    col = l * 4 + dirn * 2 + gg
                    P.ts("dve", fgate[:, gg, :], e_[:, gg, :], omlb[:, col:col + 1], lbt[:, col:col + 1], ALU.mult, ALU.add,
                         [ekey, "omlb", "lbt"], [fkey])
                yield
            P.act(fl(lf), fl(fgate), AF.Ln, [fkey], [lfkey])
            kk, kkkey = w2[3].next()
            P.act(fl(kk), fl(fgate), AF.Identity, [fkey], [kkkey], bias=1.0, scale=-1.0)
            yield
        Gc, Gkey = w2[4].next()
        m2 = scanmask2[:, mi, :]
        if bwd:
            P.op("dve", _scan(fl(Gc)[:, ::-1], m2, fl(lf)[:, ::-1]), [lfkey, "scanmask2"], [Gkey])
        else:
            P.op("dve", _scan(fl(Gc), m2, fl(lf)), [lfkey, "scanmask2"], [Gkey])
        yield
        Gv = c3(fl(Gc))
        n2 = 2 * nch
        mid = (C // 2) if bwd else (C // 2 - 1)
        lastp = 0 if bwd else C - 1
        sm, smkey = sm_r.next()
        P.tt("dve", sm[:, 1, 0:n2], Gv[:, :, lastp], Gv[:, :, mid], ALU.subtract, [Gkey], [smkey])
        Gm, Gmkey = w2[5].next()
        P.tt("dve", c3(fl(Gm)), Gv, Gv[:, :, mid:mid + 1].to_broadcast([128, n2, C]), ALU.subtract, [Gkey], [Gmkey])
        yield
        P.act(sm[:, 0, 0:n2], Gv[:, :, mid], AF.Exp, [Gkey], [smkey], scale=kappa)
        P.act(sm[:, 2, 0:n2], Gv[:, :, lastp], AF.Exp, [Gkey], [smkey], scale=kappa)
        yield
        P.act(sm[:, 1, 0:n2], sm[:, 1, 0:n2], AF.Exp, [smkey], [smkey], scale=kappa)
        A_, Akey = w2[6].next(); B_, Bkey = w2[7].next()
        P.act(fl(A_), fl(Gm), AF.Exp, [Gmkey], [Akey], scale=kappa)
        yield
        P.act(fl(B_), fl(Gm), AF.Exp, [Gmkey], [Bkey], scale=-kappa)
        yield
        Qf, Qfkey = w2[8].next()
        Kt, Ktkey = kt_r.next(); Kout, Koutkey = kout_r.next()
        if gla and not is_ctx:
            ts_ = slice((tau - 2) * 128, (tau - 1) * 128)
            cosb = ropecos[:, ts_].unsqueeze(1).to_broadcast([128, 2, 128])
            sinb = ropesin[:, ts_].unsqueeze(1).to_broadcast([128, 2, 128])
            r1, r1key = w2[2].next(); r2, r2key = w2[3].next()
            P.tt("dve", r1[:], qk[:, 0:2, :], cosb, ALU.mult, [qkkey, "ropecos"], [r1key])
            P.tt("pool", r2[:], qk[:, 2:4, :], sinb, ALU.mult, [qkkey, "ropesin"], [r2key])
            yield
            P.tt("dve", r1[:], r1[:], r2[:], ALU.add, [r1key, r2key], [r1key])
            r3, r3key = w2[2].next(); r4, r4key = w2[3].next()
            P.tt("pool", r3[:], qk[:, 4:6, :], cosb, ALU.mult, [qkkey, "ropecos"], [r3key])
            yield
            P.tt("dve", Qf[:], r1[:], A_[:], ALU.mult, [r1key, Akey], [Qfkey])
            P.tt("pool", r4[:], qk[:, 6:8, :], sinb, ALU.mult, [qkkey, "ropesin"], [r4key])
            yield
            P.tt("pool", r3[:], r3[:], r4[:], ALU.add, [r3key, r4key], [r3key])
            yield
            P.tt("pool", Kt[:], r3[:], B_[:], ALU.mult, [r3key, Bkey], [Ktkey])
            yield
        elif gla:
            P.tt("dve", Qf[:], qk[:, 0:2, :], A_[:], ALU.mult, [qkkey, Akey], [Qfkey])
            P.tt("pool", Kt[:], qk[:, 4:6, :], B_[:], ALU.mult, [qkkey, Bkey], [Ktkey])
            yield
        else:
            P.tt("dve", Qf[:], qh[:], A_[:], ALU.mult, [qhkey, Akey], [Qfkey])
            P.tt("pool", Kt[:], kk[:], B_[:], ALU.mult, [kkkey, Bkey], [Ktkey])
            yield
        Qbd, Qbdkey = qbd_r.next()
        P.cp("act", Qbd[0:64, :, 0, :], Qf[0:64, :, :], [Qfkey], [Qbdkey])
        yield
        P.cp("act", Qbd[64:128, :, 1, :], Qf[64:128, :, :], [Qfkey], [Qbdkey])
        Qin, Qinkey = qin_r.next()
        P.tt("dve" if gla else "pool", c3(fl(Qin)), c3(fl(Qf)), sm[:, 0, 0:n2].unsqueeze(2).to_broadcast([128, n2, C]), ALU.mult,
             [Qfkey, smkey], [Qinkey])
        P.tt("pool", c3(fl(Kout)), c3(fl(Kt)), sm[:, 1, 0:n2].unsqueeze(2).to_broadcast([128, n2, C]), ALU.mult,
             [Ktkey, smkey], [Koutkey])
        yield
        tpp, tpkey = tp_ps.next()
        for gg in range(2):
            P.tr(tpp[:, gg * 128:(gg + 1) * 128], Kout[:, gg, :], identb[:], [Koutkey, "identb"], [tpkey])
        koT, koTkey = koT_r.next()
        P.cp("act", koT[:], tpp[:, 0:256], [tpkey], [koTkey])
        yield
        scp, sckey = sc_ps.next()
        for gg in range(2):
            P.mm(scp[:, gg * 256:(gg + 1) * 256], Kt[:, gg, :], Qbd[:, gg, :, :].rearrange("p h t -> p (h t)"),
                 [Ktkey, Qbdkey], [sckey])
        PT, PTkey = pt_r.next()
        tmi = (2 if gla else 4) + (1 if bwd else 0)
        P.tt("dve", PT[:], scp[:, 0:512].rearrange("p (a t) -> p a t", a=4),
             trimask[:, tmi, :].unsqueeze(1).to_broadcast([128, 4, 128]), ALU.mult, [sckey, "trimask"], [PTkey])
        yield
        vx = vxkey = None
        if not gla:
            vx, vxkey = vexp_r.next()
            P.tt(VEXP_ENG, vx[:], vt2[:].rearrange("p (a d) -> p a d", a=4).unsqueeze(2).to_broadcast([128, 4, NHC, 64]),
                 cm[:].unsqueeze(1).unsqueeze(3).to_broadcast([128, 4, NHC, 64]), ALU.mult, [vkey, "cm"], [vxkey])
            yield
        for gg in range(2):
            g = g0 + gg
            ctxs[gg].update(dict(vt=vt2[:, gg * 128:(gg + 1) * 128], vkey=vkey, koT=koT[:, gg * 128:(gg + 1) * 128], koTkey=koTkey,
                                 vx=(vx[:, gg * 2:(gg + 1) * 2, :, :] if vx is not None else None), vxkey=vxkey,
                                 PT=PT[:, gg * 2:(gg + 1) * 2, :], PTkey=PTkey, Qin=Qin[:, gg, :], Qinkey=Qinkey,
                                 sm=sm[:, :, gg * nch:(gg + 1) * nch], smkey=smkey, b0=b0s[gg], nsl=NSL[g], C=C, nch=nch,
                                 gla=gla, gi=gg, bwd=bwd, is_ctx=is_ctx, cs=cs, g=g, tau=tau))

    def super_mid(ctxs):
        c0 = ctxs[0]
        gla = c0["gla"]; C = c0["C"]; nch = c0["nch"]; bwd = c0["bwd"]
        kvp, kvkey = kv_ps.next()
        W_ = 64 if gla else NHC * 64
        for c_ in ctxs:
            gg = c_["gi"]
            for h in range(2):
                if gla:
                    P.mm(kvp[h * 64:(h + 1) * 64, gg * W_:(gg + 1) * W_], c_["koT"][:, h * 64:(h + 1) * 64],
                         c_["vt"][:, h * 64:(h + 1) * 64], [c_["koTkey"], c_["vkey"]], [kvkey], tp=(0, h * 64))
                else:
                    P.mm(kvp[h * 64:(h + 1) * 64, gg * W_:(gg + 1) * W_], c_["koT"][:, h * 64:(h + 1) * 64],
                         c_["vx"][:, h, :, :].rearrange("p j d -> p (j d)"), [c_["koTkey"], c_["vxkey"]], [kvkey], tp=(0, h * 64))
            yield
        jorder = list(range(nch - 1, -1, -1)) if bwd else list(range(nch))
        for jj, j in enumerate(jorder):
            for c_ in ctxs:
                g = c_["g"]; gg = c_["gi"]; nsl = c_["nsl"]
                s_in = (c_["b0"] + jj) % nsl
                s_out = (c_["b0"] + jj + 1) % nsl
                P.stt("dve", S[g][:, s_out, :], S[g][:, s_in, :], c_["sm"][:, 2, j:j + 1],
                      kvp[:, gg * W_ + j * 64:gg * W_ + (j + 1) * 64], ALU.mult, ALU.add,
                      [f"S{g}_{s_in}", c_["smkey"], kvkey], [f"S{g}_{s_out}"])
            yield

    def super_out(ctxs):
        c0 = ctxs[0]
        gla = c0["gla"]; C = c0["C"]; nch = c0["nch"]; bwd = c0["bwd"]; is_ctx = c0["is_ctx"]; cs = c0["cs"]; tau = c0["tau"]
        g0 = c0["g"]
        op_, okey = o_ps.next()
        for c_ in ctxs:
            gg = c_["gi"]
            for h in range(2):
                P.mm(op_[h * 64:(h + 1) * 64, gg * 128:(gg + 1) * 128], c_["vt"][:, h * 64:(h + 1) * 64], c_["PT"][:, h, :],
                     [c_["vkey"], c_["PTkey"]], [okey], start=(gg == 0), stop=False, tp=(0, h * 64))
            yield
        jorder = list(range(nch - 1, -1, -1)) if bwd else list(range(nch))
        for jj, j in enumerate(jorder):
            for c_ in ctxs:
                g = c_["g"]; gg = c_["gi"]
                s_in = (c_["b0"] + jj) % c_["nsl"]
                for h in range(2):
                    P.mm(op_[h * 64:(h + 1) * 64, gg * 128 + j * C:gg * 128 + (j + 1) * C], S[g][h * 64:(h + 1) * 64, s_in, :],
                         c_["Qin"][h * 64:(h + 1) * 64, j * C:(j + 1) * C], [f"S{g}_{s_in}", c_["Qinkey"]], [okey],
                         start=False, stop=True, tp=(h * 64, h * 64))
            yield
        g2 = lambda d3: d3.rearrange("n p t -> p n t")
        if bwd:
            osb, oskey = o_r.next()
            P.cp("act", osb[:], op_[:, 0:256], [okey], [oskey])
            P.dma(g2(OB[g0:g0 + 2, :, cs]), osb[:].rearrange("p (n t) -> p n t", n=2), reads=[oskey],
                  writes=[("OB", g0, tau), ("OB", g0 + 1, tau)], eng="pool")
            yield
        elif not (is_ctx and not need_ctx):
            obt, obtkey = ob_r.next()
            P.dma(obt[:].rearrange("p (n t) -> p n t", n=2), g2(OB[g0:g0 + 2, :, cs]),
                  reads=[("OB", g0, tau), ("OB", g0 + 1, tau)], writes=[obtkey])
            gt_, gtkey = gate_r.next()
            gft = 20 if gla else 24
            P.dma(gt_[:].rearrange("p (n t) -> p n t", n=2), g2(FMB[gft:gft + 2, :, cs]),
                  reads=[("FMB", gft, _st(tau)), ("FMB", gft + 1, _st(tau))], writes=[gtkey])
            osb, oskey = o_r.next()
            P.tt("dve", osb[:], op_[:, 0:256], obt[:], ALU.add, [okey, obtkey], [oskey])
            yield
            osq, osqkey = sq_r.next()
            P.act(osq[:], osb[:], AF.Square, [oskey], [osqkey])
            yield
            zp, zkey = z_ps.next()
            P.mm(zp[:, 0:256], onesblk[:], osq[:], ["onesblk", osqkey], [zkey])
            rs, rskey = o_r.next()
            P.act(rs[:], zp[:, 0:256], AF.Ln, [zkey, "eps"], [rskey], bias=eps_t[:])
            yield
            P.act(rs[:], rs[:], AF.Exp, [rskey], [rskey], scale=-0.5)
            yield
            nw = G["glanw"] if gla else G["hgnw"]
            P.stt("dve", osb[:], osb[:], nw[:], rs[:], ALU.mult, ALU.mult, [oskey, rskey, "glanw", "hgnw"], [oskey])
            yield
            og, ogkey = og_r.next()
            P.tt("pool", og[:], osb[:], gt_[:], ALU.mult, [oskey, gtkey], [ogkey])
            P.dma(g2(OT[4 + g0:6 + g0, :, cs]), og[:].rearrange("p (n t) -> p n t", n=2), reads=[ogkey],
                  writes=[("OT", 4 + g0, tau), ("OT", 5 + g0, tau)], eng="pool")
            yield

    for dirn in DBG["p2_dirs"]:
        bwd = dirn == 1
        order = [1, 0] + list(range(NT - 1, 1, -1)) if bwd else list(range(NT))
        for g in range(4):
            base[g] = 0
            P.memset("dve", S[g][:, 0, :], 0.0, [f"S{g}_0"])
        if DBG["p2_tiles"]:
            order = order[:DBG["p2_tiles"]]
        mids = []
        outs = []
        mid_ctxs = []
        for tau in order:
            cts = [[dict(), dict()], [dict(), dict()]]
            fronts = [super_front(dirn, tau, False, cts[0]), super_front(dirn, tau, True, cts[1])]
            _interleave(outs + mids + fronts)
            outs = [super_out(c2) for c2 in mid_ctxs]
            mids = [super_mid(c2) for c2 in cts]
            mid_ctxs = cts
        _interleave(outs + mids)
        _interleave([super_out(c2) for c2 in mid_ctxs])


def _st(tau):
    return 0 if tau < 2 else 2 + 4 * ((tau - 2) // 4)


def _scan(out, d0, d1):
    return lambda e: e.tensor_tensor_scan(out, d0, d1, 0.0, ALU.mult, ALU.add)


def build_p3(P, l, need_ctx, FMB, TM, OT, ttab, ones64):
    KTb = Ring(P, "KTb", 2, [128, NTOK], BF16)
    Vd = Ring(P, "Vd", 2, [128, 68, 64], BF16)
    q_r = Ring(P, "q3", 2, [128, 512], BF16); g_r = Ring(P, "g3", 2, [128, 512], BF16)
    pe_r = Ring(P, "pe3", 4, [128, 512], BF16); p_r = Ring(P, "p3", 4, [128, 512], BF16)
    rc_r = Ring(P, "rc3", 2, [128, 512]); o_r = Ring(P, "o3", 2, [128, 512]); og_r = Ring(P, "og3", 2, [128, 512], BF16)
    s_ps = Ring(P, "s_ps", 3, [128, 512], psum=True)
    ao_ps = Ring(P, "ao_ps", 2, [128, 512], psum=True); as_ps = Ring(P, "as_ps", 2, [128, 512], psum=True)
    TMv = TM.rearrange("(r k) c -> k r c", k=64)

    def row_interval(rp):
        if rp <= 7:
            return 0, rp + 4
        if rp >= 56:
            return rp - 3, 63
        return rp - 3, rp + 4

    for hp in range(4):
        kt_, ktkey = KTb.next()
        P.dma(kt_[:], FMB[4 + hp, :, :], reads=[("FMB", 4 + hp, s) for s in [0] + [2 + 4 * i for i in range(8)]], writes=[ktkey])
        vd, vdkey = Vd.next()
        for hh in range(2):
            for part in range(4):
                r0, r1 = part * 17, (part + 1) * 17
                P.dma(vd[hh * 64:(hh + 1) * 64, r0:r1, :], TMv[:, r0:r1, hp * 128 + hh * 64:hp * 128 + (hh + 1) * 64],
                      reads=[("TM", t_) for t_ in range(NT)], writes=[vdkey])
        blocks = [("lat", b) for b in range(8)] + ([("ctx", 0)] if need_ctx else [])
        for kind, b in blocks:
            if kind == "lat":
                tok0 = CTX + b * 512
                nq = 512
                pieces = [(rho, 0, 512, None) for rho in range(4)]
                for rp in range(64):
                    lo, hi = row_interval(rp)
                    a = max(lo, 8 * b); e_ = min(hi, 8 * b + 7)
                    if a > e_:
                        continue
                    u0 = 7 - rp + a
                    pieces.append((4 + rp, (a - 8 * b) * 64, (e_ - 8 * b + 1) * 64, u0))
            else:
                tok0 = 0
                nq = 256
                pieces = [(rho, 0, 256, None) for rho in range(4)]
            qt, qkey = q_r.next(); gt_, gkey = g_r.next()
            st_reads = sorted(set(_st(t_) for t_ in range(tok0 // 128, (tok0 + nq) // 128)))
            P.dma(qt[:, 0:nq], FMB[hp, :, tok0:tok0 + nq], reads=[("FMB", hp, s) for s in st_reads], writes=[qkey])
            P.dma(gt_[:, 0:nq], FMB[8 + hp, :, tok0:tok0 + nq], reads=[("FMB", 8 + hp, s) for s in st_reads], writes=[gkey])
            ao, aokey = ao_ps.next(); as_, askey = as_ps.next()
            staged = {}

            def stage_a(pi):
                rho, c0, c1, u0 = pieces[pi]
                sp_, spkey = s_ps.next()
                for hh in range(2):
                    P.mm(sp_[hh * 64:(hh + 1) * 64, c0:c1], kt_[hh * 64:(hh + 1) * 64, rho * 64:(rho + 1) * 64],
                         qt[hh * 64:(hh + 1) * 64, c0:c1], [ktkey, qkey], [spkey], tp=(hh * 64, hh * 64))
                pe_, pekey = pe_r.next()
                P.act(pe_[:, c0:c1], sp_[:, c0:c1], AF.Exp, [spkey], [pekey])
                if u0 is not None:
                    pp, ppkey = p_r.next()
                    nr = (c1 - c0) // 64
                    P.tt("dve", pp[:, c0:c1], pe_[:, c0:c1], ttab[:, hp, u0 * 64:(u0 + nr) * 64], ALU.mult,
                         [pekey, "ttab"], [ppkey])
                else:
                    pp, ppkey = pe_, pekey
                staged[pi] = (pp, ppkey)

            def stage_b(pi):
                rho, c0, c1, u0 = pieces[pi]
                pp, ppkey = staged.pop(pi)
                for hh in range(2):
                    P.mm(ao[hh * 64:(hh + 1) * 64, c0:c1], vd[hh * 64:(hh + 1) * 64, rho, :], pp[hh * 64:(hh + 1) * 64, c0:c1],
                         [vdkey, ppkey], [aokey], start=(pi == 0), stop=(pi == len(pieces) - 1), tp=(hh * 64, hh * 64))
                    P.mm(as_[hh * 64:(hh + 1) * 64, c0:c1], ones64[hh * 64:(hh + 1) * 64, :], pp[hh * 64:(hh + 1) * 64, c0:c1],
                         ["ones64", ppkey], [askey], start=(pi == 0), stop=(pi == len(pieces) - 1), tp=(hh * 64, hh * 64))

            LA = 2
            for pi in range(min(LA, len(pieces))):
                stage_a(pi)
            for pi in range(len(pieces)):
                if pi + LA < len(pieces):
                    stage_a(pi + LA)
                stage_b(pi)
            rc, rckey = rc_r.next()
            P.recip(rc[:, 0:nq], as_[:, 0:nq], [askey], [rckey])
            ob, obkey = o_r.next()
            P.tt("dve", ob[:, 0:nq], ao[:, 0:nq], rc[:, 0:nq], ALU.mult, [aokey, rckey], [obkey])
            og, ogkey = og_r.next()
            P.tt("pool", og[:, 0:nq], ob[:, 0:nq], gt_[:, 0:nq], ALU.mult, [obkey, gkey], [ogkey])
            P.dma(OT[hp, :, tok0:tok0 + nq], og[:, 0:nq], reads=[ogkey],
                  writes=[("OT", hp, t_) for t_ in range(tok0 // 128, (tok0 + nq) // 128)], eng="pool")


_CACHE = {}


def kernel(**inputs):
    inputs = {k: np.asarray(v) for k, v in inputs.items()}
    common, per = host_layout(inputs)
    if "nc" not in _CACHE:
        _CACHE["nc"] = build_program()
    nc = _CACHE["nc"]
    in_maps = [dict(common, **p) for p in per]
    res = run_bass_kernel_spmd(nc, in_maps, core_ids=list(range(len(per))))
    out = np.stack([np.asarray(r["out"], dtype=np.float32) for r in res.results], axis=0)
    return out
```

```python
import numpy as np
import ml_dtypes
from contextlib import ExitStack
import concourse.bass as bass
import concourse.mybir as mybir
from concourse.bass_utils import run_bass_kernel_spmd

F32 = mybir.dt.float32
BF16 = mybir.dt.bfloat16
AF = mybir.ActivationFunctionType
ALU = mybir.AluOpType
NPBF = ml_dtypes.bfloat16

ENGS = ("pe", "act", "dve", "pool", "sp")
SAME_ENGINE_SYNC = {"pe": False, "act": True, "dve": True, "pool": True, "sp": False}
N_DMA_SEMS = {"sp": 20, "act": 4, "pool": 12}

D = 1024
KT = 8
T = 4096
CTX = 256
NTOK = T + CTX
NT = NTOK // 128
NEXT = 4992
N_FMB = 26
N_FMF = 5
EPS = 1e-6
CHG = 32


class Ins:
    __slots__ = ("eng", "fn", "deps", "is_dma", "dma_sem", "dma_val", "inc_idx", "needed", "pos", "guard")

    def __init__(self, eng, fn, is_dma):
        self.eng = eng
        self.fn = fn
        self.deps = []
        self.is_dma = is_dma
        self.dma_sem = None
        self.dma_val = None
        self.inc_idx = None
        self.needed = False
        self.guard = None


class Prog:
    def __init__(self):
        self.nc = bass.Bass("TRN2", target_bir_lowering=False)
        self.es = ExitStack()
        self.streams = {e: [] for e in ENGS}
        self.last_w = {}
        self.readers = {}
        self.dma_rr = {e: 0 for e in N_DMA_SEMS}
        self.dma_last = {}
        self.dma_cnt = {}
        self.n_ins = 0
        self.scopes = []

    def push(self):
        self.scopes.append(ExitStack())

    def pop(self):
        self.barrier()
        self.scopes.pop().close()

    def _ctx(self):
        return self.scopes[-1] if self.scopes else self.es

    def sb(self, name, shape, dtype=F32):
        self.uid = getattr(self, "uid", 0) + 1
        return self._ctx().enter_context(self.nc.sbuf_tensor(f"sb{self.uid}_{name}", list(shape), dtype))

    def ps(self, name, shape=(128, 512), dtype=F32):
        self.uid = getattr(self, "uid", 0) + 1
        return self._ctx().enter_context(self.nc.psum_tensor(f"ps{self.uid}_{name}", list(shape), dtype))

    def dram(self, name, shape, dtype=F32, kind="Internal"):
        return self.nc.dram_tensor(name, list(shape), dtype, kind=kind)

    def op(self, eng, fn, reads=(), writes=(), dma=False):
        ins = Ins(eng, fn, dma)
        ins.pos = self.n_ins
        self.n_ins += 1
        deps = []
        for r in reads:
            w = self.last_w.get(r)
            if w is not None:
                deps.append(w)
        for wkey in writes:
            w = self.last_w.get(wkey)
            if w is not None:
                deps.append(w)
            deps.extend(self.readers.get(wkey, {}).values())
        if dma:
            slot = self.dma_rr[eng] % N_DMA_SEMS[eng]
            self.dma_rr[eng] += 1
            key = (eng, slot)
            prev = self.dma_last.get(key)
            if prev is not None:
                ins.guard = prev
            self.dma_cnt[key] = self.dma_cnt.get(key, 0) + 1
            ins.dma_sem = key
            ins.dma_val = 16 * self.dma_cnt[key]
            self.dma_last[key] = ins
        best = {}
        for d in deps:
            if d is ins:
                continue
            k = d.dma_sem if d.is_dma else d.eng
            if k not in best or best[k].pos < d.pos:
                best[k] = d
        for d in best.values():
            ins.deps.append(d)
            d.needed = True
        if ins.guard is not None:
            ins.guard.needed = True
        mykey = ins.dma_sem if dma else eng
        for r in reads:
            self.readers.setdefault(r, {})[mykey] = ins
        for wkey in writes:
            self.last_w[wkey] = ins
            self.readers[wkey] = {}
        self.streams[eng].append(ins)
        return ins

    def barrier(self):
        lasts = []
        for e in ENGS:
            for ins in reversed(self.streams[e]):
                if ins.fn is not None and not ins.is_dma:
                    lasts.append(ins)
                    break
        lasts += list(self.dma_last.values())
        for e in ENGS:
            b = Ins(e, None, False)
            b.pos = self.n_ins
            self.n_ins += 1
            for d in lasts:
                b.deps.append(d)
                d.needed = True
            self.streams[e].append(b)
        self.last_w = {}
        self.readers = {}

    def dma(self, out, in_, reads=(), writes=(), eng="sp"):
        return self.op(eng, lambda e: e.dma_start(out=out, in_=in_), reads, writes, dma=True)

    def mm(self, out, lhsT, rhs, reads, writes, start=True, stop=True, tp=None):
        if tp is None:
            return self.op("pe", lambda e: e.matmul(out, lhsT, rhs, start=start, stop=stop), reads, writes)
        return self.op("pe", lambda e: e.matmul(out, lhsT, rhs, start=start, stop=stop, tile_position=tp,
                                                skip_group_check=True), reads, writes)

    def tr(self, out, in_, ident, reads, writes):
        return self.op("pe", lambda e: e.transpose(out, in_, ident), reads, writes)

    def act(self, out, in_, func, reads, writes, bias=None, scale=None, accum_out=None):
        kw = {}
        if bias is not None:
            kw["bias"] = bias
        if scale is not None:
            kw["scale"] = scale
        if accum_out is not None:
            kw["accum_out"] = accum_out
        return self.op("act", lambda e: e.activation(out, in_, func, **kw), reads, writes)

    def tt(self, eng, out, in0, in1, op, reads, writes):
        return self.op(eng, lambda e: e.tensor_tensor(out, in0, in1, op), reads, writes)

    def ts(self, eng, out, in0, s1, s2, op0, op1, reads, writes):
        if s2 is None:
            return self.op(eng, lambda e: e.tensor_scalar(out, in0, s1, None, op0=op0), reads, writes)
        return self.op(eng, lambda e: e.tensor_scalar(out, in0, s1, s2, op0=op0, op1=op1), reads, writes)

    def stt(self, eng, out, in0, scalar, in1, op0, op1, reads, writes):
        return self.op(eng, lambda e: e.scalar_tensor_tensor(out, in0, scalar, in1, op0=op0, op1=op1), reads, writes)

    def cp(self, eng, out, in_, reads, writes):
        if eng == "act":
            return self.op("act", lambda e: e.copy(out, in_), reads, writes)
        return self.op(eng, lambda e: e.tensor_copy(out, in_), reads, writes)

    def recip(self, out, in_, reads, writes):
        return self.op("dve", lambda e: e.reciprocal(out, in_), reads, writes)

    def memset(self, eng, ap, val, writes):
        return self.op(eng, lambda e: e.memset(ap, val), (), writes)

    def finish(self, final_waits=()):
        nc = self.nc
        es = self.es
        sems = {e: es.enter_context(nc.semaphore("s_" + e)) for e in ENGS}
        dsems = {}
        for e, n in N_DMA_SEMS.items():
            for i in range(n):
                dsems[(e, i)] = es.enter_context(nc.semaphore(f"d_{e}{i}"))
        for fw in final_waits:
            fw.needed = True
        for e in ENGS:
            c = 0
            for ins in self.streams[e]:
                if ins.is_dma or ins.fn is None:
                    continue
                if ins.needed:
                    c += 1
                    ins.inc_idx = c
        streams = self.streams

        def emit(e, eng_obj):
            known = {}

            def wait_for(d):
                if d.is_dma:
                    k = ("d",) + d.dma_sem
                    if known.get(k, 0) >= d.dma_val:
                        return
                    eng_obj.wait_ge(dsems[d.dma_sem], d.dma_val)
                    known[k] = d.dma_val
                else:
                    if d.eng == e and not SAME_ENGINE_SYNC[e]:
                        return
                    k = ("e", d.eng)
                    if known.get(k, 0) >= d.inc_idx:
                        return
                    eng_obj.wait_ge(sems[d.eng], d.inc_idx)
                    known[k] = d.inc_idx

            for ins in streams[e]:
                for d in ins.deps:
                    wait_for(d)
                if ins.guard is not None:
                    wait_for(ins.guard)
                if ins.fn is None:
                    continue
                bi = ins.fn(eng_obj)
                if ins.is_dma:
                    bi.then_inc(dsems[ins.dma_sem], 16)
                elif ins.needed:
                    bi.then_inc(sems[e], 1)
            if e == "sp":
                for fw in final_waits:
                    wait_for(fw)

        with nc.Block() as block:
            @block.tensor
            def _(eng):
                emit("pe", eng)

            @block.scalar
            def _(eng):
                emit("act", eng)

            @block.vector
            def _(eng):
                emit("dve", eng)

            @block.gpsimd
            def _(eng):
                emit("pool", eng)

            @block.sync
            def _(eng):
                emit("sp", eng)
        while self.scopes:
            self.scopes.pop().close()
        self.es.close()
        return nc


class Ring:
    def __init__(self, P, name, n, shape, dtype=F32, psum=False):
        self.name = name
        if psum:
            self.bufs = [P.ps(f"{name}{i}", shape, dtype) for i in range(n)]
        else:
            self.bufs = [P.sb(f"{name}{i}", shape, dtype) for i in range(n)]
        self.i = 0

    def next(self):
        j = self.i % len(self.bufs)
        self.i += 1
        return self.bufs[j], f"{self.name}{j}"


def ext_col_index():
    cols = []
    cols += list(range(0, 512)) + list(range(512, 1024)) + list(range(1536, 2048))

    def pad_heads(base, swap):
        out = []
        for h in range(4):
            for i in range(32):
                j = (i + 8 if (i % 16) < 8 else i - 8) if swap else i
                out.append(base + h * 32 + j)
            out += [-1] * 32
        return out
    cols += pad_heads(2048, False) + pad_heads(2048, True) + pad_heads(2176, False) + pad_heads(2176, True)
    cols += list(range(2592, 2848)) + list(range(2848, 3104)) + list(range(3872, 4128))
    cols += list(range(3104, 3360)) + list(range(3360, 3616)) + list(range(2560, 2592)) + [-1] * 96
    cols += list(range(1024, 1536)) + list(range(2304, 2560)) + list(range(3616, 3872))
    assert len(cols) == NEXT
    return np.array(cols)


def host_consts():
    c = {}
    c["identf"] = np.eye(128, dtype=np.float32)
    c["identb"] = np.eye(128, dtype=np.float32).astype(NPBF)
    i = np.arange(128)
    m16 = np.ones((128, 128), np.float32); m16[:, i % 16 == 0] = 0
    m128 = np.ones((128, 128), np.float32); m128[:, 0] = 0
    m32 = np.ones((128, 128), np.float32); m32[:, i % 32 == 0] = 0
    c["scanmask"] = np.stack([m16, m128, m32], 0)
    s = i[:, None]; t = i[None, :]
    tri = []
    for C in (16, 128, 32):
        same = (s // C) == (t // C)
        tri.append((same & (s <= t)).astype(np.float32))
        tri.append((same & (s >= t)).astype(np.float32))
    c["trimask"] = np.stack(tri, 0)
    c["cm"] = ((i[:, None] // CHG) == np.arange(128 // CHG)[None, :]).astype(np.float32).astype(NPBF)
    c["onesblk"] = (((i[:, None] // 64) == (i[None, :] // 64)).astype(np.float32) / 64.0).astype(NPBF)
    c["ones64"] = np.ones((128, 64), np.float32).astype(NPBF)
    inv_freq = (1.0 / (10000.0 ** (np.arange(0, 16, 2, dtype=np.float32) / np.float32(16)))).astype(np.float32)
    tt_ = np.arange(T)
    cos = np.zeros((128, T), np.float32); sins = np.zeros((128, T), np.float32)
    for p in range(128):
        ii = p % 64
        if ii >= 32:
            continue
        blk = ii // 16
        j = ii % 16
        f = j % 8
        pos = (tt_ // 64 if blk == 0 else tt_ % 64).astype(np.float32)
        ang = (pos * inv_freq[f]).astype(np.float32)
        cos[p] = np.cos(ang)
        sins[p] = -np.sin(ang) if j < 8 else np.sin(ang)
    c["ropecos"] = cos
    c["ropesin"] = sins
    return c


def host_layout(inp):
    f32 = np.float32
    common = dict(host_consts())
    cols = ext_col_index()
    valid = cols >= 0
    DEPTH = inp["w_in"].shape[0]
    q = np.arange(64)
    col_start = np.clip(q - 8, 0, 48)
    kc = np.arange(64)
    inwin = (kc[:, None] >= col_start[None, :]) & (kc[:, None] < col_start[None, :] + 16)
    didx = np.clip(kc[:, None] - q[None, :], -15, 15) + 15
    pidx = np.arange(128)
    for l in range(DEPTH):
        w_ext = np.zeros((D, NEXT), f32)
        w_ext[:, valid] = inp["w_in"][l][:, cols[valid]]
        common[f"w_in{l}"] = w_ext
        common[f"ada_w{l}"] = np.ascontiguousarray(inp["ada_w"][l], dtype=f32)
        common[f"ada_bT{l}"] = np.ascontiguousarray(inp["ada_b"][l].reshape(24, 128).T, dtype=f32)
        common[f"ada_bgt{l}"] = np.ascontiguousarray(inp["ada_b"][l][2048:3072].reshape(1, 1024), dtype=f32)
        common[f"norm_wT{l}"] = np.ascontiguousarray(inp["norm_w"][l].reshape(8, 128).T, dtype=f32)
        common[f"w_out{l}"] = np.ascontiguousarray(inp["w_out"][l], dtype=f32)
        rpb = inp["na_rpb"][l]
        g = rpb[:, ::-1, :][:, :, didx]
        g = np.where(inwin[None, None], g, f32(-30000.0))
        g = g.transpose(0, 2, 1, 3).reshape(4, 128, 15 * 64)
        common[f"rpbt{l}"] = np.ascontiguousarray(g, dtype=f32)
        wgk = inp["gla_w_gk"][l]
        bgk = inp["gla_b_gk"][l]
        wp = np.zeros((32, 2, 2, 128), f32)
        bp = np.zeros((128, 2, 2), f32)
        for d_ in range(2):
            for g_ in range(2):
                for p in range(128):
                    h = 2 * g_ + p // 64
                    ii = p % 64
                    if ii < 32:
                        wp[d_ * 16:(d_ + 1) * 16, d_, g_, p] = wgk[d_, :, h * 32 + ii]
                        bp[p, d_, g_] = bgk[d_, h * 32 + ii]
        common[f"wgk{l}"] = wp
        common[f"bgk{l}"] = bp
        common[f"glanw{l}"] = np.ascontiguousarray(inp["gla_norm_w"][l][pidx % 64].reshape(128, 1), dtype=f32)
        common[f"hgnw{l}"] = np.ascontiguousarray(inp["hgrn_norm_w"][l][pidx % 64].reshape(128, 1), dtype=f32)
    lb = inp["hgrn_lower_bounds"]
    common["lbraw"] = np.ascontiguousarray(lb.reshape(2, 2, 2, 128).transpose(3, 0, 1, 2).reshape(128, 8), dtype=f32)
    common["fnw"] = np.ascontiguousarray(inp["final_norm_w"].reshape(1, D), dtype=f32)
    per = []
    B = inp["x"].shape[0]
    cc = inp["c_ctx"].reshape(8, 128).T
    for b in range(B):
        cv = np.concatenate([inp["c"][b].reshape(8, 128).T, cc], axis=1)
        per.append({"x": np.ascontiguousarray(inp["x"][b], dtype=f32),
                    "ctx": np.ascontiguousarray(inp["ctx"][b], dtype=f32),
                    "cvec": np.ascontiguousarray(cv, dtype=f32)})
    return common, per


FMB_EVAC = {}
for _t in range(0, 4):
    FMB_EVAC[_t] = ("scale", 0.125)
for _t in range(4, 8):
    FMB_EVAC[_t] = ("copy", 1.0)
for _t in range(8, 12):
    FMB_EVAC[_t] = ("silu", 1.0)
for _t in range(12, 16):
    FMB_EVAC[_t] = ("scale", 32.0 ** -0.5)
for _t in range(16, 20):
    FMB_EVAC[_t] = ("copy", 1.0)
for _t in (20, 21, 24, 25):
    FMB_EVAC[_t] = ("silu", 1.0)
for _t in (22, 23):
    FMB_EVAC[_t] = ("scale", 0.125)


import os
DBG = {"skip_p1": os.environ.get("K_DBG_SKIP_P1") == "1",
       "p2_tiles": int(os.environ.get("K_DBG_P2_TILES", "0")),
       "p2_groups": [int(x) for x in os.environ.get("K_DBG_P2_GROUPS", "0,1,2,3").split(",")],
       "p2_dirs": [int(x) for x in os.environ.get("K_DBG_P2_DIRS", "1,0").split(",")],
       "cut": int(os.environ.get("K_DBG_P2_CUT", "99"))}


def build_program(depth=2, stop_after=None, debug=False):
    P = Prog()
    nc = P.nc
    kdbg = "ExternalOutput" if debug else "Internal"

    def din(name, shape, dt=F32):
        return P.dram(name, shape, dt, kind="ExternalInput").ap()

    x_d = din("x", [T, D]); ctx_d = din("ctx", [CTX, D]); cvec_d = din("cvec", [128, 16])
    identf_d = din("identf", [128, 128]); identb_d = din("identb", [128, 128], BF16)
    scanmask_d = din("scanmask", [3, 128, 128]); trimask_d = din("trimask", [6, 128, 128])
    cm_d = din("cm", [128, 128 // CHG], BF16); onesblk_d = din("onesblk", [128, 128], BF16); ones64_d = din("ones64", [128, 64], BF16)
    ropecos_d = din("ropecos", [128, T]); ropesin_d = din("ropesin", [128, T])
    lbraw_d = din("lbraw", [128, 8]); fnw_d = din("fnw", [1, D])
    LW = []
    for l in range(2):
        LW.append(dict(
            w_in=din(f"w_in{l}", [D, NEXT]), ada_w=din(f"ada_w{l}", [D, 3 * D]), ada_bT=din(f"ada_bT{l}", [128, 24]),
            ada_bgt=din(f"ada_bgt{l}", [1, D]), norm_wT=din(f"norm_wT{l}", [128, 8]), w_out=din(f"w_out{l}", [D, D]),
            rpbt=din(f"rpbt{l}", [4, 128, 960]), wgk=din(f"wgk{l}", [32, 2, 2, 128]), bgk=din(f"bgk{l}", [128, 2, 2]),
            glanw=din(f"glanw{l}", [128, 1]), hgnw=din(f"hgnw{l}", [128, 1])))
    out_d = P.dram("out", [T, D], F32, kind="ExternalOutput").ap()
    FMB = P.dram("FMB", [N_FMB, 128, NTOK], BF16, kind=kdbg).ap()
    FMF = P.dram("FMF", [N_FMF, 128, NTOK], F32, kind=kdbg).ap()
    TM = P.dram("TM", [NTOK, 1024], BF16, kind=kdbg).ap()
    OB = P.dram("OB", [4, 128, NTOK], F32, kind=kdbg).ap()
    OT = P.dram("OT", [8, 128, NTOK], BF16, kind=kdbg).ap()
    XN = P.dram("XN", [T, D], F32, kind=kdbg).ap()
    XC = P.dram("XC", [CTX, D], F32, kind=kdbg).ap()
    MODDBG = P.dram("MODDBG", [128, 64], F32, kind=kdbg).ap()

    identf = P.sb("identf", [128, 128]); identb = P.sb("identb", [128, 128], BF16)
    onesblk = P.sb("onesblk", [128, 128], BF16); ones64 = P.sb("ones64", [128, 64], BF16)
    cvec = P.sb("cvec", [128, 16]); sil = P.sb("sil", [128, 16]); silrep = P.sb("silrep", [128, 16, 128])
    lbraw = P.sb("lbraw", [128, 8]); lbt = P.sb("lbt", [128, 8]); omlb = P.sb("omlb", [128, 8])
    eps_t = P.sb("eps_t", [128, 1])
    P.memset("dve", eps_t[:], EPS, ["eps"])
    for nm, dst, src in (("identf", identf, identf_d), ("identb", identb, identb_d), ("onesblk", onesblk, onesblk_d),
                         ("ones64", ones64, ones64_d), ("cvec", cvec, cvec_d), ("lbraw", lbraw, lbraw_d)):
        P.dma(dst[:], src, writes=[nm])
    tmp16 = P.sb("tmp16", [128, 16])
    P.act(tmp16[:], cvec[:], AF.Exp, ["cvec"], ["tmp16"], scale=-1.0)
    P.ts("dve", tmp16[:], tmp16[:], 1.0, None, ALU.add, None, ["tmp16"], ["tmp16"])
    P.recip(tmp16[:], tmp16[:], ["tmp16"], ["tmp16"])
    P.tt("dve", sil[:], cvec[:], tmp16[:], ALU.mult, ["cvec", "tmp16"], ["sil"])
    for j in range(16):
        P.cp("dve", silrep[:, j, :], sil[:, j:j + 1].to_broadcast([128, 128]), ["sil"], ["silrep"])
    P.memset("dve", lbt[:], 0.0, ["lbt"])
    P.tt("dve", lbt[:, 4:8], lbraw[:, 0:4], lbraw[:, 4:8], ALU.subtract, ["lbraw", "lbt"], ["lbt"])
    P.act(lbt[:, 4:8], lbt[:, 4:8], AF.Exp, ["lbt"], ["lbt"])
    P.ts("dve", lbt[:, 4:8], lbt[:, 4:8], 1.0, None, ALU.add, None, ["lbt"], ["lbt"])
    P.recip(lbt[:, 4:8], lbt[:, 4:8], ["lbt"], ["lbt"])
    P.ts("dve", omlb[:], lbt[:], -1.0, 1.0, ALU.mult, ALU.add, ["lbt"], ["omlb"])

    last_out = []

    for l in range(depth):
        W = LW[l]
        last = (l == depth - 1)
        need_ctx = not last
        x_src = x_d if l == 0 else XN
        c_src = ctx_d if l == 0 else XC
        tiles = list(range(NT))

        P.push()
        gsh = P.sb(f"gsh{l}", [128, 2, 2, 8])
        gtrep = P.sb(f"gtrep{l}", [128, 2, D])
        woutb = P.sb(f"woutb{l}", [128, 2 if need_ctx else 1, KT, D], BF16)
        ttab = P.sb(f"ttab{l}", [128, 4, 960], BF16)
        wgk = P.sb(f"wgk{l}", [32, 2, 2, 128]); nbgk = P.sb(f"nbgk{l}", [128, 4])
        glanw = P.sb(f"glanw{l}", [128, 1]); hgnw = P.sb(f"hgnw{l}", [128, 1])

        P.push()
        winb = P.sb("winb", [128, KT, NEXT], BF16)
        win_v = W["w_in"].rearrange("(kt p) j -> p kt j", p=128)
        if not (DBG["skip_p1"] and l == 0):
            for kt in range(KT):
                for hf in range(2):
                    c0, c1 = hf * (NEXT // 2), (hf + 1) * (NEXT // 2)
                    P.dma(winb[:, kt, c0:c1], win_v[:, kt, c0:c1], writes=["winb"], eng="pool")
        tr_ps = Ring(P, "tr_ps", 2, [128, 512], psum=True)
        fm_ps = Ring(P, "fm_ps", 3, [128, 512], psum=True)
        tm_ps = Ring(P, "tm_ps", 2, [128, 512], psum=True)
        modT = P.sb("modT", [128, 16, 2]); adabT = P.sb("adabT", [128, 24]); normwT = P.sb("normwT", [128, 8])
        adabgt = P.sb("adabgt", [128, D])
        stg = Ring(P, "stg", 2, [128, KT, 256])
        mod_ps = tm_ps.bufs[0]; gt_ps = fm_ps
        P.dma(adabT[:], W["ada_bT"], writes=["adabT"]); P.dma(normwT[:], W["norm_wT"], writes=["normwT"])
        P.dma(adabgt[:], W["ada_bgt"].partition_broadcast(128), writes=["adabgt"])
        P.dma(wgk[:], W["wgk"], writes=["wgk"]); P.dma(nbgk[:], W["bgk"].rearrange("p a b -> p (a b)"), writes=["nbgk"])
        P.dma(glanw[:], W["glanw"], writes=["glanw"]); P.dma(hgnw[:], W["hgnw"], writes=["hgnw"])
        P.ts("dve", nbgk[:], nbgk[:], -1.0, None, ALU.mult, None, ["nbgk"], ["nbgk"])
        adaw_v = W["ada_w"].rearrange("(kt p) j -> p kt j", p=128)
        sil2 = sil[:].rearrange("p (w k) -> p w k", w=2)
        for ch in range(12):
            sbuf, skey = stg.next()
            P.dma(sbuf[:], adaw_v[:, :, ch * 256:(ch + 1) * 256], writes=[skey])
            if ch < 8:
                for jl in range(2):
                    jt = ch * 2 + jl
                    for kt in range(KT):
                        P.mm(mod_ps[:, jt * 2:jt * 2 + 2], sbuf[:, kt, jl * 128:(jl + 1) * 128], sil2[:, :, kt],
                             [skey, "sil"], ["tm_ps0"], start=(kt == 0), stop=(kt == KT - 1))
                    P.ts("dve", modT[:, jt, :], mod_ps[:, jt * 2:jt * 2 + 2], adabT[:, jt:jt + 1], None, ALU.add, None,
                         ["tm_ps0", "adabT"], ["modT"])
            else:
                q4 = ch - 8
                for which in range(2):
                    gp, gk_ = gt_ps.next()
                    for kt in range(KT):
                        P.mm(gp[:, 0:256], silrep[:, which * 8 + kt, :], sbuf[:, kt, :], ["silrep", skey], [gk_],
                             start=(kt == 0), stop=(kt == KT - 1))
                    P.tt("dve", gtrep[:, which, q4 * 256:(q4 + 1) * 256], gp[:, 0:256], adabgt[:, q4 * 256:(q4 + 1) * 256],
                         ALU.add, [gk_, "adabgt"], ["gtrep"])
        for which in range(2):
            P.stt("dve", gsh[:, which, 0, :], modT[:, 8:16, which], 1.0, normwT[:], ALU.add, ALU.mult,
                  ["modT", "normwT"], ["gsh"])
            P.cp("dve", gsh[:, which, 1, :], modT[:, 0:8, which], ["modT"], ["gsh"])
        if debug:
            P.dma(MODDBG[:, 0:32], gsh[:].rearrange("p a b c -> p (a b c)"), reads=["gsh"])
        wout_v = W["w_out"].rearrange("(kt p) j -> p kt j", p=128)
        for q4 in range(4):
            sbuf, skey = stg.next()
            P.dma(sbuf[:], wout_v[:, :, q4 * 256:(q4 + 1) * 256], writes=[skey])
            for which in range(2 if need_ctx else 1):
                for kt in range(KT):
                    eng = "dve" if kt % 2 == 0 else "pool"
                    P.tt(eng, woutb[:, which, kt, q4 * 256:(q4 + 1) * 256], sbuf[:, kt, :],
                         gtrep[:, which, q4 * 256:(q4 + 1) * 256], ALU.mult, [skey, "gtrep"], ["woutb"])
        for hp in range(4):
            sbuf, skey = stg.next()
            rv = sbuf[:].rearrange("p k c -> p (k c)")
            P.dma(rv[:, 0:960], W["rpbt"][hp, :, :], writes=[skey])
            P.act(ttab[:, hp, :], rv[:, 0:960], AF.Exp, [skey], ["ttab"])
        if stop_after == (l, "prep"):
            break

        xt_r = Ring(P, "xt", 2, [128, D]); xh_r = Ring(P, "xh", 2, [128, D]); sq_scr = P.sb("sq_scr", [128, D])
        ss_r = Ring(P, "ss", 4, [128, 2])
        hT_r = Ring(P, "hT", 2, [128, KT, 512], BF16)
        fmo_b = Ring(P, "fmo_b", 4, [128, 512], BF16); fmo_f = Ring(P, "fmo_f", 2, [128, 512])
        tmo = Ring(P, "tmo", 2, [128, 1024], BF16)
        supers = [(0, 2)] + [(2 + 4 * i, 4) for i in range(8)]
        if DBG["skip_p1"] and l == 0:
            supers = []
        def prologue(t0, ntl):
            is_ctx = (t0 == 0)
            which = 1 if is_ctx else 0
            ntok = ntl * 128
            hT, hkey = hT_r.next()
            for ti in range(ntl):
                tau = t0 + ti
                xt, xkey = xt_r.next()
                src = c_src[tau * 128:(tau + 1) * 128, :] if is_ctx else x_src[(tau - 2) * 128:(tau - 1) * 128, :]
                P.dma(xt[:], src, writes=[xkey])
                ss, sskey = ss_r.next()
                P.memset("dve", ss[:], 0.0, [sskey])
                P.act(sq_scr[:], xt[:], AF.Square, [xkey, sskey], ["sq_scr", sskey], accum_out=ss[:, 0:1])
                P.act(ss[:, 1:2], ss[:, 0:1], AF.Ln, [sskey, "eps"], [sskey], bias=eps_t[:], scale=1.0 / D)
                P.act(ss[:, 1:2], ss[:, 1:2], AF.Exp, [sskey], [sskey], scale=-0.5)
                xh, xhkey = xh_r.next()
                P.ts("dve", xh[:], xt[:], ss[:, 1:2], None, ALU.mult, None, [xkey, sskey], [xhkey])
                for half in range(2):
                    tp, tpkey = tr_ps.next()
                    for q4 in range(4):
                        kt = half * 4 + q4
                        P.tr(tp[:, q4 * 128:(q4 + 1) * 128], xh[:, kt * 128:(kt + 1) * 128], identf[:], [xhkey, "identf"], [tpkey])
                    for q4 in range(4):
                        kt = half * 4 + q4
                        P.act(hT[:, kt, ti * 128:(ti + 1) * 128], tp[:, q4 * 128:(q4 + 1) * 128], AF.Identity,
                              [tpkey, "gsh"], [hkey], bias=gsh[:, which, 1, kt:kt + 1], scale=gsh[:, which, 0, kt:kt + 1])
            return hT, hkey

        def mainbody(t0, ntl, hT, hkey):
            ntok = ntl * 128
            for ft in range(N_FMB + N_FMF):
                fp, fpkey = fm_ps.next()
                for kt in range(KT):
                    P.mm(fp[:, 0:ntok], winb[:, kt, ft * 128:(ft + 1) * 128], hT[:, kt, 0:ntok], ["winb", hkey], [fpkey],
                         start=(kt == 0), stop=(kt == KT - 1))
                if ft < N_FMB:
                    ob, obkey = fmo_b.next()
                    kind, sc_ = FMB_EVAC[ft]
                    if kind == "silu":
                        P.act(ob[:, 0:ntok], fp[:, 0:ntok], AF.Silu, [fpkey], [obkey])
                    else:
                        P.act(ob[:, 0:ntok], fp[:, 0:ntok], AF.Copy, [fpkey], [obkey], scale=float(sc_))
                    P.dma(FMB[ft, :, t0 * 128:t0 * 128 + ntok], ob[:, 0:ntok], reads=[obkey], writes=[("FMB", ft, t0)], eng="pool")
                else:
                    ob, obkey = fmo_f.next()
                    P.cp("dve", ob[:, 0:ntok], fp[:, 0:ntok], [fpkey], [obkey])
                    P.dma(FMF[ft - N_FMB, :, t0 * 128:t0 * 128 + ntok], ob[:, 0:ntok], reads=[obkey],
                          writes=[("FMF", ft - N_FMB, t0)], eng="pool")
            cbase = (N_FMB + N_FMF) * 128
            for ti in range(ntl):
                tau = t0 + ti
                ob, obkey = tmo.next()
                for half in range(2):
                    tp2, tp2key = tm_ps.next()
                    for kt in range(KT):
                        P.mm(tp2[:], hT[:, kt, ti * 128:(ti + 1) * 128], winb[:, kt, cbase + half * 512:cbase + (half + 1) * 512],
                             [hkey, "winb"], [tp2key], start=(kt == 0), stop=(kt == KT - 1))
                    P.cp("dve", ob[:, half * 512:(half + 1) * 512], tp2[:], [tp2key], [obkey])
                P.dma(TM[tau * 128:(tau + 1) * 128, :], ob[:], reads=[obkey], writes=[("TM", tau)], eng="pool")

        cur = prologue(*supers[0]) if supers else None
        for i_, (t0, ntl) in enumerate(supers):
            nxt = prologue(*supers[i_ + 1]) if i_ + 1 < len(supers) else None
            mainbody(t0, ntl, *cur)
            cur = nxt
        P.pop()
        if stop_after == (l, "p1"):
            break

        P.push()
        build_p2(P, l, need_ctx, FMB, FMF, TM, OB, OT, dict(
            identb=identb, onesblk=onesblk, wgk=wgk, nbgk=nbgk, glanw=glanw, hgnw=hgnw, lbt=lbt, omlb=omlb, eps_t=eps_t,
            scanmask_d=scanmask_d, trimask_d=trimask_d, cm_d=cm_d, ropecos_d=ropecos_d, ropesin_d=ropesin_d))
        P.pop()
        if stop_after == (l, "p2"):
            break

        P.push()
        build_p3(P, l, need_ctx, FMB, TM, OT, ttab, ones64)
        P.pop()
        if stop_after == (l, "p3"):
            break

        P.push()
        ot_r = Ring(P, "ot", 2, [128, 8, 128], BF16); xr = Ring(P, "x4", 2, [128, D]); xo = Ring(P, "xo", 2, [128, D])
        y_ps = Ring(P, "y_ps", 4, [128, 512], psum=True)
        ss_r = Ring(P, "ss4", 4, [128, 2]); sq_scr = P.sb("sq4", [128, D]); fnw = P.sb("fnw", [128, D])
        if last:
            P.dma(fnw[:], fnw_d.partition_broadcast(128), writes=["fnw"])
        otv = OT.rearrange("f p t -> p f t")
        for tau in (range(NT) if need_ctx else range(2, NT)):
            is_ctx = tau < 2
            which = 1 if is_ctx else 0
            ot, otkey = ot_r.next()
            P.dma(ot[:], otv[:, :, tau * 128:(tau + 1) * 128], reads=[("OT", f, tau) for f in range(8)], writes=[otkey])
            xt, xkey = xr.next()
            rows = slice(tau * 128, (tau + 1) * 128) if is_ctx else slice((tau - 2) * 128, (tau - 1) * 128)
            src = c_src[rows, :] if is_ctx else x_src[rows, :]
            P.dma(xt[:], src, reads=[("XR", l, tau)], writes=[xkey])
            xn, xnkey = xo.next()
            for half in range(2):
                yp, ypkey = y_ps.next()
                for f in range(8):
                    P.mm(yp[:], ot[:, f, :], woutb[:, which, f, half * 512:(half + 1) * 512], [otkey, "woutb"], [ypkey],
                         start=(f == 0), stop=(f == 7))
                P.tt("dve", xn[:, half * 512:(half + 1) * 512], yp[:], xt[:, half * 512:(half + 1) * 512], ALU.add,
                     [ypkey, xkey], [xnkey])
            if not last:
                dst = XC[rows, :] if is_ctx else XN[rows, :]
                P.dma(dst, xn[:], reads=[xnkey], writes=[("XR", l + 1, tau)], eng="pool")
            else:
                ss, sskey = ss_r.next()
                P.memset("dve", ss[:], 0.0, [sskey])
                P.act(sq_scr[:], xn[:], AF.Square, [xnkey, sskey], ["sq4", sskey], accum_out=ss[:, 0:1])
                P.act(ss[:, 1:2], ss[:, 0:1], AF.Ln, [sskey, "eps"], [sskey], bias=eps_t[:], scale=1.0 / D)
                P.act(ss[:, 1:2], ss[:, 1:2], AF.Exp, [sskey], [sskey], scale=-0.5)
                P.stt("dve", xn[:], xn[:], ss[:, 1:2], fnw[:], ALU.mult, ALU.mult, [xnkey, sskey, "fnw"], [xnkey])
                last_out.append(P.dma(out_d[rows, :], xn[:], reads=[xnkey], eng="pool"))
        P.pop()
        P.pop()
        if stop_after == (l, "p4"):
            break

    finals = list(last_out) + [d for d in P.dma_last.values()]
    return P.finish(final_waits=finals)


def _interleave(gens):
    gens = list(gens)
    while gens:
        for gen in list(gens):
            try:
                next(gen)
            except StopIteration:
                gens.remove(gen)


VEXP_ENG = os.environ.get("K_VEXP_ENG", "dve")


def build_p2(P, l, need_ctx, FMB, FMF, TM, OB, OT, G):
    identb = G["identb"]; onesblk = G["onesblk"]; wgk = G["wgk"]; nbgk = G["nbgk"]
    lbt = G["lbt"]; omlb = G["omlb"]; eps_t = G["eps_t"]
    scanmask = P.sb("scanmask", [128, 3, 128]); trimask = P.sb("trimask", [128, 6, 128]); cm = P.sb("cm", [128, 128 // CHG], BF16)
    ropecos = P.sb("ropecos", [128, T]); ropesin = P.sb("ropesin", [128, T])
    P.dma(scanmask[:], G["scanmask_d"].rearrange("n p t -> p n t"), writes=["scanmask"])
    P.dma(trimask[:], G["trimask_d"].rearrange("n p t -> p n t"), writes=["trimask"])
    P.dma(cm[:], G["cm_d"], writes=["cm"])
    P.dma(ropecos[:], G["ropecos_d"], writes=["ropecos"]); P.dma(ropesin[:], G["ropesin_d"], writes=["ropesin"])
    NHC = 128 // CHG
    NSL = [3, 3, 2 * NHC + 1, 2 * NHC + 1]
    S = [P.sb(f"S{g}", [128, NSL[g], 64]) for g in range(4)]
    RD = 6
    qk_r = Ring(P, "qk", 3, [128, 8, 128], BF16); gk_r = Ring(P, "gkr", 3, [32, 128]); v2_r = Ring(P, "vr", 8, [128, 256], BF16)
    qh_r = Ring(P, "qh", 3, [128, 2, 128], BF16); ff_r = Ring(P, "ff", 3, [128, 2, 128]); gate_r = Ring(P, "gate", 4, [128, 256], BF16)
    ob_r = Ring(P, "obr", 4, [128, 256])
    w2 = [Ring(P, f"w{i}", 3, [128, 2, 128]) for i in range(9)]
    qin_r = Ring(P, "qin", 7, [128, 2, 128])
    sm_r = Ring(P, "sm", 8, [128, 4, 8])
    kt_r = Ring(P, "ktr", 6, [128, 2, 128], BF16); kout_r = Ring(P, "kout", 6, [128, 2, 128], BF16)
    qbd_r = Ring(P, "qbd", 6, [128, 2, 2, 128], BF16)
    koT_r = Ring(P, "koT", 6, [128, 256], BF16); pt_r = Ring(P, "pt", 7, [128, 4, 128], BF16)
    vexp_r = Ring(P, "vexp", 4, [128, 4, NHC, 64], BF16)
    o_r = Ring(P, "osb", 6, [128, 256]); sq_r = Ring(P, "osq", 4, [128, 256], BF16); og_r = Ring(P, "og", 4, [128, 256], BF16)
    scanmask2 = P.sb("scanmask2", [128, 3, 256])
    for k_ in range(3):
        P.cp("pool", scanmask2[:, k_, :].rearrange("p (a t) -> p a t", a=2), scanmask[:, k_, :].unsqueeze(1).to_broadcast([128, 2, 128]),
             ["scanmask"], ["scanmask2"])
    z_ps = Ring(P, "z_ps", 1, [128, 512], psum=True); tp_ps = Ring(P, "tp_ps", 1, [128, 1024], BF16, psum=True)
    sc_ps = Ring(P, "sc_ps", 2, [128, 512], psum=True); kv_ps = Ring(P, "kv_ps", 2, [128, 512], psum=True)
    o_ps = Ring(P, "o_ps", 2, [128, 512], psum=True)
    for i, b in enumerate(qbd_r.bufs):
        P.memset("pool", b[:], 0.0, [f"qbd{i}"])
    base = [0, 0, 0, 0]

    def super_front(dirn, tau, gla, ctxs):
        bwd = dirn == 1
        is_ctx = tau < 2
        cs = slice(tau * 128, (tau + 1) * 128)
        C = 128 if gla else CHG
        nch = 128 // C
        mi = 1 if gla else 2
        kappa = -1.0 / 16.0 if gla else 1.0
        g0 = 0 if gla else 2
        c3 = lambda ap: ap.rearrange("p (n c) -> p n c", c=C)
        fl = lambda t: t[:].rearrange("p a b -> p (a b)")
        b0s = []
        for gg in range(2):
            g = g0 + gg
            b0s.append(base[g])
            base[g] = (base[g] + nch) % NSL[g]
        vt2, vkey = v2_r.next()
        voff = 512 if gla else 768
        P.dma(vt2[:], TM[cs, voff:voff + 256], reads=[("TM", tau)], writes=[vkey])
        lf, lfkey = w2[0].next()
        if gla:
            qk, qkkey = qk_r.next()
            P.dma(qk[:], FMB[12:20, :, cs].rearrange("n p t -> p n t"), reads=[("FMB", ft, _st(tau)) for ft in range(12, 20)],
                  writes=[qkkey])
            gkt, gkkey = gk_r.next()
            P.dma(gkt[:], FMF[4, 0:32, cs], reads=[("FMF", 4, _st(tau))], writes=[gkkey])
            yield
            zp, zkey = z_ps.next()
            e_, ekey = w2[1].next()
            for gg in range(2):
                P.mm(zp[:, gg * 128:(gg + 1) * 128], wgk[:, dirn, gg, :], gkt[:], ["wgk", gkkey], [zkey])
            for gg in range(2):
                P.act(e_[:, gg, :], zp[:, gg * 128:(gg + 1) * 128], AF.Exp, [zkey, "nbgk"], [ekey],
                      bias=nbgk[:, dirn * 2 + gg:dirn * 2 + gg + 1], scale=-1.0)
                yield
            P.act(fl(lf), fl(e_), AF.Ln, [ekey], [lfkey], bias=1.0)
            yield
            kk = kkkey = None
        else:
            qh, qhkey = qh_r.next()
            P.dma(qh[:], FMB[22:24, :, cs].rearrange("n p t -> p n t"), reads=[("FMB", 22, _st(tau)), ("FMB", 23, _st(tau))],
                  writes=[qhkey])
            fr, frkey = ff_r.next()
            fft = 2 if bwd else 0
            P.dma(fr[:], FMF[fft:fft + 2, :, cs].rearrange("n p t -> p n t"),
                  reads=[("FMF", fft, _st(tau)), ("FMF", fft + 1, _st(tau))], writes=[frkey])
            yield
            e_, ekey = w2[1].next()
            P.act(fl(e_), fl(fr), AF.Exp, [frkey], [ekey], scale=-1.0)
            yield
            P.act(fl(e_), fl(e_), AF.Ln, [ekey], [ekey], bias=1.0)
            yield
            P.act(fl(e_), fl(e_), AF.Exp, [ekey], [ekey], scale=-1.0)
            yield
            if l == 0:
                fgate, fkey = e_, ekey
            else:
                fgate, fkey = w2[2].next()
                for gg in range(2):
                    col = l * 4 + dirn * 2 + gg
                    P.ts("dve", fgate[:, gg, :], e_[:, gg, :], omlb[:, col:col + 1], lbt[:, col:col + 1], ALU.mult, ALU.add,
                         [ekey, "omlb", "lbt"], [fkey])
                yield
            P.act(fl(lf), fl(fgate), AF.Ln, [fkey], [lfkey])
            kk, kkkey = w2[3].next()
            P.act(fl(kk), fl(fgate), AF.Identity, [fkey], [kkkey], bias=1.0, scale=-1.0)
            yield
        Gc, Gkey = w2[4].next()
        m2 = scanmask2[:, mi, :]
        if bwd:
            P.op("dve", _scan(fl(Gc)[:, ::-1], m2, fl(lf)[:, ::-1]), [lfkey, "scanmask2"], [Gkey])
        else:
            P.op("dve", _scan(fl(Gc), m2, fl(lf)), [lfkey, "scanmask2"], [Gkey])
        yield
        Gv = c3(fl(Gc))
        n2 = 2 * nch
        mid = (C // 2) if bwd else (C // 2 - 1)
        lastp = 0 if bwd else C - 1
        sm, smkey = sm_r.next()
        P.tt("dve", sm[:, 1, 0:n2], Gv[:, :, lastp], Gv[:, :, mid], ALU.subtract, [Gkey], [smkey])
        Gm, Gmkey = w2[5].next()
        P.tt("dve", c3(fl(Gm)), Gv, Gv[:, :, mid:mid + 1].to_broadcast([128, n2, C]), ALU.subtract, [Gkey], [Gmkey])
        yield
        P.act(sm[:, 0, 0:n2], Gv[:, :, mid], AF.Exp, [Gkey], [smkey], scale=kappa)
        P.act(sm[:, 2, 0:n2], Gv[:, :, lastp], AF.Exp, [Gkey], [smkey], scale=kappa)
        yield
        P.act(sm[:, 1, 0:n2], sm[:, 1, 0:n2], AF.Exp, [smkey], [smkey], scale=kappa)
        A_, Akey = w2[6].next(); B_, Bkey = w2[7].next()
        P.act(fl(A_), fl(Gm), AF.Exp, [Gmkey], [Akey], scale=kappa)
        yield
        P.act(fl(B_), fl(Gm), AF.Exp, [Gmkey], [Bkey], scale=-kappa)
        yield
        Qf, Qfkey = w2[8].next()
        Kt, Ktkey = kt_r.next(); Kout, Koutkey = kout_r.next()
        if gla and not is_ctx:
            ts_ = slice((tau - 2) * 128, (tau - 1) * 128)
            cosb = ropecos[:, ts_].unsqueeze(1).to_broadcast([128, 2, 128])
            sinb = ropesin[:, ts_].unsqueeze(1).to_broadcast([128, 2, 128])
            r1, r1key = w2[2].next(); r2, r2key = w2[3].next()
            P.tt("dve", r1[:], qk[:, 0:2, :], cosb, ALU.mult, [qkkey, "ropecos"], [r1key])
            P.tt("pool", r2[:], qk[:, 2:4, :], sinb, ALU.mult, [qkkey, "ropesin"], [r2key])
            yield
            P.tt("dve", r1[:], r1[:], r2[:], ALU.add, [r1key, r2key], [r1key])
            r3, r3key = w2[2].next(); r4, r4key = w2[3].next()
            P.tt("pool", r3[:], qk[:, 4:6, :], cosb, ALU.mult, [qkkey, "ropecos"], [r3key])
            yield
            P.tt("dve", Qf[:], r1[:], A_[:], ALU.mult, [r1key, Akey], [Qfkey])
            P.tt("pool", r4[:], qk[:, 6:8, :], sinb, ALU.mult, [qkkey, "ropesin"], [r4key])
            yield
            P.tt("pool", r3[:], r3[:], r4[:], ALU.add, [r3key, r4key], [r3key])
            yield
            P.tt("pool", Kt[:], r3[:], B_[:], ALU.mult, [r3key, Bkey], [Ktkey])
            yield
        elif gla:
            P.tt("dve", Qf[:], qk[:, 0:2, :], A_[:], ALU.mult, [qkkey, Akey], [Qfkey])
            P.tt("pool", Kt[:], qk[:, 4:6, :], B_[:], ALU.mult, [qkkey, Bkey], [Ktkey])
            yield
        else:
            P.tt("dve", Qf[:], qh[:], A_[:], ALU.mult, [qhkey, Akey], [Qfkey])
            P.tt("pool", Kt[:], kk[:], B_[:], ALU.mult, [kkkey, Bkey], [Ktkey])
            yield
        Qbd, Qbdkey = qbd_r.next()
        P.cp("act", Qbd[0:64, :, 0, :], Qf[0:64, :, :], [Qfkey], [Qbdkey])
        yield
        P.cp("act", Qbd[64:128, :, 1, :], Qf[64:128, :, :], [Qfkey], [Qbdkey])
        Qin, Qinkey = qin_r.next()
        P.tt("dve" if gla else "pool", c3(fl(Qin)), c3(fl(Qf)), sm[:, 0, 0:n2].unsqueeze(2).to_broadcast([128, n2, C]), ALU.mult,
             [Qfkey, smkey], [Qinkey])
        P.tt("pool", c3(fl(Kout)), c3(fl(Kt)), sm[:, 1, 0:n2].unsqueeze(2).to_broadcast([128, n2, C]), ALU.mult,
             [Ktkey, smkey], [Koutkey])
        yield
        tpp, tpkey = tp_ps.next()
        for gg in range(2):
            P.tr(tpp[:, gg * 128:(gg + 1) * 128], Kout[:, gg, :], identb[:], [Koutkey, "identb"], [tpkey])
        koT, koTkey = koT_r.next()
        P.cp("act", koT[:], tpp[:, 0:256], [tpkey], [koTkey])
        yield
        scp, sckey = sc_ps.next()
        for gg in range(2):
            P.mm(scp[:, gg * 256:(gg + 1) * 256], Kt[:, gg, :], Qbd[:, gg, :, :].rearrange("p h t -> p (h t)"),
                 [Ktkey, Qbdkey], [sckey])
        PT, PTkey = pt_r.next()
        tmi = (2 if gla else 4) + (1 if bwd else 0)
        P.tt("dve", PT[:], scp[:, 0:512].rearrange("p (a t) -> p a t", a=4),
             trimask[:, tmi, :].unsqueeze(1).to_broadcast([128, 4, 128]), ALU.mult, [sckey, "trimask"], [PTkey])
        yield
        vx = vxkey = None
        if not gla:
            vx, vxkey = vexp_r.next()
            P.tt(VEXP_ENG, vx[:], vt2[:].rearrange("p (a d) -> p a d", a=4).unsqueeze(2).to_broadcast([128, 4, NHC, 64]),
                 cm[:].unsqueeze(1).unsqueeze(3).to_broadcast([128, 4, NHC, 64]), ALU.mult, [vkey, "cm"], [vxkey])
            yield
        for gg in range(2):
            g = g0 + gg
            ctxs[gg].update(dict(vt=vt2[:, gg * 128:(gg + 1) * 128], vkey=vkey, koT=koT[:, gg * 128:(gg + 1) * 128], koTkey=koTkey,
                                 vx=(vx[:, gg * 2:(gg + 1) * 2, :, :] if vx is not None else None), vxkey=vxkey,
                                 PT=PT[:, gg * 2:(gg + 1) * 2, :], PTkey=PTkey, Qin=Qin[:, gg, :], Qinkey=Qinkey,
                                 sm=sm[:, :, gg * nch:(gg + 1) * nch], smkey=smkey, b0=b0s[gg], nsl=NSL[g], C=C, nch=nch,
                                 gla=gla, gi=gg, bwd=bwd, is_ctx=is_ctx, cs=cs, g=g, tau=tau))

    def super_mid(ctxs):
        c0 = ctxs[0]
        gla = c0["gla"]; C = c0["C"]; nch = c0["nch"]; bwd = c0["bwd"]
        kvp, kvkey = kv_ps.next()
        W_ = 64 if gla else NHC * 64
        for c_ in ctxs:
            gg = c_["gi"]
            for h in range(2):
                if gla:
                    P.mm(kvp[h * 64:(h + 1) * 64, gg * W_:(gg + 1) * W_], c_["koT"][:, h * 64:(h + 1) * 64],
                         c_["vt"][:, h * 64:(h + 1) * 64], [c_["koTkey"], c_["vkey"]], [kvkey], tp=(0, h * 64))
                else:
                    P.mm(kvp[h * 64:(h + 1) * 64, gg * W_:(gg + 1) * W_], c_["koT"][:, h * 64:(h + 1) * 64],
                         c_["vx"][:, h, :, :].rearrange("p j d -> p (j d)"), [c_["koTkey"], c_["vxkey"]], [kvkey], tp=(0, h * 64))
            yield
        jorder = list(range(nch - 1, -1, -1)) if bwd else list(range(nch))
        for jj, j in enumerate(jorder):
            for c_ in ctxs:
                g = c_["g"]; gg = c_["gi"]; nsl = c_["nsl"]
                s_in = (c_["b0"] + jj) % nsl
                s_out = (c_["b0"] + jj + 1) % nsl
                P.stt("dve", S[g][:, s_out, :], S[g][:, s_in, :], c_["sm"][:, 2, j:j + 1],
                      kvp[:, gg * W_ + j * 64:gg * W_ + (j + 1) * 64], ALU.mult, ALU.add,
                      [f"S{g}_{s_in}", c_["smkey"], kvkey], [f"S{g}_{s_out}"])
            yield

    def super_out(ctxs):
        c0 = ctxs[0]
        gla = c0["gla"]; C = c0["C"]; nch = c0["nch"]; bwd = c0["bwd"]; is_ctx = c0["is_ctx"]; cs = c0["cs"]; tau = c0["tau"]
        g0 = c0["g"]
        op_, okey = o_ps.next()
        for c_ in ctxs:
            gg = c_["gi"]
            for h in range(2):
                P.mm(op_[h * 64:(h + 1) * 64, gg * 128:(gg + 1) * 128], c_["vt"][:, h * 64:(h + 1) * 64], c_["PT"][:, h, :],
                     [c_["vkey"], c_["PTkey"]], [okey], start=(gg == 0), stop=False, tp=(0, h * 64))
            yield
        jorder = list(range(nch - 1, -1, -1)) if bwd else list(range(nch))
        for jj, j in enumerate(jorder):
            for c_ in ctxs:
                g = c_["g"]; gg = c_["gi"]
                s_in = (c_["b0"] + jj) % c_["nsl"]
                for h in range(2):
                    P.mm(op_[h * 64:(h + 1) * 64, gg * 128 + j * C:gg * 128 + (j + 1) * C], S[g][h * 64:(h + 1) * 64, s_in, :],
                         c_["Qin"][h * 64:(h + 1) * 64, j * C:(j + 1) * C], [f"S{g}_{s_in}", c_["Qinkey"]], [okey],
                         start=False, stop=True, tp=(h * 64, h * 64))
            yield
        g2 = lambda d3: d3.rearrange("n p t -> p n t")
        if bwd:
            osb, oskey = o_r.next()
            P.cp("act", osb[:], op_[:, 0:256], [okey], [oskey])
            P.dma(g2(OB[g0:g0 + 2, :, cs]), osb[:].rearrange("p (n t) -> p n t", n=2), reads=[oskey],
                  writes=[("OB", g0, tau), ("OB", g0 + 1, tau)], eng="pool")
            yield
        elif not (is_ctx and not need_ctx):
            obt, obtkey = ob_r.next()
            P.dma(obt[:].rearrange("p (n t) -> p n t", n=2), g2(OB[g0:g0 + 2, :, cs]),
                  reads=[("OB", g0, tau), ("OB", g0 + 1, tau)], writes=[obtkey])
            gt_, gtkey = gate_r.next()
            gft = 20 if gla else 24
            P.dma(gt_[:].rearrange("p (n t) -> p n t", n=2), g2(FMB[gft:gft + 2, :, cs]),
                  reads=[("FMB", gft, _st(tau)), ("FMB", gft + 1, _st(tau))], writes=[gtkey])
            osb, oskey = o_r.next()
            P.tt("dve", osb[:], op_[:, 0:256], obt[:], ALU.add, [okey, obtkey], [oskey])
            yield
            osq, osqkey = sq_r.next()
            P.act(osq[:], osb[:], AF.Square, [oskey], [osqkey])
            yield
            zp, zkey = z_ps.next()
            P.mm(zp[:, 0:256], onesblk[:], osq[:], ["onesblk", osqkey], [zkey])
            rs, rskey = o_r.next()
            P.act(rs[:], zp[:, 0:256], AF.Ln, [zkey, "eps"], [rskey], bias=eps_t[:])
            yield
            P.act(rs[:], rs[:], AF.Exp, [rskey], [rskey], scale=-0.5)
            yield
            nw = G["glanw"] if gla else G["hgnw"]
            P.stt("dve", osb[:], osb[:], nw[:], rs[:], ALU.mult, ALU.mult, [oskey, rskey, "glanw", "hgnw"], [oskey])
            yield
            og, ogkey = og_r.next()
            P.tt("pool", og[:], osb[:], gt_[:], ALU.mult, [oskey, gtkey], [ogkey])
            P.dma(g2(OT[4 + g0:6 + g0, :, cs]), og[:].rearrange("p (n t) -> p n t", n=2), reads=[ogkey],
                  writes=[("OT", 4 + g0, tau), ("OT", 5 + g0, tau)], eng="pool")
            yield

    for dirn in DBG["p2_dirs"]:
        bwd = dirn == 1
        order = [1, 0] + list(range(NT - 1, 1, -1)) if bwd else list(range(NT))
        for g in range(4):
            base[g] = 0
            P.memset("dve", S[g][:, 0, :], 0.0, [f"S{g}_0"])
        if DBG["p2_tiles"]:
            order = order[:DBG["p2_tiles"]]
        mids = []
        outs = []
        mid_ctxs = []
        for tau in order:
            cts = [[dict(), dict()], [dict(), dict()]]
            fronts = [super_front(dirn, tau, False, cts[0]), super_front(dirn, tau, True, cts[1])]
            _interleave(outs + mids + fronts)
            outs = [super_out(c2) for c2 in mid_ctxs]
            mids = [super_mid(c2) for c2 in cts]
            mid_ctxs = cts
        _interleave(outs + mids)
        _interleave([super_out(c2) for c2 in mid_ctxs])


def _st(tau):
    return 0 if tau < 2 else 2 + 4 * ((tau - 2) // 4)


def _scan(out, d0, d1):
    return lambda e: e.tensor_tensor_scan(out, d0, d1, 0.0, ALU.mult, ALU.add)


def build_p3(P, l, need_ctx, FMB, TM, OT, ttab, ones64):
    KTb = Ring(P, "KTb", 2, [128, NTOK], BF16)
    Vd = Ring(P, "Vd", 2, [128, 68, 64], BF16)
    q_r = Ring(P, "q3", 2, [128, 512], BF16); g_r = Ring(P, "g3", 2, [128, 512], BF16)
    pe_r = Ring(P, "pe3", 4, [128, 512], BF16); p_r = Ring(P, "p3", 4, [128, 512], BF16)
    rc_r = Ring(P, "rc3", 2, [128, 512]); o_r = Ring(P, "o3", 2, [128, 512]); og_r = Ring(P, "og3", 2, [128, 512], BF16)
    s_ps = Ring(P, "s_ps", 3, [128, 512], psum=True)
    ao_ps = Ring(P, "ao_ps", 2, [128, 512], psum=True); as_ps = Ring(P, "as_ps", 2, [128, 512], psum=True)
    TMv = TM.rearrange("(r k) c -> k r c", k=64)

    def row_interval(rp):
        if rp <= 7:
            return 0, rp + 4
        if rp >= 56:
            return rp - 3, 63
        return rp - 3, rp + 4

    for hp in range(4):
        kt_, ktkey = KTb.next()
        P.dma(kt_[:], FMB[4 + hp, :, :], reads=[("FMB", 4 + hp, s) for s in [0] + [2 + 4 * i for i in range(8)]], writes=[ktkey])
        vd, vdkey = Vd.next()
        for hh in range(2):
            for part in range(4):
                r0, r1 = part * 17, (part + 1) * 17
                P.dma(vd[hh * 64:(hh + 1) * 64, r0:r1, :], TMv[:, r0:r1, hp * 128 + hh * 64:hp * 128 + (hh + 1) * 64],
                      reads=[("TM", t_) for t_ in range(NT)], writes=[vdkey])
        blocks = [("lat", b) for b in range(8)] + ([("ctx", 0)] if need_ctx else [])
        for kind, b in blocks:
            if kind == "lat":
                tok0 = CTX + b * 512
                nq = 512
                pieces = [(rho, 0, 512, None) for rho in range(4)]
                for rp in range(64):
                    lo, hi = row_interval(rp)
                    a = max(lo, 8 * b); e_ = min(hi, 8 * b + 7)
                    if a > e_:
                        continue
                    u0 = 7 - rp + a
                    pieces.append((4 + rp, (a - 8 * b) * 64, (e_ - 8 * b + 1) * 64, u0))
            else:
                tok0 = 0
                nq = 256
                pieces = [(rho, 0, 256, None) for rho in range(4)]
            qt, qkey = q_r.next(); gt_, gkey = g_r.next()
            st_reads = sorted(set(_st(t_) for t_ in range(tok0 // 128, (tok0 + nq) // 128)))
            P.dma(qt[:, 0:nq], FMB[hp, :, tok0:tok0 + nq], reads=[("FMB", hp, s) for s in st_reads], writes=[qkey])
            P.dma(gt_[:, 0:nq], FMB[8 + hp, :, tok0:tok0 + nq], reads=[("FMB", 8 + hp, s) for s in st_reads], writes=[gkey])
            ao, aokey = ao_ps.next(); as_, askey = as_ps.next()
            staged = {}

            def stage_a(pi):
                rho, c0, c1, u0 = pieces[pi]
                sp_, spkey = s_ps.next()
                for hh in range(2):
                    P.mm(sp_[hh * 64:(hh + 1) * 64, c0:c1], kt_[hh * 64:(hh + 1) * 64, rho * 64:(rho + 1) * 64],
                         qt[hh * 64:(hh + 1) * 64, c0:c1], [ktkey, qkey], [spkey], tp=(hh * 64, hh * 64))
                pe_, pekey = pe_r.next()
                P.act(pe_[:, c0:c1], sp_[:, c0:c1], AF.Exp, [spkey], [pekey])
                if u0 is not None:
                    pp, ppkey = p_r.next()
                    nr = (c1 - c0) // 64
                    P.tt("dve", pp[:, c0:c1], pe_[:, c0:c1], ttab[:, hp, u0 * 64:(u0 + nr) * 64], ALU.mult,
                         [pekey, "ttab"], [ppkey])
                else:
                    pp, ppkey = pe_, pekey
                staged[pi] = (pp, ppkey)

            def stage_b(pi):
                rho, c0, c1, u0 = pieces[pi]
                pp, ppkey = staged.pop(pi)
                for hh in range(2):
                    P.mm(ao[hh * 64:(hh + 1) * 64, c0:c1], vd[hh * 64:(hh + 1) * 64, rho, :], pp[hh * 64:(hh + 1) * 64, c0:c1],
                         [vdkey, ppkey], [aokey], start=(pi == 0), stop=(pi == len(pieces) - 1), tp=(hh * 64, hh * 64))
                    P.mm(as_[hh * 64:(hh + 1) * 64, c0:c1], ones64[hh * 64:(hh + 1) * 64, :], pp[hh * 64:(hh + 1) * 64, c0:c1],
                         ["ones64", ppkey], [askey], start=(pi == 0), stop=(pi == len(pieces) - 1), tp=(hh * 64, hh * 64))

            LA = 2
            for pi in range(min(LA, len(pieces))):
                stage_a(pi)
            for pi in range(len(pieces)):
                if pi + LA < len(pieces):
                    stage_a(pi + LA)
                stage_b(pi)
            rc, rckey = rc_r.next()
            P.recip(rc[:, 0:nq], as_[:, 0:nq], [askey], [rckey])
            ob, obkey = o_r.next()
            P.tt("dve", ob[:, 0:nq], ao[:, 0:nq], rc[:, 0:nq], ALU.mult, [aokey, rckey], [obkey])
            og, ogkey = og_r.next()
            P.tt("pool", og[:, 0:nq], ob[:, 0:nq], gt_[:, 0:nq], ALU.mult, [obkey, gkey], [ogkey])
            P.dma(OT[hp, :, tok0:tok0 + nq], og[:, 0:nq], reads=[ogkey],
                  writes=[("OT", hp, t_) for t_ in range(tok0 // 128, (tok0 + nq) // 128)], eng="pool")


_CACHE = {}


def kernel(**inputs):
    inputs = {k: np.asarray(v) for k, v in inputs.items()}
    common, per = host_layout(inputs)
    if "nc" not in _CACHE:
        _CACHE["nc"] = build_program()
    nc = _CACHE["nc"]
    in_maps = [dict(common, **p) for p in per]
    res = run_bass_kernel_spmd(nc, in_maps, core_ids=list(range(len(per))))
    out = np.stack([np.asarray(r["out"], dtype=np.float32) for r in res.results], axis=0)
    return out
```

```python
import numpy as np
import ml_dtypes
from contextlib import ExitStack
import concourse.bass as bass
import concourse.mybir as mybir
from concourse.bass_utils import run_bass_kernel_spmd

F32 = mybir.dt.float32
BF16 = mybir.dt.bfloat16
AF = mybir.ActivationFunctionType
ALU = mybir.AluOpType
NPBF = ml_dtypes.bfloat16

ENGS = ("pe", "act", "dve", "pool", "sp")
SAME_ENGINE_SYNC = {"pe": False, "act": True, "dve": True, "pool": True, "sp": False}
N_DMA_SEMS = {"sp": 20, "act": 4, "pool": 12}

D = 1024
KT = 8
T = 4096
CTX = 256
NTOK = T + CTX
NT = NTOK // 128
NEXT = 4992
N_FMB = 26
N_FMF = 5
EPS = 1e-6
CHG = 32


class Ins:
    __slots__ = ("eng", "fn", "deps", "is_dma", "dma_sem", "dma_val", "inc_idx", "needed", "pos", "guard")

    def __init__(self, eng, fn, is_dma):
        self.eng = eng
        self.fn = fn
        self.deps = []
        self.is_dma = is_dma
        self.dma_sem = None
        self.dma_val = None
        self.inc_idx = None
        self.needed = False
        self.guard = None


class Prog:
    def __init__(self):
        self.nc = bass.Bass("TRN2", target_bir_lowering=False)
        self.es = ExitStack()
        self.streams = {e: [] for e in ENGS}
        self.last_w = {}
        self.readers = {}
        self.dma_rr = {e: 0 for e in N_DMA_SEMS}
        self.dma_last = {}
        self.dma_cnt = {}
        self.n_ins = 0
        self.scopes = []

    def push(self):
        self.scopes.append(ExitStack())

    def pop(self):
        self.barrier()
        self.scopes.pop().close()

    def _ctx(self):
        return self.scopes[-1] if self.scopes else self.es

    def sb(self, name, shape, dtype=F32):
        self.uid = getattr(self, "uid", 0) + 1
        return self._ctx().enter_context(self.nc.sbuf_tensor(f"sb{self.uid}_{name}", list(shape), dtype))

    def ps(self, name, shape=(128, 512), dtype=F32):
        self.uid = getattr(self, "uid", 0) + 1
        return self._ctx().enter_context(self.nc.psum_tensor(f"ps{self.uid}_{name}", list(shape), dtype))

    def dram(self, name, shape, dtype=F32, kind="Internal"):
        return self.nc.dram_tensor(name, list(shape), dtype, kind=kind)

    def op(self, eng, fn, reads=(), writes=(), dma=False):
        ins = Ins(eng, fn, dma)
        ins.pos = self.n_ins
        self.n_ins += 1
        deps = []
        for r in reads:
            w = self.last_w.get(r)
            if w is not None:
                deps.append(w)
        for wkey in writes:
            w = self.last_w.get(wkey)
            if w is not None:
                deps.append(w)
            deps.extend(self.readers.get(wkey, {}).values())
        if dma:
            slot = self.dma_rr[eng] % N_DMA_SEMS[eng]
            self.dma_rr[eng] += 1
            key = (eng, slot)
            prev = self.dma_last.get(key)
            if prev is not None:
                ins.guard = prev
            self.dma_cnt[key] = self.dma_cnt.get(key, 0) + 1
            ins.dma_sem = key
            ins.dma_val = 16 * self.dma_cnt[key]
            self.dma_last[key] = ins
        best = {}
        for d in deps:
            if d is ins:
                continue
            k = d.dma_sem if d.is_dma else d.eng
            if k not in best or best[k].pos < d.pos:
                best[k] = d
        for d in best.values():
            ins.deps.append(d)
            d.needed = True
        if ins.guard is not None:
            ins.guard.needed = True
        mykey = ins.dma_sem if dma else eng
        for r in reads:
            self.readers.setdefault(r, {})[mykey] = ins
        for wkey in writes:
            self.last_w[wkey] = ins
            self.readers[wkey] = {}
        self.streams[eng].append(ins)
        return ins

    def barrier(self):
        lasts = []
        for e in ENGS:
            for ins in reversed(self.streams[e]):
                if ins.fn is not None and not ins.is_dma:
                    lasts.append(ins)
                    break
        lasts += list(self.dma_last.values())
        for e in ENGS:
            b = Ins(e, None, False)
            b.pos = self.n_ins
            self.n_ins += 1
            for d in lasts:
                b.deps.append(d)
                d.needed = True
            self.streams[e].append(b)
        self.last_w = {}
        self.readers = {}

    def dma(self, out, in_, reads=(), writes=(), eng="sp"):
        return self.op(eng, lambda e: e.dma_start(out=out, in_=in_), reads, writes, dma=True)

    def mm(self, out, lhsT, rhs, reads, writes, start=True, stop=True, tp=None):
        if tp is None:
            return self.op("pe", lambda e: e.matmul(out, lhsT, rhs, start=start, stop=stop), reads, writes)
        return self.op("pe", lambda e: e.matmul(out, lhsT, rhs, start=start, stop=stop, tile_position=tp,
                                                skip_group_check=True), reads, writes)

    def tr(self, out, in_, ident, reads, writes):
        return self.op("pe", lambda e: e.transpose(out, in_, ident), reads, writes)

    def act(self, out, in_, func, reads, writes, bias=None, scale=None, accum_out=None):
        kw = {}
        if bias is not None:
            kw["bias"] = bias
        if scale is not None:
            kw["scale"] = scale
        if accum_out is not None:
            kw["accum_out"] = accum_out
        return self.op("act", lambda e: e.activation(out, in_, func, **kw), reads, writes)

    def tt(self, eng, out, in0, in1, op, reads, writes):
        return self.op(eng, lambda e: e.tensor_tensor(out, in0, in1, op), reads, writes)

    def ts(self, eng, out, in0, s1, s2, op0, op1, reads, writes):
        if s2 is None:
            return self.op(eng, lambda e: e.tensor_scalar(out, in0, s1, None, op0=op0), reads, writes)
        return self.op(eng, lambda e: e.tensor_scalar(out, in0, s1, s2, op0=op0, op1=op1), reads, writes)

    def stt(self, eng, out, in0, scalar, in1, op0, op1, reads, writes):
        return self.op(eng, lambda e: e.scalar_tensor_tensor(out, in0, scalar, in1, op0=op0, op1=op1), reads, writes)

    def cp(self, eng, out, in_, reads, writes):
        if eng == "act":
            return self.op("act", lambda e: e.copy(out, in_), reads, writes)
        return self.op(eng, lambda e: e.tensor_copy(out, in_), reads, writes)

    def recip(self, out, in_, reads, writes):
        return self.op("dve", lambda e: e.reciprocal(out, in_), reads, writes)

    def memset(self, eng, ap, val, writes):
        return self.op(eng, lambda e: e.memset(ap, val), (), writes)

    def finish(self, final_waits=()):
        nc = self.nc
        es = self.es
        sems = {e: es.enter_context(nc.semaphore("s_" + e)) for e in ENGS}
        dsems = {}
        for e, n in N_DMA_SEMS.items():
            for i in range(n):
                dsems[(e, i)] = es.enter_context(nc.semaphore(f"d_{e}{i}"))
        for fw in final_waits:
            fw.needed = True
        for e in ENGS:
            c = 0
            for ins in self.streams[e]:
                if ins.is_dma or ins.fn is None:
                    continue
                if ins.needed:
                    c += 1
                    ins.inc_idx = c
        streams = self.streams

        def emit(e, eng_obj):
            known = {}

            def wait_for(d):
                if d.is_dma:
                    k = ("d",) + d.dma_sem
                    if known.get(k, 0) >= d.dma_val:
                        return
                    eng_obj.wait_ge(dsems[d.dma_sem], d.dma_val)
                    known[k] = d.dma_val
                else:
                    if d.eng == e and not SAME_ENGINE_SYNC[e]:
                        return
                    k = ("e", d.eng)
                    if known.get(k, 0) >= d.inc_idx:
                        return
                    eng_obj.wait_ge(sems[d.eng], d.inc_idx)
                    known[k] = d.inc_idx

            for ins in streams[e]:
                for d in ins.deps:
                    wait_for(d)
                if ins.guard is not None:
                    wait_for(ins.guard)
                if ins.fn is None:
                    continue
                bi = ins.fn(eng_obj)
                if ins.is_dma:
                    bi.then_inc(dsems[ins.dma_sem], 16)
                elif ins.needed:
                    bi.then_inc(sems[e], 1)
            if e == "sp":
                for fw in final_waits:
                    wait_for(fw)

        with nc.Block() as block:
            @block.tensor
            def _(eng):
                emit("pe", eng)

            @block.scalar
            def _(eng):
                emit("act", eng)

            @block.vector
            def _(eng):
                emit("dve", eng)

            @block.gpsimd
            def _(eng):
                emit("pool", eng)

            @block.sync
            def _(eng):
                emit("sp", eng)
        while self.scopes:
            self.scopes.pop().close()
        self.es.close()
        return nc


class Ring:
    def __init__(self, P, name, n, shape, dtype=F32, psum=False):
        self.name = name
        if psum:
            self.bufs = [P.ps(f"{name}{i}", shape, dtype) for i in range(n)]
        else:
            self.bufs = [P.sb(f"{name}{i}", shape, dtype) for i in range(n)]
        self.i = 0

    def next(self):
        j = self.i % len(self.bufs)
        self.i += 1
        return self.bufs[j], f"{self.name}{j}"


def ext_col_index():
    cols = []
    cols += list(range(0, 512)) + list(range(512, 1024)) + list(range(1536, 2048))

    def pad_heads(base, swap):
        out = []
        for h in range(4):
            for i in range(32):
                j = (i + 8 if (i % 16) < 8 else i - 8) if swap else i
                out.append(base + h * 32 + j)
            out += [-1] * 32
        return out
    cols += pad_heads(2048, False) + pad_heads(2048, True) + pad_heads(2176, False) + pad_heads(2176, True)
    cols += list(range(2592, 2848)) + list(range(2848, 3104)) + list(range(3872, 4128))
    cols += list(range(3104, 3360)) + list(range(3360, 3616)) + list(range(2560, 2592)) + [-1] * 96
    cols += list(range(1024, 1536)) + list(range(2304, 2560)) + list(range(3616, 3872))
    assert len(cols) == NEXT
    return np.array(cols)


def host_consts():
    c = {}
    c["identf"] = np.eye(128, dtype=np.float32)
    c["identb"] = np.eye(128, dtype=np.float32).astype(NPBF)
    i = np.arange(128)
    m16 = np.ones((128, 128), np.float32); m16[:, i % 16 == 0] = 0
    m128 = np.ones((128, 128), np.float32); m128[:, 0] = 0
    m32 = np.ones((128, 128), np.float32); m32[:, i % 32 == 0] = 0
    c["scanmask"] = np.stack([m16, m128, m32], 0)
    s = i[:, None]; t = i[None, :]
    tri = []
    for C in (16, 128, 32):
        same = (s // C) == (t // C)
        tri.append((same & (s <= t)).astype(np.float32))
        tri.append((same & (s >= t)).astype(np.float32))
    c["trimask"] = np.stack(tri, 0)
    c["cm"] = ((i[:, None] // CHG) == np.arange(128 // CHG)[None, :]).astype(np.float32).astype(NPBF)
    c["onesblk"] = (((i[:, None] // 64) == (i[None, :] // 64)).astype(np.float32) / 64.0).astype(NPBF)
    c["ones64"] = np.ones((128, 64), np.float32).astype(NPBF)
    inv_freq = (1.0 / (10000.0 ** (np.arange(0, 16, 2, dtype=np.float32) / np.float32(16)))).astype(np.float32)
    tt_ = np.arange(T)
    cos = np.zeros((128, T), np.float32); sins = np.zeros((128, T), np.float32)
    for p in range(128):
        ii = p % 64
        if ii >= 32:
            continue
        blk = ii // 16
        j = ii % 16
        f = j % 8
        pos = (tt_ // 64 if blk == 0 else tt_ % 64).astype(np.float32)
        ang = (pos * inv_freq[f]).astype(np.float32)
        cos[p] = np.cos(ang)
        sins[p] = -np.sin(ang) if j < 8 else np.sin(ang)
    c["ropecos"] = cos
    c["ropesin"] = sins
    return c


def host_layout(inp):
    f32 = np.float32
    common = dict(host_consts())
    cols = ext_col_index()
    valid = cols >= 0
    DEPTH = inp["w_in"].shape[0]
    q = np.arange(64)
    col_start = np.clip(q - 8, 0, 48)
    kc = np.arange(64)
    inwin = (kc[:, None] >= col_start[None, :]) & (kc[:, None] < col_start[None, :] + 16)
    didx = np.clip(kc[:, None] - q[None, :], -15, 15) + 15
    pidx = np.arange(128)
    for l in range(DEPTH):
        w_ext = np.zeros((D, NEXT), f32)
        w_ext[:, valid] = inp["w_in"][l][:, cols[valid]]
        common[f"w_in{l}"] = w_ext
        common[f"ada_w{l}"] = np.ascontiguousarray(inp["ada_w"][l], dtype=f32)
        common[f"ada_bT{l}"] = np.ascontiguousarray(inp["ada_b"][l].reshape(24, 128).T, dtype=f32)
        common[f"ada_bgt{l}"] = np.ascontiguousarray(inp["ada_b"][l][2048:3072].reshape(1, 1024), dtype=f32)
        common[f"norm_wT{l}"] = np.ascontiguousarray(inp["norm_w"][l].reshape(8, 128).T, dtype=f32)
        common[f"w_out{l}"] = np.ascontiguousarray(inp["w_out"][l], dtype=f32)
        rpb = inp["na_rpb"][l]
        g = rpb[:, ::-1, :][:, :, didx]
        g = np.where(inwin[None, None], g, f32(-30000.0))
        g = g.transpose(0, 2, 1, 3).reshape(4, 128, 15 * 64)
        common[f"rpbt{l}"] = np.ascontiguousarray(g, dtype=f32)
        wgk = inp["gla_w_gk"][l]
        bgk = inp["gla_b_gk"][l]
        wp = np.zeros((32, 2, 2, 128), f32)
        bp = np.zeros((128, 2, 2), f32)
        for d_ in range(2):
            for g_ in range(2):
                for p in range(128):
                    h = 2 * g_ + p // 64
                    ii = p % 64
                    if ii < 32:
                        wp[d_ * 16:(d_ + 1) * 16, d_, g_, p] = wgk[d_, :, h * 32 + ii]
                        bp[p, d_, g_] = bgk[d_, h * 32 + ii]
        common[f"wgk{l}"] = wp
        common[f"bgk{l}"] = bp
        common[f"glanw{l}"] = np.ascontiguousarray(inp["gla_norm_w"][l][pidx % 64].reshape(128, 1), dtype=f32)
        common[f"hgnw{l}"] = np.ascontiguousarray(inp["hgrn_norm_w"][l][pidx % 64].reshape(128, 1), dtype=f32)
    lb = inp["hgrn_lower_bounds"]
    common["lbraw"] = np.ascontiguousarray(lb.reshape(2, 2, 2, 128).transpose(3, 0, 1, 2).reshape(128, 8), dtype=f32)
    common["fnw"] = np.ascontiguousarray(inp["final_norm_w"].reshape(1, D), dtype=f32)
    per = []
    B = inp["x"].shape[0]
    cc = inp["c_ctx"].reshape(8, 128).T
    for b in range(B):
        cv = np.concatenate([inp["c"][b].reshape(8, 128).T, cc], axis=1)
        per.append({"x": np.ascontiguousarray(inp["x"][b], dtype=f32),
                    "ctx": np.ascontiguousarray(inp["ctx"][b], dtype=f32),
                    "cvec": np.ascontiguousarray(cv, dtype=f32)})
    return common, per


FMB_EVAC = {}
for _t in range(0, 4):
    FMB_EVAC[_t] = ("scale", 0.125)
for _t in range(4, 8):
    FMB_EVAC[_t] = ("copy", 1.0)
for _t in range(8, 12):
    FMB_EVAC[_t] = ("silu", 1.0)
for _t in range(12, 16):
    FMB_EVAC[_t] = ("scale", 32.0 ** -0.5)
for _t in range(16, 20):
    FMB_EVAC[_t] = ("copy", 1.0)
for _t in (20, 21, 24, 25):
    FMB_EVAC[_t] = ("silu", 1.0)
for _t in (22, 23):
    FMB_EVAC[_t] = ("scale", 0.125)


import os
DBG = {"skip_p1": os.environ.get("K_DBG_SKIP_P1") == "1",
       "p2_tiles": int(os.environ.get("K_DBG_P2_TILES", "0")),
       "p2_groups": [int(x) for x in os.environ.get("K_DBG_P2_GROUPS", "0,1,2,3").split(",")],
       "p2_dirs": [int(x) for x in os.environ.get("K_DBG_P2_DIRS", "1,0").split(",")],
       "cut": int(os.environ.get("K_DBG_P2_CUT", "99"))}


def build_program(depth=2, stop_after=None, debug=False):
    P = Prog()
    nc = P.nc
    kdbg = "ExternalOutput" if debug else "Internal"

    def din(name, shape, dt=F32):
        return P.dram(name, shape, dt, kind="ExternalInput").ap()

    x_d = din("x", [T, D]); ctx_d = din("ctx", [CTX, D]); cvec_d = din("cvec", [128, 16])
    identf_d = din("identf", [128, 128]); identb_d = din("identb", [128, 128], BF16)
    scanmask_d = din("scanmask", [3, 128, 128]); trimask_d = din("trimask", [6, 128, 128])
    cm_d = din("cm", [128, 128 // CHG], BF16); onesblk_d = din("onesblk", [128, 128], BF16); ones64_d = din("ones64", [128, 64], BF16)
    ropecos_d = din("ropecos", [128, T]); ropesin_d = din("ropesin", [128, T])
    lbraw_d = din("lbraw", [128, 8]); fnw_d = din("fnw", [1, D])
    LW = []
    for l in range(2):
        LW.append(dict(
            w_in=din(f"w_in{l}", [D, NEXT]), ada_w=din(f"ada_w{l}", [D, 3 * D]), ada_bT=din(f"ada_bT{l}", [128, 24]),
            ada_bgt=din(f"ada_bgt{l}", [1, D]), norm_wT=din(f"norm_wT{l}", [128, 8]), w_out=din(f"w_out{l}", [D, D]),
            rpbt=din(f"rpbt{l}", [4, 128, 960]), wgk=din(f"wgk{l}", [32, 2, 2, 128]), bgk=din(f"bgk{l}", [128, 2, 2]),
            glanw=din(f"glanw{l}", [128, 1]), hgnw=din(f"hgnw{l}", [128, 1])))
    out_d = P.dram("out", [T, D], F32, kind="ExternalOutput").ap()
    FMB = P.dram("FMB", [N_FMB, 128, NTOK], BF16, kind=kdbg).ap()
    FMF = P.dram("FMF", [N_FMF, 128, NTOK], F32, kind=kdbg).ap()
    TM = P.dram("TM", [NTOK, 1024], BF16, kind=kdbg).ap()
    OB = P.dram("OB", [4, 128, NTOK], F32, kind=kdbg).ap()
    OT = P.dram("OT", [8, 128, NTOK], BF16, kind=kdbg).ap()
    XN = P.dram("XN", [T, D], F32, kind=kdbg).ap()
    XC = P.dram("XC", [CTX, D], F32, kind=kdbg).ap()
    MODDBG = P.dram("MODDBG", [128, 64], F32, kind=kdbg).ap()

    identf = P.sb("identf", [128, 128]); identb = P.sb("identb", [128, 128], BF16)
    onesblk = P.sb("onesblk", [128, 128], BF16); ones64 = P.sb("ones64", [128, 64], BF16)
    cvec = P.sb("cvec", [128, 16]); sil = P.sb("sil", [128, 16]); silrep = P.sb("silrep", [128, 16, 128])
    lbraw = P.sb("lbraw", [128, 8]); lbt = P.sb("lbt", [128, 8]); omlb = P.sb("omlb", [128, 8])
    eps_t = P.sb("eps_t", [128, 1])
    P.memset("dve", eps_t[:], EPS, ["eps"])
    for nm, dst, src in (("identf", identf, identf_d), ("identb", identb, identb_d), ("onesblk", onesblk, onesblk_d),
                         ("ones64", ones64, ones64_d), ("cvec", cvec, cvec_d), ("lbraw", lbraw, lbraw_d)):
        P.dma(dst[:], src, writes=[nm])
    tmp16 = P.sb("tmp16", [128, 16])
    P.act(tmp16[:], cvec[:], AF.Exp, ["cvec"], ["tmp16"], scale=-1.0)
    P.ts("dve", tmp16[:], tmp16[:], 1.0, None, ALU.add, None, ["tmp16"], ["tmp16"])
    P.recip(tmp16[:], tmp16[:], ["tmp16"], ["tmp16"])
    P.tt("dve", sil[:], cvec[:], tmp16[:], ALU.mult, ["cvec", "tmp16"], ["sil"])
    for j in range(16):
        P.cp("dve", silrep[:, j, :], sil[:, j:j + 1].to_broadcast([128, 128]), ["sil"], ["silrep"])
    P.memset("dve", lbt[:], 0.0, ["lbt"])
    P.tt("dve", lbt[:, 4:8], lbraw[:, 0:4], lbraw[:, 4:8], ALU.subtract, ["lbraw", "lbt"], ["lbt"])
    P.act(lbt[:, 4:8], lbt[:, 4:8], AF.Exp, ["lbt"], ["lbt"])
    P.ts("dve", lbt[:, 4:8], lbt[:, 4:8], 1.0, None, ALU.add, None, ["lbt"], ["lbt"])
    P.recip(lbt[:, 4:8], lbt[:, 4:8], ["lbt"], ["lbt"])
    P.ts("dve", omlb[:], lbt[:], -1.0, 1.0, ALU.mult, ALU.add, ["lbt"], ["omlb"])

    last_out = []

    for l in range(depth):
        W = LW[l]
        last = (l == depth - 1)
        need_ctx = not last
        x_src = x_d if l == 0 else XN
        c_src = ctx_d if l == 0 else XC
        tiles = list(range(NT))

        P.push()
        gsh = P.sb(f"gsh{l}", [128, 2, 2, 8])
        gtrep = P.sb(f"gtrep{l}", [128, 2, D])
        woutb = P.sb(f"woutb{l}", [128, 2 if need_ctx else 1, KT, D], BF16)
        ttab = P.sb(f"ttab{l}", [128, 4, 960], BF16)
        wgk = P.sb(f"wgk{l}", [32, 2, 2, 128]); nbgk = P.sb(f"nbgk{l}", [128, 4])
        glanw = P.sb(f"glanw{l}", [128, 1]); hgnw = P.sb(f"hgnw{l}", [128, 1])

        P.push()
        winb = P.sb("winb", [128, KT, NEXT], BF16)
        win_v = W["w_in"].rearrange("(kt p) j -> p kt j", p=128)
        if not (DBG["skip_p1"] and l == 0):
            for kt in range(KT):
                for hf in range(2):
                    c0, c1 = hf * (NEXT // 2), (hf + 1) * (NEXT // 2)
                    P.dma(winb[:, kt, c0:c1], win_v[:, kt, c0:c1], writes=["winb"], eng="pool")
        trb_ps = Ring(P, "trb_ps", 2, [128, 1024], BF16, psum=True)
        fm_ps = Ring(P, "fm_ps", 3, [128, 512], psum=True)
        tm_ps = Ring(P, "tm_ps", 2, [128, 512], psum=True)
        modT = P.sb("modT", [128, 16, 2]); adabT = P.sb("adabT", [128, 24]); normwT = P.sb("normwT", [128, 8])
        adabgt = P.sb("adabgt", [128, D])
        stg = Ring(P, "stg", 2, [128, KT, 256])
        mod_ps = tm_ps.bufs[0]; gt_ps = fm_ps
        P.dma(adabT[:], W["ada_bT"], writes=["adabT"]); P.dma(normwT[:], W["norm_wT"], writes=["normwT"])
        P.dma(adabgt[:], W["ada_bgt"].partition_broadcast(128), writes=["adabgt"])
        P.dma(wgk[:], W["wgk"], writes=["wgk"]); P.dma(nbgk[:], W["bgk"].rearrange("p a b -> p (a b)"), writes=["nbgk"])
        P.dma(glanw[:], W["glanw"], writes=["glanw"]); P.dma(hgnw[:], W["hgnw"], writes=["hgnw"])
        P.ts("dve", nbgk[:], nbgk[:], -1.0, None, ALU.mult, None, ["nbgk"], ["nbgk"])
        adaw_v = W["ada_w"].rearrange("(kt p) j -> p kt j", p=128)
        sil2 = sil[:].rearrange("p (w k) -> p w k", w=2)
        for ch in range(12):
            sbuf, skey = stg.next()
            P.dma(sbuf[:], adaw_v[:, :, ch * 256:(ch + 1) * 256], writes=[skey])
            if ch < 8:
                for jl in range(2):
                    jt = ch * 2 + jl
                    for kt in range(KT):
                        P.mm(mod_ps[:, jt * 2:jt * 2 + 2], sbuf[:, kt, jl * 128:(jl + 1) * 128], sil2[:, :, kt],
                             [skey, "sil"], ["tm_ps0"], start=(kt == 0), stop=(kt == KT - 1))
                    P.ts("dve", modT[:, jt, :], mod_ps[:, jt * 2:jt * 2 + 2], adabT[:, jt:jt + 1], None, ALU.add, None,
                         ["tm_ps0", "adabT"], ["modT"])
            else:
                q4 = ch - 8
                for which in range(2):
                    gp, gk_ = gt_ps.next()
                    for kt in range(KT):
                        P.mm(gp[:, 0:256], silrep[:, which * 8 + kt, :], sbuf[:, kt, :], ["silrep", skey], [gk_],
                             start=(kt == 0), stop=(kt == KT - 1))
                    P.tt("dve", gtrep[:, which, q4 * 256:(q4 + 1) * 256], gp[:, 0:256], adabgt[:, q4 * 256:(q4 + 1) * 256],
                         ALU.add, [gk_, "adabgt"], ["gtrep"])
        for which in range(2):
            P.stt("dve", gsh[:, which, 0, :], modT[:, 8:16, which], 1.0, normwT[:], ALU.add, ALU.mult,
                  ["modT", "normwT"], ["gsh"])
            P.cp("dve", gsh[:, which, 1, :], modT[:, 0:8, which], ["modT"], ["gsh"])
        if debug:
            P.dma(MODDBG[:, 0:32], gsh[:].rearrange("p a b c -> p (a b c)"), reads=["gsh"])
        wout_v = W["w_out"].rearrange("(kt p) j -> p kt j", p=128)
        for q4 in range(4):
            sbuf, skey = stg.next()
            P.dma(sbuf[:], wout_v[:, :, q4 * 256:(q4 + 1) * 256], writes=[skey])
            for which in range(2 if need_ctx else 1):
                for kt in range(KT):
                    eng = "dve" if kt % 2 == 0 else "pool"
                    P.tt(eng, woutb[:, which, kt, q4 * 256:(q4 + 1) * 256], sbuf[:, kt, :],
                         gtrep[:, which, q4 * 256:(q4 + 1) * 256], ALU.mult, [skey, "gtrep"], ["woutb"])
        for hp in range(4):
            sbuf, skey = stg.next()
            rv = sbuf[:].rearrange("p k c -> p (k c)")
            P.dma(rv[:, 0:960], W["rpbt"][hp, :, :], writes=[skey])
            P.act(ttab[:, hp, :], rv[:, 0:960], AF.Exp, [skey], ["ttab"])
        if stop_after == (l, "prep"):
            break

        xt_r = Ring(P, "xt", 2, [128, D]); xh_r = Ring(P, "xh", 2, [128, D], BF16); sq_scr = P.sb("sq_scr", [128, D])
        ss_r = Ring(P, "ss", 4, [128, 2])
        hT_r = Ring(P, "hT", 2, [128, KT, 512], BF16)
        fmo_b = Ring(P, "fmo_b", 4, [128, 512], BF16); fmo_f = Ring(P, "fmo_f", 2, [128, 512])
        tmo = Ring(P, "tmo", 2, [128, 1024], BF16)
        supers = [(0, 2)] + [(2 + 4 * i, 4) for i in range(8)]
        if DBG["skip_p1"] and l == 0:
            supers = []
        def prologue(t0, ntl):
            is_ctx = (t0 == 0)
            which = 1 if is_ctx else 0
            ntok = ntl * 128
            hT, hkey = hT_r.next()
            for ti in range(ntl):
                tau = t0 + ti
                xt, xkey = xt_r.next()
                src = c_src[tau * 128:(tau + 1) * 128, :] if is_ctx else x_src[(tau - 2) * 128:(tau - 1) * 128, :]
                P.dma(xt[:], src, writes=[xkey])
                ss, sskey = ss_r.next()
                P.memset("dve", ss[:], 0.0, [sskey])
                P.act(sq_scr[:], xt[:], AF.Square, [xkey, sskey], ["sq_scr", sskey], accum_out=ss[:, 0:1])
                P.act(ss[:, 1:2], ss[:, 0:1], AF.Ln, [sskey, "eps"], [sskey], bias=eps_t[:], scale=1.0 / D)
                P.act(ss[:, 1:2], ss[:, 1:2], AF.Exp, [sskey], [sskey], scale=-0.5)
                xh, xhkey = xh_r.next()
                P.ts("dve", xh[:], xt[:], ss[:, 1:2], None, ALU.mult, None, [xkey, sskey], [xhkey])
                tp, tpkey = trb_ps.next()
                for kt in range(KT):
                    P.tr(tp[:, kt * 128:(kt + 1) * 128], xh[:, kt * 128:(kt + 1) * 128], identb[:], [xhkey, "identb"], [tpkey])
                for kt in range(KT):
                    P.act(hT[:, kt, ti * 128:(ti + 1) * 128], tp[:, kt * 128:(kt + 1) * 128], AF.Identity,
                          [tpkey, "gsh"], [hkey], bias=gsh[:, which, 1, kt:kt + 1], scale=gsh[:, which, 0, kt:kt + 1])
            return hT, hkey

        def mainbody(t0, ntl, hT, hkey):
            ntok = ntl * 128
            for ft in range(N_FMB + N_FMF):
                fp, fpkey = fm_ps.next()
                for kt in range(KT):
                    P.mm(fp[:, 0:ntok], winb[:, kt, ft * 128:(ft + 1) * 128], hT[:, kt, 0:ntok], ["winb", hkey], [fpkey],
                         start=(kt == 0), stop=(kt == KT - 1))
                if ft < N_FMB:
                    ob, obkey = fmo_b.next()
                    kind, sc_ = FMB_EVAC[ft]
                    if kind == "silu":
                        P.act(ob[:, 0:ntok], fp[:, 0:ntok], AF.Silu, [fpkey], [obkey])
                    else:
                        P.act(ob[:, 0:ntok], fp[:, 0:ntok], AF.Copy, [fpkey], [obkey], scale=float(sc_))
                    P.dma(FMB[ft, :, t0 * 128:t0 * 128 + ntok], ob[:, 0:ntok], reads=[obkey], writes=[("FMB", ft, t0)], eng="pool")
                else:
                    ob, obkey = fmo_f.next()
                    P.cp("dve", ob[:, 0:ntok], fp[:, 0:ntok], [fpkey], [obkey])
                    P.dma(FMF[ft - N_FMB, :, t0 * 128:t0 * 128 + ntok], ob[:, 0:ntok], reads=[obkey],
                          writes=[("FMF", ft - N_FMB, t0)], eng="pool")
            cbase = (N_FMB + N_FMF) * 128
            for ti in range(ntl):
                tau = t0 + ti
                ob, obkey = tmo.next()
                for half in range(2):
                    tp2, tp2key = tm_ps.next()
                    for kt in range(KT):
                        P.mm(tp2[:], hT[:, kt, ti * 128:(ti + 1) * 128], winb[:, kt, cbase + half * 512:cbase + (half + 1) * 512],
                             [hkey, "winb"], [tp2key], start=(kt == 0), stop=(kt == KT - 1))
                    P.cp("dve", ob[:, half * 512:(half + 1) * 512], tp2[:], [tp2key], [obkey])
                P.dma(TM[tau * 128:(tau + 1) * 128, :], ob[:], reads=[obkey], writes=[("TM", tau)], eng="pool")

        cur = prologue(*supers[0]) if supers else None
        for i_, (t0, ntl) in enumerate(supers):
            nxt = prologue(*supers[i_ + 1]) if i_ + 1 < len(supers) else None
            mainbody(t0, ntl, *cur)
            cur = nxt
        P.pop()
        if stop_after == (l, "p1"):
            break

        P.push()
        build_p2(P, l, need_ctx, FMB, FMF, TM, OB, OT, dict(
            identb=identb, onesblk=onesblk, wgk=wgk, nbgk=nbgk, glanw=glanw, hgnw=hgnw, lbt=lbt, omlb=omlb, eps_t=eps_t,
            scanmask_d=scanmask_d, trimask_d=trimask_d, cm_d=cm_d, ropecos_d=ropecos_d, ropesin_d=ropesin_d))
        P.pop()
        if stop_after == (l, "p2"):
            break

        P.push()
        build_p3(P, l, need_ctx, FMB, TM, OT, ttab, ones64)
        P.pop()
        if stop_after == (l, "p3"):
            break

        P.push()
        ot_r = Ring(P, "ot", 2, [128, 8, 128], BF16); xr = Ring(P, "x4", 2, [128, D]); xo = Ring(P, "xo", 2, [128, D])
        y_ps = Ring(P, "y_ps", 4, [128, 512], psum=True)
        ss_r = Ring(P, "ss4", 4, [128, 2]); sq_scr = P.sb("sq4", [128, D]); fnw = P.sb("fnw", [128, D])
        if last:
            P.dma(fnw[:], fnw_d.partition_broadcast(128), writes=["fnw"])
        otv = OT.rearrange("f p t -> p f t")
        for tau in (range(NT) if need_ctx else range(2, NT)):
            is_ctx = tau < 2
            which = 1 if is_ctx else 0
            ot, otkey = ot_r.next()
            P.dma(ot[:], otv[:, :, tau * 128:(tau + 1) * 128], reads=[("OT", f, tau) for f in range(8)], writes=[otkey])
            xt, xkey = xr.next()
            rows = slice(tau * 128, (tau + 1) * 128) if is_ctx else slice((tau - 2) * 128, (tau - 1) * 128)
            src = c_src[rows, :] if is_ctx else x_src[rows, :]
            P.dma(xt[:], src, reads=[("XR", l, tau)], writes=[xkey])
            xn, xnkey = xo.next()
            for half in range(2):
                yp, ypkey = y_ps.next()
                for f in range(8):
                    P.mm(yp[:], ot[:, f, :], woutb[:, which, f, half * 512:(half + 1) * 512], [otkey, "woutb"], [ypkey],
                         start=(f == 0), stop=(f == 7))
                P.tt("dve", xn[:, half * 512:(half + 1) * 512], yp[:], xt[:, half * 512:(half + 1) * 512], ALU.add,
                     [ypkey, xkey], [xnkey])
            if not last:
                dst = XC[rows, :] if is_ctx else XN[rows, :]
                P.dma(dst, xn[:], reads=[xnkey], writes=[("XR", l + 1, tau)], eng="pool")
            else:
                ss, sskey = ss_r.next()
                P.memset("dve", ss[:], 0.0, [sskey])
                P.act(sq_scr[:], xn[:], AF.Square, [xnkey, sskey], ["sq4", sskey], accum_out=ss[:, 0:1])
                P.act(ss[:, 1:2], ss[:, 0:1], AF.Ln, [sskey, "eps"], [sskey], bias=eps_t[:], scale=1.0 / D)
                P.act(ss[:, 1:2], ss[:, 1:2], AF.Exp, [sskey], [sskey], scale=-0.5)
                P.stt("dve", xn[:], xn[:], ss[:, 1:2], fnw[:], ALU.mult, ALU.mult, [xnkey, sskey, "fnw"], [xnkey])
                last_out.append(P.dma(out_d[rows, :], xn[:], reads=[xnkey], eng="pool"))
        P.pop()
        P.pop()
        if stop_after == (l, "p4"):
            break

    finals = list(last_out) + [d for d in P.dma_last.values()]
    return P.finish(final_waits=finals)


def _interleave(gens):
    gens = list(gens)
    while gens:
        for gen in list(gens):
            try:
                next(gen)
            except StopIteration:
                gens.remove(gen)


VEXP_ENG = os.environ.get("K_VEXP_ENG", "dve")


def build_p2(P, l, need_ctx, FMB, FMF, TM, OB, OT, G):
    identb = G["identb"]; onesblk = G["onesblk"]; wgk = G["wgk"]; nbgk = G["nbgk"]
    lbt = G["lbt"]; omlb = G["omlb"]; eps_t = G["eps_t"]
    scanmask = P.sb("scanmask", [128, 3, 128]); trimask = P.sb("trimask", [128, 6, 128]); cm = P.sb("cm", [128, 128 // CHG], BF16)
    ropecos = P.sb("ropecos", [128, T]); ropesin = P.sb("ropesin", [128, T])
    P.dma(scanmask[:], G["scanmask_d"].rearrange("n p t -> p n t"), writes=["scanmask"])
    P.dma(trimask[:], G["trimask_d"].rearrange("n p t -> p n t"), writes=["trimask"])
    P.dma(cm[:], G["cm_d"], writes=["cm"])
    P.dma(ropecos[:], G["ropecos_d"], writes=["ropecos"]); P.dma(ropesin[:], G["ropesin_d"], writes=["ropesin"])
    NHC = 128 // CHG
    NSL = [3, 3, 2 * NHC + 1, 2 * NHC + 1]
    S = [P.sb(f"S{g}", [128, NSL[g], 64]) for g in range(4)]
    RD = 6
    qk_r = Ring(P, "qk", 3, [128, 8, 128], BF16); gk_r = Ring(P, "gkr", 3, [32, 128]); v2_r = Ring(P, "vr", 8, [128, 256], BF16)
    qh_r = Ring(P, "qh", 3, [128, 2, 128], BF16); ff_r = Ring(P, "ff", 3, [128, 2, 128]); gate_r = Ring(P, "gate", 4, [128, 256], BF16)
    ob_r = Ring(P, "obr", 4, [128, 256])
    w2 = [Ring(P, f"w{i}", 3, [128, 2, 128]) for i in range(9)]
    qin_r = Ring(P, "qin", 7, [128, 2, 128])
    sm_r = Ring(P, "sm", 8, [128, 4, 8])
    kt_r = Ring(P, "ktr", 6, [128, 2, 128], BF16); kout_r = Ring(P, "kout", 6, [128, 2, 128], BF16)
    qbd_r = Ring(P, "qbd", 6, [128, 2, 2, 128], BF16)
    koT_r = Ring(P, "koT", 6, [128, 256], BF16); pt_r = Ring(P, "pt", 7, [128, 4, 128], BF16)
    vexp_r = Ring(P, "vexp", 4, [128, 4, NHC, 64], BF16)
    o_r = Ring(P, "osb", 6, [128, 256]); sq_r = Ring(P, "osq", 4, [128, 256], BF16); og_r = Ring(P, "og", 4, [128, 256], BF16)
    scanmask2 = P.sb("scanmask2", [128, 3, 256])
    for k_ in range(3):
        P.cp("pool", scanmask2[:, k_, :].rearrange("p (a t) -> p a t", a=2), scanmask[:, k_, :].unsqueeze(1).to_broadcast([128, 2, 128]),
             ["scanmask"], ["scanmask2"])
    z_ps = Ring(P, "z_ps", 1, [128, 512], psum=True); tp_ps = Ring(P, "tp_ps", 1, [128, 1024], BF16, psum=True)
    sc_ps = Ring(P, "sc_ps", 2, [128, 512], psum=True); kv_ps = Ring(P, "kv_ps", 2, [128, 512], psum=True)
    o_ps = Ring(P, "o_ps", 2, [128, 512], psum=True)
    for i, b in enumerate(qbd_r.bufs):
        P.memset("pool", b[:], 0.0, [f"qbd{i}"])
    base = [0, 0, 0, 0]

    def super_front(dirn, tau, gla, ctxs):
        bwd = dirn == 1
        is_ctx = tau < 2
        cs = slice(tau * 128, (tau + 1) * 128)
        C = 128 if gla else CHG
        nch = 128 // C
        mi = 1 if gla else 2
        kappa = -1.0 / 16.0 if gla else 1.0
        g0 = 0 if gla else 2
        c3 = lambda ap: ap.rearrange("p (n c) -> p n c", c=C)
        fl = lambda t: t[:].rearrange("p a b -> p (a b)")
        b0s = []
        for gg in range(2):
            g = g0 + gg
            b0s.append(base[g])
            base[g] = (base[g] + nch) % NSL[g]
        vt2, vkey = v2_r.next()
        voff = 512 if gla else 768
        P.dma(vt2[:], TM[cs, voff:voff + 256], reads=[("TM", tau)], writes=[vkey])
        lf, lfkey = w2[0].next()
        if gla:
            qk, qkkey = qk_r.next()
            P.dma(qk[:], FMB[12:20, :, cs].rearrange("n p t -> p n t"), reads=[("FMB", ft, _st(tau)) for ft in range(12, 20)],
                  writes=[qkkey])
            gkt, gkkey = gk_r.next()
            P.dma(gkt[:], FMF[4, 0:32, cs], reads=[("FMF", 4, _st(tau))], writes=[gkkey])
            yield
            zp, zkey = z_ps.next()
            e_, ekey = w2[1].next()
            for gg in range(2):
                P.mm(zp[:, gg * 128:(gg + 1) * 128], wgk[:, dirn, gg, :], gkt[:], ["wgk", gkkey], [zkey])
            for gg in range(2):
                P.act(e_[:, gg, :], zp[:, gg * 128:(gg + 1) * 128], AF.Exp, [zkey, "nbgk"], [ekey],
                      bias=nbgk[:, dirn * 2 + gg:dirn * 2 + gg + 1], scale=-1.0)
                yield
            P.act(fl(lf), fl(e_), AF.Ln, [ekey], [lfkey], bias=1.0)
            yield
            kk = kkkey = None
        else:
            qh, qhkey = qh_r.next()
            P.dma(qh[:], FMB[22:24, :, cs].rearrange("n p t -> p n t"), reads=[("FMB", 22, _st(tau)), ("FMB", 23, _st(tau))],
                  writes=[qhkey])
            fr, frkey = ff_r.next()
            fft = 2 if bwd else 0
            P.dma(fr[:], FMF[fft:fft + 2, :, cs].rearrange("n p t -> p n t"),
                  reads=[("FMF", fft, _st(tau)), ("FMF", fft + 1, _st(tau))], writes=[frkey])
            yield
            e_, ekey = w2[1].next()
            P.act(fl(e_), fl(fr), AF.Exp, [frkey], [ekey], scale=-1.0)
            yield
            P.act(fl(e_), fl(e_), AF.Ln, [ekey], [ekey], bias=1.0)
            yield
            P.act(fl(e_), fl(e_), AF.Exp, [ekey], [ekey], scale=-1.0)
            yield
            if l == 0:
                fgate, fkey = e_, ekey
            else:
                fgate, fkey = w2[2].next()
                for gg in range(2):
                    col = l * 4 + dirn * 2 + gg
                    P.ts("dve", fgate[:, gg, :], e_[:, gg, :], omlb[:, col:col + 1], lbt[:, col:col + 1], ALU.mult, ALU.add,
                         [ekey, "omlb", "lbt"], [fkey])
                yield
            P.act(fl(lf), fl(fgate), AF.Ln, [fkey], [lfkey])
            kk, kkkey = w2[3].next()
            P.act(fl(kk), fl(fgate), AF.Identity, [fkey], [kkkey], bias=1.0, scale=-1.0)
            yield
        Gc, Gkey = w2[4].next()
        m2 = scanmask2[:, mi, :]
        if bwd:
            P.op("dve", _scan(fl(Gc)[:, ::-1], m2, fl(lf)[:, ::-1]), [lfkey, "scanmask2"], [Gkey])
        else:
            P.op("dve", _scan(fl(Gc), m2, fl(lf)), [lfkey, "scanmask2"], [Gkey])
        yield
        Gv = c3(fl(Gc))
        n2 = 2 * nch
        mid = (C // 2) if bwd else (C // 2 - 1)
        lastp = 0 if bwd else C - 1
        sm, smkey = sm_r.next()
        P.tt("dve", sm[:, 1, 0:n2], Gv[:, :, lastp], Gv[:, :, mid], ALU.subtract, [Gkey], [smkey])
        Gm, Gmkey = w2[5].next()
        P.tt("dve", c3(fl(Gm)), Gv, Gv[:, :, mid:mid + 1].to_broadcast([128, n2, C]), ALU.subtract, [Gkey], [Gmkey])
        yield
        P.act(sm[:, 0, 0:n2], Gv[:, :, mid], AF.Exp, [Gkey], [smkey], scale=kappa)
        P.act(sm[:, 2, 0:n2], Gv[:, :, lastp], AF.Exp, [Gkey], [smkey], scale=kappa)
        yield
        P.act(sm[:, 1, 0:n2], sm[:, 1, 0:n2], AF.Exp, [smkey], [smkey], scale=kappa)
        A_, Akey = w2[6].next(); B_, Bkey = w2[7].next()
        P.act(fl(A_), fl(Gm), AF.Exp, [Gmkey], [Akey], scale=kappa)
        yield
        P.act(fl(B_), fl(Gm), AF.Exp, [Gmkey], [Bkey], scale=-kappa)
        yield
        Qf, Qfkey = w2[8].next()
        Kt, Ktkey = kt_r.next(); Kout, Koutkey = kout_r.next()
        if gla and not is_ctx:
            ts_ = slice((tau - 2) * 128, (tau - 1) * 128)
            cosb = ropecos[:, ts_].unsqueeze(1).to_broadcast([128, 2, 128])
            sinb = ropesin[:, ts_].unsqueeze(1).to_broadcast([128, 2, 128])
            r1, r1key = w2[2].next(); r2, r2key = w2[3].next()
            P.tt("dve", r1[:], qk[:, 0:2, :], cosb, ALU.mult, [qkkey, "ropecos"], [r1key])
            P.tt("pool", r2[:], qk[:, 2:4, :], sinb, ALU.mult, [qkkey, "ropesin"], [r2key])
            yield
            P.tt("dve", r1[:], r1[:], r2[:], ALU.add, [r1key, r2key], [r1key])
            r3, r3key = w2[2].next(); r4, r4key = w2[3].next()
            P.tt("pool", r3[:], qk[:, 4:6, :], cosb, ALU.mult, [qkkey, "ropecos"], [r3key])
            yield
            P.tt("dve", Qf[:], r1[:], A_[:], ALU.mult, [r1key, Akey], [Qfkey])
            P.tt("pool", r4[:], qk[:, 6:8, :], sinb, ALU.mult, [qkkey, "ropesin"], [r4key])
            yield
            P.tt("pool", r3[:], r3[:], r4[:], ALU.add, [r3key, r4key], [r3key])
            yield
            P.tt("pool", Kt[:], r3[:], B_[:], ALU.mult, [r3key, Bkey], [Ktkey])
            yield
        elif gla:
            P.tt("dve", Qf[:], qk[:, 0:2, :], A_[:], ALU.mult, [qkkey, Akey], [Qfkey])
            P.tt("pool", Kt[:], qk[:, 4:6, :], B_[:], ALU.mult, [qkkey, Bkey], [Ktkey])
            yield
        else:
            P.tt("dve", Qf[:], qh[:], A_[:], ALU.mult, [qhkey, Akey], [Qfkey])
            P.tt("pool", Kt[:], kk[:], B_[:], ALU.mult, [kkkey, Bkey], [Ktkey])
            yield
        Qbd, Qbdkey = qbd_r.next()
        P.cp("act", Qbd[0:64, :, 0, :], Qf[0:64, :, :], [Qfkey], [Qbdkey])
        yield
        P.cp("act", Qbd[64:128, :, 1, :], Qf[64:128, :, :], [Qfkey], [Qbdkey])
        Qin, Qinkey = qin_r.next()
        P.tt("dve" if gla else "pool", c3(fl(Qin)), c3(fl(Qf)), sm[:, 0, 0:n2].unsqueeze(2).to_broadcast([128, n2, C]), ALU.mult,
             [Qfkey, smkey], [Qinkey])
        P.tt("pool", c3(fl(Kout)), c3(fl(Kt)), sm[:, 1, 0:n2].unsqueeze(2).to_broadcast([128, n2, C]), ALU.mult,
             [Ktkey, smkey], [Koutkey])
        yield
        tpp, tpkey = tp_ps.next()
        for gg in range(2):
            P.tr(tpp[:, gg * 128:(gg + 1) * 128], Kout[:, gg, :], identb[:], [Koutkey, "identb"], [tpkey])
        koT, koTkey = koT_r.next()
        P.cp("act", koT[:], tpp[:, 0:256], [tpkey], [koTkey])
        yield
        scp, sckey = sc_ps.next()
        for gg in range(2):
            P.mm(scp[:, gg * 256:(gg + 1) * 256], Kt[:, gg, :], Qbd[:, gg, :, :].rearrange("p h t -> p (h t)"),
                 [Ktkey, Qbdkey], [sckey])
        PT, PTkey = pt_r.next()
        tmi = (2 if gla else 4) + (1 if bwd else 0)
        P.tt("dve", PT[:], scp[:, 0:512].rearrange("p (a t) -> p a t", a=4),
             trimask[:, tmi, :].unsqueeze(1).to_broadcast([128, 4, 128]), ALU.mult, [sckey, "trimask"], [PTkey])
        yield
        vx = vxkey = None
        if not gla:
            vx, vxkey = vexp_r.next()
            P.tt(VEXP_ENG, vx[:], vt2[:].rearrange("p (a d) -> p a d", a=4).unsqueeze(2).to_broadcast([128, 4, NHC, 64]),
                 cm[:].unsqueeze(1).unsqueeze(3).to_broadcast([128, 4, NHC, 64]), ALU.mult, [vkey, "cm"], [vxkey])
            yield
        for gg in range(2):
            g = g0 + gg
            ctxs[gg].update(dict(vt=vt2[:, gg * 128:(gg + 1) * 128], vkey=vkey, koT=koT[:, gg * 128:(gg + 1) * 128], koTkey=koTkey,
                                 vx=(vx[:, gg * 2:(gg + 1) * 2, :, :] if vx is not None else None), vxkey=vxkey,
                                 PT=PT[:, gg * 2:(gg + 1) * 2, :], PTkey=PTkey, Qin=Qin[:, gg, :], Qinkey=Qinkey,
                                 sm=sm[:, :, gg * nch:(gg + 1) * nch], smkey=smkey, b0=b0s[gg], nsl=NSL[g], C=C, nch=nch,
                                 gla=gla, gi=gg, bwd=bwd, is_ctx=is_ctx, cs=cs, g=g, tau=tau))

    def super_mid(ctxs):
        c0 = ctxs[0]
        gla = c0["gla"]; C = c0["C"]; nch = c0["nch"]; bwd = c0["bwd"]
        kvp, kvkey = kv_ps.next()
        W_ = 64 if gla else NHC * 64
        for c_ in ctxs:
            gg = c_["gi"]
            for h in range(2):
                if gla:
                    P.mm(kvp[h * 64:(h + 1) * 64, gg * W_:(gg + 1) * W_], c_["koT"][:, h * 64:(h + 1) * 64],
                         c_["vt"][:, h * 64:(h + 1) * 64], [c_["koTkey"], c_["vkey"]], [kvkey], tp=(0, h * 64))
                else:
                    P.mm(kvp[h * 64:(h + 1) * 64, gg * W_:(gg + 1) * W_], c_["koT"][:, h * 64:(h + 1) * 64],
                         c_["vx"][:, h, :, :].rearrange("p j d -> p (j d)"), [c_["koTkey"], c_["vxkey"]], [kvkey], tp=(0, h * 64))
            yield
        jorder = list(range(nch - 1, -1, -1)) if bwd else list(range(nch))
        for jj, j in enumerate(jorder):
            for c_ in ctxs:
                g = c_["g"]; gg = c_["gi"]; nsl = c_["nsl"]
                s_in = (c_["b0"] + jj) % nsl
                s_out = (c_["b0"] + jj + 1) % nsl
                P.stt("dve", S[g][:, s_out, :], S[g][:, s_in, :], c_["sm"][:, 2, j:j + 1],
                      kvp[:, gg * W_ + j * 64:gg * W_ + (j + 1) * 64], ALU.mult, ALU.add,
                      [f"S{g}_{s_in}", c_["smkey"], kvkey], [f"S{g}_{s_out}"])
            yield

    def super_out(ctxs):
        c0 = ctxs[0]
        gla = c0["gla"]; C = c0["C"]; nch = c0["nch"]; bwd = c0["bwd"]; is_ctx = c0["is_ctx"]; cs = c0["cs"]; tau = c0["tau"]
        g0 = c0["g"]
        op_, okey = o_ps.next()
        for c_ in ctxs:
            gg = c_["gi"]
            for h in range(2):
                P.mm(op_[h * 64:(h + 1) * 64, gg * 128:(gg + 1) * 128], c_["vt"][:, h * 64:(h + 1) * 64], c_["PT"][:, h, :],
                     [c_["vkey"], c_["PTkey"]], [okey], start=(gg == 0), stop=False, tp=(0, h * 64))
            yield
        jorder = list(range(nch - 1, -1, -1)) if bwd else list(range(nch))
        for jj, j in enumerate(jorder):
            for c_ in ctxs:
                g = c_["g"]; gg = c_["gi"]
                s_in = (c_["b0"] + jj) % c_["nsl"]
                for h in range(2):
                    P.mm(op_[h * 64:(h + 1) * 64, gg * 128 + j * C:gg * 128 + (j + 1) * C], S[g][h * 64:(h + 1) * 64, s_in, :],
                         c_["Qin"][h * 64:(h + 1) * 64, j * C:(j + 1) * C], [f"S{g}_{s_in}", c_["Qinkey"]], [okey],
                         start=False, stop=True, tp=(h * 64, h * 64))
            yield
        g2 = lambda d3: d3.rearrange("n p t -> p n t")
        if bwd:
            osb, oskey = o_r.next()
            P.cp("act", osb[:], op_[:, 0:256], [okey], [oskey])
            P.dma(g2(OB[g0:g0 + 2, :, cs]), osb[:].rearrange("p (n t) -> p n t", n=2), reads=[oskey],
                  writes=[("OB", g0, tau), ("OB", g0 + 1, tau)], eng="pool")
            yield
        elif not (is_ctx and not need_ctx):
            obt, obtkey = ob_r.next()
            P.dma(obt[:].rearrange("p (n t) -> p n t", n=2), g2(OB[g0:g0 + 2, :, cs]),
                  reads=[("OB", g0, tau), ("OB", g0 + 1, tau)], writes=[obtkey])
            gt_, gtkey = gate_r.next()
            gft = 20 if gla else 24
            P.dma(gt_[:].rearrange("p (n t) -> p n t", n=2), g2(FMB[gft:gft + 2, :, cs]),
                  reads=[("FMB", gft, _st(tau)), ("FMB", gft + 1, _st(tau))], writes=[gtkey])
            osb, oskey = o_r.next()
            P.tt("dve", osb[:], op_[:, 0:256], obt[:], ALU.add, [okey, obtkey], [oskey])
            yield
            osq, osqkey = sq_r.next()
            P.act(osq[:], osb[:], AF.Square, [oskey], [osqkey])
            yield
            zp, zkey = z_ps.next()
            P.mm(zp[:, 0:256], onesblk[:], osq[:], ["onesblk", osqkey], [zkey])
            rs, rskey = o_r.next()
            P.act(rs[:], zp[:, 0:256], AF.Ln, [zkey, "eps"], [rskey], bias=eps_t[:])
            yield
            P.act(rs[:], rs[:], AF.Exp, [rskey], [rskey], scale=-0.5)
            yield
            nw = G["glanw"] if gla else G["hgnw"]
            P.stt("dve", osb[:], osb[:], nw[:], rs[:], ALU.mult, ALU.mult, [oskey, rskey, "glanw", "hgnw"], [oskey])
            yield
            og, ogkey = og_r.next()
            P.tt("pool", og[:], osb[:], gt_[:], ALU.mult, [oskey, gtkey], [ogkey])
            P.dma(g2(OT[4 + g0:6 + g0, :, cs]), og[:].rearrange("p (n t) -> p n t", n=2), reads=[ogkey],
                  writes=[("OT", 4 + g0, tau), ("OT", 5 + g0, tau)], eng="pool")
            yield

    for dirn in DBG["p2_dirs"]:
        bwd = dirn == 1
        order = [1, 0] + list(range(NT - 1, 1, -1)) if bwd else list(range(NT))
        for g in range(4):
            base[g] = 0
            P.memset("dve", S[g][:, 0, :], 0.0, [f"S{g}_0"])
        if DBG["p2_tiles"]:
            order = order[:DBG["p2_tiles"]]
        mids = []
        outs = []
        mid_ctxs = []
        for tau in order:
            cts = [[dict(), dict()], [dict(), dict()]]
            fronts = [super_front(dirn, tau, False, cts[0]), super_front(dirn, tau, True, cts[1])]
            _interleave(outs + mids + fronts)
            outs = [super_out(c2) for c2 in mid_ctxs]
            mids = [super_mid(c2) for c2 in cts]
            mid_ctxs = cts
        _interleave(outs + mids)
        _interleave([super_out(c2) for c2 in mid_ctxs])


def _st(tau):
    return 0 if tau < 2 else 2 + 4 * ((tau - 2) // 4)


def _scan(out, d0, d1):
    return lambda e: e.tensor_tensor_scan(out, d0, d1, 0.0, ALU.mult, ALU.add)


def build_p3(P, l, need_ctx, FMB, TM, OT, ttab, ones64):
    KTb = Ring(P, "KTb", 2, [128, NTOK], BF16)
    Vd = Ring(P, "Vd", 2, [128, 68, 64], BF16)
    q_r = Ring(P, "q3", 2, [128, 512], BF16); g_r = Ring(P, "g3", 2, [128, 512], BF16)
    pe_r = Ring(P, "pe3", 4, [128, 512], BF16); p_r = Ring(P, "p3", 4, [128, 512], BF16)
    rc_r = Ring(P, "rc3", 2, [128, 512]); o_r = Ring(P, "o3", 2, [128, 512]); og_r = Ring(P, "og3", 2, [128, 512], BF16)
    s_ps = Ring(P, "s_ps", 3, [128, 512], psum=True)
    ao_ps = Ring(P, "ao_ps", 2, [128, 512], psum=True); as_ps = Ring(P, "as_ps", 2, [128, 512], psum=True)
    TMv = TM.rearrange("(r k) c -> k r c", k=64)

    def row_interval(rp):
        if rp <= 7:
            return 0, rp + 4
        if rp >= 56:
            return rp - 3, 63
        return rp - 3, rp + 4

    for hp in range(4):
        kt_, ktkey = KTb.next()
        P.dma(kt_[:], FMB[4 + hp, :, :], reads=[("FMB", 4 + hp, s) for s in [0] + [2 + 4 * i for i in range(8)]], writes=[ktkey])
        vd, vdkey = Vd.next()
        for hh in range(2):
            for part in range(4):
                r0, r1 = part * 17, (part + 1) * 17
                P.dma(vd[hh * 64:(hh + 1) * 64, r0:r1, :], TMv[:, r0:r1, hp * 128 + hh * 64:hp * 128 + (hh + 1) * 64],
                      reads=[("TM", t_) for t_ in range(NT)], writes=[vdkey])
        blocks = [("lat", b) for b in range(8)] + ([("ctx", 0)] if need_ctx else [])
        for kind, b in blocks:
            if kind == "lat":
                tok0 = CTX + b * 512
                nq = 512
                pieces = [(rho, 0, 512, None) for rho in range(4)]
                for rp in range(64):
                    lo, hi = row_interval(rp)
                    a = max(lo, 8 * b); e_ = min(hi, 8 * b + 7)
                    if a > e_:
                        continue
                    u0 = 7 - rp + a
                    pieces.append((4 + rp, (a - 8 * b) * 64, (e_ - 8 * b + 1) * 64, u0))
            else:
                tok0 = 0
                nq = 256
                pieces = [(rho, 0, 256, None) for rho in range(4)]
            qt, qkey = q_r.next(); gt_, gkey = g_r.next()
            st_reads = sorted(set(_st(t_) for t_ in range(tok0 // 128, (tok0 + nq) // 128)))
            P.dma(qt[:, 0:nq], FMB[hp, :, tok0:tok0 + nq], reads=[("FMB", hp, s) for s in st_reads], writes=[qkey])
            P.dma(gt_[:, 0:nq], FMB[8 + hp, :, tok0:tok0 + nq], reads=[("FMB", 8 + hp, s) for s in st_reads], writes=[gkey])
            ao, aokey = ao_ps.next(); as_, askey = as_ps.next()
            staged = {}

            def stage_a(pi):
                rho, c0, c1, u0 = pieces[pi]
                sp_, spkey = s_ps.next()
                for hh in range(2):
                    P.mm(sp_[hh * 64:(hh + 1) * 64, c0:c1], kt_[hh * 64:(hh + 1) * 64, rho * 64:(rho + 1) * 64],
                         qt[hh * 64:(hh + 1) * 64, c0:c1], [ktkey, qkey], [spkey], tp=(hh * 64, hh * 64))
                pe_, pekey = pe_r.next()
                P.act(pe_[:, c0:c1], sp_[:, c0:c1], AF.Exp, [spkey], [pekey])
                if u0 is not None:
                    pp, ppkey = p_r.next()
                    nr = (c1 - c0) // 64
                    P.tt("dve", pp[:, c0:c1], pe_[:, c0:c1], ttab[:, hp, u0 * 64:(u0 + nr) * 64], ALU.mult,
                         [pekey, "ttab"], [ppkey])
                else:
                    pp, ppkey = pe_, pekey
                staged[pi] = (pp, ppkey)

            def stage_b(pi):
                rho, c0, c1, u0 = pieces[pi]
                pp, ppkey = staged.pop(pi)
                for hh in range(2):
                    P.mm(ao[hh * 64:(hh + 1) * 64, c0:c1], vd[hh * 64:(hh + 1) * 64, rho, :], pp[hh * 64:(hh + 1) * 64, c0:c1],
                         [vdkey, ppkey], [aokey], start=(pi == 0), stop=(pi == len(pieces) - 1), tp=(hh * 64, hh * 64))
                    P.mm(as_[hh * 64:(hh + 1) * 64, c0:c1], ones64[hh * 64:(hh + 1) * 64, :], pp[hh * 64:(hh + 1) * 64, c0:c1],
                         ["ones64", ppkey], [askey], start=(pi == 0), stop=(pi == len(pieces) - 1), tp=(hh * 64, hh * 64))

            LA = 2
            for pi in range(min(LA, len(pieces))):
                stage_a(pi)
            for pi in range(len(pieces)):
                if pi + LA < len(pieces):
                    stage_a(pi + LA)
                stage_b(pi)
            rc, rckey = rc_r.next()
            P.recip(rc[:, 0:nq], as_[:, 0:nq], [askey], [rckey])
            ob, obkey = o_r.next()
            P.tt("dve", ob[:, 0:nq], ao[:, 0:nq], rc[:, 0:nq], ALU.mult, [aokey, rckey], [obkey])
            og, ogkey = og_r.next()
            P.tt("pool", og[:, 0:nq], ob[:, 0:nq], gt_[:, 0:nq], ALU.mult, [obkey, gkey], [ogkey])
            P.dma(OT[hp, :, tok0:tok0 + nq], og[:, 0:nq], reads=[ogkey],
                  writes=[("OT", hp, t_) for t_ in range(tok0 // 128, (tok0 + nq) // 128)], eng="pool")


_CACHE = {}


def kernel(**inputs):
    inputs = {k: np.asarray(v) for k, v in inputs.items()}
    common, per = host_layout(inputs)
    if "nc" not in _CACHE:
        _CACHE["nc"] = build_program()
    nc = _CACHE["nc"]
    in_maps = [dict(common, **p) for p in per]
    res = run_bass_kernel_spmd(nc, in_maps, core_ids=list(range(len(per))))
    out = np.stack([np.asarray(r["out"], dtype=np.float32) for r in res.results], axis=0)
    return out
```

```python
import numpy as np
import ml_dtypes
from contextlib import ExitStack
import concourse.bass as bass
import concourse.mybir as mybir
from concourse.bass_utils import run_bass_kernel_spmd

F32 = mybir.dt.float32
BF16 = mybir.dt.bfloat16
AF = mybir.ActivationFunctionType
ALU = mybir.AluOpType
NPBF = ml_dtypes.bfloat16

ENGS = ("pe", "act", "dve", "pool", "sp")
SAME_ENGINE_SYNC = {"pe": False, "act": True, "dve": True, "pool": True, "sp": False}
N_DMA_SEMS = {"sp": 20, "act": 4, "pool": 12}

D = 1024
KT = 8
T = 4096
CTX = 256
NTOK = T + CTX
NT = NTOK // 128
NEXT = 4992
N_FMB = 26
N_FMF = 5
EPS = 1e-6
CHG = 32


class Ins:
    __slots__ = ("eng", "fn", "deps", "is_dma", "dma_sem", "dma_val", "inc_idx", "needed", "pos", "guard")

    def __init__(self, eng, fn, is_dma):
        self.eng = eng
        self.fn = fn
        self.deps = []
        self.is_dma = is_dma
        self.dma_sem = None
        self.dma_val = None
        self.inc_idx = None
        self.needed = False
        self.guard = None


class Prog:
    def __init__(self):
        self.nc = bass.Bass("TRN2", target_bir_lowering=False)
        self.es = ExitStack()
        self.streams = {e: [] for e in ENGS}
        self.last_w = {}
        self.readers = {}
        self.dma_rr = {e: 0 for e in N_DMA_SEMS}
        self.dma_last = {}
        self.dma_cnt = {}
        self.n_ins = 0
        self.scopes = []

    def push(self):
        self.scopes.append(ExitStack())

    def pop(self):
        self.barrier()
        self.scopes.pop().close()

    def _ctx(self):
        return self.scopes[-1] if self.scopes else self.es

    def sb(self, name, shape, dtype=F32):
        self.uid = getattr(self, "uid", 0) + 1
        return self._ctx().enter_context(self.nc.sbuf_tensor(f"sb{self.uid}_{name}", list(shape), dtype))

    def ps(self, name, shape=(128, 512), dtype=F32):
        self.uid = getattr(self, "uid", 0) + 1
        return self._ctx().enter_context(self.nc.psum_tensor(f"ps{self.uid}_{name}", list(shape), dtype))

    def dram(self, name, shape, dtype=F32, kind="Internal"):
        return self.nc.dram_tensor(name, list(shape), dtype, kind=kind)

    def op(self, eng, fn, reads=(), writes=(), dma=False):
        ins = Ins(eng, fn, dma)
        ins.pos = self.n_ins
        self.n_ins += 1
        deps = []
        for r in reads:
            w = self.last_w.get(r)
            if w is not None:
                deps.append(w)
        for wkey in writes:
            w = self.last_w.get(wkey)
            if w is not None:
                deps.append(w)
            deps.extend(self.readers.get(wkey, {}).values())
        if dma:
            slot = self.dma_rr[eng] % N_DMA_SEMS[eng]
            self.dma_rr[eng] += 1
            key = (eng, slot)
            prev = self.dma_last.get(key)
            if prev is not None:
                ins.guard = prev
            self.dma_cnt[key] = self.dma_cnt.get(key, 0) + 1
            ins.dma_sem = key
            ins.dma_val = 16 * self.dma_cnt[key]
            self.dma_last[key] = ins
        best = {}
        for d in deps:
            if d is ins:
                continue
            k = d.dma_sem if d.is_dma else d.eng
            if k not in best or best[k].pos < d.pos:
                best[k] = d
        for d in best.values():
            ins.deps.append(d)
            d.needed = True
        if ins.guard is not None:
            ins.guard.needed = True
        mykey = ins.dma_sem if dma else eng
        for r in reads:
            self.readers.setdefault(r, {})[mykey] = ins
        for wkey in writes:
            self.last_w[wkey] = ins
            self.readers[wkey] = {}
        self.streams[eng].append(ins)
        return ins

    def barrier(self):
        lasts = []
        for e in ENGS:
            for ins in reversed(self.streams[e]):
                if ins.fn is not None and not ins.is_dma:
                    lasts.append(ins)
                    break
        lasts += list(self.dma_last.values())
        for e in ENGS:
            b = Ins(e, None, False)
            b.pos = self.n_ins
            self.n_ins += 1
            for d in lasts:
                b.deps.append(d)
                d.needed = True
            self.streams[e].append(b)
        self.last_w = {}
        self.readers = {}

    def dma(self, out, in_, reads=(), writes=(), eng="sp"):
        return self.op(eng, lambda e: e.dma_start(out=out, in_=in_), reads, writes, dma=True)

    def mm(self, out, lhsT, rhs, reads, writes, start=True, stop=True, tp=None):
        if tp is None:
            return self.op("pe", lambda e: e.matmul(out, lhsT, rhs, start=start, stop=stop), reads, writes)
        return self.op("pe", lambda e: e.matmul(out, lhsT, rhs, start=start, stop=stop, tile_position=tp,
                                                skip_group_check=True), reads, writes)

    def tr(self, out, in_, ident, reads, writes):
        return self.op("pe", lambda e: e.transpose(out, in_, ident), reads, writes)

    def act(self, out, in_, func, reads, writes, bias=None, scale=None, accum_out=None):
        kw = {}
        if bias is not None:
            kw["bias"] = bias
        if scale is not None:
            kw["scale"] = scale
        if accum_out is not None:
            kw["accum_out"] = accum_out
        return self.op("act", lambda e: e.activation(out, in_, func, **kw), reads, writes)

    def tt(self, eng, out, in0, in1, op, reads, writes):
        return self.op(eng, lambda e: e.tensor_tensor(out, in0, in1, op), reads, writes)

    def ts(self, eng, out, in0, s1, s2, op0, op1, reads, writes):
        if s2 is None:
            return self.op(eng, lambda e: e.tensor_scalar(out, in0, s1, None, op0=op0), reads, writes)
        return self.op(eng, lambda e: e.tensor_scalar(out, in0, s1, s2, op0=op0, op1=op1), reads, writes)

    def stt(self, eng, out, in0, scalar, in1, op0, op1, reads, writes):
        return self.op(eng, lambda e: e.scalar_tensor_tensor(out, in0, scalar, in1, op0=op0, op1=op1), reads, writes)

    def cp(self, eng, out, in_, reads, writes):
        if eng == "act":
            return self.op("act", lambda e: e.copy(out, in_), reads, writes)
        return self.op(eng, lambda e: e.tensor_copy(out, in_), reads, writes)

    def recip(self, out, in_, reads, writes):
        return self.op("dve", lambda e: e.reciprocal(out, in_), reads, writes)

    def memset(self, eng, ap, val, writes):
        return self.op(eng, lambda e: e.memset(ap, val), (), writes)

    def finish(self, final_waits=()):
        nc = self.nc
        es = self.es
        sems = {e: es.enter_context(nc.semaphore("s_" + e)) for e in ENGS}
        dsems = {}
        for e, n in N_DMA_SEMS.items():
            for i in range(n):
                dsems[(e, i)] = es.enter_context(nc.semaphore(f"d_{e}{i}"))
        for fw in final_waits:
            fw.needed = True
        for e in ENGS:
            c = 0
            for ins in self.streams[e]:
                if ins.is_dma or ins.fn is None:
                    continue
                if ins.needed:
                    c += 1
                    ins.inc_idx = c
        streams = self.streams

        def emit(e, eng_obj):
            known = {}

            def wait_for(d):
                if d.is_dma:
                    k = ("d",) + d.dma_sem
                    if known.get(k, 0) >= d.dma_val:
                        return
                    eng_obj.wait_ge(dsems[d.dma_sem], d.dma_val)
                    known[k] = d.dma_val
                else:
                    if d.eng == e and not SAME_ENGINE_SYNC[e]:
                        return
                    k = ("e", d.eng)
                    if known.get(k, 0) >= d.inc_idx:
                        return
                    eng_obj.wait_ge(sems[d.eng], d.inc_idx)
                    known[k] = d.inc_idx

            for ins in streams[e]:
                for d in ins.deps:
                    wait_for(d)
                if ins.guard is not None:
                    wait_for(ins.guard)
                if ins.fn is None:
                    continue
                bi = ins.fn(eng_obj)
                if ins.is_dma:
                    bi.then_inc(dsems[ins.dma_sem], 16)
                elif ins.needed:
                    bi.then_inc(sems[e], 1)
            if e == "sp":
                for fw in final_waits:
                    wait_for(fw)

        with nc.Block() as block:
            @block.tensor
            def _(eng):
                emit("pe", eng)

            @block.scalar
            def _(eng):
                emit("act", eng)

            @block.vector
            def _(eng):
                emit("dve", eng)

            @block.gpsimd
            def _(eng):
                emit("pool", eng)

            @block.sync
            def _(eng):
                emit("sp", eng)
        while self.scopes:
            self.scopes.pop().close()
        self.es.close()
        return nc


class Ring:
    def __init__(self, P, name, n, shape, dtype=F32, psum=False):
        self.name = name
        if psum:
            self.bufs = [P.ps(f"{name}{i}", shape, dtype) for i in range(n)]
        else:
            self.bufs = [P.sb(f"{name}{i}", shape, dtype) for i in range(n)]
        self.i = 0

    def next(self):
        j = self.i % len(self.bufs)
        self.i += 1
        return self.bufs[j], f"{self.name}{j}"


def ext_col_index():
    cols = []
    cols += list(range(0, 512)) + list(range(512, 1024)) + list(range(1536, 2048))

    def pad_heads(base, swap):
        out = []
        for h in range(4):
            for i in range(32):
                j = (i + 8 if (i % 16) < 8 else i - 8) if swap else i
                out.append(base + h * 32 + j)
            out += [-1] * 32
        return out
    cols += pad_heads(2048, False) + pad_heads(2048, True) + pad_heads(2176, False) + pad_heads(2176, True)
    cols += list(range(2592, 2848)) + list(range(2848, 3104)) + list(range(3872, 4128))
    cols += list(range(3104, 3360)) + list(range(3360, 3616)) + list(range(2560, 2592)) + [-1] * 96
    cols += list(range(1024, 1536)) + list(range(2304, 2560)) + list(range(3616, 3872))
    assert len(cols) == NEXT
    return np.array(cols)


def host_consts():
    c = {}
    c["identf"] = np.eye(128, dtype=np.float32)
    c["identb"] = np.eye(128, dtype=np.float32).astype(NPBF)
    i = np.arange(128)
    m16 = np.ones((128, 128), np.float32); m16[:, i % 16 == 0] = 0
    m128 = np.ones((128, 128), np.float32); m128[:, 0] = 0
    m32 = np.ones((128, 128), np.float32); m32[:, i % 32 == 0] = 0
    c["scanmask"] = np.stack([m16, m128, m32], 0)
    s = i[:, None]; t = i[None, :]
    tri = []
    for C in (16, 128, 32):
        same = (s // C) == (t // C)
        tri.append((same & (s <= t)).astype(np.float32))
        tri.append((same & (s >= t)).astype(np.float32))
    c["trimask"] = np.stack(tri, 0)
    c["cm"] = ((i[:, None] // CHG) == np.arange(128 // CHG)[None, :]).astype(np.float32).astype(NPBF)
    c["onesblk"] = (((i[:, None] // 64) == (i[None, :] // 64)).astype(np.float32) / 64.0).astype(NPBF)
    c["ones64"] = np.ones((128, 64), np.float32).astype(NPBF)
    inv_freq = (1.0 / (10000.0 ** (np.arange(0, 16, 2, dtype=np.float32) / np.float32(16)))).astype(np.float32)
    tt_ = np.arange(T)
    cos = np.zeros((128, T), np.float32); sins = np.zeros((128, T), np.float32)
    for p in range(128):
        ii = p % 64
        if ii >= 32:
            continue
        blk = ii // 16
        j = ii % 16
        f = j % 8
        pos = (tt_ // 64 if blk == 0 else tt_ % 64).astype(np.float32)
        ang = (pos * inv_freq[f]).astype(np.float32)
        cos[p] = np.cos(ang)
        sins[p] = -np.sin(ang) if j < 8 else np.sin(ang)
    c["ropecos"] = cos
    c["ropesin"] = sins
    return c


def host_layout(inp):
    f32 = np.float32
    common = dict(host_consts())
    cols = ext_col_index()
    valid = cols >= 0
    DEPTH = inp["w_in"].shape[0]
    q = np.arange(64)
    col_start = np.clip(q - 8, 0, 48)
    kc = np.arange(64)
    inwin = (kc[:, None] >= col_start[None, :]) & (kc[:, None] < col_start[None, :] + 16)
    didx = np.clip(kc[:, None] - q[None, :], -15, 15) + 15
    pidx = np.arange(128)
    for l in range(DEPTH):
        w_ext = np.zeros((D, NEXT), f32)
        w_ext[:, valid] = inp["w_in"][l][:, cols[valid]]
        common[f"w_in{l}"] = w_ext
        common[f"ada_w{l}"] = np.ascontiguousarray(inp["ada_w"][l], dtype=f32)
        common[f"ada_bT{l}"] = np.ascontiguousarray(inp["ada_b"][l].reshape(24, 128).T, dtype=f32)
        common[f"ada_bgt{l}"] = np.ascontiguousarray(inp["ada_b"][l][2048:3072].reshape(1, 1024), dtype=f32)
        common[f"norm_wT{l}"] = np.ascontiguousarray(inp["norm_w"][l].reshape(8, 128).T, dtype=f32)
        common[f"w_out{l}"] = np.ascontiguousarray(inp["w_out"][l], dtype=f32)
        rpb = inp["na_rpb"][l]
        g = rpb[:, ::-1, :][:, :, didx]
        g = np.where(inwin[None, None], g, f32(-30000.0))
        g = g.transpose(0, 2, 1, 3).reshape(4, 128, 15 * 64)
        common[f"rpbt{l}"] = np.ascontiguousarray(g, dtype=f32)
        wgk = inp["gla_w_gk"][l]
        bgk = inp["gla_b_gk"][l]
        wp = np.zeros((32, 2, 2, 128), f32)
        bp = np.zeros((128, 2, 2), f32)
        for d_ in range(2):
            for g_ in range(2):
                for p in range(128):
                    h = 2 * g_ + p // 64
                    ii = p % 64
                    if ii < 32:
                        wp[d_ * 16:(d_ + 1) * 16, d_, g_, p] = wgk[d_, :, h * 32 + ii]
                        bp[p, d_, g_] = bgk[d_, h * 32 + ii]
        common[f"wgk{l}"] = wp
        common[f"bgk{l}"] = bp
        common[f"glanw{l}"] = np.ascontiguousarray(inp["gla_norm_w"][l][pidx % 64].reshape(128, 1), dtype=f32)
        common[f"hgnw{l}"] = np.ascontiguousarray(inp["hgrn_norm_w"][l][pidx % 64].reshape(128, 1), dtype=f32)
    lb = inp["hgrn_lower_bounds"]
    common["lbraw"] = np.ascontiguousarray(lb.reshape(2, 2, 2, 128).transpose(3, 0, 1, 2).reshape(128, 8), dtype=f32)
    common["fnw"] = np.ascontiguousarray(inp["final_norm_w"].reshape(1, D), dtype=f32)
    per = []
    B = inp["x"].shape[0]
    cc = inp["c_ctx"].reshape(8, 128).T
    for b in range(B):
        cv = np.concatenate([inp["c"][b].reshape(8, 128).T, cc], axis=1)
        per.append({"x": np.ascontiguousarray(inp["x"][b], dtype=f32),
                    "ctx": np.ascontiguousarray(inp["ctx"][b], dtype=f32),
                    "cvec": np.ascontiguousarray(cv, dtype=f32)})
    return common, per


FMB_EVAC = {}
for _t in range(0, 4):
    FMB_EVAC[_t] = ("scale", 0.125)
for _t in range(4, 8):
    FMB_EVAC[_t] = ("copy", 1.0)
for _t in range(8, 12):
    FMB_EVAC[_t] = ("silu", 1.0)
for _t in range(12, 16):
    FMB_EVAC[_t] = ("scale", 32.0 ** -0.5)
for _t in range(16, 20):
    FMB_EVAC[_t] = ("copy", 1.0)
for _t in (20, 21, 24, 25):
    FMB_EVAC[_t] = ("silu", 1.0)
for _t in (22, 23):
    FMB_EVAC[_t] = ("scale", 0.125)


import os
DBG = {"skip_p1": os.environ.get("K_DBG_SKIP_P1") == "1",
       "p2_tiles": int(os.environ.get("K_DBG_P2_TILES", "0")),
       "p2_groups": [int(x) for x in os.environ.get("K_DBG_P2_GROUPS", "0,1,2,3").split(",")],
       "p2_dirs": [int(x) for x in os.environ.get("K_DBG_P2_DIRS", "1,0").split(",")],
       "cut": int(os.environ.get("K_DBG_P2_CUT", "99"))}


def build_program(depth=2, stop_after=None, debug=False):
    P = Prog()
    nc = P.nc
    kdbg = "ExternalOutput" if debug else "Internal"

    def din(name, shape, dt=F32):
        return P.dram(name, shape, dt, kind="ExternalInput").ap()

    x_d = din("x", [T, D]); ctx_d = din("ctx", [CTX, D]); cvec_d = din("cvec", [128, 16])
    identf_d = din("identf", [128, 128]); identb_d = din("identb", [128, 128], BF16)
    scanmask_d = din("scanmask", [3, 128, 128]); trimask_d = din("trimask", [6, 128, 128])
    cm_d = din("cm", [128, 128 // CHG], BF16); onesblk_d = din("onesblk", [128, 128], BF16); ones64_d = din("ones64", [128, 64], BF16)
    ropecos_d = din("ropecos", [128, T]); ropesin_d = din("ropesin", [128, T])
    lbraw_d = din("lbraw", [128, 8]); fnw_d = din("fnw", [1, D])
    LW = []
    for l in range(2):
        LW.append(dict(
            w_in=din(f"w_in{l}", [D, NEXT]), ada_w=din(f"ada_w{l}", [D, 3 * D]), ada_bT=din(f"ada_bT{l}", [128, 24]),
            ada_bgt=din(f"ada_bgt{l}", [1, D]), norm_wT=din(f"norm_wT{l}", [128, 8]), w_out=din(f"w_out{l}", [D, D]),
            rpbt=din(f"rpbt{l}", [4, 128, 960]), wgk=din(f"wgk{l}", [32, 2, 2, 128]), bgk=din(f"bgk{l}", [128, 2, 2]),
            glanw=din(f"glanw{l}", [128, 1]), hgnw=din(f"hgnw{l}", [128, 1])))
    out_d = P.dram("out", [T, D], F32, kind="ExternalOutput").ap()
    FMB = P.dram("FMB", [N_FMB, 128, NTOK], BF16, kind=kdbg).ap()
    FMF = P.dram("FMF", [N_FMF, 128, NTOK], F32, kind=kdbg).ap()
    TM = P.dram("TM", [NTOK, 1024], BF16, kind=kdbg).ap()
    OB = P.dram("OB", [4, 128, NTOK], F32, kind=kdbg).ap()
    OT = P.dram("OT", [8, 128, NTOK], BF16, kind=kdbg).ap()
    XN = P.dram("XN", [T, D], F32, kind=kdbg).ap()
    XC = P.dram("XC", [CTX, D], F32, kind=kdbg).ap()
    MODDBG = P.dram("MODDBG", [128, 64], F32, kind=kdbg).ap()

    identf = P.sb("identf", [128, 128]); identb = P.sb("identb", [128, 128], BF16)
    onesblk = P.sb("onesblk", [128, 128], BF16); ones64 = P.sb("ones64", [128, 64], BF16)
    cvec = P.sb("cvec", [128, 16]); sil = P.sb("sil", [128, 16]); silrep = P.sb("silrep", [128, 16, 128])
    lbraw = P.sb("lbraw", [128, 8]); lbt = P.sb("lbt", [128, 8]); omlb = P.sb("omlb", [128, 8])
    eps_t = P.sb("eps_t", [128, 1])
    P.memset("dve", eps_t[:], EPS, ["eps"])
    for nm, dst, src in (("identf", identf, identf_d), ("identb", identb, identb_d), ("onesblk", onesblk, onesblk_d),
                         ("ones64", ones64, ones64_d), ("cvec", cvec, cvec_d), ("lbraw", lbraw, lbraw_d)):
        P.dma(dst[:], src, writes=[nm])
    tmp16 = P.sb("tmp16", [128, 16])
    P.act(tmp16[:], cvec[:], AF.Exp, ["cvec"], ["tmp16"], scale=-1.0)
    P.ts("dve", tmp16[:], tmp16[:], 1.0, None, ALU.add, None, ["tmp16"], ["tmp16"])
    P.recip(tmp16[:], tmp16[:], ["tmp16"], ["tmp16"])
    P.tt("dve", sil[:], cvec[:], tmp16[:], ALU.mult, ["cvec", "tmp16"], ["sil"])
    for j in range(16):
        P.cp("dve", silrep[:, j, :], sil[:, j:j + 1].to_broadcast([128, 128]), ["sil"], ["silrep"])
    P.memset("dve", lbt[:], 0.0, ["lbt"])
    P.tt("dve", lbt[:, 4:8], lbraw[:, 0:4], lbraw[:, 4:8], ALU.subtract, ["lbraw", "lbt"], ["lbt"])
    P.act(lbt[:, 4:8], lbt[:, 4:8], AF.Exp, ["lbt"], ["lbt"])
    P.ts("dve", lbt[:, 4:8], lbt[:, 4:8], 1.0, None, ALU.add, None, ["lbt"], ["lbt"])
    P.recip(lbt[:, 4:8], lbt[:, 4:8], ["lbt"], ["lbt"])
    P.ts("dve", omlb[:], lbt[:], -1.0, 1.0, ALU.mult, ALU.add, ["lbt"], ["omlb"])

    last_out = []

    for l in range(depth):
        W = LW[l]
        last = (l == depth - 1)
        need_ctx = not last
        x_src = x_d if l == 0 else XN
        c_src = ctx_d if l == 0 else XC
        tiles = list(range(NT))

        P.push()
        gsh = P.sb(f"gsh{l}", [128, 2, 2, 8])
        gtrep = P.sb(f"gtrep{l}", [128, 2, D])
        woutb = P.sb(f"woutb{l}", [128, 2 if need_ctx else 1, KT, D], BF16)
        ttab = P.sb(f"ttab{l}", [128, 4, 960], BF16)
        wgk = P.sb(f"wgk{l}", [32, 2, 2, 128]); nbgk = P.sb(f"nbgk{l}", [128, 4])
        glanw = P.sb(f"glanw{l}", [128, 1]); hgnw = P.sb(f"hgnw{l}", [128, 1])

        P.push()
        winb = P.sb("winb", [128, KT, NEXT], BF16)
        win_v = W["w_in"].rearrange("(kt p) j -> p kt j", p=128)
        if not (DBG["skip_p1"] and l == 0):
            for kt in range(KT):
                for hf in range(2):
                    c0, c1 = hf * (NEXT // 2), (hf + 1) * (NEXT // 2)
                    P.dma(winb[:, kt, c0:c1], win_v[:, kt, c0:c1], writes=["winb"], eng="pool")
        tr_ps = Ring(P, "tr_ps", 2, [128, 512], psum=True)
        fm_ps = Ring(P, "fm_ps", 3, [128, 512], psum=True)
        tm_ps = Ring(P, "tm_ps", 2, [128, 512], psum=True)
        modT = P.sb("modT", [128, 16, 2]); adabT = P.sb("adabT", [128, 24]); normwT = P.sb("normwT", [128, 8])
        adabgt = P.sb("adabgt", [128, D])
        stg = Ring(P, "stg", 2, [128, KT, 256])
        mod_ps = tm_ps.bufs[0]; gt_ps = fm_ps
        P.dma(adabT[:], W["ada_bT"], writes=["adabT"]); P.dma(normwT[:], W["norm_wT"], writes=["normwT"])
        P.dma(adabgt[:], W["ada_bgt"].partition_broadcast(128), writes=["adabgt"])
        P.dma(wgk[:], W["wgk"], writes=["wgk"]); P.dma(nbgk[:], W["bgk"].rearrange("p a b -> p (a b)"), writes=["nbgk"])
        P.dma(glanw[:], W["glanw"], writes=["glanw"]); P.dma(hgnw[:], W["hgnw"], writes=["hgnw"])
        P.ts("dve", nbgk[:], nbgk[:], -1.0, None, ALU.mult, None, ["nbgk"], ["nbgk"])
        adaw_v = W["ada_w"].rearrange("(kt p) j -> p kt j", p=128)
        sil2 = sil[:].rearrange("p (w k) -> p w k", w=2)
        for ch in range(12):
            sbuf, skey = stg.next()
            P.dma(sbuf[:], adaw_v[:, :, ch * 256:(ch + 1) * 256], writes=[skey])
            if ch < 8:
                for jl in range(2):
                    jt = ch * 2 + jl
                    for kt in range(KT):
                        P.mm(mod_ps[:, jt * 2:jt * 2 + 2], sbuf[:, kt, jl * 128:(jl + 1) * 128], sil2[:, :, kt],
                             [skey, "sil"], ["tm_ps0"], start=(kt == 0), stop=(kt == KT - 1))
                    P.ts("dve", modT[:, jt, :], mod_ps[:, jt * 2:jt * 2 + 2], adabT[:, jt:jt + 1], None, ALU.add, None,
                         ["tm_ps0", "adabT"], ["modT"])
            else:
                q4 = ch - 8
                for which in range(2):
                    gp, gk_ = gt_ps.next()
                    for kt in range(KT):
                        P.mm(gp[:, 0:256], silrep[:, which * 8 + kt, :], sbuf[:, kt, :], ["silrep", skey], [gk_],
                             start=(kt == 0), stop=(kt == KT - 1))
                    P.tt("dve", gtrep[:, which, q4 * 256:(q4 + 1) * 256], gp[:, 0:256], adabgt[:, q4 * 256:(q4 + 1) * 256],
                         ALU.add, [gk_, "adabgt"], ["gtrep"])
        for which in range(2):
            P.stt("dve", gsh[:, which, 0, :], modT[:, 8:16, which], 1.0, normwT[:], ALU.add, ALU.mult,
                  ["modT", "normwT"], ["gsh"])
            P.cp("dve", gsh[:, which, 1, :], modT[:, 0:8, which], ["modT"], ["gsh"])
        if debug:
            P.dma(MODDBG[:, 0:32], gsh[:].rearrange("p a b c -> p (a b c)"), reads=["gsh"])
        wout_v = W["w_out"].rearrange("(kt p) j -> p kt j", p=128)
        for q4 in range(4):
            sbuf, skey = stg.next()
            P.dma(sbuf[:], wout_v[:, :, q4 * 256:(q4 + 1) * 256], writes=[skey])
            for which in range(2 if need_ctx else 1):
                for kt in range(KT):
                    eng = "dve" if kt % 2 == 0 else "pool"
                    P.tt(eng, woutb[:, which, kt, q4 * 256:(q4 + 1) * 256], sbuf[:, kt, :],
                         gtrep[:, which, q4 * 256:(q4 + 1) * 256], ALU.mult, [skey, "gtrep"], ["woutb"])
        for hp in range(4):
            sbuf, skey = stg.next()
            rv = sbuf[:].rearrange("p k c -> p (k c)")
            P.dma(rv[:, 0:960], W["rpbt"][hp, :, :], writes=[skey])
            P.act(ttab[:, hp, :], rv[:, 0:960], AF.Exp, [skey], ["ttab"])
        if stop_after == (l, "prep"):
            break

        xt_r = Ring(P, "xt", 2, [128, D]); xh_r = Ring(P, "xh", 2, [128, D]); sq_scr = P.sb("sq_scr", [128, D])
        ss_r = Ring(P, "ss", 4, [128, 2])
        hT_r = Ring(P, "hT", 2, [128, KT, 512], BF16)
        fmo_b = Ring(P, "fmo_b", 4, [128, 512], BF16); fmo_f = Ring(P, "fmo_f", 2, [128, 512])
        tmo = Ring(P, "tmo", 2, [128, 1024], BF16)
        supers = [(0, 2)] + [(2 + 4 * i, 4) for i in range(8)]
        if DBG["skip_p1"] and l == 0:
            supers = []
        def prologue(t0, ntl):
            is_ctx = (t0 == 0)
            which = 1 if is_ctx else 0
            ntok = ntl * 128
            hT, hkey = hT_r.next()
            for ti in range(ntl):
                tau = t0 + ti
                xt, xkey = xt_r.next()
                src = c_src[tau * 128:(tau + 1) * 128, :] if is_ctx else x_src[(tau - 2) * 128:(tau - 1) * 128, :]
                P.dma(xt[:], src, writes=[xkey])
                ss, sskey = ss_r.next()
                P.memset("dve", ss[:], 0.0, [sskey])
                P.act(sq_scr[:], xt[:], AF.Square, [xkey, sskey], ["sq_scr", sskey], accum_out=ss[:, 0:1])
                P.act(ss[:, 1:2], ss[:, 0:1], AF.Ln, [sskey, "eps"], [sskey], bias=eps_t[:], scale=1.0 / D)
                P.act(ss[:, 1:2], ss[:, 1:2], AF.Exp, [sskey], [sskey], scale=-0.5)
                xh, xhkey = xh_r.next()
                P.ts("dve", xh[:], xt[:], ss[:, 1:2], None, ALU.mult, None, [xkey, sskey], [xhkey])
                for half in range(2):
                    tp, tpkey = tr_ps.next()
                    for q4 in range(4):
                        kt = half * 4 + q4
                        P.tr(tp[:, q4 * 128:(q4 + 1) * 128], xh[:, kt * 128:(kt + 1) * 128], identf[:], [xhkey, "identf"], [tpkey])
                    for q4 in range(4):
                        kt = half * 4 + q4
                        P.act(hT[:, kt, ti * 128:(ti + 1) * 128], tp[:, q4 * 128:(q4 + 1) * 128], AF.Identity,
                              [tpkey, "gsh"], [hkey], bias=gsh[:, which, 1, kt:kt + 1], scale=gsh[:, which, 0, kt:kt + 1])
            return hT, hkey

        def mainbody(t0, ntl, hT, hkey):
            ntok = ntl * 128
            for ft in range(N_FMB + N_FMF):
                fp, fpkey = fm_ps.next()
                for kt in range(KT):
                    P.mm(fp[:, 0:ntok], winb[:, kt, ft * 128:(ft + 1) * 128], hT[:, kt, 0:ntok], ["winb", hkey], [fpkey],
                         start=(kt == 0), stop=(kt == KT - 1))
                if ft < N_FMB:
                    ob, obkey = fmo_b.next()
                    kind, sc_ = FMB_EVAC[ft]
                    if kind == "silu":
                        P.act(ob[:, 0:ntok], fp[:, 0:ntok], AF.Silu, [fpkey], [obkey])
                    else:
                        P.act(ob[:, 0:ntok], fp[:, 0:ntok], AF.Copy, [fpkey], [obkey], scale=float(sc_))
                    P.dma(FMB[ft, :, t0 * 128:t0 * 128 + ntok], ob[:, 0:ntok], reads=[obkey], writes=[("FMB", ft, t0)], eng="pool")
                else:
                    ob, obkey = fmo_f.next()
                    P.cp("dve", ob[:, 0:ntok], fp[:, 0:ntok], [fpkey], [obkey])
                    P.dma(FMF[ft - N_FMB, :, t0 * 128:t0 * 128 + ntok], ob[:, 0:ntok], reads=[obkey],
                          writes=[("FMF", ft - N_FMB, t0)], eng="pool")
            cbase = (N_FMB + N_FMF) * 128
            for ti in range(ntl):
                tau = t0 + ti
                ob, obkey = tmo.next()
                for half in range(2):
                    tp2, tp2key = tm_ps.next()
                    for kt in range(KT):
                        P.mm(tp2[:], hT[:, kt, ti * 128:(ti + 1) * 128], winb[:, kt, cbase + half * 512:cbase + (half + 1) * 512],
                             [hkey, "winb"], [tp2key], start=(kt == 0), stop=(kt == KT - 1))
                    P.cp("dve", ob[:, half * 512:(half + 1) * 512], tp2[:], [tp2key], [obkey])
                P.dma(TM[tau * 128:(tau + 1) * 128, :], ob[:], reads=[obkey], writes=[("TM", tau)], eng="pool")

        cur = prologue(*supers[0]) if supers else None
        for i_, (t0, ntl) in enumerate(supers):
            nxt = prologue(*supers[i_ + 1]) if i_ + 1 < len(supers) else None
            mainbody(t0, ntl, *cur)
            cur = nxt
        P.pop()
        if stop_after == (l, "p1"):
            break

        P.push()
        build_p2(P, l, need_ctx, FMB, FMF, TM, OB, OT, dict(
            identb=identb, onesblk=onesblk, wgk=wgk, nbgk=nbgk, glanw=glanw, hgnw=hgnw, lbt=lbt, omlb=omlb, eps_t=eps_t,
            scanmask_d=scanmask_d, trimask_d=trimask_d, cm_d=cm_d, ropecos_d=ropecos_d, ropesin_d=ropesin_d))
        P.pop()
        if stop_after == (l, "p2"):
            break

        P.push()
        build_p3(P, l, need_ctx, FMB, TM, OT, ttab, ones64)
        P.pop()
        if stop_after == (l, "p3"):
            break

        P.push()
        ot_r = Ring(P, "ot", 2, [128, 8, 128], BF16); xr = Ring(P, "x4", 2, [128, D]); xo = Ring(P, "xo", 2, [128, D])
        y_ps = Ring(P, "y_ps", 4, [128, 512], psum=True)
        ss_r = Ring(P, "ss4", 4, [128, 2]); sq_scr = P.sb("sq4", [128, D]); fnw = P.sb("fnw", [128, D])
        if last:
            P.dma(fnw[:], fnw_d.partition_broadcast(128), writes=["fnw"])
        otv = OT.rearrange("f p t -> p f t")
        for tau in (range(NT) if need_ctx else range(2, NT)):
            is_ctx = tau < 2
            which = 1 if is_ctx else 0
            ot, otkey = ot_r.next()
            P.dma(ot[:], otv[:, :, tau * 128:(tau + 1) * 128], reads=[("OT", f, tau) for f in range(8)], writes=[otkey])
            xt, xkey = xr.next()
            rows = slice(tau * 128, (tau + 1) * 128) if is_ctx else slice((tau - 2) * 128, (tau - 1) * 128)
            src = c_src[rows, :] if is_ctx else x_src[rows, :]
            P.dma(xt[:], src, reads=[("XR", l, tau)], writes=[xkey])
            xn, xnkey = xo.next()
            for half in range(2):
                yp, ypkey = y_ps.next()
                for f in range(8):
                    P.mm(yp[:], ot[:, f, :], woutb[:, which, f, half * 512:(half + 1) * 512], [otkey, "woutb"], [ypkey],
                         start=(f == 0), stop=(f == 7))
                P.tt("dve", xn[:, half * 512:(half + 1) * 512], yp[:], xt[:, half * 512:(half + 1) * 512], ALU.add,
                     [ypkey, xkey], [xnkey])
            if not last:
                dst = XC[rows, :] if is_ctx else XN[rows, :]
                P.dma(dst, xn[:], reads=[xnkey], writes=[("XR", l + 1, tau)], eng="pool")
            else:
                ss, sskey = ss_r.next()
                P.memset("dve", ss[:], 0.0, [sskey])
                P.act(sq_scr[:], xn[:], AF.Square, [xnkey, sskey], ["sq4", sskey], accum_out=ss[:, 0:1])
                P.act(ss[:, 1:2], ss[:, 0:1], AF.Ln, [sskey, "eps"], [sskey], bias=eps_t[:], scale=1.0 / D)
                P.act(ss[:, 1:2], ss[:, 1:2], AF.Exp, [sskey], [sskey], scale=-0.5)
                P.stt("dve", xn[:], xn[:], ss[:, 1:2], fnw[:], ALU.mult, ALU.mult, [xnkey, sskey, "fnw"], [xnkey])
                last_out.append(P.dma(out_d[rows, :], xn[:], reads=[xnkey], eng="pool"))
        P.pop()
        P.pop()
        if stop_after == (l, "p4"):
            break

    finals = list(last_out) + [d for d in P.dma_last.values()]
    return P.finish(final_waits=finals)


def _interleave(gens):
    gens = list(gens)
    while gens:
        for gen in list(gens):
            try:
                next(gen)
            except StopIteration:
                gens.remove(gen)


VEXP_ENG = os.environ.get("K_VEXP_ENG", "dve")


def build_p2(P, l, need_ctx, FMB, FMF, TM, OB, OT, G):
    identb = G["identb"]; onesblk = G["onesblk"]; wgk = G["wgk"]; nbgk = G["nbgk"]
    lbt = G["lbt"]; omlb = G["omlb"]; eps_t = G["eps_t"]
    scanmask = P.sb("scanmask", [128, 3, 128]); trimask = P.sb("trimask", [128, 6, 128]); cm = P.sb("cm", [128, 128 // CHG], BF16)
    ropecos = P.sb("ropecos", [128, T]); ropesin = P.sb("ropesin", [128, T])
    P.dma(scanmask[:], G["scanmask_d"].rearrange("n p t -> p n t"), writes=["scanmask"])
    P.dma(trimask[:], G["trimask_d"].rearrange("n p t -> p n t"), writes=["trimask"])
    P.dma(cm[:], G["cm_d"], writes=["cm"])
    P.dma(ropecos[:], G["ropecos_d"], writes=["ropecos"]); P.dma(ropesin[:], G["ropesin_d"], writes=["ropesin"])
    NHC = 128 // CHG
    NSL = [3, 3, 2 * NHC + 1, 2 * NHC + 1]
    S = [P.sb(f"S{g}", [128, NSL[g], 64]) for g in range(4)]
    RD = 6
    qk_r = Ring(P, "qk", 3, [128, 8, 128], BF16); gk_r = Ring(P, "gkr", 3, [32, 128]); v2_r = Ring(P, "vr", 8, [128, 256], BF16)
    qh_r = Ring(P, "qh", 3, [128, 2, 128], BF16); ff_r = Ring(P, "ff", 3, [128, 2, 128]); gate_r = Ring(P, "gate", 4, [128, 256], BF16)
    ob_r = Ring(P, "obr", 4, [128, 256])
    w2 = [Ring(P, f"w{i}", 3, [128, 2, 128]) for i in range(9)]
    qin_r = Ring(P, "qin", 7, [128, 2, 128])
    sm_r = Ring(P, "sm", 8, [128, 4, 8])
    kt_r = Ring(P, "ktr", 6, [128, 2, 128], BF16); kout_r = Ring(P, "kout", 6, [128, 2, 128], BF16)
    qbd_r = Ring(P, "qbd", 6, [128, 2, 2, 128], BF16)
    koT_r = Ring(P, "koT", 6, [128, 256], BF16); pt_r = Ring(P, "pt", 7, [128, 4, 128], BF16)
    vexp_r = Ring(P, "vexp", 4, [128, 4, NHC, 64], BF16)
    o_r = Ring(P, "osb", 6, [128, 256]); sq_r = Ring(P, "osq", 4, [128, 256], BF16); og_r = Ring(P, "og", 4, [128, 256], BF16)
    scanmask2 = P.sb("scanmask2", [128, 3, 256])
    for k_ in range(3):
        P.cp("pool", scanmask2[:, k_, :].rearrange("p (a t) -> p a t", a=2), scanmask[:, k_, :].unsqueeze(1).to_broadcast([128, 2, 128]),
             ["scanmask"], ["scanmask2"])
    z_ps = Ring(P, "z_ps", 1, [128, 512], psum=True); tp_ps = Ring(P, "tp_ps", 1, [128, 1024], BF16, psum=True)
    sc_ps = Ring(P, "sc_ps", 2, [128, 512], psum=True); kv_ps = Ring(P, "kv_ps", 2, [128, 512], psum=True)
    o_ps = Ring(P, "o_ps", 2, [128, 512], psum=True)
    for i, b in enumerate(qbd_r.bufs):
        P.memset("pool", b[:], 0.0, [f"qbd{i}"])
    base = [0, 0, 0, 0]

    def super_front(dirn, tau, gla, ctxs):
        bwd = dirn == 1
        is_ctx = tau < 2
        cs = slice(tau * 128, (tau + 1) * 128)
        C = 128 if gla else CHG
        nch = 128 // C
        mi = 1 if gla else 2
        kappa = -1.0 / 16.0 if gla else 1.0
        g0 = 0 if gla else 2
        c3 = lambda ap: ap.rearrange("p (n c) -> p n c", c=C)
        fl = lambda t: t[:].rearrange("p a b -> p (a b)")
        b0s = []
        for gg in range(2):
            g = g0 + gg
            b0s.append(base[g])
            base[g] = (base[g] + nch) % NSL[g]
        vt2, vkey = v2_r.next()
        voff = 512 if gla else 768
        P.dma(vt2[:], TM[cs, voff:voff + 256], reads=[("TM", tau)], writes=[vkey])
        lf, lfkey = w2[0].next()
        if gla:
            qk, qkkey = qk_r.next()
            P.dma(qk[:], FMB[12:20, :, cs].rearrange("n p t -> p n t"), reads=[("FMB", ft, _st(tau)) for ft in range(12, 20)],
                  writes=[qkkey])
            gkt, gkkey = gk_r.next()
            P.dma(gkt[:], FMF[4, 0:32, cs], reads=[("FMF", 4, _st(tau))], writes=[gkkey])
            yield
            zp, zkey = z_ps.next()
            e_, ekey = w2[1].next()
            for gg in range(2):
                P.mm(zp[:, gg * 128:(gg + 1) * 128], wgk[:, dirn, gg, :], gkt[:], ["wgk", gkkey], [zkey])
            for gg in range(2):
                P.act(e_[:, gg, :], zp[:, gg * 128:(gg + 1) * 128], AF.Exp, [zkey, "nbgk"], [ekey],
                      bias=nbgk[:, dirn * 2 + gg:dirn * 2 + gg + 1], scale=-1.0)
                yield
            P.act(fl(lf), fl(e_), AF.Ln, [ekey], [lfkey], bias=1.0)
            yield
            kk = kkkey = None
        else:
            qh, qhkey = qh_r.next()
            P.dma(qh[:], FMB[22:24, :, cs].rearrange("n p t -> p n t"), reads=[("FMB", 22, _st(tau)), ("FMB", 23, _st(tau))],
                  writes=[qhkey])
            fr, frkey = ff_r.next()
            fft = 2 if bwd else 0
            P.dma(fr[:], FMF[fft:fft + 2, :, cs].rearrange("n p t -> p n t"),
                  reads=[("FMF", fft, _st(tau)), ("FMF", fft + 1, _st(tau))], writes=[frkey])
            yield
            e_, ekey = w2[1].next()
            P.act(fl(e_), fl(fr), AF.Exp, [frkey], [ekey], scale=-1.0)
            yield
            P.act(fl(e_), fl(e_), AF.Ln, [ekey], [ekey], bias=1.0)
            yield
            P.act(fl(e_), fl(e_), AF.Exp, [ekey], [ekey], scale=-1.0)
            yield
            if l == 0:
                fgate, fkey = e_, ekey
            else:
                fgate, fkey = w2[2].next()
                for gg in range(2):
                    col = l * 4 + dirn * 2 + gg
                    P.ts("dve", fgate[:, gg, :], e_[:, gg, :], omlb[:, col:col + 1], lbt[:, col:col + 1], ALU.mult, ALU.add,
                         [ekey, "omlb", "lbt"], [fkey])
                yield
            P.act(fl(lf), fl(fgate), AF.Ln, [fkey], [lfkey])
            kk, kkkey = w2[3].next()
            P.act(fl(kk), fl(fgate), AF.Identity, [fkey], [kkkey], bias=1.0, scale=-1.0)
            yield
        Gc, Gkey = w2[4].next()
        m2 = scanmask2[:, mi, :]
        if bwd:
            P.op("dve", _scan(fl(Gc)[:, ::-1], m2, fl(lf)[:, ::-1]), [lfkey, "scanmask2"], [Gkey])
        else:
            P.op("dve", _scan(fl(Gc), m2, fl(lf)), [lfkey, "scanmask2"], [Gkey])
        yield
        Gv = c3(fl(Gc))
        n2 = 2 * nch
        mid = (C // 2) if bwd else (C // 2 - 1)
        lastp = 0 if bwd else C - 1
        sm, smkey = sm_r.next()
        Gm, Gmkey = w2[5].next()
        P.tt("dve", c3(fl(Gm)), Gv, Gv[:, :, mid:mid + 1].to_broadcast([128, n2, C]), ALU.subtract, [Gkey], [Gmkey])
        yield
        Gl, Glkey = w2[0].next()
        P.tt("dve", c3(fl(Gl)), Gv, Gv[:, :, lastp:lastp + 1].to_broadcast([128, n2, C]), ALU.subtract, [Gkey], [Glkey])
        P.act(sm[:, 2, 0:n2], Gv[:, :, lastp], AF.Exp, [Gkey], [smkey], scale=kappa)
        yield
        A_, Akey = w2[6].next(); B_, Bkey = w2[7].next()
        P.act(fl(B_), fl(Gm), AF.Exp, [Gmkey], [Bkey], scale=-kappa)
        yield
        P.act(fl(A_), fl(Gm), AF.Exp, [Gmkey], [Akey], scale=kappa)
        yield
        Eo, Eokey = w2[1].next()
        P.act(fl(Eo), fl(Gl), AF.Exp, [Glkey], [Eokey], scale=-kappa)
        yield
        Ei, Eikey = w2[4].next()
        P.act(fl(Ei), fl(Gc), AF.Exp, [Gkey], [Eikey], scale=kappa)
        yield
        Qf, Qfkey = w2[8].next()
        Kt, Ktkey = kt_r.next(); Kout, Koutkey = kout_r.next()
        if gla and not is_ctx:
            ts_ = slice((tau - 2) * 128, (tau - 1) * 128)
            cosb = ropecos[:, ts_].unsqueeze(1).to_broadcast([128, 2, 128])
            sinb = ropesin[:, ts_].unsqueeze(1).to_broadcast([128, 2, 128])
            r1, r1key = w2[2].next(); r2, r2key = w2[3].next()
            P.tt("dve", r1[:], qk[:, 0:2, :], cosb, ALU.mult, [qkkey, "ropecos"], [r1key])
            P.tt("pool", r2[:], qk[:, 2:4, :], sinb, ALU.mult, [qkkey, "ropesin"], [r2key])
            yield
            P.tt("dve", r1[:], r1[:], r2[:], ALU.add, [r1key, r2key], [r1key])
            r3, r3key = w2[2].next(); r4, r4key = w2[3].next()
            P.tt("pool", r3[:], qk[:, 4:6, :], cosb, ALU.mult, [qkkey, "ropecos"], [r3key])
            yield
            P.tt("dve", Qf[:], r1[:], A_[:], ALU.mult, [r1key, Akey], [Qfkey])
            P.tt("pool", r4[:], qk[:, 6:8, :], sinb, ALU.mult, [qkkey, "ropesin"], [r4key])
            yield
            P.tt("pool", r3[:], r3[:], r4[:], ALU.add, [r3key, r4key], [r3key])
            yield
            qsrc, qsrckey, ksrc, ksrckey = r1[:], r1key, r3[:], r3key
        elif gla:
            qsrc, qsrckey, ksrc, ksrckey = qk[:, 0:2, :], qkkey, qk[:, 4:6, :], qkkey
            P.tt("dve", Qf[:], qsrc, A_[:], ALU.mult, [qsrckey, Akey], [Qfkey])
        else:
            qsrc, qsrckey, ksrc, ksrckey = qh[:], qhkey, kk[:], kkkey
            P.tt("dve", Qf[:], qsrc, A_[:], ALU.mult, [qsrckey, Akey], [Qfkey])
        P.tt("pool", Kt[:], ksrc, B_[:], ALU.mult, [ksrckey, Bkey], [Ktkey])
        yield
        P.tt("pool", Kout[:], ksrc, Eo[:], ALU.mult, [ksrckey, Eokey], [Koutkey])
        yield
        Qbd, Qbdkey = qbd_r.next()
        P.cp("act", Qbd[0:64, :, 0, :], Qf[0:64, :, :], [Qfkey], [Qbdkey])
        yield
        P.cp("act", Qbd[64:128, :, 1, :], Qf[64:128, :, :], [Qfkey], [Qbdkey])
        Qin, Qinkey = qin_r.next()
        P.tt("dve" if gla else "pool", Qin[:], qsrc, Ei[:], ALU.mult, [qsrckey, Eikey], [Qinkey])
        yield
        tpp, tpkey = tp_ps.next()
        for gg in range(2):
            P.tr(tpp[:, gg * 128:(gg + 1) * 128], Kout[:, gg, :], identb[:], [Koutkey, "identb"], [tpkey])
        koT, koTkey = koT_r.next()
        P.cp("act", koT[:], tpp[:, 0:256], [tpkey], [koTkey])
        yield
        scp, sckey = sc_ps.next()
        for gg in range(2):
            P.mm(scp[:, gg * 256:(gg + 1) * 256], Kt[:, gg, :], Qbd[:, gg, :, :].rearrange("p h t -> p (h t)"),
                 [Ktkey, Qbdkey], [sckey])
        PT, PTkey = pt_r.next()
        tmi = (2 if gla else 4) + (1 if bwd else 0)
        P.tt("dve", PT[:], scp[:, 0:512].rearrange("p (a t) -> p a t", a=4),
             trimask[:, tmi, :].unsqueeze(1).to_broadcast([128, 4, 128]), ALU.mult, [sckey, "trimask"], [PTkey])
        yield
        vx = vxkey = None
        if not gla:
            vx, vxkey = vexp_r.next()
            P.tt(VEXP_ENG, vx[:], vt2[:].rearrange("p (a d) -> p a d", a=4).unsqueeze(2).to_broadcast([128, 4, NHC, 64]),
                 cm[:].unsqueeze(1).unsqueeze(3).to_broadcast([128, 4, NHC, 64]), ALU.mult, [vkey, "cm"], [vxkey])
            yield
        for gg in range(2):
            g = g0 + gg
            ctxs[gg].update(dict(vt=vt2[:, gg * 128:(gg + 1) * 128], vkey=vkey, koT=koT[:, gg * 128:(gg + 1) * 128], koTkey=koTkey,
                                 vx=(vx[:, gg * 2:(gg + 1) * 2, :, :] if vx is not None else None), vxkey=vxkey,
                                 PT=PT[:, gg * 2:(gg + 1) * 2, :], PTkey=PTkey, Qin=Qin[:, gg, :], Qinkey=Qinkey,
                                 sm=sm[:, :, gg * nch:(gg + 1) * nch], smkey=smkey, b0=b0s[gg], nsl=NSL[g], C=C, nch=nch,
                                 gla=gla, gi=gg, bwd=bwd, is_ctx=is_ctx, cs=cs, g=g, tau=tau))

    def super_mid(ctxs):
        c0 = ctxs[0]
        gla = c0["gla"]; C = c0["C"]; nch = c0["nch"]; bwd = c0["bwd"]
        kvp, kvkey = kv_ps.next()
        W_ = 64 if gla else NHC * 64
        for c_ in ctxs:
            gg = c_["gi"]
            for h in range(2):
                if gla:
                    P.mm(kvp[h * 64:(h + 1) * 64, gg * W_:(gg + 1) * W_], c_["koT"][:, h * 64:(h + 1) * 64],
                         c_["vt"][:, h * 64:(h + 1) * 64], [c_["koTkey"], c_["vkey"]], [kvkey], tp=(0, h * 64))
                else:
                    P.mm(kvp[h * 64:(h + 1) * 64, gg * W_:(gg + 1) * W_], c_["koT"][:, h * 64:(h + 1) * 64],
                         c_["vx"][:, h, :, :].rearrange("p j d -> p (j d)"), [c_["koTkey"], c_["vxkey"]], [kvkey], tp=(0, h * 64))
            yield
        jorder = list(range(nch - 1, -1, -1)) if bwd else list(range(nch))
        for jj, j in enumerate(jorder):
            for c_ in ctxs:
                g = c_["g"]; gg = c_["gi"]; nsl = c_["nsl"]
                s_in = (c_["b0"] + jj) % nsl
                s_out = (c_["b0"] + jj + 1) % nsl
                P.stt("dve", S[g][:, s_out, :], S[g][:, s_in, :], c_["sm"][:, 2, j:j + 1],
                      kvp[:, gg * W_ + j * 64:gg * W_ + (j + 1) * 64], ALU.mult, ALU.add,
                      [f"S{g}_{s_in}", c_["smkey"], kvkey], [f"S{g}_{s_out}"])
            yield

    def super_out(ctxs):
        c0 = ctxs[0]
        gla = c0["gla"]; C = c0["C"]; nch = c0["nch"]; bwd = c0["bwd"]; is_ctx = c0["is_ctx"]; cs = c0["cs"]; tau = c0["tau"]
        g0 = c0["g"]
        op_, okey = o_ps.next()
        for c_ in ctxs:
            gg = c_["gi"]
            for h in range(2):
                P.mm(op_[h * 64:(h + 1) * 64, gg * 128:(gg + 1) * 128], c_["vt"][:, h * 64:(h + 1) * 64], c_["PT"][:, h, :],
                     [c_["vkey"], c_["PTkey"]], [okey], start=(gg == 0), stop=False, tp=(0, h * 64))
            yield
        jorder = list(range(nch - 1, -1, -1)) if bwd else list(range(nch))
        for jj, j in enumerate(jorder):
            for c_ in ctxs:
                g = c_["g"]; gg = c_["gi"]
                s_in = (c_["b0"] + jj) % c_["nsl"]
                for h in range(2):
                    P.mm(op_[h * 64:(h + 1) * 64, gg * 128 + j * C:gg * 128 + (j + 1) * C], S[g][h * 64:(h + 1) * 64, s_in, :],
                         c_["Qin"][h * 64:(h + 1) * 64, j * C:(j + 1) * C], [f"S{g}_{s_in}", c_["Qinkey"]], [okey],
                         start=False, stop=True, tp=(h * 64, h * 64))
            yield
        g2 = lambda d3: d3.rearrange("n p t -> p n t")
        if bwd:
            osb, oskey = o_r.next()
            P.cp("act", osb[:], op_[:, 0:256], [okey], [oskey])
            P.dma(g2(OB[g0:g0 + 2, :, cs]), osb[:].rearrange("p (n t) -> p n t", n=2), reads=[oskey],
                  writes=[("OB", g0, tau), ("OB", g0 + 1, tau)], eng="pool")
            yield
        elif not (is_ctx and not need_ctx):
            obt, obtkey = ob_r.next()
            P.dma(obt[:].rearrange("p (n t) -> p n t", n=2), g2(OB[g0:g0 + 2, :, cs]),
                  reads=[("OB", g0, tau), ("OB", g0 + 1, tau)], writes=[obtkey])
            gt_, gtkey = gate_r.next()
            gft = 20 if gla else 24
            P.dma(gt_[:].rearrange("p (n t) -> p n t", n=2), g2(FMB[gft:gft + 2, :, cs]),
                  reads=[("FMB", gft, _st(tau)), ("FMB", gft + 1, _st(tau))], writes=[gtkey])
            osb, oskey = o_r.next()
            P.tt("dve", osb[:], op_[:, 0:256], obt[:], ALU.add, [okey, obtkey], [oskey])
            yield
            osq, osqkey = sq_r.next()
            P.act(osq[:], osb[:], AF.Square, [oskey], [osqkey])
            yield
            zp, zkey = z_ps.next()
            P.mm(zp[:, 0:256], onesblk[:], osq[:], ["onesblk", osqkey], [zkey])
            rs, rskey = o_r.next()
            P.act(rs[:], zp[:, 0:256], AF.Ln, [zkey, "eps"], [rskey], bias=eps_t[:])
            yield
            P.act(rs[:], rs[:], AF.Exp, [rskey], [rskey], scale=-0.5)
            yield
            nw = G["glanw"] if gla else G["hgnw"]
            P.stt("dve", osb[:], osb[:], nw[:], rs[:], ALU.mult, ALU.mult, [oskey, rskey, "glanw", "hgnw"], [oskey])
            yield
            og, ogkey = og_r.next()
            P.tt("pool", og[:], osb[:], gt_[:], ALU.mult, [oskey, gtkey], [ogkey])
            P.dma(g2(OT[4 + g0:6 + g0, :, cs]), og[:].rearrange("p (n t) -> p n t", n=2), reads=[ogkey],
                  writes=[("OT", 4 + g0, tau), ("OT", 5 + g0, tau)], eng="pool")
            yield

    for dirn in DBG["p2_dirs"]:
        bwd = dirn == 1
        order = [1, 0] + list(range(NT - 1, 1, -1)) if bwd else list(range(NT))
        for g in range(4):
            base[g] = 0
            P.memset("dve", S[g][:, 0, :], 0.0, [f"S{g}_0"])
        if DBG["p2_tiles"]:
            order = order[:DBG["p2_tiles"]]
        mids = []
        outs = []
        mid_ctxs = []
        for tau in order:
            cts = [[dict(), dict()], [dict(), dict()]]
            fronts = [super_front(dirn, tau, False, cts[0]), super_front(dirn, tau, True, cts[1])]
            _interleave(outs + mids + fronts)
            outs = [super_out(c2) for c2 in mid_ctxs]
            mids = [super_mid(c2) for c2 in cts]
            mid_ctxs = cts
        _interleave(outs + mids)
        _interleave([super_out(c2) for c2 in mid_ctxs])


def _st(tau):
    return 0 if tau < 2 else 2 + 4 * ((tau - 2) // 4)


def _scan(out, d0, d1):
    return lambda e: e.tensor_tensor_scan(out, d0, d1, 0.0, ALU.mult, ALU.add)


def build_p3(P, l, need_ctx, FMB, TM, OT, ttab, ones64):
    KTb = Ring(P, "KTb", 2, [128, NTOK], BF16)
    Vd = Ring(P, "Vd", 2, [128, 68, 64], BF16)
    q_r = Ring(P, "q3", 2, [128, 512], BF16); g_r = Ring(P, "g3", 2, [128, 512], BF16)
    pe_r = Ring(P, "pe3", 4, [128, 512], BF16); p_r = Ring(P, "p3", 4, [128, 512], BF16)
    rc_r = Ring(P, "rc3", 2, [128, 512]); o_r = Ring(P, "o3", 2, [128, 512]); og_r = Ring(P, "og3", 2, [128, 512], BF16)
    s_ps = Ring(P, "s_ps", 3, [128, 512], psum=True)
    ao_ps = Ring(P, "ao_ps", 2, [128, 512], psum=True); as_ps = Ring(P, "as_ps", 2, [128, 512], psum=True)
    TMv = TM.rearrange("(r k) c -> k r c", k=64)

    def row_interval(rp):
        if rp <= 7:
            return 0, rp + 4
        if rp >= 56:
            return rp - 3, 63
        return rp - 3, rp + 4

    for hp in range(4):
        kt_, ktkey = KTb.next()
        P.dma(kt_[:], FMB[4 + hp, :, :], reads=[("FMB", 4 + hp, s) for s in [0] + [2 + 4 * i for i in range(8)]], writes=[ktkey])
        vd, vdkey = Vd.next()
        for hh in range(2):
            for part in range(4):
                r0, r1 = part * 17, (part + 1) * 17
                P.dma(vd[hh * 64:(hh + 1) * 64, r0:r1, :], TMv[:, r0:r1, hp * 128 + hh * 64:hp * 128 + (hh + 1) * 64],
                      reads=[("TM", t_) for t_ in range(NT)], writes=[vdkey])
        blocks = [("lat", b) for b in range(8)] + ([("ctx", 0)] if need_ctx else [])
        for kind, b in blocks:
            if kind == "lat":
                tok0 = CTX + b * 512
                nq = 512
                pieces = [(rho, 0, 512, None) for rho in range(4)]
                for rp in range(64):
                    lo, hi = row_interval(rp)
                    a = max(lo, 8 * b); e_ = min(hi, 8 * b + 7)
                    if a > e_:
                        continue
                    u0 = 7 - rp + a
                    pieces.append((4 + rp, (a - 8 * b) * 64, (e_ - 8 * b + 1) * 64, u0))
            else:
                tok0 = 0
                nq = 256
                pieces = [(rho, 0, 256, None) for rho in range(4)]
            qt, qkey = q_r.next(); gt_, gkey = g_r.next()
            st_reads = sorted(set(_st(t_) for t_ in range(tok0 // 128, (tok0 + nq) // 128)))
            P.dma(qt[:, 0:nq], FMB[hp, :, tok0:tok0 + nq], reads=[("FMB", hp, s) for s in st_reads], writes=[qkey])
            P.dma(gt_[:, 0:nq], FMB[8 + hp, :, tok0:tok0 + nq], reads=[("FMB", 8 + hp, s) for s in st_reads], writes=[gkey])
            ao, aokey = ao_ps.next(); as_, askey = as_ps.next()
            staged = {}

            def stage_a(pi):
                rho, c0, c1, u0 = pieces[pi]
                sp_, spkey = s_ps.next()
                for hh in range(2):
                    P.mm(sp_[hh * 64:(hh + 1) * 64, c0:c1], kt_[hh * 64:(hh + 1) * 64, rho * 64:(rho + 1) * 64],
                         qt[hh * 64:(hh + 1) * 64, c0:c1], [ktkey, qkey], [spkey], tp=(hh * 64, hh * 64))
                pe_, pekey = pe_r.next()
                P.act(pe_[:, c0:c1], sp_[:, c0:c1], AF.Exp, [spkey], [pekey])
                if u0 is not None:
                    pp, ppkey = p_r.next()
                    nr = (c1 - c0) // 64
                    P.tt("dve", pp[:, c0:c1], pe_[:, c0:c1], ttab[:, hp, u0 * 64:(u0 + nr) * 64], ALU.mult,
                         [pekey, "ttab"], [ppkey])
                else:
                    pp, ppkey = pe_, pekey
                staged[pi] = (pp, ppkey)

            def stage_b(pi):
                rho, c0, c1, u0 = pieces[pi]
                pp, ppkey = staged.pop(pi)
                for hh in range(2):
                    P.mm(ao[hh * 64:(hh + 1) * 64, c0:c1], vd[hh * 64:(hh + 1) * 64, rho, :], pp[hh * 64:(hh + 1) * 64, c0:c1],
                         [vdkey, ppkey], [aokey], start=(pi == 0), stop=(pi == len(pieces) - 1), tp=(hh * 64, hh * 64))
                    P.mm(as_[hh * 64:(hh + 1) * 64, c0:c1], ones64[hh * 64:(hh + 1) * 64, :], pp[hh * 64:(hh + 1) * 64, c0:c1],
                         ["ones64", ppkey], [askey], start=(pi == 0), stop=(pi == len(pieces) - 1), tp=(hh * 64, hh * 64))

            LA = 2
            for pi in range(min(LA, len(pieces))):
                stage_a(pi)
            for pi in range(len(pieces)):
                if pi + LA < len(pieces):
                    stage_a(pi + LA)
                stage_b(pi)
            rc, rckey = rc_r.next()
            P.recip(rc[:, 0:nq], as_[:, 0:nq], [askey], [rckey])
            ob, obkey = o_r.next()
            P.tt("dve", ob[:, 0:nq], ao[:, 0:nq], rc[:, 0:nq], ALU.mult, [aokey, rckey], [obkey])
            og, ogkey = og_r.next()
            P.tt("pool", og[:, 0:nq], ob[:, 0:nq], gt_[:, 0:nq], ALU.mult, [obkey, gkey], [ogkey])
            P.dma(OT[hp, :, tok0:tok0 + nq], og[:, 0:nq], reads=[ogkey],
                  writes=[("OT", hp, t_) for t_ in range(tok0 // 128, (tok0 + nq) // 128)], eng="pool")


_CACHE = {}


def kernel(**inputs):
    inputs = {k: np.asarray(v) for k, v in inputs.items()}
    common, per = host_layout(inputs)
    if "nc" not in _CACHE:
        _CACHE["nc"] = build_program()
    nc = _CACHE["nc"]
    in_maps = [dict(common, **p) for p in per]
    res = run_bass_kernel_spmd(nc, in_maps, core_ids=list(range(len(per))))
    out = np.stack([np.asarray(r["out"], dtype=np.float32) for r in res.results], axis=0)
    return out
```

```python
import numpy as np
import ml_dtypes
from contextlib import ExitStack
import concourse.bass as bass
import concourse.mybir as mybir
from concourse.bass_utils import run_bass_kernel_spmd

F32 = mybir.dt.float32
BF16 = mybir.dt.bfloat16
AF = mybir.ActivationFunctionType
ALU = mybir.AluOpType
NPBF = ml_dtypes.bfloat16

ENGS = ("pe", "act", "dve", "pool", "sp")
SAME_ENGINE_SYNC = {"pe": False, "act": True, "dve": True, "pool": True, "sp": False}
N_DMA_SEMS = {"sp": 20, "act": 4, "pool": 12}

D = 1024
KT = 8
T = 4096
CTX = 256
NTOK = T + CTX
NT = NTOK // 128
NEXT = 4992
N_FMB = 26
N_FMF = 5
EPS = 1e-6
CHG = 32


class Ins:
    __slots__ = ("eng", "fn", "deps", "is_dma", "dma_sem", "dma_val", "inc_idx", "needed", "pos", "guard")

    def __init__(self, eng, fn, is_dma):
        self.eng = eng
        self.fn = fn
        self.deps = []
        self.is_dma = is_dma
        self.dma_sem = None
        self.dma_val = None
        self.inc_idx = None
        self.needed = False
        self.guard = None


class Prog:
    def __init__(self):
        self.nc = bass.Bass("TRN2", target_bir_lowering=False)
        self.es = ExitStack()
        self.streams = {e: [] for e in ENGS}
        self.last_w = {}
        self.readers = {}
        self.dma_rr = {e: 0 for e in N_DMA_SEMS}
        self.dma_last = {}
        self.dma_cnt = {}
        self.n_ins = 0
        self.scopes = []

    def push(self):
        self.scopes.append(ExitStack())

    def pop(self):
        self.barrier()
        self.scopes.pop().close()

    def _ctx(self):
        return self.scopes[-1] if self.scopes else self.es

    def sb(self, name, shape, dtype=F32):
        self.uid = getattr(self, "uid", 0) + 1
        return self._ctx().enter_context(self.nc.sbuf_tensor(f"sb{self.uid}_{name}", list(shape), dtype))

    def ps(self, name, shape=(128, 512), dtype=F32):
        self.uid = getattr(self, "uid", 0) + 1
        return self._ctx().enter_context(self.nc.psum_tensor(f"ps{self.uid}_{name}", list(shape), dtype))

    def dram(self, name, shape, dtype=F32, kind="Internal"):
        return self.nc.dram_tensor(name, list(shape), dtype, kind=kind)

    def op(self, eng, fn, reads=(), writes=(), dma=False):
        ins = Ins(eng, fn, dma)
        ins.pos = self.n_ins
        self.n_ins += 1
        deps = []
        for r in reads:
            w = self.last_w.get(r)
            if w is not None:
                deps.append(w)
        for wkey in writes:
            w = self.last_w.get(wkey)
            if w is not None:
                deps.append(w)
            deps.extend(self.readers.get(wkey, {}).values())
        if dma:
            slot = self.dma_rr[eng] % N_DMA_SEMS[eng]
            self.dma_rr[eng] += 1
            key = (eng, slot)
            prev = self.dma_last.get(key)
            if prev is not None:
                ins.guard = prev
            self.dma_cnt[key] = self.dma_cnt.get(key, 0) + 1
            ins.dma_sem = key
            ins.dma_val = 16 * self.dma_cnt[key]
            self.dma_last[key] = ins
        best = {}
        for d in deps:
            if d is ins:
                continue
            k = d.dma_sem if d.is_dma else d.eng
            if k not in best or best[k].pos < d.pos:
                best[k] = d
        for d in best.values():
            ins.deps.append(d)
            d.needed = True
        if ins.guard is not None:
            ins.guard.needed = True
        mykey = ins.dma_sem if dma else eng
        for r in reads:
            self.readers.setdefault(r, {})[mykey] = ins
        for wkey in writes:
            self.last_w[wkey] = ins
            self.readers[wkey] = {}
        self.streams[eng].append(ins)
        return ins

    def barrier(self):
        lasts = []
        for e in ENGS:
            for ins in reversed(self.streams[e]):
                if ins.fn is not None and not ins.is_dma:
                    lasts.append(ins)
                    break
        lasts += list(self.dma_last.values())
        for e in ENGS:
            b = Ins(e, None, False)
            b.pos = self.n_ins
            self.n_ins += 1
            for d in lasts:
                b.deps.append(d)
                d.needed = True
            self.streams[e].append(b)
        self.last_w = {}
        self.readers = {}

    def dma(self, out, in_, reads=(), writes=(), eng="sp"):
        return self.op(eng, lambda e: e.dma_start(out=out, in_=in_), reads, writes, dma=True)

    def mm(self, out, lhsT, rhs, reads, writes, start=True, stop=True, tp=None):
        if tp is None:
            return self.op("pe", lambda e: e.matmul(out, lhsT, rhs, start=start, stop=stop), reads, writes)
        return self.op("pe", lambda e: e.matmul(out, lhsT, rhs, start=start, stop=stop, tile_position=tp,
                                                skip_group_check=True), reads, writes)

    def tr(self, out, in_, ident, reads, writes):
        return self.op("pe", lambda e: e.transpose(out, in_, ident), reads, writes)

    def act(self, out, in_, func, reads, writes, bias=None, scale=None, accum_out=None):
        kw = {}
        if bias is not None:
            kw["bias"] = bias
        if scale is not None:
            kw["scale"] = scale
        if accum_out is not None:
            kw["accum_out"] = accum_out
        return self.op("act", lambda e: e.activation(out, in_, func, **kw), reads, writes)

    def tt(self, eng, out, in0, in1, op, reads, writes):
        return self.op(eng, lambda e: e.tensor_tensor(out, in0, in1, op), reads, writes)

    def ts(self, eng, out, in0, s1, s2, op0, op1, reads, writes):
        if s2 is None:
            return self.op(eng, lambda e: e.tensor_scalar(out, in0, s1, None, op0=op0), reads, writes)
        return self.op(eng, lambda e: e.tensor_scalar(out, in0, s1, s2, op0=op0, op1=op1), reads, writes)

    def stt(self, eng, out, in0, scalar, in1, op0, op1, reads, writes):
        return self.op(eng, lambda e: e.scalar_tensor_tensor(out, in0, scalar, in1, op0=op0, op1=op1), reads, writes)

    def cp(self, eng, out, in_, reads, writes):
        if eng == "act":
            return self.op("act", lambda e: e.copy(out, in_), reads, writes)
        return self.op(eng, lambda e: e.tensor_copy(out, in_), reads, writes)

    def recip(self, out, in_, reads, writes):
        return self.op("dve", lambda e: e.reciprocal(out, in_), reads, writes)

    def memset(self, eng, ap, val, writes):
        return self.op(eng, lambda e: e.memset(ap, val), (), writes)

    def finish(self, final_waits=()):
        nc = self.nc
        es = self.es
        sems = {e: es.enter_context(nc.semaphore("s_" + e)) for e in ENGS}
        dsems = {}
        for e, n in N_DMA_SEMS.items():
            for i in range(n):
                dsems[(e, i)] = es.enter_context(nc.semaphore(f"d_{e}{i}"))
        for fw in final_waits:
            fw.needed = True
        for e in ENGS:
            c = 0
            for ins in self.streams[e]:
                if ins.is_dma or ins.fn is None:
                    continue
                if ins.needed:
                    c += 1
                    ins.inc_idx = c
        streams = self.streams

        def emit(e, eng_obj):
            known = {}

            def wait_for(d):
                if d.is_dma:
                    k = ("d",) + d.dma_sem
                    if known.get(k, 0) >= d.dma_val:
                        return
                    eng_obj.wait_ge(dsems[d.dma_sem], d.dma_val)
                    known[k] = d.dma_val
                else:
                    if d.eng == e and not SAME_ENGINE_SYNC[e]:
                        return
                    k = ("e", d.eng)
                    if known.get(k, 0) >= d.inc_idx:
                        return
                    eng_obj.wait_ge(sems[d.eng], d.inc_idx)
                    known[k] = d.inc_idx

            for ins in streams[e]:
                for d in ins.deps:
                    wait_for(d)
                if ins.guard is not None:
                    wait_for(ins.guard)
                if ins.fn is None:
                    continue
                bi = ins.fn(eng_obj)
                if ins.is_dma:
                    bi.then_inc(dsems[ins.dma_sem], 16)
                elif ins.needed:
                    bi.then_inc(sems[e], 1)
            if e == "sp":
                for fw in final_waits:
                    wait_for(fw)

        with nc.Block() as block:
            @block.tensor
            def _(eng):
                emit("pe", eng)

            @block.scalar
            def _(eng):
                emit("act", eng)

            @block.vector
            def _(eng):
                emit("dve", eng)

            @block.gpsimd
            def _(eng):
                emit("pool", eng)

            @block.sync
            def _(eng):
                emit("sp", eng)
        while self.scopes:
            self.scopes.pop().close()
        self.es.close()
        return nc


class Ring:
    def __init__(self, P, name, n, shape, dtype=F32, psum=False):
        self.name = name
        if psum:
            self.bufs = [P.ps(f"{name}{i}", shape, dtype) for i in range(n)]
        else:
            self.bufs = [P.sb(f"{name}{i}", shape, dtype) for i in range(n)]
        self.i = 0

    def next(self):
        j = self.i % len(self.bufs)
        self.i += 1
        return self.bufs[j], f"{self.name}{j}"


def ext_col_index():
    cols = []
    cols += list(range(0, 512)) + list(range(512, 1024)) + list(range(1536, 2048))

    def pad_heads(base, swap):
        out = []
        for h in range(4):
            for i in range(32):
                j = (i + 8 if (i % 16) < 8 else i - 8) if swap else i
                out.append(base + h * 32 + j)
            out += [-1] * 32
        return out
    cols += pad_heads(2048, False) + pad_heads(2048, True) + pad_heads(2176, False) + pad_heads(2176, True)
    cols += list(range(2592, 2848)) + list(range(2848, 3104)) + list(range(3872, 4128))
    cols += list(range(3104, 3360)) + list(range(3360, 3616)) + list(range(2560, 2592)) + [-1] * 96
    cols += list(range(1024, 1536)) + list(range(2304, 2560)) + list(range(3616, 3872))
    assert len(cols) == NEXT
    return np.array(cols)


def host_consts():
    c = {}
    c["identf"] = np.eye(128, dtype=np.float32)
    c["identb"] = np.eye(128, dtype=np.float32).astype(NPBF)
    i = np.arange(128)
    m16 = np.ones((128, 128), np.float32); m16[:, i % 16 == 0] = 0
    m128 = np.ones((128, 128), np.float32); m128[:, 0] = 0
    m32 = np.ones((128, 128), np.float32); m32[:, i % 32 == 0] = 0
    c["scanmask"] = np.stack([m16, m128, m32], 0)
    s = i[:, None]; t = i[None, :]
    tri = []
    for C in (16, 128, 32):
        same = (s // C) == (t // C)
        tri.append((same & (s <= t)).astype(np.float32))
        tri.append((same & (s >= t)).astype(np.float32))
    c["trimask"] = np.stack(tri, 0)
    c["cm"] = ((i[:, None] // CHG) == np.arange(128 // CHG)[None, :]).astype(np.float32).astype(NPBF)
    c["onesblk"] = (((i[:, None] // 64) == (i[None, :] // 64)).astype(np.float32) / 64.0).astype(NPBF)
    c["ones64"] = np.ones((128, 64), np.float32).astype(NPBF)
    inv_freq = (1.0 / (10000.0 ** (np.arange(0, 16, 2, dtype=np.float32) / np.float32(16)))).astype(np.float32)
    tt_ = np.arange(T)
    cos = np.zeros((128, T), np.float32); sins = np.zeros((128, T), np.float32)
    for p in range(128):
        ii = p % 64
        if ii >= 32:
            continue
        blk = ii // 16
        j = ii % 16
        f = j % 8
        pos = (tt_ // 64 if blk == 0 else tt_ % 64).astype(np.float32)
        ang = (pos * inv_freq[f]).astype(np.float32)
        cos[p] = np.cos(ang)
        sins[p] = -np.sin(ang) if j < 8 else np.sin(ang)
    c["ropecos"] = cos
    c["ropesin"] = sins
    return c


def host_layout(inp):
    f32 = np.float32
    common = dict(host_consts())
    cols = ext_col_index()
    valid = cols >= 0
    DEPTH = inp["w_in"].shape[0]
    q = np.arange(64)
    col_start = np.clip(q - 8, 0, 48)
    kc = np.arange(64)
    inwin = (kc[:, None] >= col_start[None, :]) & (kc[:, None] < col_start[None, :] + 16)
    didx = np.clip(kc[:, None] - q[None, :], -15, 15) + 15
    pidx = np.arange(128)
    for l in range(DEPTH):
        w_ext = np.zeros((D, NEXT), f32)
        w_ext[:, valid] = inp["w_in"][l][:, cols[valid]]
        common[f"w_in{l}"] = w_ext
        common[f"ada_w{l}"] = np.ascontiguousarray(inp["ada_w"][l], dtype=f32)
        common[f"ada_bT{l}"] = np.ascontiguousarray(inp["ada_b"][l].reshape(24, 128).T, dtype=f32)
        common[f"ada_bgt{l}"] = np.ascontiguousarray(inp["ada_b"][l][2048:3072].reshape(1, 1024), dtype=f32)
        common[f"norm_wT{l}"] = np.ascontiguousarray(inp["norm_w"][l].reshape(8, 128).T, dtype=f32)
        common[f"w_out{l}"] = np.ascontiguousarray(inp["w_out"][l], dtype=f32)
        rpb = inp["na_rpb"][l]
        g = rpb[:, ::-1, :][:, :, didx]
        g = np.where(inwin[None, None], g, f32(-30000.0))
        g = g.transpose(0, 2, 1, 3).reshape(4, 128, 15 * 64)
        common[f"rpbt{l}"] = np.ascontiguousarray(g, dtype=f32)
        wgk = inp["gla_w_gk"][l]
        bgk = inp["gla_b_gk"][l]
        wp = np.zeros((32, 2, 2, 128), f32)
        bp = np.zeros((128, 2, 2), f32)
        for d_ in range(2):
            for g_ in range(2):
                for p in range(128):
                    h = 2 * g_ + p // 64
                    ii = p % 64
                    if ii < 32:
                        wp[d_ * 16:(d_ + 1) * 16, d_, g_, p] = wgk[d_, :, h * 32 + ii]
                        bp[p, d_, g_] = bgk[d_, h * 32 + ii]
        common[f"wgk{l}"] = wp
        common[f"bgk{l}"] = bp
        common[f"glanw{l}"] = np.ascontiguousarray(inp["gla_norm_w"][l][pidx % 64].reshape(128, 1), dtype=f32)
        common[f"hgnw{l}"] = np.ascontiguousarray(inp["hgrn_norm_w"][l][pidx % 64].reshape(128, 1), dtype=f32)
    lb = inp["hgrn_lower_bounds"]
    common["lbraw"] = np.ascontiguousarray(lb.reshape(2, 2, 2, 128).transpose(3, 0, 1, 2).reshape(128, 8), dtype=f32)
    common["fnw"] = np.ascontiguousarray(inp["final_norm_w"].reshape(1, D), dtype=f32)
    per = []
    B = inp["x"].shape[0]
    cc = inp["c_ctx"].reshape(8, 128).T
    for b in range(B):
        cv = np.concatenate([inp["c"][b].reshape(8, 128).T, cc], axis=1)
        per.append({"x": np.ascontiguousarray(inp["x"][b], dtype=f32),
                    "ctx": np.ascontiguousarray(inp["ctx"][b], dtype=f32),
                    "cvec": np.ascontiguousarray(cv, dtype=f32)})
    return common, per


FMB_EVAC = {}
for _t in range(0, 4):
    FMB_EVAC[_t] = ("scale", 0.125)
for _t in range(4, 8):
    FMB_EVAC[_t] = ("copy", 1.0)
for _t in range(8, 12):
    FMB_EVAC[_t] = ("silu", 1.0)
for _t in range(12, 16):
    FMB_EVAC[_t] = ("scale", 32.0 ** -0.5)
for _t in range(16, 20):
    FMB_EVAC[_t] = ("copy", 1.0)
for _t in (20, 21, 24, 25):
    FMB_EVAC[_t] = ("silu", 1.0)
for _t in (22, 23):
    FMB_EVAC[_t] = ("scale", 0.125)


import os
DBG = {"skip_p1": os.environ.get("K_DBG_SKIP_P1") == "1",
       "p2_tiles": int(os.environ.get("K_DBG_P2_TILES", "0")),
       "p2_groups": [int(x) for x in os.environ.get("K_DBG_P2_GROUPS", "0,1,2,3").split(",")],
       "p2_dirs": [int(x) for x in os.environ.get("K_DBG_P2_DIRS", "1,0").split(",")],
       "cut": int(os.environ.get("K_DBG_P2_CUT", "99"))}


def build_program(depth=2, stop_after=None, debug=False):
    P = Prog()
    nc = P.nc
    kdbg = "ExternalOutput" if debug else "Internal"

    def din(name, shape, dt=F32):
        return P.dram(name, shape, dt, kind="ExternalInput").ap()

    x_d = din("x", [T, D]); ctx_d = din("ctx", [CTX, D]); cvec_d = din("cvec", [128, 16])
    identf_d = din("identf", [128, 128]); identb_d = din("identb", [128, 128], BF16)
    scanmask_d = din("scanmask", [3, 128, 128]); trimask_d = din("trimask", [6, 128, 128])
    cm_d = din("cm", [128, 128 // CHG], BF16); onesblk_d = din("onesblk", [128, 128], BF16); ones64_d = din("ones64", [128, 64], BF16)
    ropecos_d = din("ropecos", [128, T]); ropesin_d = din("ropesin", [128, T])
    lbraw_d = din("lbraw", [128, 8]); fnw_d = din("fnw", [1, D])
    LW = []
    for l in range(2):
        LW.append(dict(
            w_in=din(f"w_in{l}", [D, NEXT]), ada_w=din(f"ada_w{l}", [D, 3 * D]), ada_bT=din(f"ada_bT{l}", [128, 24]),
            ada_bgt=din(f"ada_bgt{l}", [1, D]), norm_wT=din(f"norm_wT{l}", [128, 8]), w_out=din(f"w_out{l}", [D, D]),
            rpbt=din(f"rpbt{l}", [4, 128, 960]), wgk=din(f"wgk{l}", [32, 2, 2, 128]), bgk=din(f"bgk{l}", [128, 2, 2]),
            glanw=din(f"glanw{l}", [128, 1]), hgnw=din(f"hgnw{l}", [128, 1])))
    out_d = P.dram("out", [T, D], F32, kind="ExternalOutput").ap()
    FMB = P.dram("FMB", [N_FMB, 128, NTOK], BF16, kind=kdbg).ap()
    FMF = P.dram("FMF", [N_FMF, 128, NTOK], F32, kind=kdbg).ap()
    TM = P.dram("TM", [NTOK, 1024], BF16, kind=kdbg).ap()
    OB = P.dram("OB", [4, 128, NTOK], F32, kind=kdbg).ap()
    OT = P.dram("OT", [8, 128, NTOK], BF16, kind=kdbg).ap()
    XN = P.dram("XN", [T, D], F32, kind=kdbg).ap()
    XC = P.dram("XC", [CTX, D], F32, kind=kdbg).ap()
    MODDBG = P.dram("MODDBG", [128, 64], F32, kind=kdbg).ap()

    identf = P.sb("identf", [128, 128]); identb = P.sb("identb", [128, 128], BF16)
    onesblk = P.sb("onesblk", [128, 128], BF16); ones64 = P.sb("ones64", [128, 64], BF16)
    cvec = P.sb("cvec", [128, 16]); sil = P.sb("sil", [128, 16]); silrep = P.sb("silrep", [128, 16, 128])
    lbraw = P.sb("lbraw", [128, 8]); lbt = P.sb("lbt", [128, 8]); omlb = P.sb("omlb", [128, 8])
    eps_t = P.sb("eps_t", [128, 1])
    P.memset("dve", eps_t[:], EPS, ["eps"])
    for nm, dst, src in (("identf", identf, identf_d), ("identb", identb, identb_d), ("onesblk", onesblk, onesblk_d),
                         ("ones64", ones64, ones64_d), ("cvec", cvec, cvec_d), ("lbraw", lbraw, lbraw_d)):
        P.dma(dst[:], src, writes=[nm])
    tmp16 = P.sb("tmp16", [128, 16])
    P.act(tmp16[:], cvec[:], AF.Exp, ["cvec"], ["tmp16"], scale=-1.0)
    P.ts("dve", tmp16[:], tmp16[:], 1.0, None, ALU.add, None, ["tmp16"], ["tmp16"])
    P.recip(tmp16[:], tmp16[:], ["tmp16"], ["tmp16"])
    P.tt("dve", sil[:], cvec[:], tmp16[:], ALU.mult, ["cvec", "tmp16"], ["sil"])
    for j in range(16):
        P.cp("dve", silrep[:, j, :], sil[:, j:j + 1].to_broadcast([128, 128]), ["sil"], ["silrep"])
    P.memset("dve", lbt[:], 0.0, ["lbt"])
    P.tt("dve", lbt[:, 4:8], lbraw[:, 0:4], lbraw[:, 4:8], ALU.subtract, ["lbraw", "lbt"], ["lbt"])
    P.act(lbt[:, 4:8], lbt[:, 4:8], AF.Exp, ["lbt"], ["lbt"])
    P.ts("dve", lbt[:, 4:8], lbt[:, 4:8], 1.0, None, ALU.add, None, ["lbt"], ["lbt"])
    P.recip(lbt[:, 4:8], lbt[:, 4:8], ["lbt"], ["lbt"])
    P.ts("dve", omlb[:], lbt[:], -1.0, 1.0, ALU.mult, ALU.add, ["lbt"], ["omlb"])

    last_out = []

    for l in range(depth):
        W = LW[l]
        last = (l == depth - 1)
        need_ctx = not last
        x_src = x_d if l == 0 else XN
        c_src = ctx_d if l == 0 else XC
        tiles = list(range(NT))

        P.push()
        gsh = P.sb(f"gsh{l}", [128, 2, 2, 8])
        gtrep = P.sb(f"gtrep{l}", [128, 2, D])
        woutb = P.sb(f"woutb{l}", [128, 2 if need_ctx else 1, KT, D], BF16)
        ttab = P.sb(f"ttab{l}", [128, 4, 960], BF16)
        wgk = P.sb(f"wgk{l}", [32, 2, 2, 128]); nbgk = P.sb(f"nbgk{l}", [128, 4])
        glanw = P.sb(f"glanw{l}", [128, 1]); hgnw = P.sb(f"hgnw{l}", [128, 1])

        P.push()
        winb = P.sb("winb", [128, KT, NEXT], BF16)
        win_v = W["w_in"].rearrange("(kt p) j -> p kt j", p=128)
        if not (DBG["skip_p1"] and l == 0):
            for kt in range(KT):
                for hf in range(2):
                    c0, c1 = hf * (NEXT // 2), (hf + 1) * (NEXT // 2)
                    P.dma(winb[:, kt, c0:c1], win_v[:, kt, c0:c1], writes=["winb"], eng="pool")
        tr_ps = Ring(P, "tr_ps", 2, [128, 512], psum=True)
        fm_ps = Ring(P, "fm_ps", 3, [128, 512], psum=True)
        tm_ps = Ring(P, "tm_ps", 2, [128, 512], psum=True)
        modT = P.sb("modT", [128, 16, 2]); adabT = P.sb("adabT", [128, 24]); normwT = P.sb("normwT", [128, 8])
        adabgt = P.sb("adabgt", [128, D])
        stg = Ring(P, "stg", 2, [128, KT, 256])
        mod_ps = tm_ps.bufs[0]; gt_ps = fm_ps
        P.dma(adabT[:], W["ada_bT"], writes=["adabT"]); P.dma(normwT[:], W["norm_wT"], writes=["normwT"])
        P.dma(adabgt[:], W["ada_bgt"].partition_broadcast(128), writes=["adabgt"])
        P.dma(wgk[:], W["wgk"], writes=["wgk"]); P.dma(nbgk[:], W["bgk"].rearrange("p a b -> p (a b)"), writes=["nbgk"])
        P.dma(glanw[:], W["glanw"], writes=["glanw"]); P.dma(hgnw[:], W["hgnw"], writes=["hgnw"])
        P.ts("dve", nbgk[:], nbgk[:], -1.0, None, ALU.mult, None, ["nbgk"], ["nbgk"])
        adaw_v = W["ada_w"].rearrange("(kt p) j -> p kt j", p=128)
        sil2 = sil[:].rearrange("p (w k) -> p w k", w=2)
        for ch in range(12):
            sbuf, skey = stg.next()
            P.dma(sbuf[:], adaw_v[:, :, ch * 256:(ch + 1) * 256], writes=[skey])
            if ch < 8:
                for jl in range(2):
                    jt = ch * 2 + jl
                    for kt in range(KT):
                        P.mm(mod_ps[:, jt * 2:jt * 2 + 2], sbuf[:, kt, jl * 128:(jl + 1) * 128], sil2[:, :, kt],
                             [skey, "sil"], ["tm_ps0"], start=(kt == 0), stop=(kt == KT - 1))
                    P.ts("dve", modT[:, jt, :], mod_ps[:, jt * 2:jt * 2 + 2], adabT[:, jt:jt + 1], None, ALU.add, None,
                         ["tm_ps0", "adabT"], ["modT"])
            else:
                q4 = ch - 8
                for which in range(2):
                    gp, gk_ = gt_ps.next()
                    for kt in range(KT):
                        P.mm(gp[:, 0:256], silrep[:, which * 8 + kt, :], sbuf[:, kt, :], ["silrep", skey], [gk_],
                             start=(kt == 0), stop=(kt == KT - 1))
                    P.tt("dve", gtrep[:, which, q4 * 256:(q4 + 1) * 256], gp[:, 0:256], adabgt[:, q4 * 256:(q4 + 1) * 256],
                         ALU.add, [gk_, "adabgt"], ["gtrep"])
        for which in range(2):
            P.stt("dve", gsh[:, which, 0, :], modT[:, 8:16, which], 1.0, normwT[:], ALU.add, ALU.mult,
                  ["modT", "normwT"], ["gsh"])
            P.cp("dve", gsh[:, which, 1, :], modT[:, 0:8, which], ["modT"], ["gsh"])
        if debug:
            P.dma(MODDBG[:, 0:32], gsh[:].rearrange("p a b c -> p (a b c)"), reads=["gsh"])
        wout_v = W["w_out"].rearrange("(kt p) j -> p kt j", p=128)
        for q4 in range(4):
            sbuf, skey = stg.next()
            P.dma(sbuf[:], wout_v[:, :, q4 * 256:(q4 + 1) * 256], writes=[skey])
            for which in range(2 if need_ctx else 1):
                for kt in range(KT):
                    eng = "dve" if kt % 2 == 0 else "pool"
                    P.tt(eng, woutb[:, which, kt, q4 * 256:(q4 + 1) * 256], sbuf[:, kt, :],
                         gtrep[:, which, q4 * 256:(q4 + 1) * 256], ALU.mult, [skey, "gtrep"], ["woutb"])
        for hp in range(4):
            sbuf, skey = stg.next()
            rv = sbuf[:].rearrange("p k c -> p (k c)")
            P.dma(rv[:, 0:960], W["rpbt"][hp, :, :], writes=[skey])
            P.act(ttab[:, hp, :], rv[:, 0:960], AF.Exp, [skey], ["ttab"])
        if stop_after == (l, "prep"):
            break

        xt_r = Ring(P, "xt", 2, [128, D]); xh_r = Ring(P, "xh", 2, [128, D]); sq_scr = P.sb("sq_scr", [128, D])
        ss_r = Ring(P, "ss", 4, [128, 2])
        hT_r = Ring(P, "hT", 2, [128, KT, 512], BF16)
        fmo_b = Ring(P, "fmo_b", 4, [128, 512], BF16); fmo_f = Ring(P, "fmo_f", 2, [128, 512])
        tmo = Ring(P, "tmo", 2, [128, 1024], BF16)
        supers = [(0, 2)] + [(2 + 4 * i, 4) for i in range(8)]
        if DBG["skip_p1"] and l == 0:
            supers = []
        def prologue(t0, ntl):
            is_ctx = (t0 == 0)
            which = 1 if is_ctx else 0
            ntok = ntl * 128
            hT, hkey = hT_r.next()
            for ti in range(ntl):
                tau = t0 + ti
                xt, xkey = xt_r.next()
                src = c_src[tau * 128:(tau + 1) * 128, :] if is_ctx else x_src[(tau - 2) * 128:(tau - 1) * 128, :]
                P.dma(xt[:], src, writes=[xkey])
                ss, sskey = ss_r.next()
                P.memset("dve", ss[:], 0.0, [sskey])
                P.act(sq_scr[:], xt[:], AF.Square, [xkey, sskey], ["sq_scr", sskey], accum_out=ss[:, 0:1])
                P.act(ss[:, 1:2], ss[:, 0:1], AF.Ln, [sskey, "eps"], [sskey], bias=eps_t[:], scale=1.0 / D)
                P.act(ss[:, 1:2], ss[:, 1:2], AF.Exp, [sskey], [sskey], scale=-0.5)
                xh, xhkey = xh_r.next()
                P.ts("dve", xh[:], xt[:], ss[:, 1:2], None, ALU.mult, None, [xkey, sskey], [xhkey])
                for half in range(2):
                    tp, tpkey = tr_ps.next()
                    for q4 in range(4):
                        kt = half * 4 + q4
                        P.tr(tp[:, q4 * 128:(q4 + 1) * 128], xh[:, kt * 128:(kt + 1) * 128], identf[:], [xhkey, "identf"], [tpkey])
                    for q4 in range(4):
                        kt = half * 4 + q4
                        P.act(hT[:, kt, ti * 128:(ti + 1) * 128], tp[:, q4 * 128:(q4 + 1) * 128], AF.Identity,
                              [tpkey, "gsh"], [hkey], bias=gsh[:, which, 1, kt:kt + 1], scale=gsh[:, which, 0, kt:kt + 1])
            return hT, hkey

        def mainbody(t0, ntl, hT, hkey):
            ntok = ntl * 128
            for ft in range(N_FMB + N_FMF):
                fp, fpkey = fm_ps.next()
                for kt in range(KT):
                    P.mm(fp[:, 0:ntok], winb[:, kt, ft * 128:(ft + 1) * 128], hT[:, kt, 0:ntok], ["winb", hkey], [fpkey],
                         start=(kt == 0), stop=(kt == KT - 1))
                if ft < N_FMB:
                    ob, obkey = fmo_b.next()
                    kind, sc_ = FMB_EVAC[ft]
                    if kind == "silu":
                        P.act(ob[:, 0:ntok], fp[:, 0:ntok], AF.Silu, [fpkey], [obkey])
                    else:
                        P.act(ob[:, 0:ntok], fp[:, 0:ntok], AF.Copy, [fpkey], [obkey], scale=float(sc_))
                    P.dma(FMB[ft, :, t0 * 128:t0 * 128 + ntok], ob[:, 0:ntok], reads=[obkey], writes=[("FMB", ft, t0)], eng="pool")
                else:
                    ob, obkey = fmo_f.next()
                    P.cp("dve", ob[:, 0:ntok], fp[:, 0:ntok], [fpkey], [obkey])
                    P.dma(FMF[ft - N_FMB, :, t0 * 128:t0 * 128 + ntok], ob[:, 0:ntok], reads=[obkey],
                          writes=[("FMF", ft - N_FMB, t0)], eng="pool")
            cbase = (N_FMB + N_FMF) * 128
            for ti in range(ntl):
                tau = t0 + ti
                ob, obkey = tmo.next()
                for half in range(2):
                    tp2, tp2key = tm_ps.next()
                    for kt in range(KT):
                        P.mm(tp2[:], hT[:, kt, ti * 128:(ti + 1) * 128], winb[:, kt, cbase + half * 512:cbase + (half + 1) * 512],
                             [hkey, "winb"], [tp2key], start=(kt == 0), stop=(kt == KT - 1))
                    P.cp("dve", ob[:, half * 512:(half + 1) * 512], tp2[:], [tp2key], [obkey])
                P.dma(TM[tau * 128:(tau + 1) * 128, :], ob[:], reads=[obkey], writes=[("TM", tau)], eng="pool")

        cur = prologue(*supers[0]) if supers else None
        for i_, (t0, ntl) in enumerate(supers):
            nxt = prologue(*supers[i_ + 1]) if i_ + 1 < len(supers) else None
            mainbody(t0, ntl, *cur)
            cur = nxt
        P.pop()
        if stop_after == (l, "p1"):
            break

        P.push()
        build_p2(P, l, need_ctx, FMB, FMF, TM, OB, OT, dict(
            identb=identb, onesblk=onesblk, wgk=wgk, nbgk=nbgk, glanw=glanw, hgnw=hgnw, lbt=lbt, omlb=omlb, eps_t=eps_t,
            scanmask_d=scanmask_d, trimask_d=trimask_d, cm_d=cm_d, ropecos_d=ropecos_d, ropesin_d=ropesin_d))
        P.pop()
        if stop_after == (l, "p2"):
            break

        P.push()
        build_p3(P, l, need_ctx, FMB, TM, OT, ttab, ones64)
        P.pop()
        if stop_after == (l, "p3"):
            break

        P.push()
        ot_r = Ring(P, "ot", 2, [128, 8, 128], BF16); xr = Ring(P, "x4", 2, [128, D]); xo = Ring(P, "xo", 2, [128, D])
        y_ps = Ring(P, "y_ps", 4, [128, 512], psum=True)
        ss_r = Ring(P, "ss4", 4, [128, 2]); sq_scr = P.sb("sq4", [128, D]); fnw = P.sb("fnw", [128, D])
        if last:
            P.dma(fnw[:], fnw_d.partition_broadcast(128), writes=["fnw"])
        otv = OT.rearrange("f p t -> p f t")
        for tau in (range(NT) if need_ctx else range(2, NT)):
            is_ctx = tau < 2
            which = 1 if is_ctx else 0
            ot, otkey = ot_r.next()
            P.dma(ot[:], otv[:, :, tau * 128:(tau + 1) * 128], reads=[("OT", f, tau) for f in range(8)], writes=[otkey])
            xt, xkey = xr.next()
            rows = slice(tau * 128, (tau + 1) * 128) if is_ctx else slice((tau - 2) * 128, (tau - 1) * 128)
            src = c_src[rows, :] if is_ctx else x_src[rows, :]
            P.dma(xt[:], src, reads=[("XR", l, tau)], writes=[xkey])
            xn, xnkey = xo.next()
            for half in range(2):
                yp, ypkey = y_ps.next()
                for f in range(8):
                    P.mm(yp[:], ot[:, f, :], woutb[:, which, f, half * 512:(half + 1) * 512], [otkey, "woutb"], [ypkey],
                         start=(f == 0), stop=(f == 7))
                P.tt("dve", xn[:, half * 512:(half + 1) * 512], yp[:], xt[:, half * 512:(half + 1) * 512], ALU.add,
                     [ypkey, xkey], [xnkey])
            if not last:
                dst = XC[rows, :] if is_ctx else XN[rows, :]
                P.dma(dst, xn[:], reads=[xnkey], writes=[("XR", l + 1, tau)], eng="pool")
            else:
                ss, sskey = ss_r.next()
                P.memset("dve", ss[:], 0.0, [sskey])
                P.act(sq_scr[:], xn[:], AF.Square, [xnkey, sskey], ["sq4", sskey], accum_out=ss[:, 0:1])
                P.act(ss[:, 1:2], ss[:, 0:1], AF.Ln, [sskey, "eps"], [sskey], bias=eps_t[:], scale=1.0 / D)
                P.act(ss[:, 1:2], ss[:, 1:2], AF.Exp, [sskey], [sskey], scale=-0.5)
                P.stt("dve", xn[:], xn[:], ss[:, 1:2], fnw[:], ALU.mult, ALU.mult, [xnkey, sskey, "fnw"], [xnkey])
                last_out.append(P.dma(out_d[rows, :], xn[:], reads=[xnkey], eng="pool"))
        P.pop()
        P.pop()
        if stop_after == (l, "p4"):
            break

    finals = list(last_out) + [d for d in P.dma_last.values()]
    return P.finish(final_waits=finals)


def _interleave(gens):
    gens = list(gens)
    while gens:
        for gen in list(gens):
            try:
                next(gen)
            except StopIteration:
                gens.remove(gen)


VEXP_ENG = os.environ.get("K_VEXP_ENG", "dve")


def build_p2(P, l, need_ctx, FMB, FMF, TM, OB, OT, G):
    identb = G["identb"]; onesblk = G["onesblk"]; wgk = G["wgk"]; nbgk = G["nbgk"]
    lbt = G["lbt"]; omlb = G["omlb"]; eps_t = G["eps_t"]
    scanmask = P.sb("scanmask", [128, 3, 128]); trimask = P.sb("trimask", [128, 6, 128]); cm = P.sb("cm", [128, 128 // CHG], BF16)
    ropecos = P.sb("ropecos", [128, T]); ropesin = P.sb("ropesin", [128, T])
    P.dma(scanmask[:], G["scanmask_d"].rearrange("n p t -> p n t"), writes=["scanmask"])
    P.dma(trimask[:], G["trimask_d"].rearrange("n p t -> p n t"), writes=["trimask"])
    P.dma(cm[:], G["cm_d"], writes=["cm"])
    P.dma(ropecos[:], G["ropecos_d"], writes=["ropecos"]); P.dma(ropesin[:], G["ropesin_d"], writes=["ropesin"])
    NHC = 128 // CHG
    NSL = [3, 3, 2 * NHC + 1, 2 * NHC + 1]
    S = [P.sb(f"S{g}", [128, NSL[g], 64]) for g in range(4)]
    RD = 6
    qk_r = Ring(P, "qk", 3, [128, 8, 128], BF16); gk_r = Ring(P, "gkr", 3, [32, 128]); v2_r = Ring(P, "vr", 8, [128, 256], BF16)
    qh_r = Ring(P, "qh", 3, [128, 2, 128], BF16); ff_r = Ring(P, "ff", 3, [128, 2, 128]); gate_r = Ring(P, "gate", 4, [128, 256], BF16)
    ob_r = Ring(P, "obr", 4, [128, 256])
    w2 = [Ring(P, f"w{i}", 3, [128, 2, 128]) for i in range(9)]
    qin_r = Ring(P, "qin", 7, [128, 2, 128])
    sm_r = Ring(P, "sm", 8, [128, 4, 8])
    kt_r = Ring(P, "ktr", 6, [128, 2, 128], BF16); kout_r = Ring(P, "kout", 6, [128, 2, 128], BF16)
    qbd_r = Ring(P, "qbd", 6, [128, 2, 2, 128], BF16)
    koT_r = Ring(P, "koT", 6, [128, 256], BF16); pt_r = Ring(P, "pt", 7, [128, 4, 128], BF16)
    vexp_r = Ring(P, "vexp", 4, [128, 4, NHC, 64], BF16)
    o_r = Ring(P, "osb", 6, [128, 256]); sq_r = Ring(P, "osq", 4, [128, 256], BF16); og_r = Ring(P, "og", 4, [128, 256], BF16)
    scanmask2 = P.sb("scanmask2", [128, 3, 256])
    for k_ in range(3):
        P.cp("pool", scanmask2[:, k_, :].rearrange("p (a t) -> p a t", a=2), scanmask[:, k_, :].unsqueeze(1).to_broadcast([128, 2, 128]),
             ["scanmask"], ["scanmask2"])
    z_ps = Ring(P, "z_ps", 1, [128, 512], psum=True); tp_ps = Ring(P, "tp_ps", 1, [128, 1024], BF16, psum=True)
    sc_ps = Ring(P, "sc_ps", 2, [128, 512], psum=True); kv_ps = Ring(P, "kv_ps", 2, [128, 512], psum=True)
    o_ps = Ring(P, "o_ps", 2, [128, 512], psum=True)
    for i, b in enumerate(qbd_r.bufs):
        P.memset("pool", b[:], 0.0, [f"qbd{i}"])
    base = [0, 0, 0, 0]

    def super_front(dirn, tau, gla, ctxs):
        bwd = dirn == 1
        is_ctx = tau < 2
        cs = slice(tau * 128, (tau + 1) * 128)
        C = 128 if gla else CHG
        nch = 128 // C
        mi = 1 if gla else 2
        kappa = -1.0 / 16.0 if gla else 1.0
        g0 = 0 if gla else 2
        c3 = lambda ap: ap.rearrange("p (n c) -> p n c", c=C)
        fl = lambda t: t[:].rearrange("p a b -> p (a b)")
        b0s = []
        for gg in range(2):
            g = g0 + gg
            b0s.append(base[g])
            base[g] = (base[g] + nch) % NSL[g]
        vt2, vkey = v2_r.next()
        voff = 512 if gla else 768
        P.dma(vt2[:], TM[cs, voff:voff + 256], reads=[("TM", tau)], writes=[vkey])
        lf, lfkey = w2[0].next()
        if gla:
            qk, qkkey = qk_r.next()
            P.dma(qk[:], FMB[12:20, :, cs].rearrange("n p t -> p n t"), reads=[("FMB", ft, _st(tau)) for ft in range(12, 20)],
                  writes=[qkkey])
            gkt, gkkey = gk_r.next()
            P.dma(gkt[:], FMF[4, 0:32, cs], reads=[("FMF", 4, _st(tau))], writes=[gkkey])
            yield
            zp, zkey = z_ps.next()
            e_, ekey = w2[1].next()
            for gg in range(2):
                P.mm(zp[:, gg * 128:(gg + 1) * 128], wgk[:, dirn, gg, :], gkt[:], ["wgk", gkkey], [zkey])
            for gg in range(2):
                P.act(e_[:, gg, :], zp[:, gg * 128:(gg + 1) * 128], AF.Exp, [zkey, "nbgk"], [ekey],
                      bias=nbgk[:, dirn * 2 + gg:dirn * 2 + gg + 1], scale=-1.0)
                yield
            P.act(fl(lf), fl(e_), AF.Ln, [ekey], [lfkey], bias=1.0)
            yield
            kk = kkkey = None
        else:
            qh, qhkey = qh_r.next()
            P.dma(qh[:], FMB[22:24, :, cs].rearrange("n p t -> p n t"), reads=[("FMB", 22, _st(tau)), ("FMB", 23, _st(tau))],
                  writes=[qhkey])
            fr, frkey = ff_r.next()
            fft = 2 if bwd else 0
            P.dma(fr[:], FMF[fft:fft + 2, :, cs].rearrange("n p t -> p n t"),
                  reads=[("FMF", fft, _st(tau)), ("FMF", fft + 1, _st(tau))], writes=[frkey])
            yield
            e_, ekey = w2[1].next()
            P.act(fl(e_), fl(fr), AF.Exp, [frkey], [ekey], scale=-1.0)
            yield
            P.act(fl(e_), fl(e_), AF.Ln, [ekey], [ekey], bias=1.0)
            yield
            P.act(fl(e_), fl(e_), AF.Exp, [ekey], [ekey], scale=-1.0)
            yield
            if l == 0:
                fgate, fkey = e_, ekey
            else:
                fgate, fkey = w2[2].next()
                for gg in range(2):
                    col = l * 4 + dirn * 2 + gg
                    P.ts("dve", fgate[:, gg, :], e_[:, gg, :], omlb[:, col:col + 1], lbt[:, col:col + 1], ALU.mult, ALU.add,
                         [ekey, "omlb", "lbt"], [fkey])
                yield
            P.act(fl(lf), fl(fgate), AF.Ln, [fkey], [lfkey])
            kk, kkkey = w2[3].next()
            P.act(fl(kk), fl(fgate), AF.Identity, [fkey], [kkkey], bias=1.0, scale=-1.0)
            yield
        Gc, Gkey = w2[4].next()
        m2 = scanmask2[:, mi, :]
        if bwd:
            P.op("dve", _scan(fl(Gc)[:, ::-1], m2, fl(lf)[:, ::-1]), [lfkey, "scanmask2"], [Gkey])
        else:
            P.op("dve", _scan(fl(Gc), m2, fl(lf)), [lfkey, "scanmask2"], [Gkey])
        yield
        Gv = c3(fl(Gc))
        n2 = 2 * nch
        mid = (C // 2) if bwd else (C // 2 - 1)
        lastp = 0 if bwd else C - 1
        sm, smkey = sm_r.next()
        P.tt("dve", sm[:, 1, 0:n2], Gv[:, :, lastp], Gv[:, :, mid], ALU.subtract, [Gkey], [smkey])
        Gm, Gmkey = w2[5].next()
        P.tt("dve", c3(fl(Gm)), Gv, Gv[:, :, mid:mid + 1].to_broadcast([128, n2, C]), ALU.subtract, [Gkey], [Gmkey])
        yield
        P.act(sm[:, 0, 0:n2], Gv[:, :, mid], AF.Exp, [Gkey], [smkey], scale=kappa)
        P.act(sm[:, 2, 0:n2], Gv[:, :, lastp], AF.Exp, [Gkey], [smkey], scale=kappa)
        yield
        P.act(sm[:, 1, 0:n2], sm[:, 1, 0:n2], AF.Exp, [smkey], [smkey], scale=kappa)
        A_, Akey = w2[6].next(); B_, Bkey = w2[7].next()
        P.act(fl(A_), fl(Gm), AF.Exp, [Gmkey], [Akey], scale=kappa)
        yield
        P.act(fl(B_), fl(Gm), AF.Exp, [Gmkey], [Bkey], scale=-kappa)
        yield
        Qf, Qfkey = w2[8].next()
        Kt, Ktkey = kt_r.next(); Kout, Koutkey = kout_r.next()
        if gla and not is_ctx:
            ts_ = slice((tau - 2) * 128, (tau - 1) * 128)
            cosb = ropecos[:, ts_].unsqueeze(1).to_broadcast([128, 2, 128])
            sinb = ropesin[:, ts_].unsqueeze(1).to_broadcast([128, 2, 128])
            r1, r1key = w2[2].next(); r2, r2key = w2[3].next()
            P.tt("dve", r1[:], qk[:, 0:2, :], cosb, ALU.mult, [qkkey, "ropecos"], [r1key])
            P.tt("pool", r2[:], qk[:, 2:4, :], sinb, ALU.mult, [qkkey, "ropesin"], [r2key])
            yield
            P.tt("dve", r1[:], r1[:], r2[:], ALU.add, [r1key, r2key], [r1key])
            r3, r3key = w2[2].next(); r4, r4key = w2[3].next()
            P.tt("pool", r3[:], qk[:, 4:6, :], cosb, ALU.mult, [qkkey, "ropecos"], [r3key])
            yield
            P.tt("dve", Qf[:], r1[:], A_[:], ALU.mult, [r1key, Akey], [Qfkey])
            P.tt("pool", r4[:], qk[:, 6:8, :], sinb, ALU.mult, [qkkey, "ropesin"], [r4key])
            yield
            P.tt("pool", r3[:], r3[:], r4[:], ALU.add, [r3key, r4key], [r3key])
            yield
            P.tt("pool", Kt[:], r3[:], B_[:], ALU.mult, [r3key, Bkey], [Ktkey])
            yield
        elif gla:
            P.tt("dve", Qf[:], qk[:, 0:2, :], A_[:], ALU.mult, [qkkey, Akey], [Qfkey])
            P.tt("pool", Kt[:], qk[:, 4:6, :], B_[:], ALU.mult, [qkkey, Bkey], [Ktkey])
            yield
        else:
            P.tt("dve", Qf[:], qh[:], A_[:], ALU.mult, [qhkey, Akey], [Qfkey])
            P.tt("pool", Kt[:], kk[:], B_[:], ALU.mult, [kkkey, Bkey], [Ktkey])
            yield
        Qbd, Qbdkey = qbd_r.next()
        P.cp("act", Qbd[0:64, :, 0, :], Qf[0:64, :, :], [Qfkey], [Qbdkey])
        yield
        P.cp("act", Qbd[64:128, :, 1, :], Qf[64:128, :, :], [Qfkey], [Qbdkey])
        Qin, Qinkey = qin_r.next()
        P.tt("dve" if gla else "pool", c3(fl(Qin)), c3(fl(Qf)), sm[:, 0, 0:n2].unsqueeze(2).to_broadcast([128, n2, C]), ALU.mult,
             [Qfkey, smkey], [Qinkey])
        P.tt("pool", c3(fl(Kout)), c3(fl(Kt)), sm[:, 1, 0:n2].unsqueeze(2).to_broadcast([128, n2, C]), ALU.mult,
             [Ktkey, smkey], [Koutkey])
        yield
        tpp, tpkey = tp_ps.next()
        for gg in range(2):
            P.tr(tpp[:, gg * 128:(gg + 1) * 128], Kout[:, gg, :], identb[:], [Koutkey, "identb"], [tpkey])
        koT, koTkey = koT_r.next()
        P.cp("act", koT[:], tpp[:, 0:256], [tpkey], [koTkey])
        yield
        scp, sckey = sc_ps.next()
        for gg in range(2):
            P.mm(scp[:, gg * 256:(gg + 1) * 256], Kt[:, gg, :], Qbd[:, gg, :, :].rearrange("p h t -> p (h t)"),
                 [Ktkey, Qbdkey], [sckey])
        PT, PTkey = pt_r.next()
        tmi = (2 if gla else 4) + (1 if bwd else 0)
        P.tt("dve", PT[:], scp[:, 0:512].rearrange("p (a t) -> p a t", a=4),
             trimask[:, tmi, :].unsqueeze(1).to_broadcast([128, 4, 128]), ALU.mult, [sckey, "trimask"], [PTkey])
        yield
        vx = vxkey = None
        if not gla:
            vx, vxkey = vexp_r.next()
            P.tt(VEXP_ENG, vx[:], vt2[:].rearrange("p (a d) -> p a d", a=4).unsqueeze(2).to_broadcast([128, 4, NHC, 64]),
                 cm[:].unsqueeze(1).unsqueeze(3).to_broadcast([128, 4, NHC, 64]), ALU.mult, [vkey, "cm"], [vxkey])
            yield
        for gg in range(2):
            g = g0 + gg
            ctxs[gg].update(dict(vt=vt2[:, gg * 128:(gg + 1) * 128], vkey=vkey, koT=koT[:, gg * 128:(gg + 1) * 128], koTkey=koTkey,
                                 vx=(vx[:, gg * 2:(gg + 1) * 2, :, :] if vx is not None else None), vxkey=vxkey,
                                 PT=PT[:, gg * 2:(gg + 1) * 2, :], PTkey=PTkey, Qin=Qin[:, gg, :], Qinkey=Qinkey,
                                 sm=sm[:, :, gg * nch:(gg + 1) * nch], smkey=smkey, b0=b0s[gg], nsl=NSL[g], C=C, nch=nch,
                                 gla=gla, gi=gg, bwd=bwd, is_ctx=is_ctx, cs=cs, g=g, tau=tau))

    def super_mid(ctxs):
        c0 = ctxs[0]
        gla = c0["gla"]; C = c0["C"]; nch = c0["nch"]; bwd = c0["bwd"]
        kvp, kvkey = kv_ps.next()
        W_ = 64 if gla else NHC * 64
        for c_ in ctxs:
            gg = c_["gi"]
            for h in range(2):
                if gla:
                    P.mm(kvp[h * 64:(h + 1) * 64, gg * W_:(gg + 1) * W_], c_["koT"][:, h * 64:(h + 1) * 64],
                         c_["vt"][:, h * 64:(h + 1) * 64], [c_["koTkey"], c_["vkey"]], [kvkey], tp=(0, h * 64))
                else:
                    P.mm(kvp[h * 64:(h + 1) * 64, gg * W_:(gg + 1) * W_], c_["koT"][:, h * 64:(h + 1) * 64],
                         c_["vx"][:, h, :, :].rearrange("p j d -> p (j d)"), [c_["koTkey"], c_["vxkey"]], [kvkey], tp=(0, h * 64))
            yield
        jorder = list(range(nch - 1, -1, -1)) if bwd else list(range(nch))
        for jj, j in enumerate(jorder):
            for c_ in ctxs:
                g = c_["g"]; gg = c_["gi"]; nsl = c_["nsl"]
                s_in = (c_["b0"] + jj) % nsl
                s_out = (c_["b0"] + jj + 1) % nsl
                P.stt("dve", S[g][:, s_out, :], S[g][:, s_in, :], c_["sm"][:, 2, j:j + 1],
                      kvp[:, gg * W_ + j * 64:gg * W_ + (j + 1) * 64], ALU.mult, ALU.add,
                      [f"S{g}_{s_in}", c_["smkey"], kvkey], [f"S{g}_{s_out}"])
            yield

    def super_out(ctxs):
        c0 = ctxs[0]
        gla = c0["gla"]; C = c0["C"]; nch = c0["nch"]; bwd = c0["bwd"]; is_ctx = c0["is_ctx"]; cs = c0["cs"]; tau = c0["tau"]
        g0 = c0["g"]
        op_, okey = o_ps.next()
        for c_ in ctxs:
            gg = c_["gi"]
            for h in range(2):
                P.mm(op_[h * 64:(h + 1) * 64, gg * 128:(gg + 1) * 128], c_["vt"][:, h * 64:(h + 1) * 64], c_["PT"][:, h, :],
                     [c_["vkey"], c_["PTkey"]], [okey], start=(gg == 0), stop=False, tp=(0, h * 64))
            yield
        jorder = list(range(nch - 1, -1, -1)) if bwd else list(range(nch))
        for jj, j in enumerate(jorder):
            for c_ in ctxs:
                g = c_["g"]; gg = c_["gi"]
                s_in = (c_["b0"] + jj) % c_["nsl"]
                for h in range(2):
                    P.mm(op_[h * 64:(h + 1) * 64, gg * 128 + j * C:gg * 128 + (j + 1) * C], S[g][h * 64:(h + 1) * 64, s_in, :],
                         c_["Qin"][h * 64:(h + 1) * 64, j * C:(j + 1) * C], [f"S{g}_{s_in}", c_["Qinkey"]], [okey],
                         start=False, stop=True, tp=(h * 64, h * 64))
            yield
        g2 = lambda d3: d3.rearrange("n p t -> p n t")
        if bwd:
            osb, oskey = o_r.next()
            P.cp("act", osb[:], op_[:, 0:256], [okey], [oskey])
            P.dma(g2(OB[g0:g0 + 2, :, cs]), osb[:].rearrange("p (n t) -> p n t", n=2), reads=[oskey],
                  writes=[("OB", g0, tau), ("OB", g0 + 1, tau)], eng="pool")
            yield
        elif not (is_ctx and not need_ctx):
            obt, obtkey = ob_r.next()
            P.dma(obt[:].rearrange("p (n t) -> p n t", n=2), g2(OB[g0:g0 + 2, :, cs]),
                  reads=[("OB", g0, tau), ("OB", g0 + 1, tau)], writes=[obtkey])
            gt_, gtkey = gate_r.next()
            gft = 20 if gla else 24
            P.dma(gt_[:].rearrange("p (n t) -> p n t", n=2), g2(FMB[gft:gft + 2, :, cs]),
                  reads=[("FMB", gft, _st(tau)), ("FMB", gft + 1, _st(tau))], writes=[gtkey])
            osb, oskey = o_r.next()
            P.tt("dve", osb[:], op_[:, 0:256], obt[:], ALU.add, [okey, obtkey], [oskey])
            yield
            osq, osqkey = sq_r.next()
            P.act(osq[:], osb[:], AF.Square, [oskey], [osqkey])
            yield
            zp, zkey = z_ps.next()
            P.mm(zp[:, 0:256], onesblk[:], osq[:], ["onesblk", osqkey], [zkey])
            rs, rskey = o_r.next()
            P.act(rs[:], zp[:, 0:256], AF.Ln, [zkey, "eps"], [rskey], bias=eps_t[:])
            yield
            P.act(rs[:], rs[:], AF.Exp, [rskey], [rskey], scale=-0.5)
            yield
            nw = G["glanw"] if gla else G["hgnw"]
            P.stt("dve", osb[:], osb[:], nw[:], rs[:], ALU.mult, ALU.mult, [oskey, rskey, "glanw", "hgnw"], [oskey])
            yield
            og, ogkey = og_r.next()
            P.tt("pool", og[:], osb[:], gt_[:], ALU.mult, [oskey, gtkey], [ogkey])
            P.dma(g2(OT[4 + g0:6 + g0, :, cs]), og[:].rearrange("p (n t) -> p n t", n=2), reads=[ogkey],
                  writes=[("OT", 4 + g0, tau), ("OT", 5 + g0, tau)], eng="pool")
            yield

    for dirn in DBG["p2_dirs"]:
        bwd = dirn == 1
        order = [1, 0] + list(range(NT - 1, 1, -1)) if bwd else list(range(NT))
        for g in range(4):
            base[g] = 0
            P.memset("dve", S[g][:, 0, :], 0.0, [f"S{g}_0"])
        if DBG["p2_tiles"]:
            order = order[:DBG["p2_tiles"]]
        mids = []
        outs = []
        mid_ctxs = []
        for tau in order:
            cts = [[dict(), dict()], [dict(), dict()]]
            fronts = [super_front(dirn, tau, False, cts[0]), super_front(dirn, tau, True, cts[1])]
            _interleave(outs + mids + fronts)
            outs = [super_out(c2) for c2 in mid_ctxs]
            mids = [super_mid(c2) for c2 in cts]
            mid_ctxs = cts
        _interleave(outs + mids)
        _interleave([super_out(c2) for c2 in mid_ctxs])


def _st(tau):
    return 0 if tau < 2 else 2 + 4 * ((tau - 2) // 4)


def _scan(out, d0, d1):
    return lambda e: e.tensor_tensor_scan(out, d0, d1, 0.0, ALU.mult, ALU.add)


def build_p3(P, l, need_ctx, FMB, TM, OT, ttab, ones64):
    KTb = Ring(P, "KTb", 2, [128, NTOK], BF16)
    Vd = Ring(P, "Vd", 2, [128, 68, 64], BF16)
    q_r = Ring(P, "q3", 2, [128, 512], BF16); g_r = Ring(P, "g3", 2, [128, 512], BF16)
    pe_r = Ring(P, "pe3", 5, [128, 512], BF16); p_r = Ring(P, "p3", 5, [128, 512], BF16)
    rc_r = Ring(P, "rc3", 2, [128, 512]); o_r = Ring(P, "o3", 2, [128, 512]); og_r = Ring(P, "og3", 2, [128, 512], BF16)
    s_ps = Ring(P, "s_ps", 4, [128, 512], psum=True)
    ao_ps = Ring(P, "ao_ps", 2, [128, 512], psum=True); as_ps = Ring(P, "as_ps", 2, [128, 512], psum=True)
    TMv = TM.rearrange("(r k) c -> k r c", k=64)

    def row_interval(rp):
        if rp <= 7:
            return 0, rp + 4
        if rp >= 56:
            return rp - 3, 63
        return rp - 3, rp + 4

    for hp in range(4):
        kt_, ktkey = KTb.next()
        P.dma(kt_[:], FMB[4 + hp, :, :], reads=[("FMB", 4 + hp, s) for s in [0] + [2 + 4 * i for i in range(8)]], writes=[ktkey])
        vd, vdkey = Vd.next()
        for hh in range(2):
            for part in range(4):
                r0, r1 = part * 17, (part + 1) * 17
                P.dma(vd[hh * 64:(hh + 1) * 64, r0:r1, :], TMv[:, r0:r1, hp * 128 + hh * 64:hp * 128 + (hh + 1) * 64],
                      reads=[("TM", t_) for t_ in range(NT)], writes=[vdkey])
        blocks = [("lat", b) for b in range(8)] + ([("ctx", 0)] if need_ctx else [])
        for kind, b in blocks:
            if kind == "lat":
                tok0 = CTX + b * 512
                nq = 512
                pieces = [(rho, 0, 512, None) for rho in range(4)]
                for rp in range(64):
                    lo, hi = row_interval(rp)
                    a = max(lo, 8 * b); e_ = min(hi, 8 * b + 7)
                    if a > e_:
                        continue
                    u0 = 7 - rp + a
                    pieces.append((4 + rp, (a - 8 * b) * 64, (e_ - 8 * b + 1) * 64, u0))
            else:
                tok0 = 0
                nq = 256
                pieces = [(rho, 0, 256, None) for rho in range(4)]
            qt, qkey = q_r.next(); gt_, gkey = g_r.next()
            st_reads = sorted(set(_st(t_) for t_ in range(tok0 // 128, (tok0 + nq) // 128)))
            P.dma(qt[:, 0:nq], FMB[hp, :, tok0:tok0 + nq], reads=[("FMB", hp, s) for s in st_reads], writes=[qkey])
            P.dma(gt_[:, 0:nq], FMB[8 + hp, :, tok0:tok0 + nq], reads=[("FMB", 8 + hp, s) for s in st_reads], writes=[gkey])
            ao, aokey = ao_ps.next(); as_, askey = as_ps.next()
            staged = {}

            def stage_a(pi):
                rho, c0, c1, u0 = pieces[pi]
                sp_, spkey = s_ps.next()
                for hh in range(2):
                    P.mm(sp_[hh * 64:(hh + 1) * 64, c0:c1], kt_[hh * 64:(hh + 1) * 64, rho * 64:(rho + 1) * 64],
                         qt[hh * 64:(hh + 1) * 64, c0:c1], [ktkey, qkey], [spkey], tp=(hh * 64, hh * 64))
                pe_, pekey = pe_r.next()
                P.act(pe_[:, c0:c1], sp_[:, c0:c1], AF.Exp, [spkey], [pekey])
                if u0 is not None:
                    pp, ppkey = p_r.next()
                    nr = (c1 - c0) // 64
                    P.tt("dve", pp[:, c0:c1], pe_[:, c0:c1], ttab[:, hp, u0 * 64:(u0 + nr) * 64], ALU.mult,
                         [pekey, "ttab"], [ppkey])
                else:
                    pp, ppkey = pe_, pekey
                staged[pi] = (pp, ppkey)

            def stage_b(pi):
                rho, c0, c1, u0 = pieces[pi]
                pp, ppkey = staged.pop(pi)
                for hh in range(2):
                    P.mm(ao[hh * 64:(hh + 1) * 64, c0:c1], vd[hh * 64:(hh + 1) * 64, rho, :], pp[hh * 64:(hh + 1) * 64, c0:c1],
                         [vdkey, ppkey], [aokey], start=(pi == 0), stop=(pi == len(pieces) - 1), tp=(hh * 64, hh * 64))
                    P.mm(as_[hh * 64:(hh + 1) * 64, c0:c1], ones64[hh * 64:(hh + 1) * 64, :], pp[hh * 64:(hh + 1) * 64, c0:c1],
                         ["ones64", ppkey], [askey], start=(pi == 0), stop=(pi == len(pieces) - 1), tp=(hh * 64, hh * 64))

            LA = 3
            for pi in range(min(LA, len(pieces))):
                stage_a(pi)
            for pi in range(len(pieces)):
                if pi + LA < len(pieces):
                    stage_a(pi + LA)
                stage_b(pi)
            rc, rckey = rc_r.next()
            P.recip(rc[:, 0:nq], as_[:, 0:nq], [askey], [rckey])
            ob, obkey = o_r.next()
            P.tt("dve", ob[:, 0:nq], ao[:, 0:nq], rc[:, 0:nq], ALU.mult, [aokey, rckey], [obkey])
            og, ogkey = og_r.next()
            P.tt("pool", og[:, 0:nq], ob[:, 0:nq], gt_[:, 0:nq], ALU.mult, [obkey, gkey], [ogkey])
            P.dma(OT[hp, :, tok0:tok0 + nq], og[:, 0:nq], reads=[ogkey],
                  writes=[("OT", hp, t_) for t_ in range(tok0 // 128, (tok0 + nq) // 128)], eng="pool")


_CACHE = {}


def kernel(**inputs):
    inputs = {k: np.asarray(v) for k, v in inputs.items()}
    common, per = host_layout(inputs)
    if "nc" not in _CACHE:
        _CACHE["nc"] = build_program()
    nc = _CACHE["nc"]
    in_maps = [dict(common, **p) for p in per]
    res = run_bass_kernel_spmd(nc, in_maps, core_ids=list(range(len(per))))
    out = np.stack([np.asarray(r["out"], dtype=np.float32) for r in res.results], axis=0)
    return out
```

```python
import numpy as np
import ml_dtypes
from contextlib import ExitStack
import concourse.bass as bass
import concourse.mybir as mybir
from concourse.bass_utils import run_bass_kernel_spmd

F32 = mybir.dt.float32
BF16 = mybir.dt.bfloat16
AF = mybir.ActivationFunctionType
ALU = mybir.AluOpType
NPBF = ml_dtypes.bfloat16

ENGS = ("pe", "act", "dve", "pool", "sp")
SAME_ENGINE_SYNC = {"pe": False, "act": True, "dve": True, "pool": True, "sp": False}
N_DMA_SEMS = {"sp": 20, "act": 4, "pool": 12}

D = 1024
KT = 8
T = 4096
CTX = 256
NTOK = T + CTX
NT = NTOK // 128
NEXT = 4992
N_FMB = 26
N_FMF = 5
EPS = 1e-6
CHG = 32


class Ins:
    __slots__ = ("eng", "fn", "deps", "is_dma", "dma_sem", "dma_val", "inc_idx", "needed", "pos", "guard")

    def __init__(self, eng, fn, is_dma):
        self.eng = eng
        self.fn = fn
        self.deps = []
        self.is_dma = is_dma
        self.dma_sem = None
        self.dma_val = None
        self.inc_idx = None
        self.needed = False
        self.guard = None


class Prog:
    def __init__(self):
        self.nc = bass.Bass("TRN2", target_bir_lowering=False)
        self.es = ExitStack()
        self.streams = {e: [] for e in ENGS}
        self.last_w = {}
        self.readers = {}
        self.dma_rr = {e: 0 for e in N_DMA_SEMS}
        self.dma_last = {}
        self.dma_cnt = {}
        self.n_ins = 0
        self.scopes = []

    def push(self):
        self.scopes.append(ExitStack())

    def pop(self):
        self.barrier()
        self.scopes.pop().close()

    def _ctx(self):
        return self.scopes[-1] if self.scopes else self.es

    def sb(self, name, shape, dtype=F32):
        self.uid = getattr(self, "uid", 0) + 1
        return self._ctx().enter_context(self.nc.sbuf_tensor(f"sb{self.uid}_{name}", list(shape), dtype))

    def ps(self, name, shape=(128, 512), dtype=F32):
        self.uid = getattr(self, "uid", 0) + 1
        return self._ctx().enter_context(self.nc.psum_tensor(f"ps{self.uid}_{name}", list(shape), dtype))

    def dram(self, name, shape, dtype=F32, kind="Internal"):
        return self.nc.dram_tensor(name, list(shape), dtype, kind=kind)

    def op(self, eng, fn, reads=(), writes=(), dma=False):
        ins = Ins(eng, fn, dma)
        ins.pos = self.n_ins
        self.n_ins += 1
        deps = []
        for r in reads:
            w = self.last_w.get(r)
            if w is not None:
                deps.append(w)
        for wkey in writes:
            w = self.last_w.get(wkey)
            if w is not None:
                deps.append(w)
            deps.extend(self.readers.get(wkey, {}).values())
        if dma:
            slot = self.dma_rr[eng] % N_DMA_SEMS[eng]
            self.dma_rr[eng] += 1
            key = (eng, slot)
            prev = self.dma_last.get(key)
            if prev is not None:
                ins.guard = prev
            self.dma_cnt[key] = self.dma_cnt.get(key, 0) + 1
            ins.dma_sem = key
            ins.dma_val = 16 * self.dma_cnt[key]
            self.dma_last[key] = ins
        best = {}
        for d in deps:
            if d is ins:
                continue
            k = d.dma_sem if d.is_dma else d.eng
            if k not in best or best[k].pos < d.pos:
                best[k] = d
        for d in best.values():
            ins.deps.append(d)
            d.needed = True
        if ins.guard is not None:
            ins.guard.needed = True
        mykey = ins.dma_sem if dma else eng
        for r in reads:
            self.readers.setdefault(r, {})[mykey] = ins
        for wkey in writes:
            self.last_w[wkey] = ins
            self.readers[wkey] = {}
        self.streams[eng].append(ins)
        return ins

    def barrier(self):
        lasts = []
        for e in ENGS:
            for ins in reversed(self.streams[e]):
                if ins.fn is not None and not ins.is_dma:
                    lasts.append(ins)
                    break
        lasts += list(self.dma_last.values())
        for e in ENGS:
            b = Ins(e, None, False)
            b.pos = self.n_ins
            self.n_ins += 1
            for d in lasts:
                b.deps.append(d)
                d.needed = True
            self.streams[e].append(b)
        self.last_w = {}
        self.readers = {}

    def dma(self, out, in_, reads=(), writes=(), eng="sp"):
        return self.op(eng, lambda e: e.dma_start(out=out, in_=in_), reads, writes, dma=True)

    def mm(self, out, lhsT, rhs, reads, writes, start=True, stop=True, tp=None):
        if tp is None:
            return self.op("pe", lambda e: e.matmul(out, lhsT, rhs, start=start, stop=stop), reads, writes)
        return self.op("pe", lambda e: e.matmul(out, lhsT, rhs, start=start, stop=stop, tile_position=tp,
                                                skip_group_check=True), reads, writes)

    def tr(self, out, in_, ident, reads, writes):
        return self.op("pe", lambda e: e.transpose(out, in_, ident), reads, writes)

    def act(self, out, in_, func, reads, writes, bias=None, scale=None, accum_out=None):
        kw = {}
        if bias is not None:
            kw["bias"] = bias
        if scale is not None:
            kw["scale"] = scale
        if accum_out is not None:
            kw["accum_out"] = accum_out
        return self.op("act", lambda e: e.activation(out, in_, func, **kw), reads, writes)

    def tt(self, eng, out, in0, in1, op, reads, writes):
        return self.op(eng, lambda e: e.tensor_tensor(out, in0, in1, op), reads, writes)

    def ts(self, eng, out, in0, s1, s2, op0, op1, reads, writes):
        if s2 is None:
            return self.op(eng, lambda e: e.tensor_scalar(out, in0, s1, None, op0=op0), reads, writes)
        return self.op(eng, lambda e: e.tensor_scalar(out, in0, s1, s2, op0=op0, op1=op1), reads, writes)

    def stt(self, eng, out, in0, scalar, in1, op0, op1, reads, writes):
        return self.op(eng, lambda e: e.scalar_tensor_tensor(out, in0, scalar, in1, op0=op0, op1=op1), reads, writes)

    def cp(self, eng, out, in_, reads, writes):
        if eng == "act":
            return self.op("act", lambda e: e.copy(out, in_), reads, writes)
        return self.op(eng, lambda e: e.tensor_copy(out, in_), reads, writes)

    def recip(self, out, in_, reads, writes):
        return self.op("dve", lambda e: e.reciprocal(out, in_), reads, writes)

    def memset(self, eng, ap, val, writes):
        return self.op(eng, lambda e: e.memset(ap, val), (), writes)

    def finish(self, final_waits=()):
        nc = self.nc
        es = self.es
        sems = {e: es.enter_context(nc.semaphore("s_" + e)) for e in ENGS}
        dsems = {}
        for e, n in N_DMA_SEMS.items():
            for i in range(n):
                dsems[(e, i)] = es.enter_context(nc.semaphore(f"d_{e}{i}"))
        for fw in final_waits:
            fw.needed = True
        for e in ENGS:
            c = 0
            for ins in self.streams[e]:
                if ins.is_dma or ins.fn is None:
                    continue
                if ins.needed:
                    c += 1
                    ins.inc_idx = c
        streams = self.streams

        def emit(e, eng_obj):
            known = {}

            def wait_for(d):
                if d.is_dma:
                    k = ("d",) + d.dma_sem
                    if known.get(k, 0) >= d.dma_val:
                        return
                    eng_obj.wait_ge(dsems[d.dma_sem], d.dma_val)
                    known[k] = d.dma_val
                else:
                    if d.eng == e and not SAME_ENGINE_SYNC[e]:
                        return
                    k = ("e", d.eng)
                    if known.get(k, 0) >= d.inc_idx:
                        return
                    eng_obj.wait_ge(sems[d.eng], d.inc_idx)
                    known[k] = d.inc_idx

            for ins in streams[e]:
                for d in ins.deps:
                    wait_for(d)
                if ins.guard is not None:
                    wait_for(ins.guard)
                if ins.fn is None:
                    continue
                bi = ins.fn(eng_obj)
                if ins.is_dma:
                    bi.then_inc(dsems[ins.dma_sem], 16)
                elif ins.needed:
                    bi.then_inc(sems[e], 1)
            if e == "sp":
                for fw in final_waits:
                    wait_for(fw)

        with nc.Block() as block:
            @block.tensor
            def _(eng):
                emit("pe", eng)

            @block.scalar
            def _(eng):
                emit("act", eng)

            @block.vector
            def _(eng):
                emit("dve", eng)

            @block.gpsimd
            def _(eng):
                emit("pool", eng)

            @block.sync
            def _(eng):
                emit("sp", eng)
        while self.scopes:
            self.scopes.pop().close()
        self.es.close()
        return nc


class Ring:
    def __init__(self, P, name, n, shape, dtype=F32, psum=False):
        self.name = name
        if psum:
            self.bufs = [P.ps(f"{name}{i}", shape, dtype) for i in range(n)]
        else:
            self.bufs = [P.sb(f"{name}{i}", shape, dtype) for i in range(n)]
        self.i = 0

    def next(self):
        j = self.i % len(self.bufs)
        self.i += 1
        return self.bufs[j], f"{self.name}{j}"


def ext_col_index():
    cols = []
    cols += list(range(0, 512)) + list(range(512, 1024)) + list(range(1536, 2048))

    def pad_heads(base, swap):
        out = []
        for h in range(4):
            for i in range(32):
                j = (i + 8 if (i % 16) < 8 else i - 8) if swap else i
                out.append(base + h * 32 + j)
            out += [-1] * 32
        return out
    cols += pad_heads(2048, False) + pad_heads(2048, True) + pad_heads(2176, False) + pad_heads(2176, True)
    cols += list(range(2592, 2848)) + list(range(2848, 3104)) + list(range(3872, 4128))
    cols += list(range(3104, 3360)) + list(range(3360, 3616)) + list(range(2560, 2592)) + [-1] * 96
    cols += list(range(1024, 1536)) + list(range(2304, 2560)) + list(range(3616, 3872))
    assert len(cols) == NEXT
    return np.array(cols)


def host_consts():
    c = {}
    c["identf"] = np.eye(128, dtype=np.float32)
    c["identb"] = np.eye(128, dtype=np.float32).astype(NPBF)
    i = np.arange(128)
    m16 = np.ones((128, 128), np.float32); m16[:, i % 16 == 0] = 0
    m128 = np.ones((128, 128), np.float32); m128[:, 0] = 0
    m32 = np.ones((128, 128), np.float32); m32[:, i % 32 == 0] = 0
    c["scanmask"] = np.stack([m16, m128, m32], 0)
    s = i[:, None]; t = i[None, :]
    tri = []
    for C in (16, 128, 32):
        same = (s // C) == (t // C)
        tri.append((same & (s <= t)).astype(np.float32))
        tri.append((same & (s >= t)).astype(np.float32))
    c["trimask"] = np.stack(tri, 0)
    c["cm"] = ((i[:, None] // CHG) == np.arange(128 // CHG)[None, :]).astype(np.float32).astype(NPBF)
    c["onesblk"] = (((i[:, None] // 64) == (i[None, :] // 64)).astype(np.float32) / 64.0).astype(NPBF)
    c["ones64"] = np.ones((128, 64), np.float32).astype(NPBF)
    inv_freq = (1.0 / (10000.0 ** (np.arange(0, 16, 2, dtype=np.float32) / np.float32(16)))).astype(np.float32)
    tt_ = np.arange(T)
    cos = np.zeros((128, T), np.float32); sins = np.zeros((128, T), np.float32)
    for p in range(128):
        ii = p % 64
        if ii >= 32:
            continue
        blk = ii // 16
        j = ii % 16
        f = j % 8
        pos = (tt_ // 64 if blk == 0 else tt_ % 64).astype(np.float32)
        ang = (pos * inv_freq[f]).astype(np.float32)
        cos[p] = np.cos(ang)
        sins[p] = -np.sin(ang) if j < 8 else np.sin(ang)
    c["ropecos"] = cos
    c["ropesin"] = sins
    return c


def host_layout(inp):
    f32 = np.float32
    common = dict(host_consts())
    cols = ext_col_index()
    valid = cols >= 0
    DEPTH = inp["w_in"].shape[0]
    q = np.arange(64)
    col_start = np.clip(q - 8, 0, 48)
    kc = np.arange(64)
    inwin = (kc[:, None] >= col_start[None, :]) & (kc[:, None] < col_start[None, :] + 16)
    didx = np.clip(kc[:, None] - q[None, :], -15, 15) + 15
    pidx = np.arange(128)
    for l in range(DEPTH):
        w_ext = np.zeros((D, NEXT), f32)
        w_ext[:, valid] = inp["w_in"][l][:, cols[valid]]
        common[f"w_in{l}"] = w_ext
        common[f"ada_w{l}"] = np.ascontiguousarray(inp["ada_w"][l], dtype=f32)
        common[f"ada_bT{l}"] = np.ascontiguousarray(inp["ada_b"][l].reshape(24, 128).T, dtype=f32)
        common[f"ada_bgt{l}"] = np.ascontiguousarray(inp["ada_b"][l][2048:3072].reshape(1, 1024), dtype=f32)
        common[f"norm_wT{l}"] = np.ascontiguousarray(inp["norm_w"][l].reshape(8, 128).T, dtype=f32)
        common[f"w_out{l}"] = np.ascontiguousarray(inp["w_out"][l], dtype=f32)
        rpb = inp["na_rpb"][l]
        g = rpb[:, ::-1, :][:, :, didx]
        g = np.where(inwin[None, None], g, f32(-30000.0))
        g = g.transpose(0, 2, 1, 3).reshape(4, 128, 15 * 64)
        common[f"rpbt{l}"] = np.ascontiguousarray(g, dtype=f32)
        wgk = inp["gla_w_gk"][l]
        bgk = inp["gla_b_gk"][l]
        wp = np.zeros((32, 2, 2, 128), f32)
        bp = np.zeros((128, 2, 2), f32)
        for d_ in range(2):
            for g_ in range(2):
                for p in range(128):
                    h = 2 * g_ + p // 64
                    ii = p % 64
                    if ii < 32:
                        wp[d_ * 16:(d_ + 1) * 16, d_, g_, p] = wgk[d_, :, h * 32 + ii]
                        bp[p, d_, g_] = bgk[d_, h * 32 + ii]
        common[f"wgk{l}"] = wp
        common[f"bgk{l}"] = bp
        common[f"glanw{l}"] = np.ascontiguousarray(inp["gla_norm_w"][l][pidx % 64].reshape(128, 1), dtype=f32)
        common[f"hgnw{l}"] = np.ascontiguousarray(inp["hgrn_norm_w"][l][pidx % 64].reshape(128, 1), dtype=f32)
    lb = inp["hgrn_lower_bounds"]
    common["lbraw"] = np.ascontiguousarray(lb.reshape(2, 2, 2, 128).transpose(3, 0, 1, 2).reshape(128, 8), dtype=f32)
    common["fnw"] = np.ascontiguousarray(inp["final_norm_w"].reshape(1, D), dtype=f32)
    per = []
    B = inp["x"].shape[0]
    cc = inp["c_ctx"].reshape(8, 128).T
    for b in range(B):
        cv = np.concatenate([inp["c"][b].reshape(8, 128).T, cc], axis=1)
        per.append({"x": np.ascontiguousarray(inp["x"][b], dtype=f32),
                    "ctx": np.ascontiguousarray(inp["ctx"][b], dtype=f32),
                    "cvec": np.ascontiguousarray(cv, dtype=f32)})
    return common, per


FMB_EVAC = {}
for _t in range(0, 4):
    FMB_EVAC[_t] = ("scale", 0.125)
for _t in range(4, 8):
    FMB_EVAC[_t] = ("copy", 1.0)
for _t in range(8, 12):
    FMB_EVAC[_t] = ("silu", 1.0)
for _t in range(12, 16):
    FMB_EVAC[_t] = ("scale", 32.0 ** -0.5)
for _t in range(16, 20):
    FMB_EVAC[_t] = ("copy", 1.0)
for _t in (20, 21, 24, 25):
    FMB_EVAC[_t] = ("silu", 1.0)
for _t in (22, 23):
    FMB_EVAC[_t] = ("scale", 0.125)


import os
DBG = {"skip_p1": os.environ.get("K_DBG_SKIP_P1") == "1",
       "p2_tiles": int(os.environ.get("K_DBG_P2_TILES", "0")),
       "p2_groups": [int(x) for x in os.environ.get("K_DBG_P2_GROUPS", "0,1,2,3").split(",")],
       "p2_dirs": [int(x) for x in os.environ.get("K_DBG_P2_DIRS", "1,0").split(",")],
       "cut": int(os.environ.get("K_DBG_P2_CUT", "99"))}


def build_program(depth=2, stop_after=None, debug=False):
    P = Prog()
    nc = P.nc
    kdbg = "ExternalOutput" if debug else "Internal"

    def din(name, shape, dt=F32):
        return P.dram(name, shape, dt, kind="ExternalInput").ap()

    x_d = din("x", [T, D]); ctx_d = din("ctx", [CTX, D]); cvec_d = din("cvec", [128, 16])
    identf_d = din("identf", [128, 128]); identb_d = din("identb", [128, 128], BF16)
    scanmask_d = din("scanmask", [3, 128, 128]); trimask_d = din("trimask", [6, 128, 128])
    cm_d = din("cm", [128, 128 // CHG], BF16); onesblk_d = din("onesblk", [128, 128], BF16); ones64_d = din("ones64", [128, 64], BF16)
    ropecos_d = din("ropecos", [128, T]); ropesin_d = din("ropesin", [128, T])
    lbraw_d = din("lbraw", [128, 8]); fnw_d = din("fnw", [1, D])
    LW = []
    for l in range(2):
        LW.append(dict(
            w_in=din(f"w_in{l}", [D, NEXT]), ada_w=din(f"ada_w{l}", [D, 3 * D]), ada_bT=din(f"ada_bT{l}", [128, 24]),
            ada_bgt=din(f"ada_bgt{l}", [1, D]), norm_wT=din(f"norm_wT{l}", [128, 8]), w_out=din(f"w_out{l}", [D, D]),
            rpbt=din(f"rpbt{l}", [4, 128, 960]), wgk=din(f"wgk{l}", [32, 2, 2, 128]), bgk=din(f"bgk{l}", [128, 2, 2]),
            glanw=din(f"glanw{l}", [128, 1]), hgnw=din(f"hgnw{l}", [128, 1])))
    out_d = P.dram("out", [T, D], F32, kind="ExternalOutput").ap()
    FMB = P.dram("FMB", [N_FMB, 128, NTOK], BF16, kind=kdbg).ap()
    FMF = P.dram("FMF", [N_FMF, 128, NTOK], F32, kind=kdbg).ap()
    TM = P.dram("TM", [NTOK, 1024], BF16, kind=kdbg).ap()
    OB = P.dram("OB", [4, 128, NTOK], F32, kind=kdbg).ap()
    OT = P.dram("OT", [8, 128, NTOK], BF16, kind=kdbg).ap()
    XN = P.dram("XN", [T, D], F32, kind=kdbg).ap()
    XC = P.dram("XC", [CTX, D], F32, kind=kdbg).ap()
    MODDBG = P.dram("MODDBG", [128, 64], F32, kind=kdbg).ap()

    identf = P.sb("identf", [128, 128]); identb = P.sb("identb", [128, 128], BF16)
    onesblk = P.sb("onesblk", [128, 128], BF16); ones64 = P.sb("ones64", [128, 64], BF16)
    cvec = P.sb("cvec", [128, 16]); sil = P.sb("sil", [128, 16]); silrep = P.sb("silrep", [128, 16, 128])
    lbraw = P.sb("lbraw", [128, 8]); lbt = P.sb("lbt", [128, 8]); omlb = P.sb("omlb", [128, 8])
    eps_t = P.sb("eps_t", [128, 1])
    P.memset("dve", eps_t[:], EPS, ["eps"])
    for nm, dst, src in (("identf", identf, identf_d), ("identb", identb, identb_d), ("onesblk", onesblk, onesblk_d),
                         ("ones64", ones64, ones64_d), ("cvec", cvec, cvec_d), ("lbraw", lbraw, lbraw_d)):
        P.dma(dst[:], src, writes=[nm])
    tmp16 = P.sb("tmp16", [128, 16])
    P.act(tmp16[:], cvec[:], AF.Exp, ["cvec"], ["tmp16"], scale=-1.0)
    P.ts("dve", tmp16[:], tmp16[:], 1.0, None, ALU.add, None, ["tmp16"], ["tmp16"])
    P.recip(tmp16[:], tmp16[:], ["tmp16"], ["tmp16"])
    P.tt("dve", sil[:], cvec[:], tmp16[:], ALU.mult, ["cvec", "tmp16"], ["sil"])
    for j in range(16):
        P.cp("dve", silrep[:, j, :], sil[:, j:j + 1].to_broadcast([128, 128]), ["sil"], ["silrep"])
    P.memset("dve", lbt[:], 0.0, ["lbt"])
    P.tt("dve", lbt[:, 4:8], lbraw[:, 0:4], lbraw[:, 4:8], ALU.subtract, ["lbraw", "lbt"], ["lbt"])
    P.act(lbt[:, 4:8], lbt[:, 4:8], AF.Exp, ["lbt"], ["lbt"])
    P.ts("dve", lbt[:, 4:8], lbt[:, 4:8], 1.0, None, ALU.add, None, ["lbt"], ["lbt"])
    P.recip(lbt[:, 4:8], lbt[:, 4:8], ["lbt"], ["lbt"])
    P.ts("dve", omlb[:], lbt[:], -1.0, 1.0, ALU.mult, ALU.add, ["lbt"], ["omlb"])

    last_out = []

    for l in range(depth):
        W = LW[l]
        last = (l == depth - 1)
        need_ctx = not last
        x_src = x_d if l == 0 else XN
        c_src = ctx_d if l == 0 else XC
        tiles = list(range(NT))

        P.push()
        gsh = P.sb(f"gsh{l}", [128, 2, 2, 8])
        gtrep = P.sb(f"gtrep{l}", [128, 2, D])
        woutb = P.sb(f"woutb{l}", [128, 2 if need_ctx else 1, KT, D], BF16)
        ttab = P.sb(f"ttab{l}", [128, 4, 960], BF16)
        wgk = P.sb(f"wgk{l}", [32, 2, 2, 128]); nbgk = P.sb(f"nbgk{l}", [128, 4])
        glanw = P.sb(f"glanw{l}", [128, 1]); hgnw = P.sb(f"hgnw{l}", [128, 1])

        P.push()
        winb = P.sb("winb", [128, KT, NEXT], BF16)
        win_v = W["w_in"].rearrange("(kt p) j -> p kt j", p=128)
        if not (DBG["skip_p1"] and l == 0):
            for kt in range(KT):
                for hf in range(2):
                    c0, c1 = hf * (NEXT // 2), (hf + 1) * (NEXT // 2)
                    P.dma(winb[:, kt, c0:c1], win_v[:, kt, c0:c1], writes=["winb"], eng="pool")
        tr_ps = Ring(P, "tr_ps", 2, [128, 512], psum=True)
        fm_ps = Ring(P, "fm_ps", 3, [128, 512], psum=True)
        tm_ps = Ring(P, "tm_ps", 2, [128, 512], psum=True)
        modT = P.sb("modT", [128, 16, 2]); adabT = P.sb("adabT", [128, 24]); normwT = P.sb("normwT", [128, 8])
        adabgt = P.sb("adabgt", [128, D])
        stg = Ring(P, "stg", 2, [128, KT, 256])
        mod_ps = tm_ps.bufs[0]; gt_ps = fm_ps
        P.dma(adabT[:], W["ada_bT"], writes=["adabT"]); P.dma(normwT[:], W["norm_wT"], writes=["normwT"])
        P.dma(adabgt[:], W["ada_bgt"].partition_broadcast(128), writes=["adabgt"])
        P.dma(wgk[:], W["wgk"], writes=["wgk"]); P.dma(nbgk[:], W["bgk"].rearrange("p a b -> p (a b)"), writes=["nbgk"])
        P.dma(glanw[:], W["glanw"], writes=["glanw"]); P.dma(hgnw[:], W["hgnw"], writes=["hgnw"])
        P.ts("dve", nbgk[:], nbgk[:], -1.0, None, ALU.mult, None, ["nbgk"], ["nbgk"])
        adaw_v = W["ada_w"].rearrange("(kt p) j -> p kt j", p=128)
        sil2 = sil[:].rearrange("p (w k) -> p w k", w=2)
        for ch in range(12):
            sbuf, skey = stg.next()
            P.dma(sbuf[:], adaw_v[:, :, ch * 256:(ch + 1) * 256], writes=[skey])
            if ch < 8:
                for jl in range(2):
                    jt = ch * 2 + jl
                    for kt in range(KT):
                        P.mm(mod_ps[:, jt * 2:jt * 2 + 2], sbuf[:, kt, jl * 128:(jl + 1) * 128], sil2[:, :, kt],
                             [skey, "sil"], ["tm_ps0"], start=(kt == 0), stop=(kt == KT - 1))
                    P.ts("dve", modT[:, jt, :], mod_ps[:, jt * 2:jt * 2 + 2], adabT[:, jt:jt + 1], None, ALU.add, None,
                         ["tm_ps0", "adabT"], ["modT"])
            else:
                q4 = ch - 8
                for which in range(2):
                    gp, gk_ = gt_ps.next()
                    for kt in range(KT):
                        P.mm(gp[:, 0:256], silrep[:, which * 8 + kt, :], sbuf[:, kt, :], ["silrep", skey], [gk_],
                             start=(kt == 0), stop=(kt == KT - 1))
                    P.tt("dve", gtrep[:, which, q4 * 256:(q4 + 1) * 256], gp[:, 0:256], adabgt[:, q4 * 256:(q4 + 1) * 256],
                         ALU.add, [gk_, "adabgt"], ["gtrep"])
        for which in range(2):
            P.stt("dve", gsh[:, which, 0, :], modT[:, 8:16, which], 1.0, normwT[:], ALU.add, ALU.mult,
                  ["modT", "normwT"], ["gsh"])
            P.cp("dve", gsh[:, which, 1, :], modT[:, 0:8, which], ["modT"], ["gsh"])
        if debug:
            P.dma(MODDBG[:, 0:32], gsh[:].rearrange("p a b c -> p (a b c)"), reads=["gsh"])
        wout_v = W["w_out"].rearrange("(kt p) j -> p kt j", p=128)
        for q4 in range(4):
            sbuf, skey = stg.next()
            P.dma(sbuf[:], wout_v[:, :, q4 * 256:(q4 + 1) * 256], writes=[skey])
            for which in range(2 if need_ctx else 1):
                for kt in range(KT):
                    eng = "dve" if kt % 2 == 0 else "pool"
                    P.tt(eng, woutb[:, which, kt, q4 * 256:(q4 + 1) * 256], sbuf[:, kt, :],
                         gtrep[:, which, q4 * 256:(q4 + 1) * 256], ALU.mult, [skey, "gtrep"], ["woutb"])
        for hp in range(4):
            sbuf, skey = stg.next()
            rv = sbuf[:].rearrange("p k c -> p (k c)")
            P.dma(rv[:, 0:960], W["rpbt"][hp, :, :], writes=[skey])
            P.act(ttab[:, hp, :], rv[:, 0:960], AF.Exp, [skey], ["ttab"])
        if stop_after == (l, "prep"):
            break

        xt_r = Ring(P, "xt", 2, [128, D]); xh_r = Ring(P, "xh", 2, [128, D]); sq_scr = P.sb("sq_scr", [128, D])
        ss_r = Ring(P, "ss", 4, [128, 2])
        hT_r = Ring(P, "hT", 2, [128, KT, 512], BF16)
        fmo_b = Ring(P, "fmo_b", 4, [128, 512], BF16); fmo_f = Ring(P, "fmo_f", 2, [128, 512])
        tmo = Ring(P, "tmo", 2, [128, 1024], BF16)
        supers = [(0, 2)] + [(2 + 4 * i, 4) for i in range(8)]
        if DBG["skip_p1"] and l == 0:
            supers = []
        def prologue(t0, ntl):
            is_ctx = (t0 == 0)
            which = 1 if is_ctx else 0
            ntok = ntl * 128
            hT, hkey = hT_r.next()
            for ti in range(ntl):
                tau = t0 + ti
                xt, xkey = xt_r.next()
                src = c_src[tau * 128:(tau + 1) * 128, :] if is_ctx else x_src[(tau - 2) * 128:(tau - 1) * 128, :]
                P.dma(xt[:], src, writes=[xkey])
                ss, sskey = ss_r.next()
                P.memset("dve", ss[:], 0.0, [sskey])
                P.act(sq_scr[:], xt[:], AF.Square, [xkey, sskey], ["sq_scr", sskey], accum_out=ss[:, 0:1])
                P.act(ss[:, 1:2], ss[:, 0:1], AF.Ln, [sskey, "eps"], [sskey], bias=eps_t[:], scale=1.0 / D)
                P.act(ss[:, 1:2], ss[:, 1:2], AF.Exp, [sskey], [sskey], scale=-0.5)
                xh, xhkey = xh_r.next()
                P.ts("dve", xh[:], xt[:], ss[:, 1:2], None, ALU.mult, None, [xkey, sskey], [xhkey])
                for half in range(2):
                    tp, tpkey = tr_ps.next()
                    for q4 in range(4):
                        kt = half * 4 + q4
                        P.tr(tp[:, q4 * 128:(q4 + 1) * 128], xh[:, kt * 128:(kt + 1) * 128], identf[:], [xhkey, "identf"], [tpkey])
                    for q4 in range(4):
                        kt = half * 4 + q4
                        P.act(hT[:, kt, ti * 128:(ti + 1) * 128], tp[:, q4 * 128:(q4 + 1) * 128], AF.Identity,
                              [tpkey, "gsh"], [hkey], bias=gsh[:, which, 1, kt:kt + 1], scale=gsh[:, which, 0, kt:kt + 1])
            return hT, hkey

        def mainbody(t0, ntl, hT, hkey):
            ntok = ntl * 128
            for ft in range(N_FMB + N_FMF):
                fp, fpkey = fm_ps.next()
                for kt in range(KT):
                    P.mm(fp[:, 0:ntok], winb[:, kt, ft * 128:(ft + 1) * 128], hT[:, kt, 0:ntok], ["winb", hkey], [fpkey],
                         start=(kt == 0), stop=(kt == KT - 1))
                if ft < N_FMB:
                    ob, obkey = fmo_b.next()
                    kind, sc_ = FMB_EVAC[ft]
                    if kind == "silu":
                        P.act(ob[:, 0:ntok], fp[:, 0:ntok], AF.Silu, [fpkey], [obkey])
                    else:
                        P.act(ob[:, 0:ntok], fp[:, 0:ntok], AF.Copy, [fpkey], [obkey], scale=float(sc_))
                    P.dma(FMB[ft, :, t0 * 128:t0 * 128 + ntok], ob[:, 0:ntok], reads=[obkey], writes=[("FMB", ft, t0)], eng="pool")
                else:
                    ob, obkey = fmo_f.next()
                    P.cp("dve", ob[:, 0:ntok], fp[:, 0:ntok], [fpkey], [obkey])
                    P.dma(FMF[ft - N_FMB, :, t0 * 128:t0 * 128 + ntok], ob[:, 0:ntok], reads=[obkey],
                          writes=[("FMF", ft - N_FMB, t0)], eng="pool")
            cbase = (N_FMB + N_FMF) * 128
            for ti in range(ntl):
                tau = t0 + ti
                ob, obkey = tmo.next()
                for half in range(2):
                    tp2, tp2key = tm_ps.next()
                    for kt in range(KT):
                        P.mm(tp2[:], hT[:, kt, ti * 128:(ti + 1) * 128], winb[:, kt, cbase + half * 512:cbase + (half + 1) * 512],
                             [hkey, "winb"], [tp2key], start=(kt == 0), stop=(kt == KT - 1))
                    P.cp("dve", ob[:, half * 512:(half + 1) * 512], tp2[:], [tp2key], [obkey])
                P.dma(TM[tau * 128:(tau + 1) * 128, :], ob[:], reads=[obkey], writes=[("TM", tau)], eng="pool")

        cur = prologue(*supers[0]) if supers else None
        for i_, (t0, ntl) in enumerate(supers):
            nxt = prologue(*supers[i_ + 1]) if i_ + 1 < len(supers) else None
            mainbody(t0, ntl, *cur)
            cur = nxt
        P.pop()
        if stop_after == (l, "p1"):
            break

        P.push()
        build_p2(P, l, need_ctx, FMB, FMF, TM, OB, OT, dict(
            identb=identb, onesblk=onesblk, wgk=wgk, nbgk=nbgk, glanw=glanw, hgnw=hgnw, lbt=lbt, omlb=omlb, eps_t=eps_t,
            scanmask_d=scanmask_d, trimask_d=trimask_d, cm_d=cm_d, ropecos_d=ropecos_d, ropesin_d=ropesin_d))
        P.pop()
        if stop_after == (l, "p2"):
            break

        P.push()
        build_p3(P, l, need_ctx, FMB, TM, OT, ttab, ones64)
        P.pop()
        if stop_after == (l, "p3"):
            break

        P.push()
        ot_r = Ring(P, "ot", 2, [128, 8, 128], BF16); xr = Ring(P, "x4", 2, [128, D]); xo = Ring(P, "xo", 2, [128, D])
        y_ps = Ring(P, "y_ps", 4, [128, 512], psum=True)
        ss_r = Ring(P, "ss4", 4, [128, 2]); sq_scr = P.sb("sq4", [128, D]); fnw = P.sb("fnw", [128, D])
        if last:
            P.dma(fnw[:], fnw_d.partition_broadcast(128), writes=["fnw"])
        otv = OT.rearrange("f p t -> p f t")
        for tau in (range(NT) if need_ctx else range(2, NT)):
            is_ctx = tau < 2
            which = 1 if is_ctx else 0
            ot, otkey = ot_r.next()
            P.dma(ot[:], otv[:, :, tau * 128:(tau + 1) * 128], reads=[("OT", f, tau) for f in range(8)], writes=[otkey])
            xt, xkey = xr.next()
            rows = slice(tau * 128, (tau + 1) * 128) if is_ctx else slice((tau - 2) * 128, (tau - 1) * 128)
            src = c_src[rows, :] if is_ctx else x_src[rows, :]
            P.dma(xt[:], src, reads=[("XR", l, tau)], writes=[xkey])
            xn, xnkey = xo.next()
            for half in range(2):
                yp, ypkey = y_ps.next()
                for f in range(8):
                    P.mm(yp[:], ot[:, f, :], woutb[:, which, f, half * 512:(half + 1) * 512], [otkey, "woutb"], [ypkey],
                         start=(f == 0), stop=(f == 7))
                P.tt("dve", xn[:, half * 512:(half + 1) * 512], yp[:], xt[:, half * 512:(half + 1) * 512], ALU.add,
                     [ypkey, xkey], [xnkey])
            if not last:
                dst = XC[rows, :] if is_ctx else XN[rows, :]
                P.dma(dst, xn[:], reads=[xnkey], writes=[("XR", l + 1, tau)], eng="pool")
            else:
                ss, sskey = ss_r.next()
                P.memset("dve", ss[:], 0.0, [sskey])
                P.act(sq_scr[:], xn[:], AF.Square, [xnkey, sskey], ["sq4", sskey], accum_out=ss[:, 0:1])
                P.act(ss[:, 1:2], ss[:, 0:1], AF.Ln, [sskey, "eps"], [sskey], bias=eps_t[:], scale=1.0 / D)
                P.act(ss[:, 1:2], ss[:, 1:2], AF.Exp, [sskey], [sskey], scale=-0.5)
                P.stt("dve", xn[:], xn[:], ss[:, 1:2], fnw[:], ALU.mult, ALU.mult, [xnkey, sskey, "fnw"], [xnkey])
                last_out.append(P.dma(out_d[rows, :], xn[:], reads=[xnkey], eng="pool"))
        P.pop()
        P.pop()
        if stop_after == (l, "p4"):
            break

    finals = list(last_out) + [d for d in P.dma_last.values()]
    return P.finish(final_waits=finals)


def _interleave(gens, strides=None):
    gens = list(gens)
    strides = list(strides) if strides is not None else [1] * len(gens)
    rnd = 0
    while gens:
        for gen, st in list(zip(gens, strides)):
            if rnd % st:
                continue
            try:
                next(gen)
            except StopIteration:
                i = gens.index(gen)
                gens.pop(i)
                strides.pop(i)
        rnd += 1


SLOW_STRIDE = int(os.environ.get("K_SLOW_STRIDE", "2"))
VEXP_ENG = os.environ.get("K_VEXP_ENG", "dve")


def build_p2(P, l, need_ctx, FMB, FMF, TM, OB, OT, G):
    identb = G["identb"]; onesblk = G["onesblk"]; wgk = G["wgk"]; nbgk = G["nbgk"]
    lbt = G["lbt"]; omlb = G["omlb"]; eps_t = G["eps_t"]
    scanmask = P.sb("scanmask", [128, 3, 128]); trimask = P.sb("trimask", [128, 6, 128]); cm = P.sb("cm", [128, 128 // CHG], BF16)
    ropecos = P.sb("ropecos", [128, T]); ropesin = P.sb("ropesin", [128, T])
    P.dma(scanmask[:], G["scanmask_d"].rearrange("n p t -> p n t"), writes=["scanmask"])
    P.dma(trimask[:], G["trimask_d"].rearrange("n p t -> p n t"), writes=["trimask"])
    P.dma(cm[:], G["cm_d"], writes=["cm"])
    P.dma(ropecos[:], G["ropecos_d"], writes=["ropecos"]); P.dma(ropesin[:], G["ropesin_d"], writes=["ropesin"])
    NHC = 128 // CHG
    NSL = [3, 3, 2 * NHC + 1, 2 * NHC + 1]
    S = [P.sb(f"S{g}", [128, NSL[g], 64]) for g in range(4)]
    RD = 6
    qk_r = Ring(P, "qk", 3, [128, 8, 128], BF16); gk_r = Ring(P, "gkr", 3, [32, 128]); v2_r = Ring(P, "vr", 8, [128, 256], BF16)
    qh_r = Ring(P, "qh", 3, [128, 2, 128], BF16); ff_r = Ring(P, "ff", 3, [128, 2, 128]); gate_r = Ring(P, "gate", 4, [128, 256], BF16)
    ob_r = Ring(P, "obr", 4, [128, 256])
    w2 = [Ring(P, f"w{i}", 3, [128, 2, 128]) for i in range(9)]
    qin_r = Ring(P, "qin", 7, [128, 2, 128])
    sm_r = Ring(P, "sm", 8, [128, 4, 8])
    kt_r = Ring(P, "ktr", 6, [128, 2, 128], BF16); kout_r = Ring(P, "kout", 6, [128, 2, 128], BF16)
    qbd_r = Ring(P, "qbd", 6, [128, 2, 2, 128], BF16)
    koT_r = Ring(P, "koT", 6, [128, 256], BF16); pt_r = Ring(P, "pt", 7, [128, 4, 128], BF16)
    vexp_r = Ring(P, "vexp", 4, [128, 4, NHC, 64], BF16)
    o_r = Ring(P, "osb", 6, [128, 256]); sq_r = Ring(P, "osq", 4, [128, 256], BF16); og_r = Ring(P, "og", 4, [128, 256], BF16)
    scanmask2 = P.sb("scanmask2", [128, 3, 256])
    for k_ in range(3):
        P.cp("pool", scanmask2[:, k_, :].rearrange("p (a t) -> p a t", a=2), scanmask[:, k_, :].unsqueeze(1).to_broadcast([128, 2, 128]),
             ["scanmask"], ["scanmask2"])
    z_ps = Ring(P, "z_ps", 1, [128, 512], psum=True); tp_ps = Ring(P, "tp_ps", 1, [128, 1024], BF16, psum=True)
    sc_ps = Ring(P, "sc_ps", 2, [128, 512], psum=True); kv_ps = Ring(P, "kv_ps", 2, [128, 512], psum=True)
    o_ps = Ring(P, "o_ps", 2, [128, 512], psum=True)
    for i, b in enumerate(qbd_r.bufs):
        P.memset("pool", b[:], 0.0, [f"qbd{i}"])
    base = [0, 0, 0, 0]

    def super_front(dirn, tau, gla, ctxs):
        bwd = dirn == 1
        is_ctx = tau < 2
        cs = slice(tau * 128, (tau + 1) * 128)
        C = 128 if gla else CHG
        nch = 128 // C
        mi = 1 if gla else 2
        kappa = -1.0 / 16.0 if gla else 1.0
        g0 = 0 if gla else 2
        c3 = lambda ap: ap.rearrange("p (n c) -> p n c", c=C)
        fl = lambda t: t[:].rearrange("p a b -> p (a b)")
        b0s = []
        for gg in range(2):
            g = g0 + gg
            b0s.append(base[g])
            base[g] = (base[g] + nch) % NSL[g]
        vt2, vkey = v2_r.next()
        voff = 512 if gla else 768
        P.dma(vt2[:], TM[cs, voff:voff + 256], reads=[("TM", tau)], writes=[vkey])
        lf, lfkey = w2[0].next()
        if gla:
            qk, qkkey = qk_r.next()
            P.dma(qk[:], FMB[12:20, :, cs].rearrange("n p t -> p n t"), reads=[("FMB", ft, _st(tau)) for ft in range(12, 20)],
                  writes=[qkkey])
            gkt, gkkey = gk_r.next()
            P.dma(gkt[:], FMF[4, 0:32, cs], reads=[("FMF", 4, _st(tau))], writes=[gkkey])
            yield
            zp, zkey = z_ps.next()
            e_, ekey = w2[1].next()
            for gg in range(2):
                P.mm(zp[:, gg * 128:(gg + 1) * 128], wgk[:, dirn, gg, :], gkt[:], ["wgk", gkkey], [zkey])
            for gg in range(2):
                P.act(e_[:, gg, :], zp[:, gg * 128:(gg + 1) * 128], AF.Exp, [zkey, "nbgk"], [ekey],
                      bias=nbgk[:, dirn * 2 + gg:dirn * 2 + gg + 1], scale=-1.0)
                yield
            P.act(fl(lf), fl(e_), AF.Ln, [ekey], [lfkey], bias=1.0)
            yield
            kk = kkkey = None
        else:
            qh, qhkey = qh_r.next()
            P.dma(qh[:], FMB[22:24, :, cs].rearrange("n p t -> p n t"), reads=[("FMB", 22, _st(tau)), ("FMB", 23, _st(tau))],
                  writes=[qhkey])
            fr, frkey = ff_r.next()
            fft = 2 if bwd else 0
            P.dma(fr[:], FMF[fft:fft + 2, :, cs].rearrange("n p t -> p n t"),
                  reads=[("FMF", fft, _st(tau)), ("FMF", fft + 1, _st(tau))], writes=[frkey])
            yield
            e_, ekey = w2[1].next()
            P.act(fl(e_), fl(fr), AF.Exp, [frkey], [ekey], scale=-1.0)
            yield
            P.act(fl(e_), fl(e_), AF.Ln, [ekey], [ekey], bias=1.0)
            yield
            P.act(fl(e_), fl(e_), AF.Exp, [ekey], [ekey], scale=-1.0)
            yield
            if l == 0:
                fgate, fkey = e_, ekey
            else:
                fgate, fkey = w2[2].next()
                for gg in range(2):
                    col = l * 4 + dirn * 2 + gg
                    P.ts("dve", fgate[:, gg, :], e_[:, gg, :], omlb[:, col:col + 1], lbt[:, col:col + 1], ALU.mult, ALU.add,
                         [ekey, "omlb", "lbt"], [fkey])
                yield
            P.act(fl(lf), fl(fgate), AF.Ln, [fkey], [lfkey])
            kk, kkkey = w2[3].next()
            P.act(fl(kk), fl(fgate), AF.Identity, [fkey], [kkkey], bias=1.0, scale=-1.0)
            yield
        Gc, Gkey = w2[4].next()
        m2 = scanmask2[:, mi, :]
        if bwd:
            P.op("dve", _scan(fl(Gc)[:, ::-1], m2, fl(lf)[:, ::-1]), [lfkey, "scanmask2"], [Gkey])
        else:
            P.op("dve", _scan(fl(Gc), m2, fl(lf)), [lfkey, "scanmask2"], [Gkey])
        yield
        Gv = c3(fl(Gc))
        n2 = 2 * nch
        mid = (C // 2) if bwd else (C // 2 - 1)
        lastp = 0 if bwd else C - 1
        sm, smkey = sm_r.next()
        P.tt("dve", sm[:, 1, 0:n2], Gv[:, :, lastp], Gv[:, :, mid], ALU.subtract, [Gkey], [smkey])
        Gm, Gmkey = w2[5].next()
        P.tt("dve", c3(fl(Gm)), Gv, Gv[:, :, mid:mid + 1].to_broadcast([128, n2, C]), ALU.subtract, [Gkey], [Gmkey])
        yield
        P.act(sm[:, 0, 0:n2], Gv[:, :, mid], AF.Exp, [Gkey], [smkey], scale=kappa)
        P.act(sm[:, 2, 0:n2], Gv[:, :, lastp], AF.Exp, [Gkey], [smkey], scale=kappa)
        yield
        P.act(sm[:, 1, 0:n2], sm[:, 1, 0:n2], AF.Exp, [smkey], [smkey], scale=kappa)
        A_, Akey = w2[6].next(); B_, Bkey = w2[7].next()
        P.act(fl(A_), fl(Gm), AF.Exp, [Gmkey], [Akey], scale=kappa)
        yield
        P.act(fl(B_), fl(Gm), AF.Exp, [Gmkey], [Bkey], scale=-kappa)
        yield
        Qf, Qfkey = w2[8].next()
        Kt, Ktkey = kt_r.next(); Kout, Koutkey = kout_r.next()
        if gla and not is_ctx:
            ts_ = slice((tau - 2) * 128, (tau - 1) * 128)
            cosb = ropecos[:, ts_].unsqueeze(1).to_broadcast([128, 2, 128])
            sinb = ropesin[:, ts_].unsqueeze(1).to_broadcast([128, 2, 128])
            r1, r1key = w2[2].next(); r2, r2key = w2[3].next()
            P.tt("dve", r1[:], qk[:, 0:2, :], cosb, ALU.mult, [qkkey, "ropecos"], [r1key])
            P.tt("pool", r2[:], qk[:, 2:4, :], sinb, ALU.mult, [qkkey, "ropesin"], [r2key])
            yield
            P.tt("dve", r1[:], r1[:], r2[:], ALU.add, [r1key, r2key], [r1key])
            r3, r3key = w2[2].next(); r4, r4key = w2[3].next()
            P.tt("pool", r3[:], qk[:, 4:6, :], cosb, ALU.mult, [qkkey, "ropecos"], [r3key])
            yield
            P.tt("dve", Qf[:], r1[:], A_[:], ALU.mult, [r1key, Akey], [Qfkey])
            P.tt("pool", r4[:], qk[:, 6:8, :], sinb, ALU.mult, [qkkey, "ropesin"], [r4key])
            yield
            P.tt("pool", r3[:], r3[:], r4[:], ALU.add, [r3key, r4key], [r3key])
            yield
            P.tt("pool", Kt[:], r3[:], B_[:], ALU.mult, [r3key, Bkey], [Ktkey])
            yield
        elif gla:
            P.tt("dve", Qf[:], qk[:, 0:2, :], A_[:], ALU.mult, [qkkey, Akey], [Qfkey])
            P.tt("pool", Kt[:], qk[:, 4:6, :], B_[:], ALU.mult, [qkkey, Bkey], [Ktkey])
            yield
        else:
            P.tt("dve", Qf[:], qh[:], A_[:], ALU.mult, [qhkey, Akey], [Qfkey])
            P.tt("pool", Kt[:], kk[:], B_[:], ALU.mult, [kkkey, Bkey], [Ktkey])
            yield
        Qbd, Qbdkey = qbd_r.next()
        P.cp("act", Qbd[0:64, :, 0, :], Qf[0:64, :, :], [Qfkey], [Qbdkey])
        yield
        P.cp("act", Qbd[64:128, :, 1, :], Qf[64:128, :, :], [Qfkey], [Qbdkey])
        Qin, Qinkey = qin_r.next()
        P.tt("dve" if gla else "pool", c3(fl(Qin)), c3(fl(Qf)), sm[:, 0, 0:n2].unsqueeze(2).to_broadcast([128, n2, C]), ALU.mult,
             [Qfkey, smkey], [Qinkey])
        P.tt("pool", c3(fl(Kout)), c3(fl(Kt)), sm[:, 1, 0:n2].unsqueeze(2).to_broadcast([128, n2, C]), ALU.mult,
             [Ktkey, smkey], [Koutkey])
        yield
        tpp, tpkey = tp_ps.next()
        for gg in range(2):
            P.tr(tpp[:, gg * 128:(gg + 1) * 128], Kout[:, gg, :], identb[:], [Koutkey, "identb"], [tpkey])
        koT, koTkey = koT_r.next()
        P.cp("act", koT[:], tpp[:, 0:256], [tpkey], [koTkey])
        yield
        scp, sckey = sc_ps.next()
        for gg in range(2):
            P.mm(scp[:, gg * 256:(gg + 1) * 256], Kt[:, gg, :], Qbd[:, gg, :, :].rearrange("p h t -> p (h t)"),
                 [Ktkey, Qbdkey], [sckey])
        PT, PTkey = pt_r.next()
        tmi = (2 if gla else 4) + (1 if bwd else 0)
        P.tt("dve", PT[:], scp[:, 0:512].rearrange("p (a t) -> p a t", a=4),
             trimask[:, tmi, :].unsqueeze(1).to_broadcast([128, 4, 128]), ALU.mult, [sckey, "trimask"], [PTkey])
        yield
        vx = vxkey = None
        if not gla:
            vx, vxkey = vexp_r.next()
            P.tt(VEXP_ENG, vx[:], vt2[:].rearrange("p (a d) -> p a d", a=4).unsqueeze(2).to_broadcast([128, 4, NHC, 64]),
                 cm[:].unsqueeze(1).unsqueeze(3).to_broadcast([128, 4, NHC, 64]), ALU.mult, [vkey, "cm"], [vxkey])
            yield
        for gg in range(2):
            g = g0 + gg
            ctxs[gg].update(dict(vt=vt2[:, gg * 128:(gg + 1) * 128], vkey=vkey, koT=koT[:, gg * 128:(gg + 1) * 128], koTkey=koTkey,
                                 vx=(vx[:, gg * 2:(gg + 1) * 2, :, :] if vx is not None else None), vxkey=vxkey,
                                 PT=PT[:, gg * 2:(gg + 1) * 2, :], PTkey=PTkey, Qin=Qin[:, gg, :], Qinkey=Qinkey,
                                 sm=sm[:, :, gg * nch:(gg + 1) * nch], smkey=smkey, b0=b0s[gg], nsl=NSL[g], C=C, nch=nch,
                                 gla=gla, gi=gg, bwd=bwd, is_ctx=is_ctx, cs=cs, g=g, tau=tau))

    def super_mid(ctxs):
        c0 = ctxs[0]
        gla = c0["gla"]; C = c0["C"]; nch = c0["nch"]; bwd = c0["bwd"]
        kvp, kvkey = kv_ps.next()
        W_ = 64 if gla else NHC * 64
        for c_ in ctxs:
            gg = c_["gi"]
            for h in range(2):
                if gla:
                    P.mm(kvp[h * 64:(h + 1) * 64, gg * W_:(gg + 1) * W_], c_["koT"][:, h * 64:(h + 1) * 64],
                         c_["vt"][:, h * 64:(h + 1) * 64], [c_["koTkey"], c_["vkey"]], [kvkey], tp=(0, h * 64))
                else:
                    P.mm(kvp[h * 64:(h + 1) * 64, gg * W_:(gg + 1) * W_], c_["koT"][:, h * 64:(h + 1) * 64],
                         c_["vx"][:, h, :, :].rearrange("p j d -> p (j d)"), [c_["koTkey"], c_["vxkey"]], [kvkey], tp=(0, h * 64))
            yield
        jorder = list(range(nch - 1, -1, -1)) if bwd else list(range(nch))
        for jj, j in enumerate(jorder):
            for c_ in ctxs:
                g = c_["g"]; gg = c_["gi"]; nsl = c_["nsl"]
                s_in = (c_["b0"] + jj) % nsl
                s_out = (c_["b0"] + jj + 1) % nsl
                P.stt("dve", S[g][:, s_out, :], S[g][:, s_in, :], c_["sm"][:, 2, j:j + 1],
                      kvp[:, gg * W_ + j * 64:gg * W_ + (j + 1) * 64], ALU.mult, ALU.add,
                      [f"S{g}_{s_in}", c_["smkey"], kvkey], [f"S{g}_{s_out}"])
            yield

    def super_out(ctxs):
        c0 = ctxs[0]
        gla = c0["gla"]; C = c0["C"]; nch = c0["nch"]; bwd = c0["bwd"]; is_ctx = c0["is_ctx"]; cs = c0["cs"]; tau = c0["tau"]
        g0 = c0["g"]
        op_, okey = o_ps.next()
        for c_ in ctxs:
            gg = c_["gi"]
            for h in range(2):
                P.mm(op_[h * 64:(h + 1) * 64, gg * 128:(gg + 1) * 128], c_["vt"][:, h * 64:(h + 1) * 64], c_["PT"][:, h, :],
                     [c_["vkey"], c_["PTkey"]], [okey], start=(gg == 0), stop=False, tp=(0, h * 64))
            yield
        jorder = list(range(nch - 1, -1, -1)) if bwd else list(range(nch))
        for jj, j in enumerate(jorder):
            for c_ in ctxs:
                g = c_["g"]; gg = c_["gi"]
                s_in = (c_["b0"] + jj) % c_["nsl"]
                for h in range(2):
                    P.mm(op_[h * 64:(h + 1) * 64, gg * 128 + j * C:gg * 128 + (j + 1) * C], S[g][h * 64:(h + 1) * 64, s_in, :],
                         c_["Qin"][h * 64:(h + 1) * 64, j * C:(j + 1) * C], [f"S{g}_{s_in}", c_["Qinkey"]], [okey],
                         start=False, stop=True, tp=(h * 64, h * 64))
            yield
        g2 = lambda d3: d3.rearrange("n p t -> p n t")
        if bwd:
            osb, oskey = o_r.next()
            P.cp("act", osb[:], op_[:, 0:256], [okey], [oskey])
            P.dma(g2(OB[g0:g0 + 2, :, cs]), osb[:].rearrange("p (n t) -> p n t", n=2), reads=[oskey],
                  writes=[("OB", g0, tau), ("OB", g0 + 1, tau)], eng="pool")
            yield
        elif not (is_ctx and not need_ctx):
            obt, obtkey = ob_r.next()
            P.dma(obt[:].rearrange("p (n t) -> p n t", n=2), g2(OB[g0:g0 + 2, :, cs]),
                  reads=[("OB", g0, tau), ("OB", g0 + 1, tau)], writes=[obtkey])
            gt_, gtkey = gate_r.next()
            gft = 20 if gla else 24
            P.dma(gt_[:].rearrange("p (n t) -> p n t", n=2), g2(FMB[gft:gft + 2, :, cs]),
                  reads=[("FMB", gft, _st(tau)), ("FMB", gft + 1, _st(tau))], writes=[gtkey])
            osb, oskey = o_r.next()
            P.tt("dve", osb[:], op_[:, 0:256], obt[:], ALU.add, [okey, obtkey], [oskey])
            yield
            osq, osqkey = sq_r.next()
            P.act(osq[:], osb[:], AF.Square, [oskey], [osqkey])
            yield
            zp, zkey = z_ps.next()
            P.mm(zp[:, 0:256], onesblk[:], osq[:], ["onesblk", osqkey], [zkey])
            rs, rskey = o_r.next()
            P.act(rs[:], zp[:, 0:256], AF.Ln, [zkey, "eps"], [rskey], bias=eps_t[:])
            yield
            P.act(rs[:], rs[:], AF.Exp, [rskey], [rskey], scale=-0.5)
            yield
            nw = G["glanw"] if gla else G["hgnw"]
            P.stt("dve", osb[:], osb[:], nw[:], rs[:], ALU.mult, ALU.mult, [oskey, rskey, "glanw", "hgnw"], [oskey])
            yield
            og, ogkey = og_r.next()
            P.tt("pool", og[:], osb[:], gt_[:], ALU.mult, [oskey, gtkey], [ogkey])
            P.dma(g2(OT[4 + g0:6 + g0, :, cs]), og[:].rearrange("p (n t) -> p n t", n=2), reads=[ogkey],
                  writes=[("OT", 4 + g0, tau), ("OT", 5 + g0, tau)], eng="pool")
            yield

    for dirn in DBG["p2_dirs"]:
        bwd = dirn == 1
        order = [1, 0] + list(range(NT - 1, 1, -1)) if bwd else list(range(NT))
        for g in range(4):
            base[g] = 0
            P.memset("dve", S[g][:, 0, :], 0.0, [f"S{g}_0"])
        if DBG["p2_tiles"]:
            order = order[:DBG["p2_tiles"]]
        mids = []
        outs = []
        mid_ctxs = []
        for tau in order:
            cts = [[dict(), dict()], [dict(), dict()]]
            fronts = [super_front(dirn, tau, False, cts[0]), super_front(dirn, tau, True, cts[1])]
            _interleave(outs + mids + fronts, [SLOW_STRIDE] * (len(outs) + len(mids)) + [1] * len(fronts))
            outs = [super_out(c2) for c2 in mid_ctxs]
            mids = [super_mid(c2) for c2 in cts]
            mid_ctxs = cts
        _interleave(outs + mids)
        _interleave([super_out(c2) for c2 in mid_ctxs])


def _st(tau):
    return 0 if tau < 2 else 2 + 4 * ((tau - 2) // 4)


def _scan(out, d0, d1):
    return lambda e: e.tensor_tensor_scan(out, d0, d1, 0.0, ALU.mult, ALU.add)


def build_p3(P, l, need_ctx, FMB, TM, OT, ttab, ones64):
    KTb = Ring(P, "KTb", 2, [128, NTOK], BF16)
    Vd = Ring(P, "Vd", 2, [128, 68, 64], BF16)
    q_r = Ring(P, "q3", 2, [128, 512], BF16); g_r = Ring(P, "g3", 2, [128, 512], BF16)
    pe_r = Ring(P, "pe3", 5, [128, 512], BF16); p_r = Ring(P, "p3", 5, [128, 512], BF16)
    rc_r = Ring(P, "rc3", 2, [128, 512]); o_r = Ring(P, "o3", 2, [128, 512]); og_r = Ring(P, "og3", 2, [128, 512], BF16)
    s_ps = Ring(P, "s_ps", 4, [128, 512], psum=True)
    ao_ps = Ring(P, "ao_ps", 2, [128, 512], psum=True); as_ps = Ring(P, "as_ps", 2, [128, 512], psum=True)
    TMv = TM.rearrange("(r k) c -> k r c", k=64)

    def row_interval(rp):
        if rp <= 7:
            return 0, rp + 4
        if rp >= 56:
            return rp - 3, 63
        return rp - 3, rp + 4

    for hp in range(4):
        kt_, ktkey = KTb.next()
        P.dma(kt_[:], FMB[4 + hp, :, :], reads=[("FMB", 4 + hp, s) for s in [0] + [2 + 4 * i for i in range(8)]], writes=[ktkey])
        vd, vdkey = Vd.next()
        for hh in range(2):
            for part in range(4):
                r0, r1 = part * 17, (part + 1) * 17
                P.dma(vd[hh * 64:(hh + 1) * 64, r0:r1, :], TMv[:, r0:r1, hp * 128 + hh * 64:hp * 128 + (hh + 1) * 64],
                      reads=[("TM", t_) for t_ in range(NT)], writes=[vdkey])
        blocks = [("lat", b) for b in range(8)] + ([("ctx", 0)] if need_ctx else [])
        for kind, b in blocks:
            if kind == "lat":
                tok0 = CTX + b * 512
                nq = 512
                pieces = [(rho, 0, 512, None) for rho in range(4)]
                for rp in range(64):
                    lo, hi = row_interval(rp)
                    a = max(lo, 8 * b); e_ = min(hi, 8 * b + 7)
                    if a > e_:
                        continue
                    u0 = 7 - rp + a
                    pieces.append((4 + rp, (a - 8 * b) * 64, (e_ - 8 * b + 1) * 64, u0))
            else:
                tok0 = 0
                nq = 256
                pieces = [(rho, 0, 256, None) for rho in range(4)]
            qt, qkey = q_r.next(); gt_, gkey = g_r.next()
            st_reads = sorted(set(_st(t_) for t_ in range(tok0 // 128, (tok0 + nq) // 128)))
            P.dma(qt[:, 0:nq], FMB[hp, :, tok0:tok0 + nq], reads=[("FMB", hp, s) for s in st_reads], writes=[qkey])
            P.dma(gt_[:, 0:nq], FMB[8 + hp, :, tok0:tok0 + nq], reads=[("FMB", 8 + hp, s) for s in st_reads], writes=[gkey])
            ao, aokey = ao_ps.next(); as_, askey = as_ps.next()
            staged = {}

            def stage_a(pi):
                rho, c0, c1, u0 = pieces[pi]
                sp_, spkey = s_ps.next()
                for hh in range(2):
                    P.mm(sp_[hh * 64:(hh + 1) * 64, c0:c1], kt_[hh * 64:(hh + 1) * 64, rho * 64:(rho + 1) * 64],
                         qt[hh * 64:(hh + 1) * 64, c0:c1], [ktkey, qkey], [spkey], tp=(hh * 64, hh * 64))
                pe_, pekey = pe_r.next()
                P.act(pe_[:, c0:c1], sp_[:, c0:c1], AF.Exp, [spkey], [pekey])
                if u0 is not None:
                    pp, ppkey = p_r.next()
                    nr = (c1 - c0) // 64
                    P.tt("dve", pp[:, c0:c1], pe_[:, c0:c1], ttab[:, hp, u0 * 64:(u0 + nr) * 64], ALU.mult,
                         [pekey, "ttab"], [ppkey])
                else:
                    pp, ppkey = pe_, pekey
                staged[pi] = (pp, ppkey)

            def stage_b(pi):
                rho, c0, c1, u0 = pieces[pi]
                pp, ppkey = staged.pop(pi)
                for hh in range(2):
                    P.mm(ao[hh * 64:(hh + 1) * 64, c0:c1], vd[hh * 64:(hh + 1) * 64, rho, :], pp[hh * 64:(hh + 1) * 64, c0:c1],
                         [vdkey, ppkey], [aokey], start=(pi == 0), stop=(pi == len(pieces) - 1), tp=(hh * 64, hh * 64))
                    P.mm(as_[hh * 64:(hh + 1) * 64, c0:c1], ones64[hh * 64:(hh + 1) * 64, :], pp[hh * 64:(hh + 1) * 64, c0:c1],
                         ["ones64", ppkey], [askey], start=(pi == 0), stop=(pi == len(pieces) - 1), tp=(hh * 64, hh * 64))

            LA = 3
            for pi in range(min(LA, len(pieces))):
                stage_a(pi)
            for pi in range(len(pieces)):
                if pi + LA < len(pieces):
                    stage_a(pi + LA)
                stage_b(pi)
            rc, rckey = rc_r.next()
            P.recip(rc[:, 0:nq], as_[:, 0:nq], [askey], [rckey])
            ob, obkey = o_r.next()
            P.tt("dve", ob[:, 0:nq], ao[:, 0:nq], rc[:, 0:nq], ALU.mult, [aokey, rckey], [obkey])
            og, ogkey = og_r.next()
            P.tt("pool", og[:, 0:nq], ob[:, 0:nq], gt_[:, 0:nq], ALU.mult, [obkey, gkey], [ogkey])
            P.dma(OT[hp, :, tok0:tok0 + nq], og[:, 0:nq], reads=[ogkey],
                  writes=[("OT", hp, t_) for t_ in range(tok0 // 128, (tok0 + nq) // 128)], eng="pool")


_CACHE = {}


def kernel(**inputs):
    inputs = {k: np.asarray(v) for k, v in inputs.items()}
    common, per = host_layout(inputs)
    if "nc" not in _CACHE:
        _CACHE["nc"] = build_program()
    nc = _CACHE["nc"]
    in_maps = [dict(common, **p) for p in per]
    res = run_bass_kernel_spmd(nc, in_maps, core_ids=list(range(len(per))))
    out = np.stack([np.asarray(r["out"], dtype=np.float32) for r in res.results], axis=0)
    return out
```
